# Optimizing a Trainium2 kernel written in Bass

```python
import math
import jax
import jax.numpy as jnp
from jax import lax

D_MODEL = 1024
BATCH = 16
SEQ = 2048
DEPTH = 2

NORM_EPS = 1e-6
GLA_HEADS = 4
GLA_DK = 64
GLA_DV = 128
GLA_QK = GLA_HEADS * GLA_DK
GLA_V = GLA_HEADS * GLA_DV
GLA_LOWRANK = 16
GLA_GATE_NORMALIZER = 16.0
GLA_CHUNK = 64
SGU_GROUPS = 4
SGU_GROUP_DIM = 128
SGU_DIM = SGU_GROUPS * SGU_GROUP_DIM
SGU_CHUNK = 128
AB_SPLIT = (GLA_QK, GLA_QK, GLA_V, GLA_V, GLA_LOWRANK, GLA_LOWRANK, SGU_DIM, SGU_DIM)
AB_IN_DIM = sum(AB_SPLIT)
AB_MIX_DIM = GLA_V + SGU_DIM
GDN_HEADS = 8
GDN_DK = 128
GDN_DV = 128
GDN_QK = GDN_HEADS * GDN_DK
GDN_V = GDN_HEADS * GDN_DV
GDN_CONV = 3
GDN_CHUNK = 64
GDN_CONV_DIM = 2 * GDN_QK + GDN_V
GDN_SPLIT = (GDN_CONV_DIM, GDN_V, GDN_HEADS, GDN_HEADS, GDN_HEADS, GDN_HEADS)
GDN_IN_DIM = sum(GDN_SPLIT)
FFN_DIM = 2816
FFN_CONV = 3
N_EVEN = (DEPTH + 1) // 2
N_ODD = DEPTH // 2

kernel_name = "hybrid_gla_sgu_gdn_encoder"


def rms_norm(x, gain):
    xf = x.astype(jnp.float32)
    y = xf * lax.rsqrt(jnp.mean(xf * xf, axis=-1, keepdims=True) + NORM_EPS)
    return (y * gain.astype(jnp.float32)).astype(x.dtype)


def layer_norm(x, gain, bias):
    xf = x.astype(jnp.float32)
    mu = jnp.mean(xf, axis=-1, keepdims=True)
    var = jnp.mean(jnp.square(xf - mu), axis=-1, keepdims=True)
    y = (xf - mu) * lax.rsqrt(var + NORM_EPS)
    return y * gain.astype(jnp.float32) + bias.astype(jnp.float32)


def l2_normalize(x):
    return x * lax.rsqrt(jnp.sum(x * x, axis=-1, keepdims=True) + NORM_EPS)


def split_cols(t, sizes):
    out, start = [], 0
    for s in sizes:
        out.append(t[..., start:start + s])
        start += s
    return out


def depthwise_conv(x, w):
    c = x.shape[-1]
    return lax.conv_general_dilated(
        x, w[:, None, :].astype(x.dtype), window_strides=(1,), padding='SAME',
        dimension_numbers=('NWC', 'WIO', 'NWC'), feature_group_count=c)


def gla_scan(q, k, v, log_a):
    b_, s_, h_, dk = q.shape
    dv = v.shape[-1]
    c = GLA_CHUNK
    n = s_ // c
    q, k, v, log_a = (t.reshape(b_, n, c, h_, t.shape[-1]) for t in (q, k, v, log_a))
    cum = jnp.cumsum(log_a, axis=2)
    last = cum[:, :, -1:]
    q_dec = q * jnp.exp(cum)
    k_inv = k * jnp.exp(-cum)
    k_end = k * jnp.exp(last - cum)
    incl_lower = jnp.tril(jnp.ones((c, c), dtype=bool))
    scores = jnp.einsum('bnihd,bnjhd->bnhij', q_dec, k_inv)
    scores = jnp.where(incl_lower, scores, 0.0)
    o_intra = jnp.einsum('bnhij,bnjhv->bnihv', scores, v)
    chunk_kv = jnp.einsum('bnjhd,bnjhv->nbhdv', k_end, v)
    chunk_decay = jnp.exp(last[:, :, 0]).transpose(1, 0, 2, 3)

    def step(state, inp):
        decay, kv = inp
        return state * decay[..., None] + kv, state

    _, states = lax.scan(step, jnp.zeros((b_, h_, dk, dv), q.dtype), (chunk_decay, chunk_kv))
    o_inter = jnp.einsum('bnihd,nbhdv->bnihv', q_dec, states)
    return (o_intra + o_inter).reshape(b_, s_, h_, dv)


def gated_delta_scan(q, k, v, beta, g):
    b_, s_, h_, dk = q.shape
    dv = v.shape[-1]
    c = GDN_CHUNK
    n = s_ // c
    q, k, v = (t.reshape(b_, n, c, h_, t.shape[-1]) for t in (q, k, v))
    beta = beta.reshape(b_, n, c, h_)
    cum = jnp.cumsum(g.reshape(b_, n, c, h_), axis=2)
    cum_h = cum.transpose(0, 1, 3, 2)
    incl_lower = jnp.tril(jnp.ones((c, c), dtype=bool))
    strict_lower = jnp.tril(jnp.ones((c, c), dtype=bool), -1)
    diff = cum_h[..., :, None] - cum_h[..., None, :]
    decay = jnp.exp(jnp.where(incl_lower, diff, -jnp.inf))
    kb = k * beta[..., None]
    a_kk = jnp.einsum('bnihd,bnjhd->bnhij', kb, k) * decay
    tmat = jnp.where(strict_lower, a_kk, 0.0) + jnp.eye(c, dtype=q.dtype)
    rhs = jnp.concatenate([v * beta[..., None], kb * jnp.exp(cum)[..., None]], axis=-1)
    rhs = rhs.transpose(0, 1, 3, 2, 4)
    sol = lax.linalg.triangular_solve(tmat, rhs, left_side=True, lower=True, unit_diagonal=True)
    u = sol[..., :dv]
    w = sol[..., dv:]
    a_qk = jnp.einsum('bnihd,bnjhd->bnhij', q, k) * decay
    q_dec = (q * jnp.exp(cum)[..., None]).transpose(0, 1, 3, 2, 4)
    k_end = (k * jnp.exp(cum[:, :, -1:] - cum)[..., None]).transpose(0, 1, 3, 2, 4)
    chunk_decay = jnp.exp(cum[:, :, -1])
    xs = tuple(jnp.moveaxis(t, 1, 0) for t in (u, w, q_dec, a_qk, k_end, chunk_decay))

    def step(state, inp):
        u_c, w_c, qd_c, aqk_c, ke_c, dec_c = inp
        v_new = u_c - jnp.einsum('bhid,bhdv->bhiv', w_c, state)
        o = jnp.einsum('bhid,bhdv->bhiv', qd_c, state) + jnp.einsum('bhij,bhjv->bhiv', aqk_c, v_new)
        state = state * dec_c[..., None, None] + jnp.einsum('bhjd,bhjv->bhdv', ke_c, v_new)
        return state, o

    _, o = lax.scan(step, jnp.zeros((b_, h_, dk, dv), q.dtype), xs)
    return o.transpose(1, 0, 3, 2, 4).reshape(b_, s_, h_, dv)


def gla_sgu_mixer(h, w_in, w_gate_fwd, b_gate_fwd, w_gate_bwd, b_gate_bwd, gla_norm,
                  sgu_ln_g, sgu_ln_b, sgu_w_s, sgu_b_s, w_out):
    f32 = jnp.float32
    b_, s_, _ = h.shape
    q, k, v, gate, lr_f, lr_b, su, sv = split_cols(h @ w_in, AB_SPLIT)
    qh = (q.astype(f32) * GLA_DK ** -0.5).reshape(b_, s_, GLA_HEADS, GLA_DK)
    kh = k.astype(f32).reshape(b_, s_, GLA_HEADS, GLA_DK)
    vh = v.astype(f32).reshape(b_, s_, GLA_HEADS, GLA_DV)

    def log_decay(lr, w, b):
        z = lr.astype(f32) @ w.astype(f32) + b.astype(f32)
        return (jax.nn.log_sigmoid(z) / GLA_GATE_NORMALIZER).reshape(b_, s_, GLA_HEADS, GLA_DK)

    o_f = gla_scan(qh, kh, vh, log_decay(lr_f, w_gate_fwd, b_gate_fwd))
    o_b = jnp.flip(gla_scan(jnp.flip(qh, 1), jnp.flip(kh, 1), jnp.flip(vh, 1),
                            jnp.flip(log_decay(lr_b, w_gate_bwd, b_gate_bwd), 1)), 1)
    o_a = rms_norm(o_f + o_b, gla_norm).reshape(b_, s_, GLA_V) * jax.nn.silu(gate.astype(f32))
    u = jax.nn.gelu(su.astype(f32))
    vv = layer_norm(jax.nn.gelu(sv.astype(f32)), sgu_ln_g, sgu_ln_b)
    vv = vv.reshape(b_, s_ // SGU_CHUNK, SGU_CHUNK, SGU_GROUPS, SGU_GROUP_DIM)
    mixed = jnp.einsum('gij,bnjgc->bnigc', sgu_w_s.astype(f32), vv) \
        + sgu_b_s.astype(f32).T[None, None, :, :, None]
    o_b_mix = u * mixed.reshape(b_, s_, SGU_DIM)
    return jnp.concatenate([o_a, o_b_mix], axis=-1).astype(h.dtype) @ w_out


def gdn_mixer(h, w_in, conv_w, a_log_fwd, dt_bias_fwd, a_log_bwd, dt_bias_bwd, norm_g, w_out):
    f32 = jnp.float32
    b_, s_, _ = h.shape
    qkv, z, beta_f, beta_b, a_f, a_b = split_cols(h @ w_in, GDN_SPLIT)
    qkv = jax.nn.silu(depthwise_conv(qkv, conv_w)).astype(f32)
    q, k, v = split_cols(qkv, (GDN_QK, GDN_QK, GDN_V))
    q = l2_normalize(q.reshape(b_, s_, GDN_HEADS, GDN_DK)) * GDN_DK ** -0.5
    k = l2_normalize(k.reshape(b_, s_, GDN_HEADS, GDN_DK))
    v = v.reshape(b_, s_, GDN_HEADS, GDN_DV)

    def log_decay(a, a_log, dt_bias):
        return -jnp.exp(a_log.astype(f32)) * jax.nn.softplus(a.astype(f32) + dt_bias.astype(f32))

    o_f = gated_delta_scan(q, k, v, jax.nn.sigmoid(beta_f.astype(f32)),
                           log_decay(a_f, a_log_fwd, dt_bias_fwd))
    o_b = jnp.flip(gated_delta_scan(jnp.flip(q, 1), jnp.flip(k, 1), jnp.flip(v, 1),
                                    jnp.flip(jax.nn.sigmoid(beta_b.astype(f32)), 1),
                                    jnp.flip(log_decay(a_b, a_log_bwd, dt_bias_bwd), 1)), 1)
    zg = jax.nn.silu(z.astype(f32).reshape(b_, s_, GDN_HEADS, GDN_DV))
    o = rms_norm(o_f + o_b, norm_g) * zg
    return o.reshape(b_, s_, GDN_V).astype(h.dtype) @ w_out


def conv_glu_ffn(h, w_up, conv_w, conv_b, w_down):
    gate, up = split_cols(h @ w_up, (FFN_DIM, FFN_DIM))
    gate = depthwise_conv(gate, conv_w) + conv_b.astype(gate.dtype)
    return (jax.nn.silu(gate) * up) @ w_down


def setup_inputs(seed: int = 0) -> dict:
    key = jax.random.key(seed)
    keys = list(jax.random.split(key, 32))

    def normal(shape, scale):
        return jax.random.normal(keys.pop(), shape, jnp.float32) * scale

    def gain(shape):
        return 1.0 + normal(shape, 0.02)

    def a_log():
        return jnp.log(jax.random.uniform(keys.pop(), (N_ODD, GDN_HEADS), jnp.float32, 1.0, 16.0))

    def dt_bias():
        dt = jnp.exp(jax.random.uniform(keys.pop(), (N_ODD, GDN_HEADS), jnp.float32,
                                        math.log(1e-3), math.log(1e-1)))
        return jnp.log(jnp.expm1(dt))

    return {
        'x': normal((BATCH, SEQ, D_MODEL), 1.0),
        'norm_mix': gain((DEPTH, D_MODEL)),
        'norm_ffn': gain((DEPTH, D_MODEL)),
        'norm_final': gain((D_MODEL,)),
        'ab_w_in': normal((N_EVEN, D_MODEL, AB_IN_DIM), D_MODEL ** -0.5),
        'gla_w_gate_fwd': normal((N_EVEN, GLA_LOWRANK, GLA_QK), GLA_LOWRANK ** -0.5),
        'gla_b_gate_fwd': normal((N_EVEN, GLA_QK), 0.1),
        'gla_w_gate_bwd': normal((N_EVEN, GLA_LOWRANK, GLA_QK), GLA_LOWRANK ** -0.5),
        'gla_b_gate_bwd': normal((N_EVEN, GLA_QK), 0.1),
        'gla_norm': gain((N_EVEN, GLA_DV)),
        'sgu_ln_g': gain((N_EVEN, SGU_DIM)),
        'sgu_ln_b': normal((N_EVEN, SGU_DIM), 0.02),
        'sgu_w_s': normal((N_EVEN, SGU_GROUPS, SGU_CHUNK, SGU_CHUNK), SGU_CHUNK ** -0.5),
        'sgu_b_s': gain((N_EVEN, SGU_GROUPS, SGU_CHUNK)),
        'ab_w_out': normal((N_EVEN, AB_MIX_DIM, D_MODEL), AB_MIX_DIM ** -0.5),
        'gdn_w_in': normal((N_ODD, D_MODEL, GDN_IN_DIM), D_MODEL ** -0.5),
        'gdn_conv_w': normal((N_ODD, GDN_CONV, GDN_CONV_DIM), GDN_CONV ** -0.5),
        'gdn_a_log_fwd': a_log(),
        'gdn_dt_bias_fwd': dt_bias(),
        'gdn_a_log_bwd': a_log(),
        'gdn_dt_bias_bwd': dt_bias(),
        'gdn_norm': gain((N_ODD, GDN_DV)),
        'gdn_w_out': normal((N_ODD, GDN_V, D_MODEL), GDN_V ** -0.5),
        'ffn_w_up': normal((DEPTH, D_MODEL, 2 * FFN_DIM), D_MODEL ** -0.5),
        'ffn_conv_w': normal((DEPTH, FFN_CONV, FFN_DIM), FFN_CONV ** -0.5),
        'ffn_conv_b': normal((DEPTH, FFN_DIM), 0.02),
        'ffn_w_down': normal((DEPTH, FFN_DIM, D_MODEL), FFN_DIM ** -0.5),
    }


def reference(x, norm_mix, norm_ffn, norm_final, ab_w_in, gla_w_gate_fwd, gla_b_gate_fwd,
              gla_w_gate_bwd, gla_b_gate_bwd, gla_norm, sgu_ln_g, sgu_ln_b, sgu_w_s, sgu_b_s,
              ab_w_out, gdn_w_in, gdn_conv_w, gdn_a_log_fwd, gdn_dt_bias_fwd, gdn_a_log_bwd,
              gdn_dt_bias_bwd, gdn_norm, gdn_w_out, ffn_w_up, ffn_conv_w, ffn_conv_b, ffn_w_down):
    h = x
    for layer in range(DEPTH):
        i = layer // 2
        hn = rms_norm(h, norm_mix[layer])
        if layer % 2 == 0:
            mix = gla_sgu_mixer(hn, ab_w_in[i], gla_w_gate_fwd[i], gla_b_gate_fwd[i],
                                gla_w_gate_bwd[i], gla_b_gate_bwd[i], gla_norm[i],
                                sgu_ln_g[i], sgu_ln_b[i], sgu_w_s[i], sgu_b_s[i], ab_w_out[i])
        else:
            mix = gdn_mixer(hn, gdn_w_in[i], gdn_conv_w[i], gdn_a_log_fwd[i], gdn_dt_bias_fwd[i],
                            gdn_a_log_bwd[i], gdn_dt_bias_bwd[i], gdn_norm[i], gdn_w_out[i])
        h = h + mix
        hn = rms_norm(h, norm_ffn[layer])
        h = h + conv_glu_ffn(hn, ffn_w_up[layer], ffn_conv_w[layer], ffn_conv_b[layer], ffn_w_down[layer])
    return rms_norm(h, norm_final)
```

```python
import numpy as np
import concourse.bass as bass
import concourse.mybir as mybir
from concourse.bass_utils import run_bass_kernel_spmd

F32 = mybir.dt.float32
BF16 = mybir.dt.bfloat16
AF = mybir.ActivationFunctionType
ALU = mybir.AluOpType
AX = mybir.AxisListType

D = 1024
S = 2048
NT = S // 128
FF = 2816
NCB = FF // 128
NCONST = 12
HPG = 2
EPS = 1e-6
N_CORES = 8


class Buf:
    __slots__ = ("name", "w", "r", "wsem", "wcnt", "rsem", "rcnt", "excl")

    def __init__(self, name):
        self.name = name
        self.excl = False
        self.w = None
        self.r = []
        self.wsem = None
        self.wcnt = 0
        self.rsem = None
        self.rcnt = 0


class Op:
    __slots__ = ("eng", "fn", "deps", "dma", "event", "src", "dst", "id")


class Prog:
    MAXV = 60000
    MAXD = 56000

    def __init__(self, nc):
        self.nc = nc
        self.engs = {"pe": nc.tensor, "act": nc.scalar, "dve": nc.vector,
                     "pool": nc.gpsimd, "sp": nc.sync}
        self.ops = []
        self.bufs = {}
        self.last = {}
        self.dma_out = []
        self.nsem = 0

    def buf(self, *key):
        b = self.bufs.get(key)
        if b is None:
            b = Buf(key)
            self.bufs[key] = b
        return b

    def _deps(self, o, r, w, is_dma):
        deps = set()
        for b in r:
            if b.w is not None:
                deps.add(b.w)
            if b.excl:
                for x in b.r:
                    if self.ops[x].eng != o.eng:
                        deps.add(x)
        for b in w:
            if b.w is not None:
                deps.add(b.w)
            last = {}
            for x in b.r:
                ox = self.ops[x]
                if ox.dma:
                    deps.add(x)
                else:
                    last[ox.eng] = max(last.get(ox.eng, -1), x)
            deps.update(last.values())
        if not is_dma:
            raw = set(b.w for b in r if b.w is not None)
            deps = set(d for d in deps
                       if d in raw or self.ops[d].eng != o.eng or self.ops[d].dma)
        o.deps = deps
        for b in r:
            b.r.append(o.id)
        for b in w:
            b.w = o.id
            b.r = []

    def op(self, eng, fn, r=(), w=()):
        o = Op()
        o.id = len(self.ops)
        o.eng = eng
        o.fn = fn
        o.dma = False
        o.event = None
        o.src = o.dst = None
        self._deps(o, r, w, False)
        self.ops.append(o)
        self.last[eng] = o.id
        return o.id

    def dma(self, q, out, in_, src=None, dst=None):
        o = Op()
        o.id = len(self.ops)
        o.eng = q
        o.fn = lambda e, out=out, in_=in_: e.dma_start(out=out, in_=in_)
        o.dma = True
        o.event = None
        o.src = src
        o.dst = dst
        self._deps(o, [src] if src is not None else [], [dst] if dst is not None else [], True)
        self.ops.append(o)
        self.dma_out.append(o.id)
        return o.id

    def barrier(self):
        deps = set(self.last.values()) | set(self.dma_out)
        for eng in ("pe", "act", "dve", "pool", "sp"):
            o = Op()
            o.id = len(self.ops)
            o.eng = eng
            o.fn = None
            o.dma = False
            o.event = None
            o.src = o.dst = None
            o.deps = set(deps)
            self.ops.append(o)
        self.dma_out = []
        for b in self.bufs.values():
            b.w = None
            b.r = []

    def _newsem(self):
        self.nsem += 1
        return self.nc.alloc_semaphore("s%d" % self.nsem)

    def emit(self):
        nc = self.nc
        has_dep = set()
        for o in self.ops:
            has_dep |= o.deps
        cur = {}
        cnt = {}
        known = {e: {} for e in self.engs}
        for o in self.ops:
            E = self.engs[o.eng]
            waits = {}
            for d in o.deps:
                ev = self.ops[d].event
                if ev is None:
                    continue
                s, v = ev
                k = id(s)
                if known[o.eng].get(k, 0) >= v:
                    continue
                if k not in waits or waits[k][1] < v:
                    waits[k] = (s, v)
            wl = list(waits.values())
            for s, v in wl:
                known[o.eng][id(s)] = v
            if o.fn is None:
                for s, v in wl:
                    E.wait_ge(s, v)
                continue
            for s, v in wl[:-1]:
                E.wait_ge(s, v)
            ins = o.fn(E)
            if wl:
                ins._wait_ge(wl[-1][0], wl[-1][1])
            if o.dma:
                b = o.dst if o.dst is not None else o.src
                if o.dst is not None:
                    if b.wsem is None or b.wcnt >= self.MAXD:
                        b.wsem = self._newsem()
                        b.wcnt = 0
                    b.wcnt += 16
                    ins.then_inc(b.wsem, 16)
                    o.event = (b.wsem, b.wcnt)
                else:
                    if b.rsem is None or b.rcnt >= self.MAXD:
                        b.rsem = self._newsem()
                        b.rcnt = 0
                    b.rcnt += 16
                    ins.then_inc(b.rsem, 16)
                    o.event = (b.rsem, b.rcnt)
            elif o.id in has_dep:
                if o.eng not in cur or cnt[o.eng] >= self.MAXV:
                    cur[o.eng] = self._newsem()
                    cnt[o.eng] = 0
                cnt[o.eng] += 1
                ins.then_inc(cur[o.eng], 1)
                o.event = (cur[o.eng], cnt[o.eng])


class Alloc:
    def __init__(self, nc, lo, hi):
        self.nc = nc
        self.lo = lo
        self.hi = hi
        self.p = lo
        self.n = 0

    def reset(self):
        self.p = self.lo

    def __call__(self, shape, dtype, name=None):
        nb = 4 if dtype == F32 else 2
        sz = nb
        for s in shape[1:]:
            sz *= s
        sz = (sz + 63) // 64 * 64
        off = self.p
        self.p += sz
        assert self.p <= self.hi, "SBUF overflow %s %d > %d" % (name, self.p, self.hi)
        self.n += 1
        return self.nc.alloc_sbuf_tensor_at("%s_%d" % (name or "t", self.n), list(shape), dtype,
                                            offset=off)


def rsqrt_from_psum(P, out_ap, ps_ap, ps_buf, out_buf, eps=EPS, scale=1.0):
    P.op("act", lambda e: e.activation(out=out_ap, in_=ps_ap, func=AF.Sqrt, bias=eps, scale=scale),
         r=[ps_buf], w=[out_buf])
    P.op("dve", lambda e: e.reciprocal(out=out_ap, in_=out_ap), r=[out_buf], w=[out_buf])


class Ctx:
    pass


def build_program(nseq, stages="all", debug=False):
    nc = bass.Bass("TRN2", target_bir_lowering=False)
    P = Prog(nc)
    C = Ctx()
    C.nc, C.P = nc, P
    C.debug = debug
    import os
    C.gdn_stop = int(os.environ.get('GDN_STOP', '9'))
    C.gdn_sub = int(os.environ.get('GDN_SUB', '99'))
    C.x = nc.dram_tensor("x", [nseq, S, D], F32, kind="ExternalInput").ap()
    C.out = nc.dram_tensor("out", [nseq, S, D], F32, kind="ExternalOutput").ap()
    vecs = nc.dram_tensor("vecs", [128, NVEC], F32, kind="ExternalInput").ap()
    C.wup_d = nc.dram_tensor("ffn_wup", [2, 2 * NCB, 128, 8, 128], F32, kind="ExternalInput").ap()
    C.wdn_d = nc.dram_tensor("ffn_wdn", [2, NCB, 128, D], F32, kind="ExternalInput").ap()
    consts_d = nc.dram_tensor("consts", [128, NCONST * 128], F32, kind="ExternalInput").ap()
    C.ab_win_d = nc.dram_tensor("ab_win", [128, 8, 2592], F32, kind="ExternalInput").ap()
    C.ab_wout_d = nc.dram_tensor("ab_wout", [128, 8, 1024], F32, kind="ExternalInput").ap()
    C.gla_wg_d = nc.dram_tensor("gla_wg", [2, 32, 256], F32, kind="ExternalInput").ap()
    C.sgu_wsT_d = nc.dram_tensor("sgu_wsT", [128, 4, 128], F32, kind="ExternalInput").ap()
    C.sgu_ln_d = nc.dram_tensor("sgu_ln", [2, 512], F32, kind="ExternalInput").ap()
    C.sgu_bs_d = nc.dram_tensor("sgu_bs", [1, 512], F32, kind="ExternalInput").ap()
    C.gdn_win_d = nc.dram_tensor("gdn_win", [32, 128, 8, 128], F32, kind="ExternalInput").ap()
    C.gdn_wsm_d = nc.dram_tensor("gdn_wsm", [128, 8, 32], F32, kind="ExternalInput").ap()
    C.gdn_wout_d = nc.dram_tensor("gdn_wout", [128, 8, 1024], F32, kind="ExternalInput").ap()
    C.gdn_vec_d = nc.dram_tensor("gdn_vec", [1, 32], F32, kind="ExternalInput").ap()

    persist = Alloc(nc, 16 * 1024, 96 * 1024)

    C.ps = [nc.alloc_psum_tensor("ps%d" % i, [128, 512], F32) for i in range(8)]
    C.psb = [P.buf("ps", i) for i in range(8)]
    for b in C.psb:
        b.excl = True
    C.psrr = 0
    C.ps_res = set()

    C.hT = persist([128, 8, S], F32, "hT")
    C.consts = persist([128, NCONST * 128], F32, "consts")
    C.ident = C.consts[:, 0:128]
    C.maskf = C.consts[:, 128:256]
    C.maskb = C.consts[:, 256:384]
    C.tri = [C.consts[:, 384:512], C.consts[:, 512:640]]
    C.mask01 = [C.maskf, C.maskb]
    C.mneg_incl = [C.consts[:, 640:768], C.consts[:, 896:1024]]
    C.mneg_strict = [C.consts[:, 768:896], C.consts[:, 1024:1152]]
    C.esel = [C.consts[:, 1152:1280], C.consts[:, 1280:1408]]
    C.onesf = C.consts[:, 1408:1536]
    C.ones1 = persist([128, 128], BF16, "ones1")
    C.identb = persist([128, 128], BF16, "identb")
    C.onesD = persist([128, 128], BF16, "onesD")
    C.ones128 = persist([128, 128], BF16, "ones128")
    C.vec = persist([128, NVEC], F32, "vecs")
    C.bconst = P.buf("const")
    C.phase = Alloc(nc, persist.p, 224 * 1024 - 256)

    P.dma("sp", C.consts[:], consts_d[:, :], dst=C.bconst)
    P.dma("sp", C.vec[:], vecs[:, :], dst=C.bconst)
    P.op("dve", lambda e: e.tensor_copy(out=C.identb[:], in_=C.ident), r=[C.bconst], w=[P.buf("identb")])
    P.op("dve", lambda e: e.memset(C.onesD[:], 1.0 / D), w=[P.buf("ones")])
    P.op("dve", lambda e: e.memset(C.ones128[:], 1.0 / 128), w=[P.buf("ones")])
    P.op("dve", lambda e: e.memset(C.ones1[:], 1.0), w=[P.buf("ones")])

    for sq in range(nseq):
        load_x(C, sq)
        P.barrier()
        C.phase.reset()
        if stages in ("mix0", "all"):
            mixer0(C)
            P.barrier()
            C.phase.reset()
        if stages in ("ffn0", "all"):
            ffn(C, 0)
            P.barrier()
            C.phase.reset()
        if stages in ("mix1", "all"):
            mixer1(C)
            P.barrier()
            C.phase.reset()
        if stages in ("ffn1", "all"):
            ffn(C, 1)
            P.barrier()
            C.phase.reset()
        final_norm_store(C, sq)
        P.barrier()
        C.phase.reset()

    P.barrier()
    P.emit()
    return nc


def dump(C, name, t, buf_list):
    if not C.debug:
        return
    shape = list(t.shape)
    d = C.nc.dram_tensor("dbg_" + name, shape, t.dtype, kind="ExternalOutput").ap()
    C.P.barrier()
    C.P.dma("sp", d, t[:], src=None, dst=C.P.buf("dbg", name))
    C.P.barrier()


def next_ps(C):
    while True:
        i = C.psrr % 8
        C.psrr += 1
        if i not in C.ps_res:
            return i


def reserve_ps(C, n):
    r = []
    for _ in range(n):
        i = next_ps(C)
        C.ps_res.add(i)
        r.append(i)
    return r


def release_ps(C, banks):
    for i in banks:
        C.ps_res.discard(i)


def hbuf(C, kb, mt):
    return C.P.buf("hT", kb, mt)


def vcol(C, name, j=0, n=1):
    o = VLAY[name][0] + j
    return C.vec[:, o:o + n]


def load_x(C, sq):
    P, ps, psb = C.P, C.ps, C.psb
    xs = [C.phase([128, D], F32, "xin%d" % i) for i in range(2)]
    for tt in range(NT):
        xb = P.buf("xin", tt % 2)
        xt = xs[tt % 2]
        P.dma("sp", xt[:], C.x[sq, tt * 128:(tt + 1) * 128, :], dst=xb)
        for half in range(2):
            pi = next_ps(C)
            for j in range(4):
                kb = half * 4 + j
                P.op("pe", lambda e, pi=pi, j=j, kb=kb, xt=xt: e.matmul(
                    ps[pi][:, j * 128:(j + 1) * 128], xt[:, kb * 128:(kb + 1) * 128], C.ident,
                    start=True, stop=True), r=[xb, C.bconst], w=[psb[pi]])
            mt = tt // 4
            P.op("act", lambda e, pi=pi, half=half, tt=tt: e.activation(
                out=C.hT[:, half * 4:half * 4 + 4, tt * 128:(tt + 1) * 128],
                in_=ps[pi][:].rearrange("p (j t) -> p j t", j=4), func=AF.Copy),
                r=[psb[pi]], w=[hbuf(C, half * 4 + j, mt) for j in range(4)])


def rms_rstd(C, mt, sqt, rt, rb):
    P, ps, psb = C.P, C.ps, C.psb
    tsl = slice(mt * 512, (mt + 1) * 512)
    pi = next_ps(C)
    for kb in range(8):
        sb = P.buf("sqt", kb % 2)
        st = sqt[kb % 2]
        P.op("act", lambda e, st=st, kb=kb: e.activation(out=st[:], in_=C.hT[:, kb, tsl], func=AF.Square),
             r=[hbuf(C, kb, mt)], w=[sb])
        P.op("pe", lambda e, st=st, kb=kb: e.matmul(
            ps[pi][:], C.onesD[:], st[:], start=(kb == 0), stop=(kb == 7)),
            r=[sb, P.buf("ones")], w=[psb[pi]])
    rsqrt_from_psum(P, rt[:], ps[pi][:], psb[pi], rb)


def rmsnorm_to_bf16(C, gname, hn, hnb):
    P = C.P
    sqt = [C.phase([128, 512], BF16, "sq%d" % i) for i in range(2)]
    rstd = [C.phase([128, 512], F32, "rstd%d" % i) for i in range(2)]
    for mt in range(4):
        tsl = slice(mt * 512, (mt + 1) * 512)
        rb = P.buf("rstd", mt % 2)
        rt = rstd[mt % 2]
        rms_rstd(C, mt, sqt, rt, rb)
        for kb in range(8):
            P.op("dve", lambda e, kb=kb, rt=rt, tsl=tsl: e.scalar_tensor_tensor(
                out=hn[:, kb, tsl], in0=C.hT[:, kb, tsl], scalar=vcol(C, gname, kb),
                in1=rt[:], op0=ALU.mult, op1=ALU.mult), r=[hbuf(C, kb, mt), rb, C.bconst], w=[hnb(kb, mt)])


def ffn(C, l):
    P, ps, psb = C.P, C.ps, C.psb
    ph = C.phase
    hn = ph([128, 8, S], BF16, "hn")
    hnb = lambda kb, mt: P.buf("hn", kb, mt)
    rmsnorm_to_bf16(C, "norm_ffn%d" % l, hn, hnb)
    GRP = 8
    act = [ph([128, S], BF16, "act%d" % i) for i in range(GRP)]
    wd = [ph([128, D], BF16, "wd%d" % i) for i in range(2 * GRP)]
    wg = [ph([128, 8, 128], BF16, "wg%d" % i) for i in range(2)]
    wu = [ph([128, 8, 128], BF16, "wu%d" % i) for i in range(2)]
    G = ph([128, S + 2], F32, "gbuf")
    cv = [ph([128, 512], F32, "cv%d" % i) for i in range(2)]
    sg = [ph([128, 512], F32, "sg%d" % i) for i in range(2)]
    gb = lambda mt: P.buf("gbuf", mt)
    gedge = P.buf("gbuf_edge")
    P.op("dve", lambda e: e.memset(G[:, 0:1], 0.0), w=[gedge])
    P.op("dve", lambda e: e.memset(G[:, S + 1:S + 2], 0.0), w=[gedge])
    cwo = VLAY["ffn_conv_w%d" % l][0]
    cbo = VLAY["ffn_conv_b%d" % l][0]
    groups = []
    c0 = 0
    while c0 < NCB:
        groups.append(list(range(c0, min(c0 + GRP, NCB))))
        c0 += GRP
    wdslot = 0
    for grp in groups:
        slots = {}
        for cb in grp:
            a_t = act[cb % GRP]
            ab = lambda mt, cb=cb: P.buf("act", cb % GRP, mt)
            wgt, wut = wg[cb % 2], wu[cb % 2]
            wgb, wub = P.buf("wg", cb % 2), P.buf("wu", cb % 2)
            ws = wdslot % (2 * GRP)
            wdslot += 1
            slots[cb] = ws
            wdb = P.buf("wd", ws)
            P.dma("pool", wgt[:], C.wup_d[l, cb], dst=wgb)
            P.dma("pool", wut[:], C.wup_d[l, NCB + cb], dst=wub)
            P.dma("pool", wd[ws][:], C.wdn_d[l, cb], dst=wdb)
            for mt in range(4):
                pi = next_ps(C)
                for kc in range(8):
                    P.op("pe", lambda e, pi=pi, kc=kc, mt=mt, wgt=wgt: e.matmul(
                        ps[pi][:], wgt[:, kc, :], hn[:, kc, mt * 512:(mt + 1) * 512],
                        start=(kc == 0), stop=(kc == 7)), r=[wgb, hnb(kc, mt)], w=[psb[pi]])
                P.op("act", lambda e, pi=pi, mt=mt: e.activation(
                    out=G[:, 1 + mt * 512:1 + (mt + 1) * 512], in_=ps[pi][:], func=AF.Copy),
                    r=[psb[pi]], w=[gb(mt)])
            for mt in range(4):
                pi = next_ps(C)
                for kc in range(8):
                    P.op("pe", lambda e, pi=pi, kc=kc, mt=mt, wut=wut: e.matmul(
                        ps[pi][:], wut[:, kc, :], hn[:, kc, mt * 512:(mt + 1) * 512],
                        start=(kc == 0), stop=(kc == 7)), r=[wub, hnb(kc, mt)], w=[psb[pi]])
                cvt, cvb = cv[mt % 2], P.buf("cv", mt % 2)
                sgt, sgb = sg[mt % 2], P.buf("sg", mt % 2)
                rd = [gb(mt), gedge, C.bconst]
                if mt > 0:
                    rd.append(gb(mt - 1))
                if mt < 3:
                    rd.append(gb(mt + 1))
                b0 = mt * 512
                w_ = lambda k, cb=cb: C.vec[:, cwo + cb * 3 + k:cwo + cb * 3 + k + 1]
                P.op("dve", lambda e, cvt=cvt, b0=b0, w_=w_: e.tensor_scalar(
                    out=cvt[:], in0=G[:, b0:b0 + 512], scalar1=w_(0), scalar2=None, op0=ALU.mult),
                    r=rd, w=[cvb])
                P.op("dve", lambda e, cvt=cvt, b0=b0, w_=w_: e.scalar_tensor_tensor(
                    out=cvt[:], in0=G[:, b0 + 1:b0 + 513], scalar=w_(1), in1=cvt[:],
                    op0=ALU.mult, op1=ALU.add), r=rd + [cvb], w=[cvb])
                P.op("dve", lambda e, cvt=cvt, b0=b0, w_=w_: e.scalar_tensor_tensor(
                    out=cvt[:], in0=G[:, b0 + 2:b0 + 514], scalar=w_(2), in1=cvt[:],
                    op0=ALU.mult, op1=ALU.add), r=rd + [cvb], w=[cvb])
                P.op("act", lambda e, cvt=cvt, sgt=sgt, cb=cb: e.activation(
                    out=sgt[:], in_=cvt[:], func=AF.Silu, bias=C.vec[:, cbo + cb:cbo + cb + 1], scale=1.0),
                    r=[cvb, C.bconst], w=[sgb])
                P.op("dve", lambda e, sgt=sgt, pi=pi, a_t=a_t, b0=b0: e.tensor_tensor(
                    out=a_t[:, b0:b0 + 512], in0=sgt[:], in1=ps[pi][:], op=ALU.mult),
                    r=[sgb, psb[pi]], w=[ab(mt)])
        for mt in range(4):
            for db in range(8):
                pi = next_ps(C)
                for i, cb in enumerate(grp):
                    P.op("pe", lambda e, pi=pi, cb=cb, db=db, mt=mt, i=i, n=len(grp), ws=slots[cb]: e.matmul(
                        ps[pi][:], wd[ws][:, db * 128:(db + 1) * 128], act[cb % GRP][:, mt * 512:(mt + 1) * 512],
                        start=(i == 0), stop=(i == n - 1)),
                        r=[P.buf("wd", slots[cb]), P.buf("act", cb % GRP, mt)], w=[psb[pi]])
                P.op("dve", lambda e, pi=pi, db=db, mt=mt: e.tensor_tensor(
                    out=C.hT[:, db, mt * 512:(mt + 1) * 512], in0=C.hT[:, db, mt * 512:(mt + 1) * 512],
                    in1=ps[pi][:], op=ALU.add), r=[psb[pi], hbuf(C, db, mt)], w=[hbuf(C, db, mt)])


def MM(out, lhsT, rhs, start=True, stop=True):
    return lambda e: e.matmul(out, lhsT, rhs, start=start, stop=stop)


def ACTF(out, in_, func, **kw):
    return lambda e: e.activation(out=out, in_=in_, func=func, **kw)


def TT(out, in0, in1, op):
    return lambda e: e.tensor_tensor(out=out, in0=in0, in1=in1, op=op)


def STT(out, in0, scalar, in1, op0, op1):
    return lambda e: e.scalar_tensor_tensor(out=out, in0=in0, scalar=scalar, in1=in1, op0=op0, op1=op1)


def TS(out, in0, s1, s2, op0, op1=None):
    if op1 is None:
        return lambda e: e.tensor_scalar(out=out, in0=in0, scalar1=s1, scalar2=None, op0=op0)
    return lambda e: e.tensor_scalar(out=out, in0=in0, scalar1=s1, scalar2=s2, op0=op0, op1=op1)


def CP(out, in_):
    return lambda e: e.tensor_copy(out=out, in_=in_)


def MS(out, val):
    return lambda e: e.memset(out, val)


def RCP(out, in_):
    return lambda e: e.reciprocal(out=out, in_=in_)


def mixer0(C):
    P, ps, psb, ph = C.P, C.ps, C.psb, C.phase
    hn = ph([128, 8, S], BF16, "hn")
    hnb = lambda kb, mt: P.buf("hn", kb, mt)
    oa = ph([128, 4, S], BF16, "oa")
    oab = lambda h, mt: P.buf("oa", h, mt)
    mark1 = ph.p
    rmsnorm_to_bf16(C, "norm_mix0", hn, hnb)
    P.barrier()
    ph.p = mark1
    qd = [ph([128, 2, S], BF16, "qd%d" % d) for d in range(2)]
    ki = [ph([128, 2, S], BF16, "ki%d" % d) for d in range(2)]
    qdb = lambda d, blk, mt: P.buf("qd", d, blk, mt)
    kib = lambda d, blk, mt: P.buf("ki", d, blk, mt)
    vt = ph([128, NT, 512], BF16, "vt")
    vtb = lambda tt: P.buf("vt", tt)
    elast = ph([128, 2, 2, NT], F32, "elast")
    elb = lambda mt: P.buf("elast", mt)
    mark2 = ph.p

    win = ph([128, 8, 1056], BF16, "win")
    winb = P.buf("win")
    for kc in range(8):
        P.dma("pool", win[:, kc, 0:1024], C.ab_win_d[:, kc, 0:1024], dst=winb)
    P.dma("pool", win[:, :, 1024:1056], C.ab_win_d[:, :, 1536:1568], dst=winb)
    wgg = ph([32, 2, 256], BF16, "wgg")
    wggb = P.buf("wgg")
    P.dma("pool", wgg[:], C.gla_wg_d.rearrange("d r c -> r d c"), dst=wggb)
    lra = [ph([32, 512], BF16, "lra%d" % d) for d in range(2)]
    lrab = [P.buf("lra", d) for d in range(2)]
    for d in range(2):
        P.op("dve", MS(lra[d][:], 1.0), w=[lrab[d]])
    e_t = [ph([128, 512], F32, "e%d" % i) for i in range(2)]
    sp_t = [ph([128, 512], F32, "sp%d" % i) for i in range(2)]
    Ep = [ph([128, 512], F32, "Ep%d" % d) for d in range(2)]
    Em = [ph([128, 512], F32, "Em%d" % d) for d in range(2)]
    for mt in range(4):
        tsl = slice(mt * 512, (mt + 1) * 512)
        for d in range(2):
            pi = next_ps(C)
            for kc in range(8):
                P.op("pe", MM(ps[pi][0:16, :], win[:, kc, 1024 + d * 16:1024 + (d + 1) * 16], hn[:, kc, tsl],
                              kc == 0, kc == 7), r=[winb, hnb(kc, mt)], w=[psb[pi]])
            P.op("act", ACTF(lra[d][0:16, :], ps[pi][0:16, :], AF.Copy), r=[psb[pi]], w=[lrab[d]])
        cpi = reserve_ps(C, 4)
        for q in range(4):
            tt = mt * 4 + q
            pi = next_ps(C)
            for d in range(2):
                P.op("pe", MM(ps[pi][:, d * 256:(d + 1) * 256], lra[d][:, q * 128:(q + 1) * 128], wgg[:, d, :]),
                     r=[lrab[d], wggb], w=[psb[pi]])
            et, etb = e_t[q % 2], P.buf("e_t", q % 2)
            spt, spb = sp_t[q % 2], P.buf("sp_t", q % 2)
            P.op("act", ACTF(et[:], ps[pi][:], AF.Exp, scale=-1.0), r=[psb[pi]], w=[etb])
            P.op("act", ACTF(spt[:], et[:], AF.Ln, bias=1.0), r=[etb], w=[spb])
            for d in range(2):
                for blk in range(2):
                    b = cpi[d * 2 + blk]
                    P.op("pe", MM(ps[b][:, q * 128:(q + 1) * 128],
                                  spt[:, d * 256 + blk * 128:d * 256 + (blk + 1) * 128], C.tri[d]),
                         r=[spb, C.bconst], w=[psb[b]])
            pi = next_ps(C)
            for kc in range(8):
                P.op("pe", MM(ps[pi][:], hn[:, kc, tt * 128:(tt + 1) * 128], win[:, kc, 512:1024], kc == 0, kc == 7),
                     r=[winb, hnb(kc, mt)], w=[psb[pi]])
            P.op("act", ACTF(vt[:, tt, :], ps[pi][:], AF.Copy), r=[psb[pi]], w=[vtb(tt)])
        for blk in range(2):
            pq = next_ps(C)
            for kc in range(8):
                P.op("pe", MM(ps[pq][:], win[:, kc, blk * 128:(blk + 1) * 128], hn[:, kc, tsl], kc == 0, kc == 7),
                     r=[winb, hnb(kc, mt)], w=[psb[pq]])
            pk = next_ps(C)
            for kc in range(8):
                P.op("pe", MM(ps[pk][:], win[:, kc, 256 + blk * 128:256 + (blk + 1) * 128], hn[:, kc, tsl],
                              kc == 0, kc == 7), r=[winb, hnb(kc, mt)], w=[psb[pk]])
            for d in range(2):
                b = cpi[d * 2 + blk]
                ep_, epb = Ep[d], P.buf("Ep", d)
                em_, emb = Em[d], P.buf("Em", d)
                P.op("act", ACTF(ep_[:], ps[b][:], AF.Exp), r=[psb[b]], w=[epb])
                P.op("act", ACTF(em_[:], ps[b][:], AF.Exp, scale=-1.0), r=[psb[b]], w=[emb])
                col = 127 if d == 0 else 0
                P.op("dve", CP(elast[:, d, blk, mt * 4:(mt + 1) * 4],
                               ep_[:].rearrange("p (n i) -> p n i", i=128)[:, :, col]), r=[epb], w=[elb(mt)])
                P.op("dve", STT(qd[d][:, blk, tsl], ps[pq][:], 0.125, ep_[:], ALU.mult, ALU.mult),
                     r=[psb[pq], epb], w=[qdb(d, blk, mt)])
                P.op("dve", TT(ki[d][:, blk, tsl], ps[pk][:], em_[:], ALU.mult),
                     r=[psb[pk], emb], w=[kib(d, blk, mt)])
        release_ps(C, cpi)
    dump(C, "qd0", qd[0], None)
    dump(C, "qd1", qd[1], None)
    dump(C, "ki0", ki[0], None)
    dump(C, "ki1", ki[1], None)
    dump(C, "vt", vt, None)
    P.barrier()
    ph.p = mark2

    Sall = [ph([128, NT, 2, 128], BF16, "Sall%d" % d) for d in range(2)]
    mark3 = ph.p
    S2 = [[ph([128, 128], F32, "S2_%d%d" % (d, hp)) for hp in range(2)] for d in range(2)]
    sab = lambda d, n, hp: P.buf("Sall", d, n, hp)
    A_t = [[ph([128, 128], F32, "A%d%d" % (d, hp)) for hp in range(2)] for d in range(2)]
    kin = [ph([128, 128], BF16, "kin%d" % i) for i in range(4)]
    for d in range(2):
        n0 = 0 if d == 0 else NT - 1
        for hp in range(2):
            P.op("dve", MS(S2[d][hp][:], 0.0), w=[P.buf("S2", d, hp)])
            P.op("dve", MS(Sall[d][:, n0, hp, :], 0.0), w=[sab(d, n0, hp)])
    kk = 0
    for step in range(NT):
        for d in range(2):
            n = step if d == 0 else NT - 1 - step
            nxt = n + 1 if d == 0 else n - 1
            if not (0 <= nxt < NT):
                continue
            mt = n // 4
            for hp in range(2):
                pi = next_ps(C)
                kt, ktb = kin[kk % 4], P.buf("kin", kk % 4)
                kk += 1
                P.op("pe", MM(ps[pi][:, 0:128], ki[d][:, hp, n * 128:(n + 1) * 128], C.identb[:]),
                     r=[kib(d, hp, mt), P.buf("identb")], w=[psb[pi]])
                P.op("act", ACTF(kt[:], ps[pi][:, 0:128], AF.Copy), r=[psb[pi]], w=[ktb])
                P.op("pe", MM(ps[pi][:, 128:384], kt[:], vt[:, n, hp * 256:(hp + 1) * 256]),
                     r=[ktb, vtb(n)], w=[psb[pi]])
                sb_ = P.buf("S2", d, hp)
                ab_ = P.buf("A", d, hp)
                P.op("act", ACTF(A_t[d][hp][:], S2[d][hp][:], AF.Copy, scale=elast[:, d, hp, n:n + 1]),
                     r=[sb_, elb(mt)], w=[ab_])
                for hh in range(2):
                    rows = slice(hh * 64, hh * 64 + 64)
                    P.op("dve", STT(S2[d][hp][rows, :], ps[pi][rows, 128 + hh * 128:256 + hh * 128],
                                    elast[rows, d, hp, n:n + 1], A_t[d][hp][rows, :], ALU.mult, ALU.add),
                         r=[psb[pi], ab_, elb(mt)], w=[sb_])
                P.op("act", ACTF(Sall[d][:, nxt, hp, :], S2[d][hp][:], AF.Copy), r=[sb_], w=[sab(d, nxt, hp)])
    dump(C, "Sall0", Sall[0], None)
    dump(C, "Sall1", Sall[1], None)
    P.barrier()
    ph.p = mark3

    wgate = ph([128, 8, 512], BF16, "wgate")
    wgb = P.buf("wgate")
    for kc in range(8):
        P.dma("pool", wgate[:, kc, :], C.ab_win_d[:, kc, 1024:1536], dst=wgb)
    mask4 = [ph([128, 512], BF16, "mask4_%d" % d) for d in range(2)]
    m4b = P.buf("mask4")
    for d in range(2):
        src = C.maskf if d == 0 else C.maskb
        for q in range(4):
            P.op("dve", CP(mask4[d][:, q * 128:(q + 1) * 128], src), r=[C.bconst], w=[m4b])
    PT = [[ph([128, 512], BF16, "PT%d%d" % (d, i)) for i in range(2)] for d in range(2)]
    sqh = [ph([128, 512], BF16, "sqh")] * 2
    rsh = [ph([128, 512], F32, "rsh%d" % i) for i in range(2)]
    sgt = ph([128, 512], F32, "sgt")
    tmp = ph([128, 512], F32, "tmp")
    it = 0
    for mt in range(4):
        tsl = slice(mt * 512, (mt + 1) * 512)
        for h in range(4):
            hp, hh = h // 2, h % 2
            rows = slice(hh * 64, hh * 64 + 64)
            par = it % 2
            it += 1
            for d in range(2):
                pi = next_ps(C)
                for q in range(4):
                    ns = slice((mt * 4 + q) * 128, (mt * 4 + q + 1) * 128)
                    P.op("pe", MM(ps[pi][:, q * 128:(q + 1) * 128], ki[d][rows, hp, ns], qd[d][rows, hp, ns]),
                         r=[kib(d, hp, mt), qdb(d, hp, mt)], w=[psb[pi]])
                P.op("dve", TT(PT[d][par][:], ps[pi][:], mask4[d][:], ALU.mult),
                     r=[psb[pi], m4b], w=[P.buf("PT", d, par)])
            po = next_ps(C)
            for q in range(4):
                n = mt * 4 + q
                cs = slice(q * 128, (q + 1) * 128)
                ns = slice(n * 128, (n + 1) * 128)
                P.op("pe", MM(ps[po][:, cs], vt[:, n, h * 128:(h + 1) * 128], PT[0][par][:, cs], True, False),
                     r=[vtb(n), P.buf("PT", 0, par)], w=[psb[po]])
                P.op("pe", MM(ps[po][:, cs], vt[:, n, h * 128:(h + 1) * 128], PT[1][par][:, cs], False, False),
                     r=[vtb(n), P.buf("PT", 1, par)], w=[psb[po]])
                for d in range(2):
                    P.op("pe", MM(ps[po][:, cs], Sall[d][rows, n, hp, :], qd[d][rows, hp, ns], False, d == 1),
                         r=[sab(d, n, hp), qdb(d, hp, mt)], w=[psb[po]])
            sqt_, sqb_ = sqh[par], P.buf("sqh", 0)
            P.op("act", ACTF(sqt_[:], ps[po][:], AF.Square), r=[psb[po]], w=[sqb_])
            pn = next_ps(C)
            P.op("pe", MM(ps[pn][:], C.ones128[:], sqt_[:]), r=[sqb_, P.buf("ones")], w=[psb[pn]])
            rt_, rb_ = rsh[par], P.buf("rsh", par)
            rsqrt_from_psum(P, rt_[:], ps[pn][:], psb[pn], rb_)
            pg = next_ps(C)
            for kc in range(8):
                P.op("pe", MM(ps[pg][:], wgate[:, kc, h * 128:(h + 1) * 128], hn[:, kc, tsl], kc == 0, kc == 7),
                     r=[wgb, hnb(kc, mt)], w=[psb[pg]])
            sgb_, tmb_ = P.buf("sgt"), P.buf("tmp")
            P.op("act", ACTF(sgt[:], ps[pg][:], AF.Silu), r=[psb[pg]], w=[sgb_])
            P.op("dve", STT(tmp[:], ps[po][:], vcol(C, "gla_norm"), rt_[:], ALU.mult, ALU.mult),
                 r=[psb[po], rb_, C.bconst], w=[tmb_])
            P.op("dve", TT(oa[:, h, tsl], tmp[:], sgt[:], ALU.mult), r=[tmb_, sgb_], w=[oab(h, mt)])
    dump(C, "oa", oa, None)
    P.barrier()
    ph.p = mark1

    wsu = ph([128, 8, 512], BF16, "wsu")
    wsv = ph([128, 8, 512], BF16, "wsv")
    wout = ph([128, 8, D], BF16, "wout")
    wsub, wsvb, woutb = P.buf("wsu"), P.buf("wsv"), P.buf("wout")
    for kc in range(8):
        P.dma("pool", wsu[:, kc, :], C.ab_win_d[:, kc, 1568:2080], dst=wsub)
        P.dma("pool", wsv[:, kc, :], C.ab_win_d[:, kc, 2080:2592], dst=wsvb)
        P.dma("pool", wout[:, kc, :], C.ab_wout_d[:, kc, :], dst=woutb)
    wsT = ph([128, 4, 128], BF16, "wsT")
    wsTb = P.buf("wsT")
    P.dma("pool", wsT[:], C.sgu_wsT_d, dst=wsTb)
    lng = ph([128, 512], F32, "lng")
    lnbt = ph([128, 512], F32, "lnb")
    bsb4 = ph([128, 4, 4, 128], F32, "bsb4")
    lnbuf = P.buf("lnconst")
    P.dma("sp", lng[:], C.sgu_ln_d[0:1, :].partition_broadcast(128), dst=lnbuf)
    P.dma("sp", lnbt[:], C.sgu_ln_d[1:2, :].partition_broadcast(128), dst=lnbuf)
    for n in range(4):
        P.dma("sp", bsb4[:, :, n, :], C.sgu_bs_d.rearrange("o (g i) -> o g i", g=4).partition_broadcast(128),
              dst=lnbuf)
    ut = [ph([128, 4, 512], BF16, "ut%d" % i) for i in range(2)]
    sgu = [ph([128, 4, 512], BF16, "sgu%d" % i) for i in range(2)]
    gv = [ph([128, 512], F32, "gv%d" % i) for i in range(2)]
    vv = [ph([128, 512], BF16, "vv%d" % i) for i in range(2)]
    st6 = [ph([128, 6], F32, "st6_%d" % i) for i in range(2)]
    mv = [ph([128, 2], F32, "mv%d" % i) for i in range(2)]
    rs1 = [ph([128, 1], F32, "rs1_%d" % i) for i in range(2)]
    tmq = [ph([128, 512], F32, "tmq%d" % i) for i in range(2)]
    for mt in range(4):
        tsl = slice(mt * 512, (mt + 1) * 512)
        u_, sg4 = ut[mt % 2], sgu[mt % 2]
        for g in range(4):
            pu = next_ps(C)
            for kc in range(8):
                P.op("pe", MM(ps[pu][:], wsu[:, kc, g * 128:(g + 1) * 128], hn[:, kc, tsl], kc == 0, kc == 7),
                     r=[wsub, hnb(kc, mt)], w=[psb[pu]])
            P.op("act", ACTF(u_[:, g, :], ps[pu][:], AF.Gelu_apprx_tanh), r=[psb[pu]], w=[P.buf("ut", mt % 2, g)])
        pm = reserve_ps(C, 4)
        for q in range(4):
            tt = mt * 4 + q
            par = q % 2
            pv = next_ps(C)
            for kc in range(8):
                P.op("pe", MM(ps[pv][:], hn[:, kc, tt * 128:(tt + 1) * 128], wsv[:, kc, :], kc == 0, kc == 7),
                     r=[wsvb, hnb(kc, mt)], w=[psb[pv]])
            gvb = P.buf("gv", par)
            P.op("act", ACTF(gv[par][:], ps[pv][:], AF.Gelu_apprx_tanh), r=[psb[pv]], w=[gvb])
            stb, mvb, rsb = P.buf("st6", par), P.buf("mv", par), P.buf("rs1", par)
            P.op("dve", lambda e, par=par: e.bn_stats(out=st6[par][:], in_=gv[par][:]), r=[gvb], w=[stb])
            P.op("dve", lambda e, par=par: e.bn_aggr(out=mv[par][:], in_=st6[par][:]), r=[stb], w=[mvb])
            P.op("act", ACTF(rs1[par][:], mv[par][:, 1:2], AF.Sqrt, bias=EPS), r=[mvb], w=[rsb])
            P.op("dve", RCP(rs1[par][:], rs1[par][:]), r=[rsb], w=[rsb])
            P.op("dve", TS(gv[par][:], gv[par][:], mv[par][:, 0:1], rs1[par][:, 0:1], ALU.subtract, ALU.mult),
                 r=[gvb, mvb, rsb], w=[gvb])
            P.op("dve", TT(gv[par][:], gv[par][:], lng[:], ALU.mult), r=[gvb, lnbuf], w=[gvb])
            vvb = P.buf("vv", par)
            P.op("dve", TT(vv[par][:], gv[par][:], lnbt[:], ALU.add), r=[gvb, lnbuf], w=[vvb])
            for g in range(4):
                P.op("pe", MM(ps[pm[g]][:, q * 128:(q + 1) * 128], vv[par][:, g * 128:(g + 1) * 128], wsT[:, g, :]),
                     r=[vvb, wsTb], w=[psb[pm[g]]])
        for g in range(4):
            tq, tqb = tmq[g % 2], P.buf("tmq", g % 2)
            P.op("dve", TT(tq[:], ps[pm[g]][:], bsb4[:, g, :, :].rearrange("p n i -> p (n i)"), ALU.add),
                 r=[psb[pm[g]], lnbuf], w=[tqb])
            P.op("dve", TT(sg4[:, g, :], tq[:], u_[:, g, :], ALU.mult),
                 r=[tqb, P.buf("ut", mt % 2, g)], w=[P.buf("sgu", mt % 2, g)])
        release_ps(C, pm)
        for db in range(8):
            po = next_ps(C)
            for kc in range(8):
                if kc < 4:
                    rhs, rb_ = oa[:, kc, tsl], oab(kc, mt)
                else:
                    rhs, rb_ = sg4[:, kc - 4, :], P.buf("sgu", mt % 2, kc - 4)
                P.op("pe", MM(ps[po][:], wout[:, kc, db * 128:(db + 1) * 128], rhs, kc == 0, kc == 7),
                     r=[woutb, rb_], w=[psb[po]])
            P.op("dve", TT(C.hT[:, db, tsl], C.hT[:, db, tsl], ps[po][:], ALU.add),
                 r=[psb[po], hbuf(C, db, mt)], w=[hbuf(C, db, mt)])
        if mt == 3:
            dump(C, "sgu", sg4, None)


def mixer1(C):
    P, ps, psb, ph = C.P, C.ps, C.psb, C.phase
    W = HPG * 128
    hn = ph([128, 8, S], BF16, "hn")
    hnb = lambda kb, mt: P.buf("hn", kb, mt)
    on = ph([128, 8, S], BF16, "on")
    onb = lambda h, mt: P.buf("on", h, mt)
    sc = {}
    for nm in ("g", "lnb", "beta", "c", "negc", "nec", "ECL", "ekend"):
        sc[nm] = ph([128, NT, 16], F32, "sc_" + nm)
    scb = P.buf("gdn_scalars")
    mark0 = ph.p
    mni = [ph([128, W], BF16, "mni%d" % d) for d in range(2)]
    mns = [ph([128, W], BF16, "mns%d" % d) for d in range(2)]
    idr = ph([128, W], BF16, "idr")
    id2 = ph([128, W], BF16, "id2")
    repb = P.buf("rep")
    for hl in range(HPG):
        cs = slice(hl * 128, (hl + 1) * 128)
        for d in range(2):
            P.op("dve", CP(mni[d][:, cs], C.mneg_incl[d]), r=[C.bconst], w=[repb])
            P.op("dve", CP(mns[d][:, cs], C.mneg_strict[d]), r=[C.bconst], w=[repb])
        P.op("dve", CP(idr[:, cs], C.ident), r=[C.bconst], w=[repb])
        P.op("dve", TS(id2[:, cs], C.ident, 2.0, None, ALU.mult), r=[C.bconst], w=[repb])
    markg0 = ph.p
    wsm = ph([128, 8, 32], F32, "wsm")
    wsmb = P.buf("wsm")
    P.dma("sp", wsm[:], C.gdn_wsm_d, dst=wsmb)
    gvec = ph([128, 32], F32, "gvec")
    gvb = P.buf("gvec")
    P.dma("sp", gvec[:], C.gdn_vec_d.partition_broadcast(128), dst=gvb)
    nea = ph([128, 16], F32, "nea")
    neab = P.buf("nea")
    P.op("act", ACTF(nea[:], gvec[:, 16:32], AF.Exp), r=[gvb], w=[neab])
    P.op("dve", TS(nea[:], nea[:], -1.0, None, ALU.mult), r=[neab], w=[neab])
    ab_tm = ph([128, NT, 32], F32, "ab_tm")
    abb = P.buf("ab_tm")
    t1 = ph([128, NT, 16], F32, "t1")
    t2 = ph([128, NT, 16], F32, "t2")
    t1b, t2b = P.buf("t1"), P.buf("t2")
    hn32 = ph([128, 8, 512], F32, "hn32")
    h32b = lambda kb: P.buf("hn32", kb)
    sqt = [ph([128, 512], BF16, "sq%d" % i) for i in range(2)]
    rstd = [ph([128, 512], F32, "rstd%d" % i) for i in range(2)]
    pa = reserve_ps(C, 1)[0]
    for mt in range(4):
        tsl = slice(mt * 512, (mt + 1) * 512)
        rb = P.buf("rstd", mt % 2)
        rt = rstd[mt % 2]
        rms_rstd(C, mt, sqt, rt, rb)
        for kb in range(8):
            P.op("dve", STT(hn32[:, kb, :], C.hT[:, kb, tsl], vcol(C, "norm_mix1", kb), rt[:], ALU.mult, ALU.mult),
                 r=[hbuf(C, kb, mt), rb, C.bconst], w=[h32b(kb)])
            P.op("act", ACTF(hn[:, kb, tsl], hn32[:, kb, :], AF.Copy), r=[h32b(kb)], w=[hnb(kb, mt)])
        for q in range(4):
            tt = mt * 4 + q
            for kc in range(8):
                P.op("pe", MM(ps[pa][:, tt * 32:(tt + 1) * 32], hn32[:, kc, q * 128:(q + 1) * 128], wsm[:, kc, :],
                              kc == 0, kc == 7), r=[wsmb, h32b(kc)], w=[psb[pa]])
    release_ps(C, [pa])
    P.op("act", ACTF(ab_tm[:].rearrange("p t c -> p (t c)"), ps[pa][:], AF.Copy), r=[psb[pa]], w=[abb])
    for tt in range(NT):
        P.op("dve", TT(t1[:, tt, :], ab_tm[:, tt, 16:32], gvec[:, 0:16], ALU.add), r=[abb, gvb], w=[t1b])
    fl = lambda t: t[:].rearrange("p t c -> p (t c)")
    P.op("act", ACTF(fl(t1), fl(t1), AF.Exp), r=[t1b], w=[t1b])
    P.op("act", ACTF(fl(t1), fl(t1), AF.Ln, bias=1.0), r=[t1b], w=[t1b])
    for tt in range(NT):
        P.op("dve", TT(sc["g"][:, tt, :], t1[:, tt, :], nea[:], ALU.mult), r=[t1b, neab], w=[scb])
    P.op("act", ACTF(t2[:], ab_tm[:, :, 0:16], AF.Exp, scale=-1.0), r=[abb], w=[t2b])
    P.op("act", ACTF(fl(t2), fl(t2), AF.Ln, bias=1.0), r=[t2b], w=[t2b])
    P.op("dve", TS(fl(sc["lnb"]), fl(t2), -1.0, None, ALU.mult), r=[t2b], w=[scb])
    P.op("act", ACTF(fl(sc["beta"]), fl(sc["lnb"]), AF.Exp), r=[scb], w=[scb])
    pc = next_ps(C)
    for tt in range(NT):
        for d in range(2):
            P.op("pe", MM(ps[pc][:, tt * 16 + d * 8:tt * 16 + d * 8 + 8], C.mask01[d], sc["g"][:, tt, d * 8:(d + 1) * 8]),
                 r=[scb, C.bconst], w=[psb[pc]])
    P.op("act", ACTF(fl(sc["c"]), ps[pc][:, 0:NT * 16], AF.Copy), r=[psb[pc]], w=[scb])
    P.op("dve", TS(fl(sc["negc"]), fl(sc["c"]), -1.0, None, ALU.mult), r=[scb], w=[scb])
    P.op("act", ACTF(fl(sc["nec"]), fl(sc["c"]), AF.Exp), r=[scb], w=[scb])
    P.op("dve", TS(fl(sc["nec"]), fl(sc["nec"]), -1.0, None, ALU.mult), r=[scb], w=[scb])
    pl = next_ps(C)
    for d in range(2):
        P.op("pe", MM(ps[pl][:, d * 128:(d + 1) * 128], C.esel[d], sc["c"][:, :, d * 8:(d + 1) * 8]),
             r=[scb, C.bconst], w=[psb[pl]])
    for d in range(2):
        cl = ps[pl][:, d * 128:(d + 1) * 128].rearrange("p (t h) -> p t h", h=8)
        P.op("act", ACTF(sc["ECL"][:, :, d * 8:(d + 1) * 8], cl, AF.Exp), r=[psb[pl]], w=[scb])
        P.op("dve", TT(t1[:, :, d * 8:(d + 1) * 8], cl, sc["c"][:, :, d * 8:(d + 1) * 8], ALU.subtract),
             r=[psb[pl], scb], w=[t1b])
    P.op("act", ACTF(fl(sc["ekend"]), fl(t1), AF.Exp), r=[t1b], w=[scb])
    P.barrier()
    ph.p = markg0
    mark1 = ph.p
    cwo = VLAY["gdn_conv_w"][0]
    if C.gdn_stop == 0:
        return

    for grp in range(8 // HPG):
        h0 = grp * HPG
        ph.p = mark1
        qT = ph([128, HPG, S], BF16, "qT")
        kT = ph([128, HPG, S], BF16, "kT")
        vtm = ph([128, NT, W], BF16, "vtm")
        ob = ph([128, HPG, S], BF16, "ob")
        qTb = lambda hl, mt: P.buf("qT", hl, mt)
        kTb = lambda hl, mt: P.buf("kT", hl, mt)
        vtb = lambda tt: P.buf("vtm", tt)
        obb = lambda hl, n: P.buf("ob", hl, n)
        mark2 = ph.p
        wblk = [ph([128, 8, 128], BF16, "wblk%d" % i) for i in range(2)]
        Gb = ph([128, S + 2], F32, "gbuf")
        cv = [ph([128, 512], F32, "cv%d" % i) for i in range(2)]
        so = [ph([128, 512], F32, "so%d" % i) for i in range(2)]
        sob = [ph([128, 512], BF16, "sob")] * 2
        sqn = [ph([128, 512], BF16, "sqn")] * 2
        rsn = [ph([128, 512], F32, "rsn")] * 2
        gb_ = lambda mt: P.buf("gbuf", mt)
        gedge = P.buf("gbuf_edge")
        P.op("dve", MS(Gb[:, 0:1], 0.0), w=[gedge])
        P.op("dve", MS(Gb[:, S + 1:S + 2], 0.0), w=[gedge])
        wi = 0
        for kind in range(3):
            for hl in range(HPG):
                blk = kind * 8 + h0 + hl
                wt, wtb = wblk[wi % 2], P.buf("wblk", wi % 2)
                wi += 1
                P.dma("pool", wt[:], C.gdn_win_d[blk], dst=wtb)
                for mt in range(4):
                    pi = next_ps(C)
                    for kc in range(8):
                        P.op("pe", MM(ps[pi][:], wt[:, kc, :], hn[:, kc, mt * 512:(mt + 1) * 512], kc == 0, kc == 7),
                             r=[wtb, hnb(kc, mt)], w=[psb[pi]])
                    P.op("act", ACTF(Gb[:, 1 + mt * 512:1 + (mt + 1) * 512], ps[pi][:], AF.Copy),
                         r=[psb[pi]], w=[gb_(mt)])
                for mt in range(4):
                    par = mt % 2
                    tsl = slice(mt * 512, (mt + 1) * 512)
                    cvt, cvb = cv[par], P.buf("cv", par)
                    rd = [gb_(mt), gedge, C.bconst]
                    if mt > 0:
                        rd.append(gb_(mt - 1))
                    if mt < 3:
                        rd.append(gb_(mt + 1))
                    b0 = mt * 512
                    w_ = lambda k: C.vec[:, cwo + blk * 3 + k:cwo + blk * 3 + k + 1]
                    P.op("dve", TS(cvt[:], Gb[:, b0:b0 + 512], w_(0), None, ALU.mult), r=rd, w=[cvb])
                    P.op("dve", STT(cvt[:], Gb[:, b0 + 1:b0 + 513], w_(1), cvt[:], ALU.mult, ALU.add), r=rd + [cvb], w=[cvb])
                    P.op("dve", STT(cvt[:], Gb[:, b0 + 2:b0 + 514], w_(2), cvt[:], ALU.mult, ALU.add), r=rd + [cvb], w=[cvb])
                    if kind < 2:
                        sot, sotb = so[par], P.buf("so", par)
                        P.op("act", ACTF(sot[:], cvt[:], AF.Silu), r=[cvb], w=[sotb])
                        sqt_, sqb_ = sqn[par], P.buf("sqn", 0)
                        P.op("act", ACTF(sqt_[:], sot[:], AF.Square), r=[sotb], w=[sqb_])
                        pn = next_ps(C)
                        P.op("pe", MM(ps[pn][:], C.ones1[:], sqt_[:]), r=[sqb_, P.buf("ones")], w=[psb[pn]])
                        rt_, rb_ = rsn[par], P.buf("rsn", 0)
                        rsqrt_from_psum(P, rt_[:], ps[pn][:], psb[pn], rb_)
                        if kind == 0:
                            P.op("dve", STT(qT[:, hl, tsl], sot[:], 128.0 ** -0.5, rt_[:], ALU.mult, ALU.mult),
                                 r=[sotb, rb_], w=[qTb(hl, mt)])
                        else:
                            P.op("dve", TT(kT[:, hl, tsl], sot[:], rt_[:], ALU.mult), r=[sotb, rb_], w=[kTb(hl, mt)])
                    else:
                        sbt, sbtb = sob[par], P.buf("sob", 0)
                        P.op("act", ACTF(sbt[:], cvt[:], AF.Silu), r=[cvb], w=[sbtb])
                        pt = next_ps(C)
                        for q in range(4):
                            P.op("pe", MM(ps[pt][:, q * 128:(q + 1) * 128], sbt[:, q * 128:(q + 1) * 128], C.identb[:]),
                                 r=[sbtb, P.buf("identb")], w=[psb[pt]])
                        P.op("act", ACTF(vtm[:, mt * 4:(mt + 1) * 4, hl * 128:(hl + 1) * 128],
                                         ps[pt][:].rearrange("p (q v) -> p q v", v=128), AF.Copy),
                             r=[psb[pt]], w=[vtb(mt * 4 + q_) for q_ in range(4)])
        P.barrier()
        ph.p = mark2
        if C.gdn_stop == 1:
            return

        gbt = ph([128, W], F32, "gbt")
        lbt = ph([128, W], F32, "lbt")
        tmpA = ph([128, W], F32, "tmpA")
        tmpB = ph([128, W], F32, "tmpB")
        DTi = ph([128, W], F32, "DTi")
        ETs = ph([128, W], F32, "ETs")
        ECb = ph([128, W], F32, "ECb")
        Xt = [ph([128, W], BF16, "Xt%d" % i) for i in range(2)]
        Yt = [ph([128, W], BF16, "Yt%d" % i) for i in range(2)]
        Pt = [ph([128, W], BF16, "Pt%d" % i) for i in range(2)]
        TN = ph([128, W], BF16, "TN")
        ZT = ph([128, W], BF16, "ZT")
        Rn = ph([128, W], BF16, "Rn")
        cset = [{nm: ph([128, W], BF16, "%s%d" % (nm, i)) for nm in ("MT", "Aqk", "kend", "qdec")} for i in range(2)]
        Rt = ph([128, W], BF16, "Rt")
        vn = ph([128, W], BF16, "vn")
        Sm = ph([128, W], F32, "Sm")
        Sbf = ph([128, W], BF16, "Sbf")
        Slo = ph([128, W], BF16, "Slo")
        osum = ph([128, HPG, 512], F32, "osum")
        sqo = ph([128, 512], BF16, "sqo")
        rso = ph([128, 512], F32, "rso")
        zs = ph([128, 512], BF16, "zs")
        wz = [ph([128, 8, 128], BF16, "wz")] * 2
        B_ = lambda nm, *k: P.buf("g2_" + nm, *k)
        v3 = lambda t: t[:].rearrange("p (h i) -> p h i", i=128)
        wzi = 0

        def pre(d, n, cs_):
            cols = slice(n * 128, (n + 1) * 128)
            mt = n // 4
            for hl in range(HPG):
                col = d * 8 + h0 + hl
                hs = slice(hl * 128, (hl + 1) * 128)
                P.op("dve", TS(gbt[:, hs], C.onesf, sc["g"][:, n, col:col + 1], None, ALU.mult),
                     r=[scb, C.bconst], w=[B_("gbt")])
                P.op("dve", TS(lbt[:, hs], C.onesf, sc["lnb"][:, n, col:col + 1], None, ALU.mult),
                     r=[scb, C.bconst], w=[B_("lbt")])
            pA, pB = next_ps(C), next_ps(C)
            for hl in range(HPG):
                hs = slice(hl * 128, (hl + 1) * 128)
                P.op("pe", MM(ps[pA][:, hs], gbt[:, hs], C.mask01[d]), r=[B_("gbt"), C.bconst], w=[psb[pA]])
            for hl in range(HPG):
                hs = slice(hl * 128, (hl + 1) * 128)
                P.op("pe", MM(ps[pB][:, hs], gbt[:, hs], C.mask01[d], True, False), r=[B_("gbt"), C.bconst], w=[psb[pB]])
                P.op("pe", MM(ps[pB][:, hs], lbt[:, hs], C.ident, False, True), r=[B_("lbt"), C.bconst], w=[psb[pB]])
            if C.gdn_sub == 1:
                return
            P.op("dve", TT(tmpA[:], ps[pA][:, 0:W], mni[d][:], ALU.add), r=[psb[pA], repb], w=[B_("tmpA")])
            P.op("dve", TT(tmpB[:], ps[pB][:, 0:W], mns[d][:], ALU.add), r=[psb[pB], repb], w=[B_("tmpB")])
            P.op("act", ACTF(ECb[:], ps[pA][:, 0:W], AF.Exp), r=[psb[pA]], w=[B_("ECb")])
            for hl in range(HPG):
                col = d * 8 + h0 + hl
                hs = slice(hl * 128, (hl + 1) * 128)
                P.op("act", ACTF(DTi[:, hs], tmpA[:, hs], AF.Exp, bias=sc["negc"][:, n, col:col + 1]),
                     r=[B_("tmpA"), scb], w=[B_("DTi")])
                P.op("act", ACTF(ETs[:, hs], tmpB[:, hs], AF.Exp, bias=sc["negc"][:, n, col:col + 1]),
                     r=[B_("tmpB"), scb], w=[B_("ETs")])
            if C.gdn_sub == 2:
                return
            P.op("dve", TT(v3(cs_["qdec"]), qT[:, :, cols], v3(ECb), ALU.mult),
                 r=[B_("ECb")] + [qTb(hl, mt) for hl in range(HPG)], w=[B_("qdec", id(cs_))])
            if C.gdn_sub == 3:
                return
            pG, pQ = next_ps(C), next_ps(C)
            for hl in range(HPG):
                hs = slice(hl * 128, (hl + 1) * 128)
                P.op("pe", MM(ps[pG][:, hs], kT[:, hl, cols], kT[:, hl, cols]), r=[kTb(hl, mt)], w=[psb[pG]])
                P.op("pe", MM(ps[pQ][:, hs], kT[:, hl, cols], qT[:, hl, cols]), r=[kTb(hl, mt), qTb(hl, mt)], w=[psb[pQ]])
            xb = lambda i: B_("X", i)
            yb = lambda i: B_("Y", i)
            pb = lambda i: B_("P", i)
            P.op("dve", STT(Xt[0][:], ps[pG][:, 0:W], -1.0, ETs[:], ALU.mult, ALU.mult), r=[psb[pG], B_("ETs")], w=[xb(0)])
            P.op("dve", TT(cs_["Aqk"][:], ps[pQ][:, 0:W], DTi[:], ALU.mult), r=[psb[pQ], B_("DTi")], w=[B_("Aqk", id(cs_))])
            pY = next_ps(C)
            for hl in range(HPG):
                hs = slice(hl * 128, (hl + 1) * 128)
                P.op("pe", MM(ps[pY][:, hs], Xt[0][:, hs], C.identb[:]), r=[xb(0), P.buf("identb")], w=[psb[pY]])
            P.op("act", ACTF(Yt[0][:], ps[pY][:, 0:W], AF.Copy), r=[psb[pY]], w=[yb(0)])
            P.op("dve", TT(TN[:], idr[:], Yt[0][:], ALU.subtract), r=[yb(0), repb], w=[B_("TN")])
            P.op("dve", TT(Pt[0][:], Xt[0][:], idr[:], ALU.add), r=[xb(0), repb], w=[pb(0)])
            if C.gdn_sub == 4:
                return
            cur = 0
            for k in range(1, 7):
                nx = 1 - cur
                pX, pY, pP = next_ps(C), next_ps(C), next_ps(C)
                for hl in range(HPG):
                    hs = slice(hl * 128, (hl + 1) * 128)
                    P.op("pe", MM(ps[pX][:, hs], Yt[cur][:, hs], Xt[cur][:, hs]), r=[xb(cur), yb(cur)], w=[psb[pX]])
                for hl in range(HPG):
                    hs = slice(hl * 128, (hl + 1) * 128)
                    P.op("pe", MM(ps[pY][:, hs], Xt[cur][:, hs], Yt[cur][:, hs]), r=[xb(cur), yb(cur)], w=[psb[pY]])
                P.op("act", ACTF(Xt[nx][:], ps[pX][:, 0:W], AF.Copy), r=[psb[pX]], w=[xb(nx)])
                P.op("dve", CP(Yt[nx][:], ps[pY][:, 0:W]), r=[psb[pY]], w=[yb(nx)])
                for hl in range(HPG):
                    hs = slice(hl * 128, (hl + 1) * 128)
                    P.op("pe", MM(ps[pP][:, hs], C.identb[:], Pt[cur][:, hs], True, False),
                         r=[pb(cur), P.buf("identb")], w=[psb[pP]])
                    P.op("pe", MM(ps[pP][:, hs], Yt[nx][:, hs], Pt[cur][:, hs], False, True),
                         r=[pb(cur), yb(nx)], w=[psb[pP]])
                P.op("act", ACTF(Pt[nx][:], ps[pP][:, 0:W], AF.Copy), r=[psb[pP]], w=[pb(nx)])
                cur = nx
            Z = Pt[cur]
            pZ, pE = next_ps(C), next_ps(C)
            for hl in range(HPG):
                hs = slice(hl * 128, (hl + 1) * 128)
                P.op("pe", MM(ps[pZ][:, hs], Z[:, hs], C.identb[:]), r=[pb(cur), P.buf("identb")], w=[psb[pZ]])
                P.op("pe", MM(ps[pE][:, hs], TN[:, hs], Z[:, hs]), r=[pb(cur), B_("TN")], w=[psb[pE]])
            P.op("act", ACTF(ZT[:], ps[pZ][:, 0:W], AF.Copy), r=[psb[pZ]], w=[B_("ZT")])
            P.op("dve", STT(Rn[:], ps[pE][:, 0:W], -1.0, id2[:], ALU.mult, ALU.add), r=[psb[pE], repb], w=[B_("Rn")])
            pM = next_ps(C)
            for hl in range(HPG):
                hs = slice(hl * 128, (hl + 1) * 128)
                P.op("pe", MM(ps[pM][:, hs], ZT[:, hs], Rn[:, hs]), r=[B_("ZT"), B_("Rn")], w=[psb[pM]])
            for hl in range(HPG):
                col = d * 8 + h0 + hl
                hs = slice(hl * 128, (hl + 1) * 128)
                P.op("act", ACTF(cs_["MT"][:, hs], ps[pM][:, hs], AF.Copy, scale=sc["beta"][:, n, col:col + 1]),
                     r=[psb[pM], scb], w=[B_("MT", id(cs_))])
            if C.gdn_sub == 5:
                return
            pK = next_ps(C)
            for hl in range(HPG):
                hs = slice(hl * 128, (hl + 1) * 128)
                P.op("pe", MM(ps[pK][:, hs], kT[:, hl, cols], C.identb[:]), r=[kTb(hl, mt), P.buf("identb")], w=[psb[pK]])
            for hl in range(HPG):
                col = d * 8 + h0 + hl
                hs = slice(hl * 128, (hl + 1) * 128)
                P.op("act", ACTF(cs_["kend"][:, hs], ps[pK][:, hs], AF.Copy, scale=sc["ekend"][:, n, col:col + 1]),
                     r=[psb[pK], scb], w=[B_("kend", id(cs_))])

        def scan(d, n, cs_):
            cols = slice(n * 128, (n + 1) * 128)
            mt, q = n // 4, n % 4
            csb = [B_(nm, id(cs_)) for nm in ("MT", "Aqk", "kend", "qdec")]
            p1 = next_ps(C)
            for hl in range(HPG):
                hs = slice(hl * 128, (hl + 1) * 128)
                P.op("pe", MM(ps[p1][:, hs], kT[:, hl, cols], Sbf[:, hs], True, False), r=[kTb(hl, mt), B_("Sbf")], w=[psb[p1]])
                P.op("pe", MM(ps[p1][:, hs], kT[:, hl, cols], Slo[:, hs], False, True), r=[kTb(hl, mt), B_("Slo")], w=[psb[p1]])
            for hl in range(HPG):
                col = d * 8 + h0 + hl
                hs = slice(hl * 128, (hl + 1) * 128)
                P.op("dve", STT(Rt[:, hs], ps[p1][:, hs], sc["nec"][:, n, col:col + 1], vtm[:, n, hs], ALU.mult, ALU.add),
                     r=[psb[p1], scb, vtb(n)], w=[B_("Rt")])
            p2 = next_ps(C)
            for hl in range(HPG):
                hs = slice(hl * 128, (hl + 1) * 128)
                P.op("pe", MM(ps[p2][:, hs], cs_["MT"][:, hs], Rt[:, hs]), r=[csb[0], B_("Rt")], w=[psb[p2]])
            P.op("act", ACTF(vn[:], ps[p2][:, 0:W], AF.Copy), r=[psb[p2]], w=[B_("vn")])
            p3, p4 = next_ps(C), next_ps(C)
            for hl in range(HPG):
                hs = slice(hl * 128, (hl + 1) * 128)
                P.op("pe", MM(ps[p3][:, hs], Sbf[:, hs], cs_["qdec"][:, hs], True, False), r=[B_("Sbf"), csb[3]], w=[psb[p3]])
                P.op("pe", MM(ps[p3][:, hs], Slo[:, hs], cs_["qdec"][:, hs], False, False), r=[B_("Slo"), csb[3]], w=[psb[p3]])
                P.op("pe", MM(ps[p3][:, hs], vn[:, hs], cs_["Aqk"][:, hs], False, True), r=[B_("vn"), csb[1]], w=[psb[p3]])
            for hl in range(HPG):
                hs = slice(hl * 128, (hl + 1) * 128)
                P.op("pe", MM(ps[p4][:, hs], cs_["kend"][:, hs], vn[:, hs]), r=[csb[2], B_("vn")], w=[psb[p4]])
            for hl in range(HPG):
                col = d * 8 + h0 + hl
                hs = slice(hl * 128, (hl + 1) * 128)
                P.op("dve", STT(Sm[:, hs], Sm[:, hs], sc["ECL"][:, n, col:col + 1], ps[p4][:, hs], ALU.mult, ALU.add),
                     r=[psb[p4], scb, B_("Sm")], w=[B_("Sm")])
            P.op("act", ACTF(Sbf[:], Sm[:], AF.Copy), r=[B_("Sm")], w=[B_("Sbf")])
            P.op("dve", TT(Slo[:], Sm[:], Sbf[:], ALU.subtract), r=[B_("Sm"), B_("Sbf")], w=[B_("Slo")])
            if d == 1:
                P.op("act", ACTF(ob[:, :, cols], v3(ps[p3][:, 0:W]), AF.Copy), r=[psb[p3]],
                     w=[obb(hl, n) for hl in range(HPG)])
            else:
                P.op("dve", TT(osum[:, :, q * 128:(q + 1) * 128], v3(ps[p3][:, 0:W]), ob[:, :, cols], ALU.add),
                     r=[psb[p3]] + [obb(hl, n) for hl in range(HPG)], w=[B_("osum", q)])

        for d in (1, 0):
            order = list(range(NT)) if d == 0 else list(range(NT - 1, -1, -1))
            P.op("dve", MS(Sm[:], 0.0), w=[B_("Sm")])
            P.op("dve", MS(Sbf[:], 0.0), w=[B_("Sbf")])
            P.op("dve", MS(Slo[:], 0.0), w=[B_("Slo")])
            pre(d, order[0], cset[0])
            if C.gdn_stop == 2:
                return
            for i, n in enumerate(order):
                if i + 1 < NT:
                    pre(d, order[i + 1], cset[(i + 1) % 2])
                if C.gdn_stop == 3:
                    continue
                scan(d, n, cset[i % 2])
                if C.gdn_stop == 4:
                    return
                if d == 0 and n % 4 == 3:
                    mt = n // 4
                    tsl = slice(mt * 512, (mt + 1) * 512)
                    for hl in range(HPG):
                        h = h0 + hl
                        wzt, wzb = wz[wzi % 2], P.buf("wz", 0)
                        wzi += 1
                        P.dma("pool", wzt[:], C.gdn_win_d[24 + h], dst=wzb)
                        osb = [B_("osum", q_) for q_ in range(4)]
                        P.op("act", ACTF(sqo[:], osum[:, hl, :], AF.Square), r=osb, w=[B_("sqo")])
                        pn = next_ps(C)
                        P.op("pe", MM(ps[pn][:], C.ones128[:], sqo[:]), r=[B_("sqo"), P.buf("ones")], w=[psb[pn]])
                        rsqrt_from_psum(P, rso[:], ps[pn][:], psb[pn], B_("rso"))
                        pz = next_ps(C)
                        for kc in range(8):
                            P.op("pe", MM(ps[pz][:], wzt[:, kc, :], hn[:, kc, tsl], kc == 0, kc == 7),
                                 r=[wzb, hnb(kc, mt)], w=[psb[pz]])
                        P.op("act", ACTF(zs[:], ps[pz][:], AF.Silu), r=[psb[pz]], w=[B_("zs")])
                        P.op("dve", TT(rso[:], rso[:], zs[:], ALU.mult), r=[B_("rso"), B_("zs")], w=[B_("rso")])
                        P.op("dve", STT(on[:, h, tsl], osum[:, hl, :], vcol(C, "gdn_norm"), rso[:], ALU.mult, ALU.mult),
                             r=osb + [B_("rso"), C.bconst], w=[onb(h, mt)])
        P.barrier()

    dump(C, "on", on, None)
    for nm in ("g", "c", "beta", "ECL", "ekend"):
        dump(C, "sc_" + nm, sc[nm], None)
    ph.p = mark1
    wout = ph([128, 8, D], BF16, "gwout")
    woutb = P.buf("gwout")
    for kc in range(8):
        P.dma("pool", wout[:, kc, :], C.gdn_wout_d[:, kc, :], dst=woutb)
    for mt in range(4):
        tsl = slice(mt * 512, (mt + 1) * 512)
        for db in range(8):
            po = next_ps(C)
            for kc in range(8):
                P.op("pe", MM(ps[po][:], wout[:, kc, db * 128:(db + 1) * 128], on[:, kc, tsl], kc == 0, kc == 7),
                     r=[woutb, onb(kc, mt)], w=[psb[po]])
            P.op("dve", TT(C.hT[:, db, tsl], C.hT[:, db, tsl], ps[po][:], ALU.add),
                 r=[psb[po], hbuf(C, db, mt)], w=[hbuf(C, db, mt)])


def final_norm_store(C, sq):
    P, ps, psb = C.P, C.ps, C.psb
    ph = C.phase
    sqt = [ph([128, 512], BF16, "sq%d" % i) for i in range(2)]
    rstd = [ph([128, 512], F32, "rstd%d" % i) for i in range(2)]
    yt = ph([128, 8, 512], F32, "y")
    ot = [ph([128, D], F32, "ot%d" % i) for i in range(2)]
    for mt in range(4):
        tsl = slice(mt * 512, (mt + 1) * 512)
        rb = P.buf("rstd", mt % 2)
        rt = rstd[mt % 2]
        rms_rstd(C, mt, sqt, rt, rb)
        yb = P.buf("fy")
        for kb in range(8):
            P.op("dve", lambda e, kb=kb, rt=rt, tsl=tsl: e.scalar_tensor_tensor(
                out=yt[:, kb, :], in0=C.hT[:, kb, tsl], scalar=vcol(C, "norm_final", kb),
                in1=rt[:], op0=ALU.mult, op1=ALU.mult), r=[hbuf(C, kb, mt), rb, C.bconst], w=[yb])
        for q in range(4):
            tt = mt * 4 + q
            ob = P.buf("fot", tt % 2)
            o_t = ot[tt % 2]
            for half in range(2):
                pi2 = next_ps(C)
                for j in range(4):
                    kb = half * 4 + j
                    P.op("pe", lambda e, pi2=pi2, j=j, kb=kb, q=q: e.matmul(
                        ps[pi2][:, j * 128:(j + 1) * 128], yt[:, kb, q * 128:(q + 1) * 128], C.ident,
                        start=True, stop=True), r=[yb, C.bconst], w=[psb[pi2]])
                P.op("act", lambda e, pi2=pi2, half=half, o_t=o_t: e.activation(
                    out=o_t[:, half * 512:(half + 1) * 512], in_=ps[pi2][:], func=AF.Copy),
                    r=[psb[pi2]], w=[ob])
            P.dma("sp", C.out[sq, tt * 128:(tt + 1) * 128, :], o_t[:], src=ob)


def vec_layout():
    lay = {}
    off = [0]

    def add(name, n):
        lay[name] = (off[0], n)
        off[0] += n
    add("norm_final", 8)
    for l in range(2):
        add("norm_mix%d" % l, 8)
        add("norm_ffn%d" % l, 8)
        add("ffn_conv_w%d" % l, NCB * 3)
        add("ffn_conv_b%d" % l, NCB)
    add("gla_norm", 1)
    add("gdn_conv_w", 24 * 3)
    add("gdn_norm", 1)
    return lay, off[0]


VLAY, NVEC = vec_layout()


def make_vecs(inputs):
    v = np.zeros((128, NVEC), np.float32)

    def put(name, arr):
        o, n = VLAY[name]
        v[:, o:o + n] = np.asarray(arr, np.float32).reshape(128, n)
    f = lambda a: np.asarray(a, np.float32)
    put("norm_final", f(inputs["norm_final"]).reshape(8, 128).T)
    for l in range(2):
        put("norm_mix%d" % l, f(inputs["norm_mix"])[l].reshape(8, 128).T)
        put("norm_ffn%d" % l, f(inputs["norm_ffn"])[l].reshape(8, 128).T)
        put("ffn_conv_w%d" % l, f(inputs["ffn_conv_w"])[l].reshape(3, NCB, 128).transpose(2, 1, 0))
        put("ffn_conv_b%d" % l, f(inputs["ffn_conv_b"])[l].reshape(NCB, 128).T)
    put("gla_norm", f(inputs["gla_norm"])[0].reshape(128, 1))
    put("gdn_conv_w", f(inputs["gdn_conv_w"])[0].reshape(3, 24, 128).transpose(2, 1, 0))
    put("gdn_norm", f(inputs["gdn_norm"])[0].reshape(128, 1))
    return v


def make_weights(inputs):
    f = lambda a: np.asarray(a, np.float32)
    w = {}
    wu = f(inputs["ffn_w_up"]).reshape(2, 8, 128, 2 * NCB, 128)
    w["ffn_wup"] = np.ascontiguousarray(wu.transpose(0, 3, 2, 1, 4))
    w["ffn_wdn"] = np.ascontiguousarray(f(inputs["ffn_w_down"]).reshape(2, NCB, 128, D))
    tile_k = lambda W: np.ascontiguousarray(W.reshape(8, 128, -1).transpose(1, 0, 2))
    w["ab_win"] = tile_k(f(inputs["ab_w_in"])[0])
    w["ab_wout"] = tile_k(f(inputs["ab_w_out"])[0])
    wg = np.zeros((2, 32, 256), np.float32)
    wg[0, :16] = f(inputs["gla_w_gate_fwd"])[0]
    wg[0, 16] = f(inputs["gla_b_gate_fwd"])[0]
    wg[1, :16] = f(inputs["gla_w_gate_bwd"])[0]
    wg[1, 16] = f(inputs["gla_b_gate_bwd"])[0]
    w["gla_wg"] = wg
    w["sgu_wsT"] = np.ascontiguousarray(f(inputs["sgu_w_s"])[0].transpose(2, 0, 1))
    w["sgu_ln"] = np.stack([f(inputs["sgu_ln_g"])[0], f(inputs["sgu_ln_b"])[0]])
    w["sgu_bs"] = np.ascontiguousarray(f(inputs["sgu_b_s"])[0].reshape(1, 512))
    gw = f(inputs["gdn_w_in"])[0]
    w["gdn_win"] = np.ascontiguousarray(gw[:, :4096].reshape(8, 128, 32, 128).transpose(2, 1, 0, 3))
    w["gdn_wsm"] = tile_k(gw[:, 4096:4128])
    w["gdn_wout"] = tile_k(f(inputs["gdn_w_out"])[0])
    w["gdn_vec"] = np.concatenate([f(inputs["gdn_dt_bias_fwd"])[0], f(inputs["gdn_dt_bias_bwd"])[0],
                                   f(inputs["gdn_a_log_fwd"])[0], f(inputs["gdn_a_log_bwd"])[0]]).reshape(1, 32)
    return w


def make_consts():
    c = np.zeros((128, NCONST * 128), np.float32)
    j = np.arange(128)[:, None]
    i = np.arange(128)[None, :]
    c[:, 0:128] = np.eye(128)
    c[:, 128:256] = (j <= i)
    c[:, 256:384] = (j >= i)
    c[:, 384:512] = (j <= i) * (-1.0 / 16)
    c[:, 512:640] = (j >= i) * (-1.0 / 16)
    big = 30000.0
    c[:, 640:768] = ((j <= i) - 1.0) * big
    c[:, 768:896] = ((j < i) - 1.0) * big
    c[:, 896:1024] = ((j >= i) - 1.0) * big
    c[:, 1024:1152] = ((j > i) - 1.0) * big
    c[127, 1152:1280] = 1.0
    c[0, 1280:1408] = 1.0
    c[:, 1408:1536] = 1.0
    return c


_NC_CACHE = {}


def make_in_map(inputs, c, nseq):
    x = np.asarray(inputs["x"], np.float32)
    m = {"x": np.ascontiguousarray(x[c * nseq:(c + 1) * nseq]), "vecs": make_vecs(inputs),
         "consts": make_consts()}
    m.update(make_weights(inputs))
    return m


def kernel(**inputs):
    B = np.asarray(inputs["x"]).shape[0]
    nseq = B // N_CORES
    if nseq not in _NC_CACHE:
        _NC_CACHE[nseq] = build_program(nseq)
    nc = _NC_CACHE[nseq]
    in_maps = [make_in_map(inputs, c, nseq) for c in range(N_CORES)]
    res = run_bass_kernel_spmd(nc, in_maps, core_ids=list(range(N_CORES)))
    return np.concatenate([r["out"] for r in res.results], axis=0)
```

```python
import numpy as np
import concourse.bass as bass
import concourse.mybir as mybir
from concourse.bass_utils import run_bass_kernel_spmd

F32 = mybir.dt.float32
BF16 = mybir.dt.bfloat16
AF = mybir.ActivationFunctionType
ALU = mybir.AluOpType
AX = mybir.AxisListType

D = 1024
S = 2048
NT = S // 128
FF = 2816
NCB = FF // 128
NCONST = 12
HPG = 2
EPS = 1e-6
N_CORES = 8


class Buf:
    __slots__ = ("name", "w", "r", "wsem", "wcnt", "rsem", "rcnt", "excl")

    def __init__(self, name):
        self.name = name
        self.excl = False
        self.w = None
        self.r = []
        self.wsem = None
        self.wcnt = 0
        self.rsem = None
        self.rcnt = 0


class Op:
    __slots__ = ("eng", "fn", "deps", "dma", "event", "src", "dst", "id")


class Prog:
    MAXV = 60000
    MAXD = 56000

    def __init__(self, nc):
        self.nc = nc
        self.engs = {"pe": nc.tensor, "act": nc.scalar, "dve": nc.vector,
                     "pool": nc.gpsimd, "sp": nc.sync}
        self.ops = []
        self.bufs = {}
        self.last = {}
        self.dma_out = []
        self.nsem = 0

    def buf(self, *key):
        b = self.bufs.get(key)
        if b is None:
            b = Buf(key)
            self.bufs[key] = b
        return b

    def _deps(self, o, r, w, is_dma):
        deps = set()
        for b in r:
            if b.w is not None:
                deps.add(b.w)
            if b.excl:
                for x in b.r:
                    if self.ops[x].eng != o.eng:
                        deps.add(x)
        for b in w:
            if b.w is not None:
                deps.add(b.w)
            last = {}
            for x in b.r:
                ox = self.ops[x]
                if ox.dma:
                    deps.add(x)
                else:
                    last[ox.eng] = max(last.get(ox.eng, -1), x)
            deps.update(last.values())
        if not is_dma:
            raw = set(b.w for b in r if b.w is not None)
            deps = set(d for d in deps
                       if d in raw or self.ops[d].eng != o.eng or self.ops[d].dma)
        o.deps = deps
        for b in r:
            b.r.append(o.id)
        for b in w:
            b.w = o.id
            b.r = []

    def op(self, eng, fn, r=(), w=()):
        o = Op()
        o.id = len(self.ops)
        o.eng = eng
        o.fn = fn
        o.dma = False
        o.event = None
        o.src = o.dst = None
        self._deps(o, r, w, False)
        self.ops.append(o)
        self.last[eng] = o.id
        return o.id

    def dma(self, q, out, in_, src=None, dst=None):
        o = Op()
        o.id = len(self.ops)
        o.eng = q
        o.fn = lambda e, out=out, in_=in_: e.dma_start(out=out, in_=in_)
        o.dma = True
        o.event = None
        o.src = src
        o.dst = dst
        self._deps(o, [src] if src is not None else [], [dst] if dst is not None else [], True)
        self.ops.append(o)
        self.dma_out.append(o.id)
        return o.id

    def barrier(self):
        deps = set(self.last.values()) | set(self.dma_out)
        for eng in ("pe", "act", "dve", "pool", "sp"):
            o = Op()
            o.id = len(self.ops)
            o.eng = eng
            o.fn = None
            o.dma = False
            o.event = None
            o.src = o.dst = None
            o.deps = set(deps)
            self.ops.append(o)
        self.dma_out = []
        for b in self.bufs.values():
            b.w = None
            b.r = []

    def _newsem(self):
        self.nsem += 1
        return self.nc.alloc_semaphore("s%d" % self.nsem)

    def emit(self):
        nc = self.nc
        has_dep = set()
        for o in self.ops:
            has_dep |= o.deps
        cur = {}
        cnt = {}
        known = {e: {} for e in self.engs}
        for o in self.ops:
            E = self.engs[o.eng]
            waits = {}
            for d in o.deps:
                ev = self.ops[d].event
                if ev is None:
                    continue
                s, v = ev
                k = id(s)
                if known[o.eng].get(k, 0) >= v:
                    continue
                if k not in waits or waits[k][1] < v:
                    waits[k] = (s, v)
            wl = list(waits.values())
            for s, v in wl:
                known[o.eng][id(s)] = v
            if o.fn is None:
                for s, v in wl:
                    E.wait_ge(s, v)
                continue
            for s, v in wl[:-1]:
                E.wait_ge(s, v)
            ins = o.fn(E)
            if wl:
                ins._wait_ge(wl[-1][0], wl[-1][1])
            if o.dma:
                b = o.dst if o.dst is not None else o.src
                if o.dst is not None:
                    if b.wsem is None or b.wcnt >= self.MAXD:
                        b.wsem = self._newsem()
                        b.wcnt = 0
                    b.wcnt += 16
                    ins.then_inc(b.wsem, 16)
                    o.event = (b.wsem, b.wcnt)
                else:
                    if b.rsem is None or b.rcnt >= self.MAXD:
                        b.rsem = self._newsem()
                        b.rcnt = 0
                    b.rcnt += 16
                    ins.then_inc(b.rsem, 16)
                    o.event = (b.rsem, b.rcnt)
            elif o.id in has_dep:
                if o.eng not in cur or cnt[o.eng] >= self.MAXV:
                    cur[o.eng] = self._newsem()
                    cnt[o.eng] = 0
                cnt[o.eng] += 1
                ins.then_inc(cur[o.eng], 1)
                o.event = (cur[o.eng], cnt[o.eng])


class Alloc:
    def __init__(self, nc, lo, hi):
        self.nc = nc
        self.lo = lo
        self.hi = hi
        self.p = lo
        self.n = 0

    def reset(self):
        self.p = self.lo

    def __call__(self, shape, dtype, name=None):
        nb = 4 if dtype == F32 else 2
        sz = nb
        for s in shape[1:]:
            sz *= s
        sz = (sz + 63) // 64 * 64
        off = self.p
        self.p += sz
        assert self.p <= self.hi, "SBUF overflow %s %d > %d" % (name, self.p, self.hi)
        self.n += 1
        return self.nc.alloc_sbuf_tensor_at("%s_%d" % (name or "t", self.n), list(shape), dtype,
                                            offset=off)


def rsqrt_from_psum(P, out_ap, ps_ap, ps_buf, out_buf, eps=EPS, scale=1.0):
    P.op("act", lambda e: e.activation(out=out_ap, in_=ps_ap, func=AF.Ln, bias=eps, scale=scale),
         r=[ps_buf], w=[out_buf])
    P.op("act", lambda e: e.activation(out=out_ap, in_=out_ap, func=AF.Exp, scale=-0.5), r=[out_buf], w=[out_buf])


class Ctx:
    pass


def build_program(nseq, stages="all", debug=False):
    nc = bass.Bass("TRN2", target_bir_lowering=False)
    P = Prog(nc)
    C = Ctx()
    C.nc, C.P = nc, P
    C.debug = debug
    import os
    C.gdn_stop = int(os.environ.get('GDN_STOP', '9'))
    C.gdn_sub = int(os.environ.get('GDN_SUB', '99'))
    C.x = nc.dram_tensor("x", [nseq, S, D], F32, kind="ExternalInput").ap()
    C.out = nc.dram_tensor("out", [nseq, S, D], F32, kind="ExternalOutput").ap()
    vecs = nc.dram_tensor("vecs", [128, NVEC], F32, kind="ExternalInput").ap()
    C.wup_d = nc.dram_tensor("ffn_wup", [2, 2 * NCB, 128, 8, 128], F32, kind="ExternalInput").ap()
    C.wdn_d = nc.dram_tensor("ffn_wdn", [2, NCB, 128, D], F32, kind="ExternalInput").ap()
    consts_d = nc.dram_tensor("consts", [128, NCONST * 128], F32, kind="ExternalInput").ap()
    C.ab_win_d = nc.dram_tensor("ab_win", [128, 8, 2592], F32, kind="ExternalInput").ap()
    C.ab_wout_d = nc.dram_tensor("ab_wout", [128, 8, 1024], F32, kind="ExternalInput").ap()
    C.gla_wg_d = nc.dram_tensor("gla_wg", [2, 32, 256], F32, kind="ExternalInput").ap()
    C.sgu_wsT_d = nc.dram_tensor("sgu_wsT", [128, 4, 128], F32, kind="ExternalInput").ap()
    C.sgu_ln_d = nc.dram_tensor("sgu_ln", [2, 512], F32, kind="ExternalInput").ap()
    C.sgu_bs_d = nc.dram_tensor("sgu_bs", [1, 512], F32, kind="ExternalInput").ap()
    C.gdn_win_d = nc.dram_tensor("gdn_win", [32, 128, 8, 128], F32, kind="ExternalInput").ap()
    C.gdn_wsm_d = nc.dram_tensor("gdn_wsm", [128, 8, 32], F32, kind="ExternalInput").ap()
    C.gdn_wout_d = nc.dram_tensor("gdn_wout", [128, 8, 1024], F32, kind="ExternalInput").ap()
    C.gdn_vec_d = nc.dram_tensor("gdn_vec", [1, 32], F32, kind="ExternalInput").ap()

    persist = Alloc(nc, 16 * 1024, 96 * 1024)

    C.ps = [nc.alloc_psum_tensor("ps%d" % i, [128, 512], F32) for i in range(8)]
    C.psb = [P.buf("ps", i) for i in range(8)]
    for b in C.psb:
        b.excl = True
    C.psrr = 0
    C.ps_res = set()

    C.hT = persist([128, 8, S], F32, "hT")
    C.consts = persist([128, NCONST * 128], F32, "consts")
    C.ident = C.consts[:, 0:128]
    C.maskf = C.consts[:, 128:256]
    C.maskb = C.consts[:, 256:384]
    C.tri = [C.consts[:, 384:512], C.consts[:, 512:640]]
    C.mask01 = [C.maskf, C.maskb]
    C.mneg_incl = [C.consts[:, 640:768], C.consts[:, 896:1024]]
    C.mneg_strict = [C.consts[:, 768:896], C.consts[:, 1024:1152]]
    C.esel = [C.consts[:, 1152:1280], C.consts[:, 1280:1408]]
    C.onesf = C.consts[:, 1408:1536]
    C.ones1 = persist([128, 128], BF16, "ones1")
    C.identb = persist([128, 128], BF16, "identb")
    C.onesD = persist([128, 128], BF16, "onesD")
    C.ones128 = persist([128, 128], BF16, "ones128")
    C.vec = persist([128, NVEC], F32, "vecs")
    C.bconst = P.buf("const")
    C.phase = Alloc(nc, persist.p, 224 * 1024 - 256)

    P.dma("sp", C.consts[:], consts_d[:, :], dst=C.bconst)
    P.dma("sp", C.vec[:], vecs[:, :], dst=C.bconst)
    P.op("dve", lambda e: e.tensor_copy(out=C.identb[:], in_=C.ident), r=[C.bconst], w=[P.buf("identb")])
    P.op("dve", lambda e: e.memset(C.onesD[:], 1.0 / D), w=[P.buf("ones")])
    P.op("dve", lambda e: e.memset(C.ones128[:], 1.0 / 128), w=[P.buf("ones")])
    P.op("dve", lambda e: e.memset(C.ones1[:], 1.0), w=[P.buf("ones")])

    for sq in range(nseq):
        load_x(C, sq)
        P.barrier()
        C.phase.reset()
        if stages in ("mix0", "all"):
            mixer0(C)
            P.barrier()
            C.phase.reset()
        if stages in ("ffn0", "all"):
            ffn(C, 0)
            P.barrier()
            C.phase.reset()
        if stages in ("mix1", "all"):
            mixer1(C)
            P.barrier()
            C.phase.reset()
        if stages in ("ffn1", "all"):
            ffn(C, 1)
            P.barrier()
            C.phase.reset()
        final_norm_store(C, sq)
        P.barrier()
        C.phase.reset()

    P.barrier()
    P.emit()
    return nc


def dump(C, name, t, buf_list):
    if not C.debug:
        return
    shape = list(t.shape)
    d = C.nc.dram_tensor("dbg_" + name, shape, t.dtype, kind="ExternalOutput").ap()
    C.P.barrier()
    C.P.dma("sp", d, t[:], src=None, dst=C.P.buf("dbg", name))
    C.P.barrier()


def next_ps(C):
    while True:
        i = C.psrr % 8
        C.psrr += 1
        if i not in C.ps_res:
            return i


def reserve_ps(C, n):
    r = []
    for _ in range(n):
        i = next_ps(C)
        C.ps_res.add(i)
        r.append(i)
    return r


def release_ps(C, banks):
    for i in banks:
        C.ps_res.discard(i)


def hbuf(C, kb, mt):
    return C.P.buf("hT", kb, mt)


def vcol(C, name, j=0, n=1):
    o = VLAY[name][0] + j
    return C.vec[:, o:o + n]


def load_x(C, sq):
    P, ps, psb = C.P, C.ps, C.psb
    xs = [C.phase([128, D], F32, "xin%d" % i) for i in range(2)]
    for tt in range(NT):
        xb = P.buf("xin", tt % 2)
        xt = xs[tt % 2]
        P.dma("sp", xt[:], C.x[sq, tt * 128:(tt + 1) * 128, :], dst=xb)
        for half in range(2):
            pi = next_ps(C)
            for j in range(4):
                kb = half * 4 + j
                P.op("pe", lambda e, pi=pi, j=j, kb=kb, xt=xt: e.matmul(
                    ps[pi][:, j * 128:(j + 1) * 128], xt[:, kb * 128:(kb + 1) * 128], C.ident,
                    start=True, stop=True), r=[xb, C.bconst], w=[psb[pi]])
            mt = tt // 4
            P.op("act", lambda e, pi=pi, half=half, tt=tt: e.activation(
                out=C.hT[:, half * 4:half * 4 + 4, tt * 128:(tt + 1) * 128],
                in_=ps[pi][:].rearrange("p (j t) -> p j t", j=4), func=AF.Copy),
                r=[psb[pi]], w=[hbuf(C, half * 4 + j, mt) for j in range(4)])


def rms_rstd(C, mt, sqt, rt, rb):
    P, ps, psb = C.P, C.ps, C.psb
    tsl = slice(mt * 512, (mt + 1) * 512)
    pi = next_ps(C)
    for kb in range(8):
        sb = P.buf("sqt", kb % 2)
        st = sqt[kb % 2]
        P.op("act", lambda e, st=st, kb=kb: e.activation(out=st[:], in_=C.hT[:, kb, tsl], func=AF.Square),
             r=[hbuf(C, kb, mt)], w=[sb])
        P.op("pe", lambda e, st=st, kb=kb: e.matmul(
            ps[pi][:], C.onesD[:], st[:], start=(kb == 0), stop=(kb == 7)),
            r=[sb, P.buf("ones")], w=[psb[pi]])
    rsqrt_from_psum(P, rt[:], ps[pi][:], psb[pi], rb)


def rmsnorm_to_bf16(C, gname, hn, hnb):
    P = C.P
    sqt = [C.phase([128, 512], BF16, "sq%d" % i) for i in range(2)]
    rstd = [C.phase([128, 512], F32, "rstd%d" % i) for i in range(2)]
    for mt in range(4):
        tsl = slice(mt * 512, (mt + 1) * 512)
        rb = P.buf("rstd", mt % 2)
        rt = rstd[mt % 2]
        rms_rstd(C, mt, sqt, rt, rb)
        for kb in range(8):
            P.op("dve", lambda e, kb=kb, rt=rt, tsl=tsl: e.scalar_tensor_tensor(
                out=hn[:, kb, tsl], in0=C.hT[:, kb, tsl], scalar=vcol(C, gname, kb),
                in1=rt[:], op0=ALU.mult, op1=ALU.mult), r=[hbuf(C, kb, mt), rb, C.bconst], w=[hnb(kb, mt)])


def ffn(C, l):
    P, ps, psb = C.P, C.ps, C.psb
    ph = C.phase
    hn = ph([128, 8, S], BF16, "hn")
    hnb = lambda kb, mt: P.buf("hn", kb, mt)
    rmsnorm_to_bf16(C, "norm_ffn%d" % l, hn, hnb)
    GRP = 8
    act = [ph([128, S], BF16, "act%d" % i) for i in range(GRP)]
    wd = [ph([128, D], BF16, "wd%d" % i) for i in range(2 * GRP)]
    wg = [ph([128, 8, 128], BF16, "wg%d" % i) for i in range(2)]
    wu = [ph([128, 8, 128], BF16, "wu%d" % i) for i in range(2)]
    G = ph([128, S + 2], F32, "gbuf")
    cv = [ph([128, 512], F32, "cv%d" % i) for i in range(2)]
    sg = [ph([128, 512], F32, "sg%d" % i) for i in range(2)]
    gb = lambda mt: P.buf("gbuf", mt)
    gedge = P.buf("gbuf_edge")
    P.op("dve", lambda e: e.memset(G[:, 0:1], 0.0), w=[gedge])
    P.op("dve", lambda e: e.memset(G[:, S + 1:S + 2], 0.0), w=[gedge])
    cwo = VLAY["ffn_conv_w%d" % l][0]
    cbo = VLAY["ffn_conv_b%d" % l][0]
    groups = []
    c0 = 0
    while c0 < NCB:
        groups.append(list(range(c0, min(c0 + GRP, NCB))))
        c0 += GRP
    wdslot = 0
    for grp in groups:
        slots = {}
        for cb in grp:
            a_t = act[cb % GRP]
            ab = lambda mt, cb=cb: P.buf("act", cb % GRP, mt)
            wgt, wut = wg[cb % 2], wu[cb % 2]
            wgb, wub = P.buf("wg", cb % 2), P.buf("wu", cb % 2)
            ws = wdslot % (2 * GRP)
            wdslot += 1
            slots[cb] = ws
            wdb = P.buf("wd", ws)
            P.dma("pool", wgt[:], C.wup_d[l, cb], dst=wgb)
            P.dma("pool", wut[:], C.wup_d[l, NCB + cb], dst=wub)
            P.dma("pool", wd[ws][:], C.wdn_d[l, cb], dst=wdb)
            for mt in range(4):
                pi = next_ps(C)
                for kc in range(8):
                    P.op("pe", lambda e, pi=pi, kc=kc, mt=mt, wgt=wgt: e.matmul(
                        ps[pi][:], wgt[:, kc, :], hn[:, kc, mt * 512:(mt + 1) * 512],
                        start=(kc == 0), stop=(kc == 7)), r=[wgb, hnb(kc, mt)], w=[psb[pi]])
                P.op("act", lambda e, pi=pi, mt=mt: e.activation(
                    out=G[:, 1 + mt * 512:1 + (mt + 1) * 512], in_=ps[pi][:], func=AF.Copy),
                    r=[psb[pi]], w=[gb(mt)])
            for mt in range(4):
                pi = next_ps(C)
                for kc in range(8):
                    P.op("pe", lambda e, pi=pi, kc=kc, mt=mt, wut=wut: e.matmul(
                        ps[pi][:], wut[:, kc, :], hn[:, kc, mt * 512:(mt + 1) * 512],
                        start=(kc == 0), stop=(kc == 7)), r=[wub, hnb(kc, mt)], w=[psb[pi]])
                cvt, cvb = cv[mt % 2], P.buf("cv", mt % 2)
                sgt, sgb = sg[mt % 2], P.buf("sg", mt % 2)
                rd = [gb(mt), gedge, C.bconst]
                if mt > 0:
                    rd.append(gb(mt - 1))
                if mt < 3:
                    rd.append(gb(mt + 1))
                b0 = mt * 512
                w_ = lambda k, cb=cb: C.vec[:, cwo + cb * 3 + k:cwo + cb * 3 + k + 1]
                P.op("dve", lambda e, cvt=cvt, b0=b0, w_=w_: e.tensor_scalar(
                    out=cvt[:], in0=G[:, b0:b0 + 512], scalar1=w_(0), scalar2=None, op0=ALU.mult),
                    r=rd, w=[cvb])
                P.op("dve", lambda e, cvt=cvt, b0=b0, w_=w_: e.scalar_tensor_tensor(
                    out=cvt[:], in0=G[:, b0 + 1:b0 + 513], scalar=w_(1), in1=cvt[:],
                    op0=ALU.mult, op1=ALU.add), r=rd + [cvb], w=[cvb])
                P.op("dve", lambda e, cvt=cvt, b0=b0, w_=w_: e.scalar_tensor_tensor(
                    out=cvt[:], in0=G[:, b0 + 2:b0 + 514], scalar=w_(2), in1=cvt[:],
                    op0=ALU.mult, op1=ALU.add), r=rd + [cvb], w=[cvb])
                P.op("act", lambda e, cvt=cvt, sgt=sgt, cb=cb: e.activation(
                    out=sgt[:], in_=cvt[:], func=AF.Silu, bias=C.vec[:, cbo + cb:cbo + cb + 1], scale=1.0),
                    r=[cvb, C.bconst], w=[sgb])
                P.op("dve", lambda e, sgt=sgt, pi=pi, a_t=a_t, b0=b0: e.tensor_tensor(
                    out=a_t[:, b0:b0 + 512], in0=sgt[:], in1=ps[pi][:], op=ALU.mult),
                    r=[sgb, psb[pi]], w=[ab(mt)])
        for mt in range(4):
            for db in range(8):
                pi = next_ps(C)
                for i, cb in enumerate(grp):
                    P.op("pe", lambda e, pi=pi, cb=cb, db=db, mt=mt, i=i, n=len(grp), ws=slots[cb]: e.matmul(
                        ps[pi][:], wd[ws][:, db * 128:(db + 1) * 128], act[cb % GRP][:, mt * 512:(mt + 1) * 512],
                        start=(i == 0), stop=(i == n - 1)),
                        r=[P.buf("wd", slots[cb]), P.buf("act", cb % GRP, mt)], w=[psb[pi]])
                P.op("dve", lambda e, pi=pi, db=db, mt=mt: e.tensor_tensor(
                    out=C.hT[:, db, mt * 512:(mt + 1) * 512], in0=C.hT[:, db, mt * 512:(mt + 1) * 512],
                    in1=ps[pi][:], op=ALU.add), r=[psb[pi], hbuf(C, db, mt)], w=[hbuf(C, db, mt)])


def MM(out, lhsT, rhs, start=True, stop=True):
    return lambda e: e.matmul(out, lhsT, rhs, start=start, stop=stop)


def ACTF(out, in_, func, **kw):
    return lambda e: e.activation(out=out, in_=in_, func=func, **kw)


def TT(out, in0, in1, op):
    return lambda e: e.tensor_tensor(out=out, in0=in0, in1=in1, op=op)


def STT(out, in0, scalar, in1, op0, op1):
    return lambda e: e.scalar_tensor_tensor(out=out, in0=in0, scalar=scalar, in1=in1, op0=op0, op1=op1)


def TS(out, in0, s1, s2, op0, op1=None):
    if op1 is None:
        return lambda e: e.tensor_scalar(out=out, in0=in0, scalar1=s1, scalar2=None, op0=op0)
    return lambda e: e.tensor_scalar(out=out, in0=in0, scalar1=s1, scalar2=s2, op0=op0, op1=op1)


def CP(out, in_):
    return lambda e: e.tensor_copy(out=out, in_=in_)


def MS(out, val):
    return lambda e: e.memset(out, val)


def RCP(out, in_):
    return lambda e: e.reciprocal(out=out, in_=in_)


def mixer0(C):
    P, ps, psb, ph = C.P, C.ps, C.psb, C.phase
    hn = ph([128, 8, S], BF16, "hn")
    hnb = lambda kb, mt: P.buf("hn", kb, mt)
    oa = ph([128, 4, S], BF16, "oa")
    oab = lambda h, mt: P.buf("oa", h, mt)
    mark1 = ph.p
    rmsnorm_to_bf16(C, "norm_mix0", hn, hnb)
    P.barrier()
    ph.p = mark1
    qd = [ph([128, 2, S], BF16, "qd%d" % d) for d in range(2)]
    ki = [ph([128, 2, S], BF16, "ki%d" % d) for d in range(2)]
    qdb = lambda d, blk, mt: P.buf("qd", d, blk, mt)
    kib = lambda d, blk, mt: P.buf("ki", d, blk, mt)
    vt = ph([128, NT, 512], BF16, "vt")
    vtb = lambda tt: P.buf("vt", tt)
    elast = ph([128, 2, 2, NT], F32, "elast")
    elb = lambda mt: P.buf("elast", mt)
    mark2 = ph.p

    win = ph([128, 8, 1056], BF16, "win")
    winb = P.buf("win")
    for kc in range(8):
        P.dma("pool", win[:, kc, 0:1024], C.ab_win_d[:, kc, 0:1024], dst=winb)
    P.dma("pool", win[:, :, 1024:1056], C.ab_win_d[:, :, 1536:1568], dst=winb)
    wgg = ph([32, 2, 256], BF16, "wgg")
    wggb = P.buf("wgg")
    P.dma("pool", wgg[:], C.gla_wg_d.rearrange("d r c -> r d c"), dst=wggb)
    lra = [ph([32, 512], BF16, "lra%d" % d) for d in range(2)]
    lrab = [P.buf("lra", d) for d in range(2)]
    for d in range(2):
        P.op("dve", MS(lra[d][:], 1.0), w=[lrab[d]])
    e_t = [ph([128, 512], F32, "e%d" % i) for i in range(2)]
    sp_t = [ph([128, 512], F32, "sp%d" % i) for i in range(2)]
    Ep = [ph([128, 512], F32, "Ep%d" % d) for d in range(2)]
    Em = [ph([128, 512], F32, "Em%d" % d) for d in range(2)]
    for mt in range(4):
        tsl = slice(mt * 512, (mt + 1) * 512)
        for d in range(2):
            pi = next_ps(C)
            for kc in range(8):
                P.op("pe", MM(ps[pi][0:16, :], win[:, kc, 1024 + d * 16:1024 + (d + 1) * 16], hn[:, kc, tsl],
                              kc == 0, kc == 7), r=[winb, hnb(kc, mt)], w=[psb[pi]])
            P.op("act", ACTF(lra[d][0:16, :], ps[pi][0:16, :], AF.Copy), r=[psb[pi]], w=[lrab[d]])
        cpi = reserve_ps(C, 4)
        for q in range(4):
            tt = mt * 4 + q
            pi = next_ps(C)
            for d in range(2):
                P.op("pe", MM(ps[pi][:, d * 256:(d + 1) * 256], lra[d][:, q * 128:(q + 1) * 128], wgg[:, d, :]),
                     r=[lrab[d], wggb], w=[psb[pi]])
            et, etb = e_t[q % 2], P.buf("e_t", q % 2)
            spt, spb = sp_t[q % 2], P.buf("sp_t", q % 2)
            P.op("act", ACTF(et[:], ps[pi][:], AF.Exp, scale=-1.0), r=[psb[pi]], w=[etb])
            P.op("act", ACTF(spt[:], et[:], AF.Ln, bias=1.0), r=[etb], w=[spb])
            for d in range(2):
                for blk in range(2):
                    b = cpi[d * 2 + blk]
                    P.op("pe", MM(ps[b][:, q * 128:(q + 1) * 128],
                                  spt[:, d * 256 + blk * 128:d * 256 + (blk + 1) * 128], C.tri[d]),
                         r=[spb, C.bconst], w=[psb[b]])
            pi = next_ps(C)
            for kc in range(8):
                P.op("pe", MM(ps[pi][:], hn[:, kc, tt * 128:(tt + 1) * 128], win[:, kc, 512:1024], kc == 0, kc == 7),
                     r=[winb, hnb(kc, mt)], w=[psb[pi]])
            P.op("act", ACTF(vt[:, tt, :], ps[pi][:], AF.Copy), r=[psb[pi]], w=[vtb(tt)])
        for blk in range(2):
            pq = next_ps(C)
            for kc in range(8):
                P.op("pe", MM(ps[pq][:], win[:, kc, blk * 128:(blk + 1) * 128], hn[:, kc, tsl], kc == 0, kc == 7),
                     r=[winb, hnb(kc, mt)], w=[psb[pq]])
            pk = next_ps(C)
            for kc in range(8):
                P.op("pe", MM(ps[pk][:], win[:, kc, 256 + blk * 128:256 + (blk + 1) * 128], hn[:, kc, tsl],
                              kc == 0, kc == 7), r=[winb, hnb(kc, mt)], w=[psb[pk]])
            for d in range(2):
                b = cpi[d * 2 + blk]
                ep_, epb = Ep[d], P.buf("Ep", d)
                em_, emb = Em[d], P.buf("Em", d)
                P.op("act", ACTF(ep_[:], ps[b][:], AF.Exp), r=[psb[b]], w=[epb])
                P.op("act", ACTF(em_[:], ps[b][:], AF.Exp, scale=-1.0), r=[psb[b]], w=[emb])
                col = 127 if d == 0 else 0
                P.op("dve", CP(elast[:, d, blk, mt * 4:(mt + 1) * 4],
                               ep_[:].rearrange("p (n i) -> p n i", i=128)[:, :, col]), r=[epb], w=[elb(mt)])
                P.op("dve", STT(qd[d][:, blk, tsl], ps[pq][:], 0.125, ep_[:], ALU.mult, ALU.mult),
                     r=[psb[pq], epb], w=[qdb(d, blk, mt)])
                P.op("dve", TT(ki[d][:, blk, tsl], ps[pk][:], em_[:], ALU.mult),
                     r=[psb[pk], emb], w=[kib(d, blk, mt)])
        release_ps(C, cpi)
    dump(C, "qd0", qd[0], None)
    dump(C, "qd1", qd[1], None)
    dump(C, "ki0", ki[0], None)
    dump(C, "ki1", ki[1], None)
    dump(C, "vt", vt, None)
    P.barrier()
    ph.p = mark2

    Sall = [ph([128, NT, 2, 128], BF16, "Sall%d" % d) for d in range(2)]
    mark3 = ph.p
    S2 = [[ph([128, 128], F32, "S2_%d%d" % (d, hp)) for hp in range(2)] for d in range(2)]
    sab = lambda d, n, hp: P.buf("Sall", d, n, hp)
    A_t = [[ph([128, 128], F32, "A%d%d" % (d, hp)) for hp in range(2)] for d in range(2)]
    kin = [ph([128, 128], BF16, "kin%d" % i) for i in range(4)]
    for d in range(2):
        n0 = 0 if d == 0 else NT - 1
        for hp in range(2):
            P.op("dve", MS(S2[d][hp][:], 0.0), w=[P.buf("S2", d, hp)])
            P.op("dve", MS(Sall[d][:, n0, hp, :], 0.0), w=[sab(d, n0, hp)])
    kk = 0
    for step in range(NT):
        for d in range(2):
            n = step if d == 0 else NT - 1 - step
            nxt = n + 1 if d == 0 else n - 1
            if not (0 <= nxt < NT):
                continue
            mt = n // 4
            for hp in range(2):
                pi = next_ps(C)
                kt, ktb = kin[kk % 4], P.buf("kin", kk % 4)
                kk += 1
                P.op("pe", MM(ps[pi][:, 0:128], ki[d][:, hp, n * 128:(n + 1) * 128], C.identb[:]),
                     r=[kib(d, hp, mt), P.buf("identb")], w=[psb[pi]])
                P.op("act", ACTF(kt[:], ps[pi][:, 0:128], AF.Copy), r=[psb[pi]], w=[ktb])
                P.op("pe", MM(ps[pi][:, 128:384], kt[:], vt[:, n, hp * 256:(hp + 1) * 256]),
                     r=[ktb, vtb(n)], w=[psb[pi]])
                sb_ = P.buf("S2", d, hp)
                ab_ = P.buf("A", d, hp)
                P.op("act", ACTF(A_t[d][hp][:], S2[d][hp][:], AF.Copy, scale=elast[:, d, hp, n:n + 1]),
                     r=[sb_, elb(mt)], w=[ab_])
                for hh in range(2):
                    rows = slice(hh * 64, hh * 64 + 64)
                    P.op("dve", STT(S2[d][hp][rows, :], ps[pi][rows, 128 + hh * 128:256 + hh * 128],
                                    elast[rows, d, hp, n:n + 1], A_t[d][hp][rows, :], ALU.mult, ALU.add),
                         r=[psb[pi], ab_, elb(mt)], w=[sb_])
                P.op("act", ACTF(Sall[d][:, nxt, hp, :], S2[d][hp][:], AF.Copy), r=[sb_], w=[sab(d, nxt, hp)])
    dump(C, "Sall0", Sall[0], None)
    dump(C, "Sall1", Sall[1], None)
    P.barrier()
    ph.p = mark3

    wgate = ph([128, 8, 512], BF16, "wgate")
    wgb = P.buf("wgate")
    for kc in range(8):
        P.dma("pool", wgate[:, kc, :], C.ab_win_d[:, kc, 1024:1536], dst=wgb)
    mask4 = [ph([128, 512], BF16, "mask4_%d" % d) for d in range(2)]
    m4b = P.buf("mask4")
    for d in range(2):
        src = C.maskf if d == 0 else C.maskb
        for q in range(4):
            P.op("dve", CP(mask4[d][:, q * 128:(q + 1) * 128], src), r=[C.bconst], w=[m4b])
    PT = [[ph([128, 512], BF16, "PT%d%d" % (d, i)) for i in range(2)] for d in range(2)]
    sqh = [ph([128, 512], BF16, "sqh")] * 2
    rsh = [ph([128, 512], F32, "rsh%d" % i) for i in range(2)]
    sgt = ph([128, 512], F32, "sgt")
    tmp = ph([128, 512], F32, "tmp")
    it = 0
    for mt in range(4):
        tsl = slice(mt * 512, (mt + 1) * 512)
        for h in range(4):
            hp, hh = h // 2, h % 2
            rows = slice(hh * 64, hh * 64 + 64)
            par = it % 2
            it += 1
            for d in range(2):
                pi = next_ps(C)
                for q in range(4):
                    ns = slice((mt * 4 + q) * 128, (mt * 4 + q + 1) * 128)
                    P.op("pe", MM(ps[pi][:, q * 128:(q + 1) * 128], ki[d][rows, hp, ns], qd[d][rows, hp, ns]),
                         r=[kib(d, hp, mt), qdb(d, hp, mt)], w=[psb[pi]])
                P.op("dve", TT(PT[d][par][:], ps[pi][:], mask4[d][:], ALU.mult),
                     r=[psb[pi], m4b], w=[P.buf("PT", d, par)])
            po = next_ps(C)
            for q in range(4):
                n = mt * 4 + q
                cs = slice(q * 128, (q + 1) * 128)
                ns = slice(n * 128, (n + 1) * 128)
                P.op("pe", MM(ps[po][:, cs], vt[:, n, h * 128:(h + 1) * 128], PT[0][par][:, cs], True, False),
                     r=[vtb(n), P.buf("PT", 0, par)], w=[psb[po]])
                P.op("pe", MM(ps[po][:, cs], vt[:, n, h * 128:(h + 1) * 128], PT[1][par][:, cs], False, False),
                     r=[vtb(n), P.buf("PT", 1, par)], w=[psb[po]])
                for d in range(2):
                    P.op("pe", MM(ps[po][:, cs], Sall[d][rows, n, hp, :], qd[d][rows, hp, ns], False, d == 1),
                         r=[sab(d, n, hp), qdb(d, hp, mt)], w=[psb[po]])
            sqt_, sqb_ = sqh[par], P.buf("sqh", 0)
            P.op("act", ACTF(sqt_[:], ps[po][:], AF.Square), r=[psb[po]], w=[sqb_])
            pn = next_ps(C)
            P.op("pe", MM(ps[pn][:], C.ones128[:], sqt_[:]), r=[sqb_, P.buf("ones")], w=[psb[pn]])
            rt_, rb_ = rsh[par], P.buf("rsh", par)
            rsqrt_from_psum(P, rt_[:], ps[pn][:], psb[pn], rb_)
            pg = next_ps(C)
            for kc in range(8):
                P.op("pe", MM(ps[pg][:], wgate[:, kc, h * 128:(h + 1) * 128], hn[:, kc, tsl], kc == 0, kc == 7),
                     r=[wgb, hnb(kc, mt)], w=[psb[pg]])
            sgb_, tmb_ = P.buf("sgt"), P.buf("tmp")
            P.op("act", ACTF(sgt[:], ps[pg][:], AF.Silu), r=[psb[pg]], w=[sgb_])
            P.op("dve", STT(tmp[:], ps[po][:], vcol(C, "gla_norm"), rt_[:], ALU.mult, ALU.mult),
                 r=[psb[po], rb_, C.bconst], w=[tmb_])
            P.op("dve", TT(oa[:, h, tsl], tmp[:], sgt[:], ALU.mult), r=[tmb_, sgb_], w=[oab(h, mt)])
    dump(C, "oa", oa, None)
    P.barrier()
    ph.p = mark1

    wsu = ph([128, 8, 512], BF16, "wsu")
    wsv = ph([128, 8, 512], BF16, "wsv")
    wout = ph([128, 8, D], BF16, "wout")
    wsub, wsvb, woutb = P.buf("wsu"), P.buf("wsv"), P.buf("wout")
    for kc in range(8):
        P.dma("pool", wsu[:, kc, :], C.ab_win_d[:, kc, 1568:2080], dst=wsub)
        P.dma("pool", wsv[:, kc, :], C.ab_win_d[:, kc, 2080:2592], dst=wsvb)
        P.dma("pool", wout[:, kc, :], C.ab_wout_d[:, kc, :], dst=woutb)
    wsT = ph([128, 4, 128], BF16, "wsT")
    wsTb = P.buf("wsT")
    P.dma("pool", wsT[:], C.sgu_wsT_d, dst=wsTb)
    lng = ph([128, 512], F32, "lng")
    lnbt = ph([128, 512], F32, "lnb")
    bsb4 = ph([128, 4, 4, 128], F32, "bsb4")
    lnbuf = P.buf("lnconst")
    P.dma("sp", lng[:], C.sgu_ln_d[0:1, :].partition_broadcast(128), dst=lnbuf)
    P.dma("sp", lnbt[:], C.sgu_ln_d[1:2, :].partition_broadcast(128), dst=lnbuf)
    for n in range(4):
        P.dma("sp", bsb4[:, :, n, :], C.sgu_bs_d.rearrange("o (g i) -> o g i", g=4).partition_broadcast(128),
              dst=lnbuf)
    ut = [ph([128, 4, 512], BF16, "ut%d" % i) for i in range(2)]
    sgu = [ph([128, 4, 512], BF16, "sgu%d" % i) for i in range(2)]
    gv = [ph([128, 512], F32, "gv%d" % i) for i in range(2)]
    vv = [ph([128, 512], BF16, "vv%d" % i) for i in range(2)]
    st6 = [ph([128, 6], F32, "st6_%d" % i) for i in range(2)]
    mv = [ph([128, 2], F32, "mv%d" % i) for i in range(2)]
    rs1 = [ph([128, 1], F32, "rs1_%d" % i) for i in range(2)]
    tmq = [ph([128, 512], F32, "tmq%d" % i) for i in range(2)]
    for mt in range(4):
        tsl = slice(mt * 512, (mt + 1) * 512)
        u_, sg4 = ut[mt % 2], sgu[mt % 2]
        for g in range(4):
            pu = next_ps(C)
            for kc in range(8):
                P.op("pe", MM(ps[pu][:], wsu[:, kc, g * 128:(g + 1) * 128], hn[:, kc, tsl], kc == 0, kc == 7),
                     r=[wsub, hnb(kc, mt)], w=[psb[pu]])
            P.op("act", ACTF(u_[:, g, :], ps[pu][:], AF.Gelu_apprx_tanh), r=[psb[pu]], w=[P.buf("ut", mt % 2, g)])
        pm = reserve_ps(C, 4)
        for q in range(4):
            tt = mt * 4 + q
            par = q % 2
            pv = next_ps(C)
            for kc in range(8):
                P.op("pe", MM(ps[pv][:], hn[:, kc, tt * 128:(tt + 1) * 128], wsv[:, kc, :], kc == 0, kc == 7),
                     r=[wsvb, hnb(kc, mt)], w=[psb[pv]])
            gvb = P.buf("gv", par)
            P.op("act", ACTF(gv[par][:], ps[pv][:], AF.Gelu_apprx_tanh), r=[psb[pv]], w=[gvb])
            stb, mvb, rsb = P.buf("st6", par), P.buf("mv", par), P.buf("rs1", par)
            P.op("dve", lambda e, par=par: e.bn_stats(out=st6[par][:], in_=gv[par][:]), r=[gvb], w=[stb])
            P.op("dve", lambda e, par=par: e.bn_aggr(out=mv[par][:], in_=st6[par][:]), r=[stb], w=[mvb])
            P.op("act", ACTF(rs1[par][:], mv[par][:, 1:2], AF.Sqrt, bias=EPS), r=[mvb], w=[rsb])
            P.op("dve", RCP(rs1[par][:], rs1[par][:]), r=[rsb], w=[rsb])
            P.op("dve", TS(gv[par][:], gv[par][:], mv[par][:, 0:1], rs1[par][:, 0:1], ALU.subtract, ALU.mult),
                 r=[gvb, mvb, rsb], w=[gvb])
            P.op("dve", TT(gv[par][:], gv[par][:], lng[:], ALU.mult), r=[gvb, lnbuf], w=[gvb])
            vvb = P.buf("vv", par)
            P.op("dve", TT(vv[par][:], gv[par][:], lnbt[:], ALU.add), r=[gvb, lnbuf], w=[vvb])
            for g in range(4):
                P.op("pe", MM(ps[pm[g]][:, q * 128:(q + 1) * 128], vv[par][:, g * 128:(g + 1) * 128], wsT[:, g, :]),
                     r=[vvb, wsTb], w=[psb[pm[g]]])
        for g in range(4):
            tq, tqb = tmq[g % 2], P.buf("tmq", g % 2)
            P.op("dve", TT(tq[:], ps[pm[g]][:], bsb4[:, g, :, :].rearrange("p n i -> p (n i)"), ALU.add),
                 r=[psb[pm[g]], lnbuf], w=[tqb])
            P.op("dve", TT(sg4[:, g, :], tq[:], u_[:, g, :], ALU.mult),
                 r=[tqb, P.buf("ut", mt % 2, g)], w=[P.buf("sgu", mt % 2, g)])
        release_ps(C, pm)
        for db in range(8):
            po = next_ps(C)
            for kc in range(8):
                if kc < 4:
                    rhs, rb_ = oa[:, kc, tsl], oab(kc, mt)
                else:
                    rhs, rb_ = sg4[:, kc - 4, :], P.buf("sgu", mt % 2, kc - 4)
                P.op("pe", MM(ps[po][:], wout[:, kc, db * 128:(db + 1) * 128], rhs, kc == 0, kc == 7),
                     r=[woutb, rb_], w=[psb[po]])
            P.op("dve", TT(C.hT[:, db, tsl], C.hT[:, db, tsl], ps[po][:], ALU.add),
                 r=[psb[po], hbuf(C, db, mt)], w=[hbuf(C, db, mt)])
        if mt == 3:
            dump(C, "sgu", sg4, None)


def mixer1(C):
    P, ps, psb, ph = C.P, C.ps, C.psb, C.phase
    W = HPG * 128
    hn = ph([128, 8, S], BF16, "hn")
    hnb = lambda kb, mt: P.buf("hn", kb, mt)
    sc = {}
    for nm in ("g", "lnb", "beta", "c", "negc", "nec", "ECL", "ekend"):
        sc[nm] = ph([128, NT, 16], F32, "sc_" + nm)
    scb = P.buf("gdn_scalars")
    mark0 = ph.p
    markg0 = ph.p
    wsm = ph([128, 8, 32], F32, "wsm")
    wsmb = P.buf("wsm")
    P.dma("sp", wsm[:], C.gdn_wsm_d, dst=wsmb)
    gvec = ph([128, 32], F32, "gvec")
    gvb = P.buf("gvec")
    P.dma("sp", gvec[:], C.gdn_vec_d.partition_broadcast(128), dst=gvb)
    nea = ph([128, 16], F32, "nea")
    neab = P.buf("nea")
    P.op("act", ACTF(nea[:], gvec[:, 16:32], AF.Exp), r=[gvb], w=[neab])
    P.op("dve", TS(nea[:], nea[:], -1.0, None, ALU.mult), r=[neab], w=[neab])
    ab_tm = ph([128, NT, 32], F32, "ab_tm")
    abb = P.buf("ab_tm")
    t1 = ph([128, NT, 16], F32, "t1")
    t2 = ph([128, NT, 16], F32, "t2")
    t1b, t2b = P.buf("t1"), P.buf("t2")
    hn32 = ph([128, 8, 512], F32, "hn32")
    h32b = lambda kb: P.buf("hn32", kb)
    sqt = [ph([128, 512], BF16, "sq%d" % i) for i in range(2)]
    rstd = [ph([128, 512], F32, "rstd%d" % i) for i in range(2)]
    pa = reserve_ps(C, 1)[0]
    for mt in range(4):
        tsl = slice(mt * 512, (mt + 1) * 512)
        rb = P.buf("rstd", mt % 2)
        rt = rstd[mt % 2]
        rms_rstd(C, mt, sqt, rt, rb)
        for kb in range(8):
            P.op("dve", STT(hn32[:, kb, :], C.hT[:, kb, tsl], vcol(C, "norm_mix1", kb), rt[:], ALU.mult, ALU.mult),
                 r=[hbuf(C, kb, mt), rb, C.bconst], w=[h32b(kb)])
            P.op("act", ACTF(hn[:, kb, tsl], hn32[:, kb, :], AF.Copy), r=[h32b(kb)], w=[hnb(kb, mt)])
        for q in range(4):
            tt = mt * 4 + q
            for kc in range(8):
                P.op("pe", MM(ps[pa][:, tt * 32:(tt + 1) * 32], hn32[:, kc, q * 128:(q + 1) * 128], wsm[:, kc, :],
                              kc == 0, kc == 7), r=[wsmb, h32b(kc)], w=[psb[pa]])
    release_ps(C, [pa])
    P.op("act", ACTF(ab_tm[:].rearrange("p t c -> p (t c)"), ps[pa][:], AF.Copy), r=[psb[pa]], w=[abb])
    for tt in range(NT):
        P.op("dve", TT(t1[:, tt, :], ab_tm[:, tt, 16:32], gvec[:, 0:16], ALU.add), r=[abb, gvb], w=[t1b])
    fl = lambda t: t[:].rearrange("p t c -> p (t c)")
    P.op("act", ACTF(fl(t1), fl(t1), AF.Exp), r=[t1b], w=[t1b])
    P.op("act", ACTF(fl(t1), fl(t1), AF.Ln, bias=1.0), r=[t1b], w=[t1b])
    for tt in range(NT):
        P.op("dve", TT(sc["g"][:, tt, :], t1[:, tt, :], nea[:], ALU.mult), r=[t1b, neab], w=[scb])
    P.op("act", ACTF(t2[:], ab_tm[:, :, 0:16], AF.Exp, scale=-1.0), r=[abb], w=[t2b])
    P.op("act", ACTF(fl(t2), fl(t2), AF.Ln, bias=1.0), r=[t2b], w=[t2b])
    P.op("dve", TS(fl(sc["lnb"]), fl(t2), -1.0, None, ALU.mult), r=[t2b], w=[scb])
    P.op("act", ACTF(fl(sc["beta"]), fl(sc["lnb"]), AF.Exp), r=[scb], w=[scb])
    pc = next_ps(C)
    for tt in range(NT):
        for d in range(2):
            P.op("pe", MM(ps[pc][:, tt * 16 + d * 8:tt * 16 + d * 8 + 8], C.mask01[d], sc["g"][:, tt, d * 8:(d + 1) * 8]),
                 r=[scb, C.bconst], w=[psb[pc]])
    P.op("act", ACTF(fl(sc["c"]), ps[pc][:, 0:NT * 16], AF.Copy), r=[psb[pc]], w=[scb])
    P.op("dve", TS(fl(sc["negc"]), fl(sc["c"]), -1.0, None, ALU.mult), r=[scb], w=[scb])
    P.op("act", ACTF(fl(sc["nec"]), fl(sc["c"]), AF.Exp), r=[scb], w=[scb])
    P.op("dve", TS(fl(sc["nec"]), fl(sc["nec"]), -1.0, None, ALU.mult), r=[scb], w=[scb])
    pl = next_ps(C)
    for d in range(2):
        P.op("pe", MM(ps[pl][:, d * 128:(d + 1) * 128], C.esel[d], sc["c"][:, :, d * 8:(d + 1) * 8]),
             r=[scb, C.bconst], w=[psb[pl]])
    for d in range(2):
        cl = ps[pl][:, d * 128:(d + 1) * 128].rearrange("p (t h) -> p t h", h=8)
        P.op("act", ACTF(sc["ECL"][:, :, d * 8:(d + 1) * 8], cl, AF.Exp), r=[psb[pl]], w=[scb])
        P.op("dve", TT(t1[:, :, d * 8:(d + 1) * 8], cl, sc["c"][:, :, d * 8:(d + 1) * 8], ALU.subtract),
             r=[psb[pl], scb], w=[t1b])
    P.op("act", ACTF(fl(sc["ekend"]), fl(t1), AF.Exp), r=[t1b], w=[scb])
    P.barrier()
    ph.p = markg0
    mark1 = ph.p
    cwo = VLAY["gdn_conv_w"][0]
    if C.gdn_stop == 0:
        return

    for grp in range(8 // HPG):
        h0 = grp * HPG
        ph.p = mark1
        qT = ph([128, HPG, S], BF16, "qT")
        kT = ph([128, HPG, S], BF16, "kT")
        vtm = ph([128, NT, W], BF16, "vtm")
        ob = ph([128, HPG, S], BF16, "ob")
        qTb = lambda hl, mt: P.buf("qT", hl, mt)
        kTb = lambda hl, mt: P.buf("kT", hl, mt)
        vtb = lambda tt: P.buf("vtm", tt)
        obb = lambda hl, n: P.buf("ob", hl, n)
        mark2 = ph.p
        wblk = [ph([128, 8, 128], BF16, "wblk%d" % i) for i in range(2)]
        Gb = ph([128, S + 2], F32, "gbuf")
        cv = [ph([128, 512], F32, "cv%d" % i) for i in range(2)]
        so = [ph([128, 512], F32, "so%d" % i) for i in range(2)]
        sob = [ph([128, 512], BF16, "sob")] * 2
        sqn = [ph([128, 512], BF16, "sqn")] * 2
        rsn = [ph([128, 512], F32, "rsn")] * 2
        gb_ = lambda mt: P.buf("gbuf", mt)
        gedge = P.buf("gbuf_edge")
        P.op("dve", MS(Gb[:, 0:1], 0.0), w=[gedge])
        P.op("dve", MS(Gb[:, S + 1:S + 2], 0.0), w=[gedge])
        wi = 0
        for kind in range(3):
            for hl in range(HPG):
                blk = kind * 8 + h0 + hl
                wt, wtb = wblk[wi % 2], P.buf("wblk", wi % 2)
                wi += 1
                P.dma("pool", wt[:], C.gdn_win_d[blk], dst=wtb)
                for mt in range(4):
                    pi = next_ps(C)
                    for kc in range(8):
                        P.op("pe", MM(ps[pi][:], wt[:, kc, :], hn[:, kc, mt * 512:(mt + 1) * 512], kc == 0, kc == 7),
                             r=[wtb, hnb(kc, mt)], w=[psb[pi]])
                    P.op("act", ACTF(Gb[:, 1 + mt * 512:1 + (mt + 1) * 512], ps[pi][:], AF.Copy),
                         r=[psb[pi]], w=[gb_(mt)])
                for mt in range(4):
                    par = mt % 2
                    tsl = slice(mt * 512, (mt + 1) * 512)
                    cvt, cvb = cv[par], P.buf("cv", par)
                    rd = [gb_(mt), gedge, C.bconst]
                    if mt > 0:
                        rd.append(gb_(mt - 1))
                    if mt < 3:
                        rd.append(gb_(mt + 1))
                    b0 = mt * 512
                    w_ = lambda k: C.vec[:, cwo + blk * 3 + k:cwo + blk * 3 + k + 1]
                    P.op("dve", TS(cvt[:], Gb[:, b0:b0 + 512], w_(0), None, ALU.mult), r=rd, w=[cvb])
                    P.op("dve", STT(cvt[:], Gb[:, b0 + 1:b0 + 513], w_(1), cvt[:], ALU.mult, ALU.add), r=rd + [cvb], w=[cvb])
                    P.op("dve", STT(cvt[:], Gb[:, b0 + 2:b0 + 514], w_(2), cvt[:], ALU.mult, ALU.add), r=rd + [cvb], w=[cvb])
                    if kind < 2:
                        sot, sotb = so[par], P.buf("so", par)
                        P.op("act", ACTF(sot[:], cvt[:], AF.Silu), r=[cvb], w=[sotb])
                        sqt_, sqb_ = sqn[par], P.buf("sqn", 0)
                        P.op("act", ACTF(sqt_[:], sot[:], AF.Square), r=[sotb], w=[sqb_])
                        pn = next_ps(C)
                        P.op("pe", MM(ps[pn][:], C.ones1[:], sqt_[:]), r=[sqb_, P.buf("ones")], w=[psb[pn]])
                        rt_, rb_ = rsn[par], P.buf("rsn", 0)
                        rsqrt_from_psum(P, rt_[:], ps[pn][:], psb[pn], rb_)
                        if kind == 0:
                            P.op("dve", STT(qT[:, hl, tsl], sot[:], 128.0 ** -0.5, rt_[:], ALU.mult, ALU.mult),
                                 r=[sotb, rb_], w=[qTb(hl, mt)])
                        else:
                            P.op("dve", TT(kT[:, hl, tsl], sot[:], rt_[:], ALU.mult), r=[sotb, rb_], w=[kTb(hl, mt)])
                    else:
                        sbt, sbtb = sob[par], P.buf("sob", 0)
                        P.op("act", ACTF(sbt[:], cvt[:], AF.Silu), r=[cvb], w=[sbtb])
                        pt = next_ps(C)
                        for q in range(4):
                            P.op("pe", MM(ps[pt][:, q * 128:(q + 1) * 128], sbt[:, q * 128:(q + 1) * 128], C.identb[:]),
                                 r=[sbtb, P.buf("identb")], w=[psb[pt]])
                        P.op("act", ACTF(vtm[:, mt * 4:(mt + 1) * 4, hl * 128:(hl + 1) * 128],
                                         ps[pt][:].rearrange("p (q v) -> p q v", v=128), AF.Copy),
                             r=[psb[pt]], w=[vtb(mt * 4 + q_) for q_ in range(4)])
        P.barrier()
        ph.p = mark2
        if C.gdn_stop == 1:
            return

        of_ = ph([128, HPG, S], BF16, "of")
        ofb = lambda hl, n: P.buf("of", hl, n)
        oo = [of_, ob]
        oob = [ofb, obb]
        markT = ph.p
        NV = 2 * HPG
        WW = NV * 128
        VH = [(d, hl) for d in range(2) for hl in range(HPG)]
        HS = [slice(v * 128, (v + 1) * 128) for v in range(NV)]
        t = {}
        for nm in ("Rt", "vn", "Sbf", "mniW", "mnsW", "idW", "id2W"):
            t[nm] = ph([128, WW], BF16, nm)
        t["Sm"] = ph([128, WW], F32, "Sm")
        tsets = []
        for k_ in range(2):
            ts_ = {}
            for nm in ("tmpA", "tmpB"):
                ts_[nm] = ph([128, WW], F32, "%s_%d" % (nm, k_))
            for nm in ("DTi", "ETs", "ECb", "X0", "X1", "Y0", "Y1", "P0", "P1", "TN", "ZT", "Rn"):
                ts_[nm] = ph([128, WW], BF16, "%s_%d" % (nm, k_))
            tsets.append(ts_)
        csets = [{nm: ph([128, WW], BF16, "%s%d" % (nm, i)) for nm in ("MT", "Aqk", "kend", "qdec")} for i in range(3)]
        B_ = lambda nm, *k: P.buf("g2_" + nm, *k)
        for v, (d, hl) in enumerate(VH):
            P.op("pool", CP(t["mniW"][:, HS[v]], C.mneg_incl[d]), r=[C.bconst], w=[B_("maskW")])
            P.op("pool", CP(t["mnsW"][:, HS[v]], C.mneg_strict[d]), r=[C.bconst], w=[B_("maskW")])
            P.op("pool", CP(t["idW"][:, HS[v]], C.ident), r=[C.bconst], w=[B_("maskW")])
            P.op("pool", TS(t["id2W"][:, HS[v]], C.ident, 2.0, None, ALU.mult), r=[C.bconst], w=[B_("maskW")])
        v3 = lambda tt_: tt_[:].rearrange("p (h i) -> p h i", i=128)
        v3p = lambda ap: ap.rearrange("p (h i) -> p h i", i=128)
        order = [list(range(NT)), list(range(NT - 1, -1, -1))]
        hb_pool = list(range(8))

        def hb_alloc():
            assert hb_pool, "out of PSUM banks"
            return hb_pool.pop(0)

        def hb_free(x):
            hb_pool.append(x)

        hp = lambda x: ps[x]
        hbb = lambda x: psb[x]

        def pre(i):
            ci = i % 3
            ti = i % 2
            cs_ = csets[ci]
            tp = tsets[ti]
            Bt = lambda nm, *k: P.buf("g2_" + nm, ti, *k)
            nn = [order[d][i] for (d, hl) in VH]
            cols = [slice(n * 128, (n + 1) * 128) for n in nn]
            mts = [n // 4 for n in nn]
            scol = lambda nm, v: sc[nm][:, nn[v], VH[v][0] * 8 + h0 + VH[v][1]:VH[v][0] * 8 + h0 + VH[v][1] + 1]
            bc = lambda nm, v: scol(nm, v).to_broadcast([128, 128])
            pA, pB = hb_alloc(), hb_alloc()
            for v, (d, hl) in enumerate(VH):
                P.op("pe", MM(hp(pA)[:, HS[v]], bc("g", v), C.mask01[d]), r=[scb, C.bconst], w=[hbb(pA)])
            for v, (d, hl) in enumerate(VH):
                P.op("pe", MM(hp(pB)[:, HS[v]], bc("g", v), C.mask01[d], True, False), r=[scb, C.bconst], w=[hbb(pB)])
                P.op("pe", MM(hp(pB)[:, HS[v]], bc("lnb", v), C.ident, False, True), r=[scb, C.bconst], w=[hbb(pB)])
            yield
            P.op("dve", TT(tp["tmpA"][:], hp(pA)[:], t["mniW"][:], ALU.add), r=[hbb(pA), B_("maskW")], w=[Bt("tmpA")])
            P.op("act", ACTF(tp["ECb"][:], hp(pA)[:], AF.Exp), r=[hbb(pA)], w=[Bt("ECb")])
            P.op("dve", TT(tp["tmpB"][:], hp(pB)[:], t["mnsW"][:], ALU.add), r=[hbb(pB), B_("maskW")], w=[Bt("tmpB")])
            hb_free(pA)
            hb_free(pB)
            yield
            pG, pQ = hb_alloc(), hb_alloc()
            for v, (d, hl) in enumerate(VH):
                P.op("pe", MM(hp(pG)[:, HS[v]], kT[:, hl, cols[v]], kT[:, hl, cols[v]]), r=[kTb(hl, mts[v])], w=[hbb(pG)])
                P.op("pe", MM(hp(pQ)[:, HS[v]], kT[:, hl, cols[v]], qT[:, hl, cols[v]]),
                     r=[kTb(hl, mts[v]), qTb(hl, mts[v])], w=[hbb(pQ)])
            for v in range(NV):
                P.op("act", ACTF(tp["ETs"][:, HS[v]], tp["tmpB"][:, HS[v]], AF.Exp, bias=scol("negc", v)),
                     r=[Bt("tmpB"), scb], w=[Bt("ETs")])
            for v in range(NV):
                P.op("act", ACTF(tp["DTi"][:, HS[v]], tp["tmpA"][:, HS[v]], AF.Exp, bias=scol("negc", v)),
                     r=[Bt("tmpA"), scb], w=[Bt("DTi")])
            for d in range(2):
                vs = slice(d * HPG * 128, (d + 1) * HPG * 128)
                P.op("pool", TT(v3p(cs_["qdec"][:, vs]), qT[:, :, cols[d * HPG]], v3p(tp["ECb"][:, vs]), ALU.mult),
                     r=[Bt("ECb")] + [qTb(hl, mts[d * HPG]) for hl in range(HPG)], w=[B_("qdec", ci)])
            yield
            xb = lambda k: Bt("X", k)
            yb = lambda k: Bt("Y", k)
            pb = lambda k: Bt("P", k)
            Xt, Yt, Pt = [tp["X0"], tp["X1"]], [tp["Y0"], tp["Y1"]], [tp["P0"], tp["P1"]]
            P.op("dve", STT(Xt[0][:], hp(pG)[:], -1.0, tp["ETs"][:], ALU.mult, ALU.mult),
                 r=[hbb(pG), Bt("ETs")], w=[xb(0)])
            P.op("dve", TT(cs_["Aqk"][:], hp(pQ)[:], tp["DTi"][:], ALU.mult), r=[hbb(pQ), Bt("DTi")], w=[B_("Aqk", ci)])
            hb_free(pG)
            hb_free(pQ)
            yield
            pY = hb_alloc()
            for v in range(NV):
                P.op("pe", MM(hp(pY)[:, HS[v]], Xt[0][:, HS[v]], C.identb[:]), r=[xb(0), P.buf("identb")], w=[hbb(pY)])
            P.op("pool", TT(Pt[0][:], Xt[0][:], t["idW"][:], ALU.add), r=[xb(0), B_("maskW")], w=[pb(0)])
            yield
            P.op("act", ACTF(Yt[0][:], hp(pY)[:], AF.Copy), r=[hbb(pY)], w=[yb(0)])
            hb_free(pY)
            P.op("pool", TT(tp["TN"][:], t["idW"][:], Yt[0][:], ALU.subtract), r=[yb(0), B_("maskW")], w=[Bt("TN")])
            yield
            cur = 0
            for k in range(1, 7):
                nx = 1 - cur
                pX, pY = hb_alloc(), hb_alloc()
                for v in range(NV):
                    P.op("pe", MM(hp(pX)[:, HS[v]], Yt[cur][:, HS[v]], Xt[cur][:, HS[v]]), r=[xb(cur), yb(cur)], w=[hbb(pX)])
                for v in range(NV):
                    P.op("pe", MM(hp(pY)[:, HS[v]], Xt[cur][:, HS[v]], Yt[cur][:, HS[v]]), r=[xb(cur), yb(cur)], w=[hbb(pY)])
                yield
                P.op("dve", CP(Yt[nx][:], hp(pY)[:]), r=[hbb(pY)], w=[yb(nx)])
                P.op("act", ACTF(Xt[nx][:], hp(pX)[:], AF.Copy), r=[hbb(pX)], w=[xb(nx)])
                hb_free(pX)
                hb_free(pY)
                yield
                pP = hb_alloc()
                for v in range(NV):
                    P.op("pe", MM(hp(pP)[:, HS[v]], C.identb[:], Pt[cur][:, HS[v]], True, False),
                         r=[pb(cur), P.buf("identb")], w=[hbb(pP)])
                    P.op("pe", MM(hp(pP)[:, HS[v]], Yt[nx][:, HS[v]], Pt[cur][:, HS[v]], False, True),
                         r=[pb(cur), yb(nx)], w=[hbb(pP)])
                yield
                if k % 2 == 0:
                    P.op("act", ACTF(Pt[nx][:], hp(pP)[:], AF.Copy), r=[hbb(pP)], w=[pb(nx)])
                else:
                    P.op("dve", CP(Pt[nx][:], hp(pP)[:]), r=[hbb(pP)], w=[pb(nx)])
                hb_free(pP)
                cur = nx
            yield
            Z = Pt[cur]
            pZ, pE = hb_alloc(), hb_alloc()
            for v in range(NV):
                P.op("pe", MM(hp(pZ)[:, HS[v]], Z[:, HS[v]], C.identb[:]), r=[pb(cur), P.buf("identb")], w=[hbb(pZ)])
                P.op("pe", MM(hp(pE)[:, HS[v]], tp["TN"][:, HS[v]], Z[:, HS[v]]), r=[pb(cur), Bt("TN")], w=[hbb(pE)])
            yield
            P.op("act", ACTF(tp["ZT"][:], hp(pZ)[:], AF.Copy), r=[hbb(pZ)], w=[Bt("ZT")])
            P.op("dve", STT(tp["Rn"][:], hp(pE)[:], -1.0, t["id2W"][:], ALU.mult, ALU.add),
                 r=[hbb(pE), B_("maskW")], w=[Bt("Rn")])
            hb_free(pZ)
            hb_free(pE)
            yield
            pM = hb_alloc()
            for v in range(NV):
                P.op("pe", MM(hp(pM)[:, HS[v]], tp["ZT"][:, HS[v]], tp["Rn"][:, HS[v]]), r=[Bt("ZT"), Bt("Rn")], w=[hbb(pM)])
            pK = hb_alloc()
            for v, (d, hl) in enumerate(VH):
                P.op("pe", MM(hp(pK)[:, HS[v]], kT[:, hl, cols[v]], C.identb[:]), r=[kTb(hl, mts[v]), P.buf("identb")],
                     w=[hbb(pK)])
            yield
            for v in range(NV):
                P.op("act", ACTF(cs_["kend"][:, HS[v]], hp(pK)[:, HS[v]], AF.Copy, scale=scol("ekend", v)),
                     r=[hbb(pK), scb], w=[B_("kend", ci)])
            hb_free(pK)
            for v in range(NV):
                P.op("act", ACTF(cs_["MT"][:, HS[v]], hp(pM)[:, HS[v]], AF.Copy, scale=scol("beta", v)),
                     r=[hbb(pM), scb], w=[B_("MT", ci)])
            hb_free(pM)

        def scan(i):
            ci = i % 3
            cs_ = csets[ci]
            nn = [order[d][i] for (d, hl) in VH]
            cols = [slice(n * 128, (n + 1) * 128) for n in nn]
            mts = [n // 4 for n in nn]
            scol = lambda nm, v: sc[nm][:, nn[v], VH[v][0] * 8 + h0 + VH[v][1]:VH[v][0] * 8 + h0 + VH[v][1] + 1]
            csb = {nm: B_(nm, ci) for nm in ("MT", "Aqk", "kend", "qdec")}
            Rt, vn, Sm, Sbf = t["Rt"], t["vn"], t["Sm"], t["Sbf"]
            p1 = hb_alloc()
            for v, (d, hl) in enumerate(VH):
                P.op("pe", MM(hp(p1)[:, HS[v]], kT[:, hl, cols[v]], Sbf[:, HS[v]]), r=[kTb(hl, mts[v]), B_("Sbf")], w=[hbb(p1)])
            yield
            for v, (d, hl) in enumerate(VH):
                P.op("dve", STT(Rt[:, HS[v]], hp(p1)[:, HS[v]], scol("nec", v), vtm[:, nn[v], hl * 128:(hl + 1) * 128],
                                ALU.mult, ALU.add), r=[hbb(p1), scb, vtb(nn[v])], w=[B_("Rt")])
            hb_free(p1)
            yield
            p2 = hb_alloc()
            for v in range(NV):
                P.op("pe", MM(hp(p2)[:, HS[v]], cs_["MT"][:, HS[v]], Rt[:, HS[v]]), r=[csb["MT"], B_("Rt")], w=[hbb(p2)])
            yield
            P.op("act", ACTF(vn[:], hp(p2)[:], AF.Copy), r=[hbb(p2)], w=[B_("vn")])
            hb_free(p2)
            yield
            p3, p4 = hb_alloc(), hb_alloc()
            for v in range(NV):
                P.op("pe", MM(hp(p4)[:, HS[v]], cs_["kend"][:, HS[v]], vn[:, HS[v]]), r=[csb["kend"], B_("vn")], w=[hbb(p4)])
            for v in range(NV):
                P.op("pe", MM(hp(p3)[:, HS[v]], Sbf[:, HS[v]], cs_["qdec"][:, HS[v]], True, False),
                     r=[B_("Sbf"), csb["qdec"]], w=[hbb(p3)])
                P.op("pe", MM(hp(p3)[:, HS[v]], vn[:, HS[v]], cs_["Aqk"][:, HS[v]], False, True),
                     r=[B_("vn"), csb["Aqk"]], w=[hbb(p3)])
            yield
            for v in range(NV):
                P.op("dve", STT(Sm[:, HS[v]], Sm[:, HS[v]], scol("ECL", v), hp(p4)[:, HS[v]], ALU.mult, ALU.add),
                     r=[hbb(p4), scb, B_("Sm")], w=[B_("Sm")])
            for d in range(2):
                vs = slice(d * HPG * 128, (d + 1) * HPG * 128)
                P.op("act", ACTF(oo[d][:, :, cols[d * HPG]], v3p(hp(p3)[:, vs]), AF.Copy), r=[hbb(p3)],
                     w=[oob[d](hl, nn[d * HPG]) for hl in range(HPG)])
            hb_free(p3)
            hb_free(p4)
            yield
            P.op("dve", CP(Sbf[:], Sm[:]), r=[B_("Sm")], w=[B_("Sbf")])

        def step(g_):
            try:
                next(g_)
                return True
            except StopIteration:
                return False

        P.op("dve", MS(t["Sm"][:], 0.0), w=[B_("Sm")])
        P.op("dve", MS(t["Sbf"][:], 0.0), w=[B_("Sbf")])
        pres = {0: pre(0), 1: pre(1)}
        while step(pres[0]):
            step(pres[1])
        for i in range(NT):
            if i + 2 < NT:
                pres[i + 2] = pre(i + 2)
            must = [scan(i)]
            if i + 1 < NT:
                must.append(pres[i + 1])
            extra = pres.get(i + 2)
            while must:
                must = [g_ for g_ in must if step(g_)]
                if extra is not None and not step(extra):
                    extra = None

        P.barrier()
        ph.p = markT
        osum = ph([128, 512], F32, "osum")
        sqo = ph([128, 512], BF16, "sqo")
        rso = ph([128, 512], F32, "rso")
        zs = ph([128, 512], BF16, "zs")
        wz = ph([128, 8, 128], BF16, "wz")
        ong = ph([128, HPG, S], BF16, "ong")
        ongb = lambda hl, mt: P.buf("ong", hl, mt)
        woutg = ph([128, HPG, D], BF16, "woutg")
        woutgb = P.buf("woutg")
        for hl in range(HPG):
            P.dma("pool", woutg[:, hl, :], C.gdn_wout_d[:, h0 + hl, :], dst=woutgb)
        Bn = lambda nm: P.buf("g2n_" + nm)
        for hl in range(HPG):
            h = h0 + hl
            P.dma("pool", wz[:], C.gdn_win_d[24 + h], dst=Bn("wz"))
            for mt in range(4):
                tsl = slice(mt * 512, (mt + 1) * 512)
                rdo = [oob[d](hl, mt * 4 + q_) for d in range(2) for q_ in range(4)]
                P.op("dve", TT(osum[:], of_[:, hl, tsl], ob[:, hl, tsl], ALU.add), r=rdo, w=[Bn("osum")])
                P.op("act", ACTF(sqo[:], osum[:], AF.Square), r=[Bn("osum")], w=[Bn("sqo")])
                pn = next_ps(C)
                P.op("pe", MM(ps[pn][:], C.ones128[:], sqo[:]), r=[Bn("sqo"), P.buf("ones")], w=[psb[pn]])
                rsqrt_from_psum(P, rso[:], ps[pn][:], psb[pn], Bn("rso"))
                pz = next_ps(C)
                for kc in range(8):
                    P.op("pe", MM(ps[pz][:], wz[:, kc, :], hn[:, kc, tsl], kc == 0, kc == 7),
                         r=[Bn("wz"), hnb(kc, mt)], w=[psb[pz]])
                P.op("act", ACTF(zs[:], ps[pz][:], AF.Silu), r=[psb[pz]], w=[Bn("zs")])
                P.op("dve", TT(rso[:], rso[:], zs[:], ALU.mult), r=[Bn("rso"), Bn("zs")], w=[Bn("rso")])
                P.op("dve", STT(ong[:, hl, tsl], osum[:], vcol(C, "gdn_norm"), rso[:], ALU.mult, ALU.mult),
                     r=[Bn("osum"), Bn("rso"), C.bconst], w=[ongb(hl, mt)])
        for mt in range(4):
            tsl = slice(mt * 512, (mt + 1) * 512)
            for db in range(8):
                po = next_ps(C)
                for hl in range(HPG):
                    P.op("pe", MM(ps[po][:], woutg[:, hl, db * 128:(db + 1) * 128], ong[:, hl, tsl], hl == 0, hl == HPG - 1),
                         r=[woutgb, ongb(hl, mt)], w=[psb[po]])
                P.op("dve", TT(C.hT[:, db, tsl], C.hT[:, db, tsl], ps[po][:], ALU.add),
                     r=[psb[po], hbuf(C, db, mt)], w=[hbuf(C, db, mt)])
        P.barrier()


def final_norm_store(C, sq):
    P, ps, psb = C.P, C.ps, C.psb
    ph = C.phase
    sqt = [ph([128, 512], BF16, "sq%d" % i) for i in range(2)]
    rstd = [ph([128, 512], F32, "rstd%d" % i) for i in range(2)]
    yt = ph([128, 8, 512], F32, "y")
    ot = [ph([128, D], F32, "ot%d" % i) for i in range(2)]
    for mt in range(4):
        tsl = slice(mt * 512, (mt + 1) * 512)
        rb = P.buf("rstd", mt % 2)
        rt = rstd[mt % 2]
        rms_rstd(C, mt, sqt, rt, rb)
        yb = P.buf("fy")
        for kb in range(8):
            P.op("dve", lambda e, kb=kb, rt=rt, tsl=tsl: e.scalar_tensor_tensor(
                out=yt[:, kb, :], in0=C.hT[:, kb, tsl], scalar=vcol(C, "norm_final", kb),
                in1=rt[:], op0=ALU.mult, op1=ALU.mult), r=[hbuf(C, kb, mt), rb, C.bconst], w=[yb])
        for q in range(4):
            tt = mt * 4 + q
            ob = P.buf("fot", tt % 2)
            o_t = ot[tt % 2]
            for half in range(2):
                pi2 = next_ps(C)
                for j in range(4):
                    kb = half * 4 + j
                    P.op("pe", lambda e, pi2=pi2, j=j, kb=kb, q=q: e.matmul(
                        ps[pi2][:, j * 128:(j + 1) * 128], yt[:, kb, q * 128:(q + 1) * 128], C.ident,
                        start=True, stop=True), r=[yb, C.bconst], w=[psb[pi2]])
                P.op("act", lambda e, pi2=pi2, half=half, o_t=o_t: e.activation(
                    out=o_t[:, half * 512:(half + 1) * 512], in_=ps[pi2][:], func=AF.Copy),
                    r=[psb[pi2]], w=[ob])
            P.dma("sp", C.out[sq, tt * 128:(tt + 1) * 128, :], o_t[:], src=ob)


def vec_layout():
    lay = {}
    off = [0]

    def add(name, n):
        lay[name] = (off[0], n)
        off[0] += n
    add("norm_final", 8)
    for l in range(2):
        add("norm_mix%d" % l, 8)
        add("norm_ffn%d" % l, 8)
        add("ffn_conv_w%d" % l, NCB * 3)
        add("ffn_conv_b%d" % l, NCB)
    add("gla_norm", 1)
    add("gdn_conv_w", 24 * 3)
    add("gdn_norm", 1)
    return lay, off[0]


VLAY, NVEC = vec_layout()


def make_vecs(inputs):
    v = np.zeros((128, NVEC), np.float32)

    def put(name, arr):
        o, n = VLAY[name]
        v[:, o:o + n] = np.asarray(arr, np.float32).reshape(128, n)
    f = lambda a: np.asarray(a, np.float32)
    put("norm_final", f(inputs["norm_final"]).reshape(8, 128).T)
    for l in range(2):
        put("norm_mix%d" % l, f(inputs["norm_mix"])[l].reshape(8, 128).T)
        put("norm_ffn%d" % l, f(inputs["norm_ffn"])[l].reshape(8, 128).T)
        put("ffn_conv_w%d" % l, f(inputs["ffn_conv_w"])[l].reshape(3, NCB, 128).transpose(2, 1, 0))
        put("ffn_conv_b%d" % l, f(inputs["ffn_conv_b"])[l].reshape(NCB, 128).T)
    put("gla_norm", f(inputs["gla_norm"])[0].reshape(128, 1))
    put("gdn_conv_w", f(inputs["gdn_conv_w"])[0].reshape(3, 24, 128).transpose(2, 1, 0))
    put("gdn_norm", f(inputs["gdn_norm"])[0].reshape(128, 1))
    return v


def make_weights(inputs):
    f = lambda a: np.asarray(a, np.float32)
    w = {}
    wu = f(inputs["ffn_w_up"]).reshape(2, 8, 128, 2 * NCB, 128)
    w["ffn_wup"] = np.ascontiguousarray(wu.transpose(0, 3, 2, 1, 4))
    w["ffn_wdn"] = np.ascontiguousarray(f(inputs["ffn_w_down"]).reshape(2, NCB, 128, D))
    tile_k = lambda W: np.ascontiguousarray(W.reshape(8, 128, -1).transpose(1, 0, 2))
    w["ab_win"] = tile_k(f(inputs["ab_w_in"])[0])
    w["ab_wout"] = tile_k(f(inputs["ab_w_out"])[0])
    wg = np.zeros((2, 32, 256), np.float32)
    wg[0, :16] = f(inputs["gla_w_gate_fwd"])[0]
    wg[0, 16] = f(inputs["gla_b_gate_fwd"])[0]
    wg[1, :16] = f(inputs["gla_w_gate_bwd"])[0]
    wg[1, 16] = f(inputs["gla_b_gate_bwd"])[0]
    w["gla_wg"] = wg
    w["sgu_wsT"] = np.ascontiguousarray(f(inputs["sgu_w_s"])[0].transpose(2, 0, 1))
    w["sgu_ln"] = np.stack([f(inputs["sgu_ln_g"])[0], f(inputs["sgu_ln_b"])[0]])
    w["sgu_bs"] = np.ascontiguousarray(f(inputs["sgu_b_s"])[0].reshape(1, 512))
    gw = f(inputs["gdn_w_in"])[0]
    w["gdn_win"] = np.ascontiguousarray(gw[:, :4096].reshape(8, 128, 32, 128).transpose(2, 1, 0, 3))
    w["gdn_wsm"] = tile_k(gw[:, 4096:4128])
    w["gdn_wout"] = tile_k(f(inputs["gdn_w_out"])[0])
    w["gdn_vec"] = np.concatenate([f(inputs["gdn_dt_bias_fwd"])[0], f(inputs["gdn_dt_bias_bwd"])[0],
                                   f(inputs["gdn_a_log_fwd"])[0], f(inputs["gdn_a_log_bwd"])[0]]).reshape(1, 32)
    return w


def make_consts():
    c = np.zeros((128, NCONST * 128), np.float32)
    j = np.arange(128)[:, None]
    i = np.arange(128)[None, :]
    c[:, 0:128] = np.eye(128)
    c[:, 128:256] = (j <= i)
    c[:, 256:384] = (j >= i)
    c[:, 384:512] = (j <= i) * (-1.0 / 16)
    c[:, 512:640] = (j >= i) * (-1.0 / 16)
    big = 30000.0
    c[:, 640:768] = ((j <= i) - 1.0) * big
    c[:, 768:896] = ((j < i) - 1.0) * big
    c[:, 896:1024] = ((j >= i) - 1.0) * big
    c[:, 1024:1152] = ((j > i) - 1.0) * big
    c[127, 1152:1280] = 1.0
    c[0, 1280:1408] = 1.0
    c[:, 1408:1536] = 1.0
    return c


_NC_CACHE = {}


def make_in_map(inputs, c, nseq):
    x = np.asarray(inputs["x"], np.float32)
    m = {"x": np.ascontiguousarray(x[c * nseq:(c + 1) * nseq]), "vecs": make_vecs(inputs),
         "consts": make_consts()}
    m.update(make_weights(inputs))
    return m


def kernel(**inputs):
    B = np.asarray(inputs["x"]).shape[0]
    nseq = B // N_CORES
    if nseq not in _NC_CACHE:
        _NC_CACHE[nseq] = build_program(nseq)
    nc = _NC_CACHE[nseq]
    in_maps = [make_in_map(inputs, c, nseq) for c in range(N_CORES)]
    res = run_bass_kernel_spmd(nc, in_maps, core_ids=list(range(N_CORES)))
    return np.concatenate([r["out"] for r in res.results], axis=0)
```

```python
import numpy as np
import concourse.bass as bass
import concourse.mybir as mybir
from concourse.bass_utils import run_bass_kernel_spmd

F32 = mybir.dt.float32
BF16 = mybir.dt.bfloat16
AF = mybir.ActivationFunctionType
ALU = mybir.AluOpType
AX = mybir.AxisListType

D = 1024
S = 2048
NT = S // 128
FF = 2816
NCB = FF // 128
NCONST = 12
HPG = 2
NLEV = 4
EPS = 1e-6
N_CORES = 8


class Buf:
    __slots__ = ("name", "w", "r", "wsem", "wcnt", "rsem", "rcnt", "excl")

    def __init__(self, name):
        self.name = name
        self.excl = False
        self.w = None
        self.r = []
        self.wsem = None
        self.wcnt = 0
        self.rsem = None
        self.rcnt = 0


class Op:
    __slots__ = ("eng", "fn", "deps", "dma", "event", "src", "dst", "id")


class Prog:
    MAXV = 60000
    MAXD = 56000

    def __init__(self, nc):
        self.nc = nc
        self.engs = {"pe": nc.tensor, "act": nc.scalar, "dve": nc.vector,
                     "pool": nc.gpsimd, "sp": nc.sync}
        self.ops = []
        self.bufs = {}
        self.last = {}
        self.dma_out = []
        self.nsem = 0

    def buf(self, *key):
        b = self.bufs.get(key)
        if b is None:
            b = Buf(key)
            self.bufs[key] = b
        return b

    def _deps(self, o, r, w, is_dma):
        deps = set()
        for b in r:
            if b.w is not None:
                deps.add(b.w)
            if b.excl:
                for x in b.r:
                    if self.ops[x].eng != o.eng:
                        deps.add(x)
        for b in w:
            if b.w is not None:
                deps.add(b.w)
            last = {}
            for x in b.r:
                ox = self.ops[x]
                if ox.dma:
                    deps.add(x)
                else:
                    last[ox.eng] = max(last.get(ox.eng, -1), x)
            deps.update(last.values())
        if not is_dma:
            raw = set(b.w for b in r if b.w is not None)
            deps = set(d for d in deps
                       if d in raw or self.ops[d].eng != o.eng or self.ops[d].dma)
        o.deps = deps
        for b in r:
            b.r.append(o.id)
        for b in w:
            b.w = o.id
            b.r = []

    def op(self, eng, fn, r=(), w=()):
        o = Op()
        o.id = len(self.ops)
        o.eng = eng
        o.fn = fn
        o.dma = False
        o.event = None
        o.src = o.dst = None
        self._deps(o, r, w, False)
        self.ops.append(o)
        self.last[eng] = o.id
        return o.id

    def dma(self, q, out, in_, src=None, dst=None):
        o = Op()
        o.id = len(self.ops)
        o.eng = q
        o.fn = lambda e, out=out, in_=in_: e.dma_start(out=out, in_=in_)
        o.dma = True
        o.event = None
        o.src = src
        o.dst = dst
        self._deps(o, [src] if src is not None else [], [dst] if dst is not None else [], True)
        self.ops.append(o)
        self.dma_out.append(o.id)
        return o.id

    def barrier(self):
        deps = set(self.last.values()) | set(self.dma_out)
        for eng in ("pe", "act", "dve", "pool", "sp"):
            o = Op()
            o.id = len(self.ops)
            o.eng = eng
            o.fn = None
            o.dma = False
            o.event = None
            o.src = o.dst = None
            o.deps = set(deps)
            self.ops.append(o)
        self.dma_out = []
        for b in self.bufs.values():
            b.w = None
            b.r = []

    def _newsem(self):
        self.nsem += 1
        return self.nc.alloc_semaphore("s%d" % self.nsem)

    def emit(self):
        nc = self.nc
        has_dep = set()
        for o in self.ops:
            has_dep |= o.deps
        cur = {}
        cnt = {}
        known = {e: {} for e in self.engs}
        for o in self.ops:
            E = self.engs[o.eng]
            waits = {}
            for d in o.deps:
                ev = self.ops[d].event
                if ev is None:
                    continue
                s, v = ev
                k = id(s)
                if known[o.eng].get(k, 0) >= v:
                    continue
                if k not in waits or waits[k][1] < v:
                    waits[k] = (s, v)
            wl = list(waits.values())
            for s, v in wl:
                known[o.eng][id(s)] = v
            if o.fn is None:
                for s, v in wl:
                    E.wait_ge(s, v)
                continue
            for s, v in wl[:-1]:
                E.wait_ge(s, v)
            ins = o.fn(E)
            if wl:
                ins._wait_ge(wl[-1][0], wl[-1][1])
            if o.dma:
                b = o.dst if o.dst is not None else o.src
                if o.dst is not None:
                    if b.wsem is None or b.wcnt >= self.MAXD:
                        b.wsem = self._newsem()
                        b.wcnt = 0
                    b.wcnt += 16
                    ins.then_inc(b.wsem, 16)
                    o.event = (b.wsem, b.wcnt)
                else:
                    if b.rsem is None or b.rcnt >= self.MAXD:
                        b.rsem = self._newsem()
                        b.rcnt = 0
                    b.rcnt += 16
                    ins.then_inc(b.rsem, 16)
                    o.event = (b.rsem, b.rcnt)
            elif o.id in has_dep:
                if o.eng not in cur or cnt[o.eng] >= self.MAXV:
                    cur[o.eng] = self._newsem()
                    cnt[o.eng] = 0
                cnt[o.eng] += 1
                ins.then_inc(cur[o.eng], 1)
                o.event = (cur[o.eng], cnt[o.eng])


class Alloc:
    def __init__(self, nc, lo, hi):
        self.nc = nc
        self.lo = lo
        self.hi = hi
        self.p = lo
        self.n = 0

    def reset(self):
        self.p = self.lo

    def __call__(self, shape, dtype, name=None):
        nb = 4 if dtype == F32 else 2
        sz = nb
        for s in shape[1:]:
            sz *= s
        sz = (sz + 63) // 64 * 64
        off = self.p
        self.p += sz
        assert self.p <= self.hi, "SBUF overflow %s %d > %d" % (name, self.p, self.hi)
        self.n += 1
        return self.nc.alloc_sbuf_tensor_at("%s_%d" % (name or "t", self.n), list(shape), dtype,
                                            offset=off)


def rsqrt_from_psum(P, out_ap, ps_ap, ps_buf, out_buf, eps=EPS, scale=1.0):
    P.op("act", lambda e: e.activation(out=out_ap, in_=ps_ap, func=AF.Ln, bias=eps, scale=scale),
         r=[ps_buf], w=[out_buf])
    P.op("act", lambda e: e.activation(out=out_ap, in_=out_ap, func=AF.Exp, scale=-0.5), r=[out_buf], w=[out_buf])


class Ctx:
    pass


def build_program(nseq, stages="all", debug=False):
    nc = bass.Bass("TRN2", target_bir_lowering=False)
    P = Prog(nc)
    C = Ctx()
    C.nc, C.P = nc, P
    C.debug = debug
    import os
    C.gdn_stop = int(os.environ.get('GDN_STOP', '9'))
    C.gdn_sub = int(os.environ.get('GDN_SUB', '99'))
    C.x = nc.dram_tensor("x", [nseq, S, D], F32, kind="ExternalInput").ap()
    C.out = nc.dram_tensor("out", [nseq, S, D], F32, kind="ExternalOutput").ap()
    vecs = nc.dram_tensor("vecs", [128, NVEC], F32, kind="ExternalInput").ap()
    C.wup_d = nc.dram_tensor("ffn_wup", [2, 2 * NCB, 128, 8, 128], F32, kind="ExternalInput").ap()
    C.wdn_d = nc.dram_tensor("ffn_wdn", [2, NCB, 128, D], F32, kind="ExternalInput").ap()
    consts_d = nc.dram_tensor("consts", [128, NCONST * 128], F32, kind="ExternalInput").ap()
    C.ab_win_d = nc.dram_tensor("ab_win", [128, 8, 2592], F32, kind="ExternalInput").ap()
    C.ab_wout_d = nc.dram_tensor("ab_wout", [128, 8, 1024], F32, kind="ExternalInput").ap()
    C.gla_wg_d = nc.dram_tensor("gla_wg", [2, 32, 256], F32, kind="ExternalInput").ap()
    C.sgu_wsT_d = nc.dram_tensor("sgu_wsT", [128, 4, 128], F32, kind="ExternalInput").ap()
    C.sgu_ln_d = nc.dram_tensor("sgu_ln", [2, 512], F32, kind="ExternalInput").ap()
    C.sgu_bs_d = nc.dram_tensor("sgu_bs", [1, 512], F32, kind="ExternalInput").ap()
    C.gdn_win_d = nc.dram_tensor("gdn_win", [32, 128, 8, 128], F32, kind="ExternalInput").ap()
    C.gdn_wsm_d = nc.dram_tensor("gdn_wsm", [128, 8, 32], F32, kind="ExternalInput").ap()
    C.gdn_wout_d = nc.dram_tensor("gdn_wout", [128, 8, 1024], F32, kind="ExternalInput").ap()
    C.gdn_vec_d = nc.dram_tensor("gdn_vec", [1, 32], F32, kind="ExternalInput").ap()

    persist = Alloc(nc, 16 * 1024, 96 * 1024)

    C.ps = [nc.alloc_psum_tensor("ps%d" % i, [128, 512], F32) for i in range(8)]
    C.psb = [P.buf("ps", i) for i in range(8)]
    for b in C.psb:
        b.excl = True
    C.psrr = 0
    C.ps_res = set()

    C.hT = persist([128, 8, S], F32, "hT")
    C.consts = persist([128, NCONST * 128], F32, "consts")
    C.ident = C.consts[:, 0:128]
    C.maskf = C.consts[:, 128:256]
    C.maskb = C.consts[:, 256:384]
    C.tri = [C.consts[:, 384:512], C.consts[:, 512:640]]
    C.mask01 = [C.maskf, C.maskb]
    C.mneg_incl = [C.consts[:, 640:768], C.consts[:, 896:1024]]
    C.mneg_strict = [C.consts[:, 768:896], C.consts[:, 1024:1152]]
    C.esel = [C.consts[:, 1152:1280], C.consts[:, 1280:1408]]
    C.onesf = C.consts[:, 1408:1536]
    C.ones1 = persist([128, 128], BF16, "ones1")
    C.identb = persist([128, 128], BF16, "identb")
    C.onesD = persist([128, 128], BF16, "onesD")
    C.ones128 = persist([128, 128], BF16, "ones128")
    C.vec = persist([128, NVEC], F32, "vecs")
    C.bconst = P.buf("const")
    C.phase = Alloc(nc, persist.p, 224 * 1024 - 256)

    P.dma("sp", C.consts[:], consts_d[:, :], dst=C.bconst)
    P.dma("sp", C.vec[:], vecs[:, :], dst=C.bconst)
    P.op("dve", lambda e: e.tensor_copy(out=C.identb[:], in_=C.ident), r=[C.bconst], w=[P.buf("identb")])
    P.op("dve", lambda e: e.memset(C.onesD[:], 1.0 / D), w=[P.buf("ones")])
    P.op("dve", lambda e: e.memset(C.ones128[:], 1.0 / 128), w=[P.buf("ones")])
    P.op("dve", lambda e: e.memset(C.ones1[:], 1.0), w=[P.buf("ones")])

    for sq in range(nseq):
        load_x(C, sq)
        P.barrier()
        C.phase.reset()
        if stages in ("mix0", "all"):
            mixer0(C)
            P.barrier()
            C.phase.reset()
        if stages in ("ffn0", "all"):
            ffn(C, 0)
            P.barrier()
            C.phase.reset()
        if stages in ("mix1", "all"):
            mixer1(C)
            P.barrier()
            C.phase.reset()
        if stages in ("ffn1", "all"):
            ffn(C, 1)
            P.barrier()
            C.phase.reset()
        final_norm_store(C, sq)
        P.barrier()
        C.phase.reset()

    P.barrier()
    P.emit()
    return nc


def dump(C, name, t, buf_list):
    if not C.debug:
        return
    shape = list(t.shape)
    d = C.nc.dram_tensor("dbg_" + name, shape, t.dtype, kind="ExternalOutput").ap()
    C.P.barrier()
    C.P.dma("sp", d, t[:], src=None, dst=C.P.buf("dbg", name))
    C.P.barrier()


def next_ps(C):
    while True:
        i = C.psrr % 8
        C.psrr += 1
        if i not in C.ps_res:
            return i


def reserve_ps(C, n):
    r = []
    for _ in range(n):
        i = next_ps(C)
        C.ps_res.add(i)
        r.append(i)
    return r


def release_ps(C, banks):
    for i in banks:
        C.ps_res.discard(i)


def hbuf(C, kb, mt):
    return C.P.buf("hT", kb, mt)


def vcol(C, name, j=0, n=1):
    o = VLAY[name][0] + j
    return C.vec[:, o:o + n]


def load_x(C, sq):
    P, ps, psb = C.P, C.ps, C.psb
    xs = [C.phase([128, D], F32, "xin%d" % i) for i in range(2)]
    for tt in range(NT):
        xb = P.buf("xin", tt % 2)
        xt = xs[tt % 2]
        P.dma("sp", xt[:], C.x[sq, tt * 128:(tt + 1) * 128, :], dst=xb)
        for half in range(2):
            pi = next_ps(C)
            for j in range(4):
                kb = half * 4 + j
                P.op("pe", lambda e, pi=pi, j=j, kb=kb, xt=xt: e.matmul(
                    ps[pi][:, j * 128:(j + 1) * 128], xt[:, kb * 128:(kb + 1) * 128], C.ident,
                    start=True, stop=True), r=[xb, C.bconst], w=[psb[pi]])
            mt = tt // 4
            P.op("act", lambda e, pi=pi, half=half, tt=tt: e.activation(
                out=C.hT[:, half * 4:half * 4 + 4, tt * 128:(tt + 1) * 128],
                in_=ps[pi][:].rearrange("p (j t) -> p j t", j=4), func=AF.Copy),
                r=[psb[pi]], w=[hbuf(C, half * 4 + j, mt) for j in range(4)])


def rms_rstd(C, mt, sqt, rt, rb):
    P, ps, psb = C.P, C.ps, C.psb
    tsl = slice(mt * 512, (mt + 1) * 512)
    pi = next_ps(C)
    for kb in range(8):
        sb = P.buf("sqt", kb % 2)
        st = sqt[kb % 2]
        P.op("act", lambda e, st=st, kb=kb: e.activation(out=st[:], in_=C.hT[:, kb, tsl], func=AF.Square),
             r=[hbuf(C, kb, mt)], w=[sb])
        P.op("pe", lambda e, st=st, kb=kb: e.matmul(
            ps[pi][:], C.onesD[:], st[:], start=(kb == 0), stop=(kb == 7)),
            r=[sb, P.buf("ones")], w=[psb[pi]])
    rsqrt_from_psum(P, rt[:], ps[pi][:], psb[pi], rb)


def rmsnorm_to_bf16(C, gname, hn, hnb):
    P = C.P
    sqt = [C.phase([128, 512], BF16, "sq%d" % i) for i in range(2)]
    rstd = [C.phase([128, 512], F32, "rstd%d" % i) for i in range(2)]
    for mt in range(4):
        tsl = slice(mt * 512, (mt + 1) * 512)
        rb = P.buf("rstd", mt % 2)
        rt = rstd[mt % 2]
        rms_rstd(C, mt, sqt, rt, rb)
        for kb in range(8):
            P.op("dve", lambda e, kb=kb, rt=rt, tsl=tsl: e.scalar_tensor_tensor(
                out=hn[:, kb, tsl], in0=C.hT[:, kb, tsl], scalar=vcol(C, gname, kb),
                in1=rt[:], op0=ALU.mult, op1=ALU.mult), r=[hbuf(C, kb, mt), rb, C.bconst], w=[hnb(kb, mt)])


def ffn(C, l):
    P, ps, psb = C.P, C.ps, C.psb
    ph = C.phase
    hn = ph([128, 8, S], BF16, "hn")
    hnb = lambda kb, mt: P.buf("hn", kb, mt)
    rmsnorm_to_bf16(C, "norm_ffn%d" % l, hn, hnb)
    GRP = 8
    act = [ph([128, S], BF16, "act%d" % i) for i in range(GRP)]
    wd = [ph([128, D], BF16, "wd%d" % i) for i in range(2 * GRP)]
    wg = [ph([128, 8, 128], BF16, "wg%d" % i) for i in range(2)]
    wu = [ph([128, 8, 128], BF16, "wu%d" % i) for i in range(2)]
    G = ph([128, S + 2], F32, "gbuf")
    cv = [ph([128, 512], F32, "cv%d" % i) for i in range(2)]
    sg = [ph([128, 512], F32, "sg%d" % i) for i in range(2)]
    gb = lambda mt: P.buf("gbuf", mt)
    gedge = P.buf("gbuf_edge")
    P.op("dve", lambda e: e.memset(G[:, 0:1], 0.0), w=[gedge])
    P.op("dve", lambda e: e.memset(G[:, S + 1:S + 2], 0.0), w=[gedge])
    cwo = VLAY["ffn_conv_w%d" % l][0]
    cbo = VLAY["ffn_conv_b%d" % l][0]
    groups = []
    c0 = 0
    while c0 < NCB:
        groups.append(list(range(c0, min(c0 + GRP, NCB))))
        c0 += GRP
    wdslot = 0
    for grp in groups:
        slots = {}
        for cb in grp:
            a_t = act[cb % GRP]
            ab = lambda mt, cb=cb: P.buf("act", cb % GRP, mt)
            wgt, wut = wg[cb % 2], wu[cb % 2]
            wgb, wub = P.buf("wg", cb % 2), P.buf("wu", cb % 2)
            ws = wdslot % (2 * GRP)
            wdslot += 1
            slots[cb] = ws
            wdb = P.buf("wd", ws)
            P.dma("pool", wgt[:], C.wup_d[l, cb], dst=wgb)
            P.dma("pool", wut[:], C.wup_d[l, NCB + cb], dst=wub)
            P.dma("pool", wd[ws][:], C.wdn_d[l, cb], dst=wdb)
            for mt in range(4):
                pi = next_ps(C)
                for kc in range(8):
                    P.op("pe", lambda e, pi=pi, kc=kc, mt=mt, wgt=wgt: e.matmul(
                        ps[pi][:], wgt[:, kc, :], hn[:, kc, mt * 512:(mt + 1) * 512],
                        start=(kc == 0), stop=(kc == 7)), r=[wgb, hnb(kc, mt)], w=[psb[pi]])
                P.op("act", lambda e, pi=pi, mt=mt: e.activation(
                    out=G[:, 1 + mt * 512:1 + (mt + 1) * 512], in_=ps[pi][:], func=AF.Copy),
                    r=[psb[pi]], w=[gb(mt)])
            for mt in range(4):
                pi = next_ps(C)
                for kc in range(8):
                    P.op("pe", lambda e, pi=pi, kc=kc, mt=mt, wut=wut: e.matmul(
                        ps[pi][:], wut[:, kc, :], hn[:, kc, mt * 512:(mt + 1) * 512],
                        start=(kc == 0), stop=(kc == 7)), r=[wub, hnb(kc, mt)], w=[psb[pi]])
                cvt, cvb = cv[mt % 2], P.buf("cv", mt % 2)
                sgt, sgb = sg[mt % 2], P.buf("sg", mt % 2)
                rd = [gb(mt), gedge, C.bconst]
                if mt > 0:
                    rd.append(gb(mt - 1))
                if mt < 3:
                    rd.append(gb(mt + 1))
                b0 = mt * 512
                w_ = lambda k, cb=cb: C.vec[:, cwo + cb * 3 + k:cwo + cb * 3 + k + 1]
                P.op("dve", lambda e, cvt=cvt, b0=b0, w_=w_: e.tensor_scalar(
                    out=cvt[:], in0=G[:, b0:b0 + 512], scalar1=w_(0), scalar2=None, op0=ALU.mult),
                    r=rd, w=[cvb])
                P.op("dve", lambda e, cvt=cvt, b0=b0, w_=w_: e.scalar_tensor_tensor(
                    out=cvt[:], in0=G[:, b0 + 1:b0 + 513], scalar=w_(1), in1=cvt[:],
                    op0=ALU.mult, op1=ALU.add), r=rd + [cvb], w=[cvb])
                P.op("dve", lambda e, cvt=cvt, b0=b0, w_=w_: e.scalar_tensor_tensor(
                    out=cvt[:], in0=G[:, b0 + 2:b0 + 514], scalar=w_(2), in1=cvt[:],
                    op0=ALU.mult, op1=ALU.add), r=rd + [cvb], w=[cvb])
                P.op("act", lambda e, cvt=cvt, sgt=sgt, cb=cb: e.activation(
                    out=sgt[:], in_=cvt[:], func=AF.Silu, bias=C.vec[:, cbo + cb:cbo + cb + 1], scale=1.0),
                    r=[cvb, C.bconst], w=[sgb])
                P.op("dve", lambda e, sgt=sgt, pi=pi, a_t=a_t, b0=b0: e.tensor_tensor(
                    out=a_t[:, b0:b0 + 512], in0=sgt[:], in1=ps[pi][:], op=ALU.mult),
                    r=[sgb, psb[pi]], w=[ab(mt)])
        for mt in range(4):
            for db in range(8):
                pi = next_ps(C)
                for i, cb in enumerate(grp):
                    P.op("pe", lambda e, pi=pi, cb=cb, db=db, mt=mt, i=i, n=len(grp), ws=slots[cb]: e.matmul(
                        ps[pi][:], wd[ws][:, db * 128:(db + 1) * 128], act[cb % GRP][:, mt * 512:(mt + 1) * 512],
                        start=(i == 0), stop=(i == n - 1)),
                        r=[P.buf("wd", slots[cb]), P.buf("act", cb % GRP, mt)], w=[psb[pi]])
                P.op("dve", lambda e, pi=pi, db=db, mt=mt: e.tensor_tensor(
                    out=C.hT[:, db, mt * 512:(mt + 1) * 512], in0=C.hT[:, db, mt * 512:(mt + 1) * 512],
                    in1=ps[pi][:], op=ALU.add), r=[psb[pi], hbuf(C, db, mt)], w=[hbuf(C, db, mt)])


def MM(out, lhsT, rhs, start=True, stop=True):
    return lambda e: e.matmul(out, lhsT, rhs, start=start, stop=stop)


def ACTF(out, in_, func, **kw):
    return lambda e: e.activation(out=out, in_=in_, func=func, **kw)


def TT(out, in0, in1, op):
    return lambda e: e.tensor_tensor(out=out, in0=in0, in1=in1, op=op)


def STT(out, in0, scalar, in1, op0, op1):
    return lambda e: e.scalar_tensor_tensor(out=out, in0=in0, scalar=scalar, in1=in1, op0=op0, op1=op1)


def TS(out, in0, s1, s2, op0, op1=None):
    if op1 is None:
        return lambda e: e.tensor_scalar(out=out, in0=in0, scalar1=s1, scalar2=None, op0=op0)
    return lambda e: e.tensor_scalar(out=out, in0=in0, scalar1=s1, scalar2=s2, op0=op0, op1=op1)


def CP(out, in_):
    return lambda e: e.tensor_copy(out=out, in_=in_)


def MS(out, val):
    return lambda e: e.memset(out, val)


def RCP(out, in_):
    return lambda e: e.reciprocal(out=out, in_=in_)


def mixer0(C):
    P, ps, psb, ph = C.P, C.ps, C.psb, C.phase
    hn = ph([128, 8, S], BF16, "hn")
    hnb = lambda kb, mt: P.buf("hn", kb, mt)
    oa = ph([128, 4, S], BF16, "oa")
    oab = lambda h, mt: P.buf("oa", h, mt)
    mark1 = ph.p
    rmsnorm_to_bf16(C, "norm_mix0", hn, hnb)
    P.barrier()
    ph.p = mark1
    qd = [ph([128, 2, S], BF16, "qd%d" % d) for d in range(2)]
    ki = [ph([128, 2, S], BF16, "ki%d" % d) for d in range(2)]
    qdb = lambda d, blk, mt: P.buf("qd", d, blk, mt)
    kib = lambda d, blk, mt: P.buf("ki", d, blk, mt)
    vt = ph([128, NT, 512], BF16, "vt")
    vtb = lambda tt: P.buf("vt", tt)
    elast = ph([128, 2, 2, NT], F32, "elast")
    elb = lambda mt: P.buf("elast", mt)
    mark2 = ph.p

    win = ph([128, 8, 1056], BF16, "win")
    winb = P.buf("win")
    for kc in range(8):
        P.dma("pool", win[:, kc, 0:1024], C.ab_win_d[:, kc, 0:1024], dst=winb)
    P.dma("pool", win[:, :, 1024:1056], C.ab_win_d[:, :, 1536:1568], dst=winb)
    wgg = ph([32, 2, 256], BF16, "wgg")
    wggb = P.buf("wgg")
    P.dma("pool", wgg[:], C.gla_wg_d.rearrange("d r c -> r d c"), dst=wggb)
    lra = [ph([32, 512], BF16, "lra%d" % d) for d in range(2)]
    lrab = [P.buf("lra", d) for d in range(2)]
    for d in range(2):
        P.op("dve", MS(lra[d][:], 1.0), w=[lrab[d]])
    e_t = [ph([128, 512], F32, "e%d" % i) for i in range(2)]
    sp_t = [ph([128, 512], F32, "sp%d" % i) for i in range(2)]
    Ep = [ph([128, 512], F32, "Ep%d" % d) for d in range(2)]
    Em = [ph([128, 512], F32, "Em%d" % d) for d in range(2)]
    for mt in range(4):
        tsl = slice(mt * 512, (mt + 1) * 512)
        for d in range(2):
            pi = next_ps(C)
            for kc in range(8):
                P.op("pe", MM(ps[pi][0:16, :], win[:, kc, 1024 + d * 16:1024 + (d + 1) * 16], hn[:, kc, tsl],
                              kc == 0, kc == 7), r=[winb, hnb(kc, mt)], w=[psb[pi]])
            P.op("act", ACTF(lra[d][0:16, :], ps[pi][0:16, :], AF.Copy), r=[psb[pi]], w=[lrab[d]])
        cpi = reserve_ps(C, 4)
        for q in range(4):
            tt = mt * 4 + q
            pi = next_ps(C)
            for d in range(2):
                P.op("pe", MM(ps[pi][:, d * 256:(d + 1) * 256], lra[d][:, q * 128:(q + 1) * 128], wgg[:, d, :]),
                     r=[lrab[d], wggb], w=[psb[pi]])
            et, etb = e_t[q % 2], P.buf("e_t", q % 2)
            spt, spb = sp_t[q % 2], P.buf("sp_t", q % 2)
            P.op("act", ACTF(et[:], ps[pi][:], AF.Exp, scale=-1.0), r=[psb[pi]], w=[etb])
            P.op("act", ACTF(spt[:], et[:], AF.Ln, bias=1.0), r=[etb], w=[spb])
            for d in range(2):
                for blk in range(2):
                    b = cpi[d * 2 + blk]
                    P.op("pe", MM(ps[b][:, q * 128:(q + 1) * 128],
                                  spt[:, d * 256 + blk * 128:d * 256 + (blk + 1) * 128], C.tri[d]),
                         r=[spb, C.bconst], w=[psb[b]])
            pi = next_ps(C)
            for kc in range(8):
                P.op("pe", MM(ps[pi][:], hn[:, kc, tt * 128:(tt + 1) * 128], win[:, kc, 512:1024], kc == 0, kc == 7),
                     r=[winb, hnb(kc, mt)], w=[psb[pi]])
            P.op("act", ACTF(vt[:, tt, :], ps[pi][:], AF.Copy), r=[psb[pi]], w=[vtb(tt)])
        for blk in range(2):
            pq = next_ps(C)
            for kc in range(8):
                P.op("pe", MM(ps[pq][:], win[:, kc, blk * 128:(blk + 1) * 128], hn[:, kc, tsl], kc == 0, kc == 7),
                     r=[winb, hnb(kc, mt)], w=[psb[pq]])
            pk = next_ps(C)
            for kc in range(8):
                P.op("pe", MM(ps[pk][:], win[:, kc, 256 + blk * 128:256 + (blk + 1) * 128], hn[:, kc, tsl],
                              kc == 0, kc == 7), r=[winb, hnb(kc, mt)], w=[psb[pk]])
            for d in range(2):
                b = cpi[d * 2 + blk]
                ep_, epb = Ep[d], P.buf("Ep", d)
                em_, emb = Em[d], P.buf("Em", d)
                P.op("act", ACTF(ep_[:], ps[b][:], AF.Exp), r=[psb[b]], w=[epb])
                P.op("act", ACTF(em_[:], ps[b][:], AF.Exp, scale=-1.0), r=[psb[b]], w=[emb])
                col = 127 if d == 0 else 0
                P.op("dve", CP(elast[:, d, blk, mt * 4:(mt + 1) * 4],
                               ep_[:].rearrange("p (n i) -> p n i", i=128)[:, :, col]), r=[epb], w=[elb(mt)])
                P.op("dve", STT(qd[d][:, blk, tsl], ps[pq][:], 0.125, ep_[:], ALU.mult, ALU.mult),
                     r=[psb[pq], epb], w=[qdb(d, blk, mt)])
                P.op("dve", TT(ki[d][:, blk, tsl], ps[pk][:], em_[:], ALU.mult),
                     r=[psb[pk], emb], w=[kib(d, blk, mt)])
        release_ps(C, cpi)
    dump(C, "qd0", qd[0], None)
    dump(C, "qd1", qd[1], None)
    dump(C, "ki0", ki[0], None)
    dump(C, "ki1", ki[1], None)
    dump(C, "vt", vt, None)
    P.barrier()
    ph.p = mark2

    Sall = [ph([128, NT, 2, 128], BF16, "Sall%d" % d) for d in range(2)]
    mark3 = ph.p
    S2 = [[ph([128, 128], F32, "S2_%d%d" % (d, hp)) for hp in range(2)] for d in range(2)]
    sab = lambda d, n, hp: P.buf("Sall", d, n, hp)
    A_t = [[ph([128, 128], F32, "A%d%d" % (d, hp)) for hp in range(2)] for d in range(2)]
    kin = [ph([128, 128], BF16, "kin%d" % i) for i in range(4)]
    for d in range(2):
        n0 = 0 if d == 0 else NT - 1
        for hp in range(2):
            P.op("dve", MS(S2[d][hp][:], 0.0), w=[P.buf("S2", d, hp)])
            P.op("dve", MS(Sall[d][:, n0, hp, :], 0.0), w=[sab(d, n0, hp)])
    kk = 0
    for step in range(NT):
        for d in range(2):
            n = step if d == 0 else NT - 1 - step
            nxt = n + 1 if d == 0 else n - 1
            if not (0 <= nxt < NT):
                continue
            mt = n // 4
            for hp in range(2):
                pi = next_ps(C)
                kt, ktb = kin[kk % 4], P.buf("kin", kk % 4)
                kk += 1
                P.op("pe", MM(ps[pi][:, 0:128], ki[d][:, hp, n * 128:(n + 1) * 128], C.identb[:]),
                     r=[kib(d, hp, mt), P.buf("identb")], w=[psb[pi]])
                P.op("act", ACTF(kt[:], ps[pi][:, 0:128], AF.Copy), r=[psb[pi]], w=[ktb])
                P.op("pe", MM(ps[pi][:, 128:384], kt[:], vt[:, n, hp * 256:(hp + 1) * 256]),
                     r=[ktb, vtb(n)], w=[psb[pi]])
                sb_ = P.buf("S2", d, hp)
                ab_ = P.buf("A", d, hp)
                P.op("act", ACTF(A_t[d][hp][:], S2[d][hp][:], AF.Copy, scale=elast[:, d, hp, n:n + 1]),
                     r=[sb_, elb(mt)], w=[ab_])
                for hh in range(2):
                    rows = slice(hh * 64, hh * 64 + 64)
                    P.op("dve", STT(S2[d][hp][rows, :], ps[pi][rows, 128 + hh * 128:256 + hh * 128],
                                    elast[rows, d, hp, n:n + 1], A_t[d][hp][rows, :], ALU.mult, ALU.add),
                         r=[psb[pi], ab_, elb(mt)], w=[sb_])
                P.op("act", ACTF(Sall[d][:, nxt, hp, :], S2[d][hp][:], AF.Copy), r=[sb_], w=[sab(d, nxt, hp)])
    dump(C, "Sall0", Sall[0], None)
    dump(C, "Sall1", Sall[1], None)
    P.barrier()
    ph.p = mark3

    wgate = ph([128, 8, 512], BF16, "wgate")
    wgb = P.buf("wgate")
    for kc in range(8):
        P.dma("pool", wgate[:, kc, :], C.ab_win_d[:, kc, 1024:1536], dst=wgb)
    mask4 = [ph([128, 512], BF16, "mask4_%d" % d) for d in range(2)]
    m4b = P.buf("mask4")
    for d in range(2):
        src = C.maskf if d == 0 else C.maskb
        for q in range(4):
            P.op("dve", CP(mask4[d][:, q * 128:(q + 1) * 128], src), r=[C.bconst], w=[m4b])
    PT = [[ph([128, 512], BF16, "PT%d%d" % (d, i)) for i in range(2)] for d in range(2)]
    sqh = [ph([128, 512], BF16, "sqh")] * 2
    rsh = [ph([128, 512], F32, "rsh%d" % i) for i in range(2)]
    sgt = ph([128, 512], F32, "sgt")
    tmp = ph([128, 512], F32, "tmp")
    it = 0
    for mt in range(4):
        tsl = slice(mt * 512, (mt + 1) * 512)
        for h in range(4):
            hp, hh = h // 2, h % 2
            rows = slice(hh * 64, hh * 64 + 64)
            par = it % 2
            it += 1
            for d in range(2):
                pi = next_ps(C)
                for q in range(4):
                    ns = slice((mt * 4 + q) * 128, (mt * 4 + q + 1) * 128)
                    P.op("pe", MM(ps[pi][:, q * 128:(q + 1) * 128], ki[d][rows, hp, ns], qd[d][rows, hp, ns]),
                         r=[kib(d, hp, mt), qdb(d, hp, mt)], w=[psb[pi]])
                P.op("dve", TT(PT[d][par][:], ps[pi][:], mask4[d][:], ALU.mult),
                     r=[psb[pi], m4b], w=[P.buf("PT", d, par)])
            po = next_ps(C)
            for q in range(4):
                n = mt * 4 + q
                cs = slice(q * 128, (q + 1) * 128)
                ns = slice(n * 128, (n + 1) * 128)
                P.op("pe", MM(ps[po][:, cs], vt[:, n, h * 128:(h + 1) * 128], PT[0][par][:, cs], True, False),
                     r=[vtb(n), P.buf("PT", 0, par)], w=[psb[po]])
                P.op("pe", MM(ps[po][:, cs], vt[:, n, h * 128:(h + 1) * 128], PT[1][par][:, cs], False, False),
                     r=[vtb(n), P.buf("PT", 1, par)], w=[psb[po]])
                for d in range(2):
                    P.op("pe", MM(ps[po][:, cs], Sall[d][rows, n, hp, :], qd[d][rows, hp, ns], False, d == 1),
                         r=[sab(d, n, hp), qdb(d, hp, mt)], w=[psb[po]])
            sqt_, sqb_ = sqh[par], P.buf("sqh", 0)
            P.op("act", ACTF(sqt_[:], ps[po][:], AF.Square), r=[psb[po]], w=[sqb_])
            pn = next_ps(C)
            P.op("pe", MM(ps[pn][:], C.ones128[:], sqt_[:]), r=[sqb_, P.buf("ones")], w=[psb[pn]])
            rt_, rb_ = rsh[par], P.buf("rsh", par)
            rsqrt_from_psum(P, rt_[:], ps[pn][:], psb[pn], rb_)
            pg = next_ps(C)
            for kc in range(8):
                P.op("pe", MM(ps[pg][:], wgate[:, kc, h * 128:(h + 1) * 128], hn[:, kc, tsl], kc == 0, kc == 7),
                     r=[wgb, hnb(kc, mt)], w=[psb[pg]])
            sgb_, tmb_ = P.buf("sgt"), P.buf("tmp")
            P.op("act", ACTF(sgt[:], ps[pg][:], AF.Silu), r=[psb[pg]], w=[sgb_])
            P.op("dve", STT(tmp[:], ps[po][:], vcol(C, "gla_norm"), rt_[:], ALU.mult, ALU.mult),
                 r=[psb[po], rb_, C.bconst], w=[tmb_])
            P.op("dve", TT(oa[:, h, tsl], tmp[:], sgt[:], ALU.mult), r=[tmb_, sgb_], w=[oab(h, mt)])
    dump(C, "oa", oa, None)
    P.barrier()
    ph.p = mark1

    wsu = ph([128, 8, 512], BF16, "wsu")
    wsv = ph([128, 8, 512], BF16, "wsv")
    wout = ph([128, 8, D], BF16, "wout")
    wsub, wsvb, woutb = P.buf("wsu"), P.buf("wsv"), P.buf("wout")
    for kc in range(8):
        P.dma("pool", wsu[:, kc, :], C.ab_win_d[:, kc, 1568:2080], dst=wsub)
        P.dma("pool", wsv[:, kc, :], C.ab_win_d[:, kc, 2080:2592], dst=wsvb)
        P.dma("pool", wout[:, kc, :], C.ab_wout_d[:, kc, :], dst=woutb)
    wsT = ph([128, 4, 128], BF16, "wsT")
    wsTb = P.buf("wsT")
    P.dma("pool", wsT[:], C.sgu_wsT_d, dst=wsTb)
    lng = ph([128, 512], F32, "lng")
    lnbt = ph([128, 512], F32, "lnb")
    bsb4 = ph([128, 4, 4, 128], F32, "bsb4")
    lnbuf = P.buf("lnconst")
    P.dma("sp", lng[:], C.sgu_ln_d[0:1, :].partition_broadcast(128), dst=lnbuf)
    P.dma("sp", lnbt[:], C.sgu_ln_d[1:2, :].partition_broadcast(128), dst=lnbuf)
    for n in range(4):
        P.dma("sp", bsb4[:, :, n, :], C.sgu_bs_d.rearrange("o (g i) -> o g i", g=4).partition_broadcast(128),
              dst=lnbuf)
    ut = [ph([128, 4, 512], BF16, "ut%d" % i) for i in range(2)]
    sgu = [ph([128, 4, 512], BF16, "sgu%d" % i) for i in range(2)]
    gv = [ph([128, 512], F32, "gv%d" % i) for i in range(2)]
    vv = [ph([128, 512], BF16, "vv%d" % i) for i in range(2)]
    st6 = [ph([128, 6], F32, "st6_%d" % i) for i in range(2)]
    mv = [ph([128, 2], F32, "mv%d" % i) for i in range(2)]
    rs1 = [ph([128, 1], F32, "rs1_%d" % i) for i in range(2)]
    tmq = [ph([128, 512], F32, "tmq%d" % i) for i in range(2)]
    for mt in range(4):
        tsl = slice(mt * 512, (mt + 1) * 512)
        u_, sg4 = ut[mt % 2], sgu[mt % 2]
        for g in range(4):
            pu = next_ps(C)
            for kc in range(8):
                P.op("pe", MM(ps[pu][:], wsu[:, kc, g * 128:(g + 1) * 128], hn[:, kc, tsl], kc == 0, kc == 7),
                     r=[wsub, hnb(kc, mt)], w=[psb[pu]])
            P.op("act", ACTF(u_[:, g, :], ps[pu][:], AF.Gelu_apprx_tanh), r=[psb[pu]], w=[P.buf("ut", mt % 2, g)])
        pm = reserve_ps(C, 4)
        for q in range(4):
            tt = mt * 4 + q
            par = q % 2
            pv = next_ps(C)
            for kc in range(8):
                P.op("pe", MM(ps[pv][:], hn[:, kc, tt * 128:(tt + 1) * 128], wsv[:, kc, :], kc == 0, kc == 7),
                     r=[wsvb, hnb(kc, mt)], w=[psb[pv]])
            gvb = P.buf("gv", par)
            P.op("act", ACTF(gv[par][:], ps[pv][:], AF.Gelu_apprx_tanh), r=[psb[pv]], w=[gvb])
            stb, mvb, rsb = P.buf("st6", par), P.buf("mv", par), P.buf("rs1", par)
            P.op("dve", lambda e, par=par: e.bn_stats(out=st6[par][:], in_=gv[par][:]), r=[gvb], w=[stb])
            P.op("dve", lambda e, par=par: e.bn_aggr(out=mv[par][:], in_=st6[par][:]), r=[stb], w=[mvb])
            P.op("act", ACTF(rs1[par][:], mv[par][:, 1:2], AF.Sqrt, bias=EPS), r=[mvb], w=[rsb])
            P.op("dve", RCP(rs1[par][:], rs1[par][:]), r=[rsb], w=[rsb])
            P.op("dve", TS(gv[par][:], gv[par][:], mv[par][:, 0:1], rs1[par][:, 0:1], ALU.subtract, ALU.mult),
                 r=[gvb, mvb, rsb], w=[gvb])
            P.op("dve", TT(gv[par][:], gv[par][:], lng[:], ALU.mult), r=[gvb, lnbuf], w=[gvb])
            vvb = P.buf("vv", par)
            P.op("dve", TT(vv[par][:], gv[par][:], lnbt[:], ALU.add), r=[gvb, lnbuf], w=[vvb])
            for g in range(4):
                P.op("pe", MM(ps[pm[g]][:, q * 128:(q + 1) * 128], vv[par][:, g * 128:(g + 1) * 128], wsT[:, g, :]),
                     r=[vvb, wsTb], w=[psb[pm[g]]])
        for g in range(4):
            tq, tqb = tmq[g % 2], P.buf("tmq", g % 2)
            P.op("dve", TT(tq[:], ps[pm[g]][:], bsb4[:, g, :, :].rearrange("p n i -> p (n i)"), ALU.add),
                 r=[psb[pm[g]], lnbuf], w=[tqb])
            P.op("dve", TT(sg4[:, g, :], tq[:], u_[:, g, :], ALU.mult),
                 r=[tqb, P.buf("ut", mt % 2, g)], w=[P.buf("sgu", mt % 2, g)])
        release_ps(C, pm)
        for db in range(8):
            po = next_ps(C)
            for kc in range(8):
                if kc < 4:
                    rhs, rb_ = oa[:, kc, tsl], oab(kc, mt)
                else:
                    rhs, rb_ = sg4[:, kc - 4, :], P.buf("sgu", mt % 2, kc - 4)
                P.op("pe", MM(ps[po][:], wout[:, kc, db * 128:(db + 1) * 128], rhs, kc == 0, kc == 7),
                     r=[woutb, rb_], w=[psb[po]])
            P.op("dve", TT(C.hT[:, db, tsl], C.hT[:, db, tsl], ps[po][:], ALU.add),
                 r=[psb[po], hbuf(C, db, mt)], w=[hbuf(C, db, mt)])
        if mt == 3:
            dump(C, "sgu", sg4, None)


def mixer1(C):
    P, ps, psb, ph = C.P, C.ps, C.psb, C.phase
    W = HPG * 128
    hn = ph([128, 8, S], BF16, "hn")
    hnb = lambda kb, mt: P.buf("hn", kb, mt)
    sc = {}
    for nm in ("g", "lnb", "beta", "c", "negc", "nec", "ECL", "ekend"):
        sc[nm] = ph([128, NT, 16], F32, "sc_" + nm)
    scb = P.buf("gdn_scalars")
    mark0 = ph.p
    markg0 = ph.p
    wsm = ph([128, 8, 32], F32, "wsm")
    wsmb = P.buf("wsm")
    P.dma("sp", wsm[:], C.gdn_wsm_d, dst=wsmb)
    gvec = ph([128, 32], F32, "gvec")
    gvb = P.buf("gvec")
    P.dma("sp", gvec[:], C.gdn_vec_d.partition_broadcast(128), dst=gvb)
    nea = ph([128, 16], F32, "nea")
    neab = P.buf("nea")
    P.op("act", ACTF(nea[:], gvec[:, 16:32], AF.Exp), r=[gvb], w=[neab])
    P.op("dve", TS(nea[:], nea[:], -1.0, None, ALU.mult), r=[neab], w=[neab])
    ab_tm = ph([128, NT, 32], F32, "ab_tm")
    abb = P.buf("ab_tm")
    t1 = ph([128, NT, 16], F32, "t1")
    t2 = ph([128, NT, 16], F32, "t2")
    t1b, t2b = P.buf("t1"), P.buf("t2")
    hn32 = ph([128, 8, 512], F32, "hn32")
    h32b = lambda kb: P.buf("hn32", kb)
    sqt = [ph([128, 512], BF16, "sq%d" % i) for i in range(2)]
    rstd = [ph([128, 512], F32, "rstd%d" % i) for i in range(2)]
    pa = reserve_ps(C, 1)[0]
    for mt in range(4):
        tsl = slice(mt * 512, (mt + 1) * 512)
        rb = P.buf("rstd", mt % 2)
        rt = rstd[mt % 2]
        rms_rstd(C, mt, sqt, rt, rb)
        for kb in range(8):
            P.op("dve", STT(hn32[:, kb, :], C.hT[:, kb, tsl], vcol(C, "norm_mix1", kb), rt[:], ALU.mult, ALU.mult),
                 r=[hbuf(C, kb, mt), rb, C.bconst], w=[h32b(kb)])
            P.op("act", ACTF(hn[:, kb, tsl], hn32[:, kb, :], AF.Copy), r=[h32b(kb)], w=[hnb(kb, mt)])
        for q in range(4):
            tt = mt * 4 + q
            for kc in range(8):
                P.op("pe", MM(ps[pa][:, tt * 32:(tt + 1) * 32], hn32[:, kc, q * 128:(q + 1) * 128], wsm[:, kc, :],
                              kc == 0, kc == 7), r=[wsmb, h32b(kc)], w=[psb[pa]])
    release_ps(C, [pa])
    P.op("act", ACTF(ab_tm[:].rearrange("p t c -> p (t c)"), ps[pa][:], AF.Copy), r=[psb[pa]], w=[abb])
    for tt in range(NT):
        P.op("dve", TT(t1[:, tt, :], ab_tm[:, tt, 16:32], gvec[:, 0:16], ALU.add), r=[abb, gvb], w=[t1b])
    fl = lambda t: t[:].rearrange("p t c -> p (t c)")
    P.op("act", ACTF(fl(t1), fl(t1), AF.Exp), r=[t1b], w=[t1b])
    P.op("act", ACTF(fl(t1), fl(t1), AF.Ln, bias=1.0), r=[t1b], w=[t1b])
    for tt in range(NT):
        P.op("dve", TT(sc["g"][:, tt, :], t1[:, tt, :], nea[:], ALU.mult), r=[t1b, neab], w=[scb])
    P.op("act", ACTF(t2[:], ab_tm[:, :, 0:16], AF.Exp, scale=-1.0), r=[abb], w=[t2b])
    P.op("act", ACTF(fl(t2), fl(t2), AF.Ln, bias=1.0), r=[t2b], w=[t2b])
    P.op("dve", TS(fl(sc["lnb"]), fl(t2), -1.0, None, ALU.mult), r=[t2b], w=[scb])
    P.op("act", ACTF(fl(sc["beta"]), fl(sc["lnb"]), AF.Exp), r=[scb], w=[scb])
    pc = next_ps(C)
    for tt in range(NT):
        for d in range(2):
            P.op("pe", MM(ps[pc][:, tt * 16 + d * 8:tt * 16 + d * 8 + 8], C.mask01[d], sc["g"][:, tt, d * 8:(d + 1) * 8]),
                 r=[scb, C.bconst], w=[psb[pc]])
    P.op("act", ACTF(fl(sc["c"]), ps[pc][:, 0:NT * 16], AF.Copy), r=[psb[pc]], w=[scb])
    P.op("dve", TS(fl(sc["negc"]), fl(sc["c"]), -1.0, None, ALU.mult), r=[scb], w=[scb])
    P.op("act", ACTF(fl(sc["nec"]), fl(sc["c"]), AF.Exp), r=[scb], w=[scb])
    P.op("dve", TS(fl(sc["nec"]), fl(sc["nec"]), -1.0, None, ALU.mult), r=[scb], w=[scb])
    pl = next_ps(C)
    for d in range(2):
        P.op("pe", MM(ps[pl][:, d * 128:(d + 1) * 128], C.esel[d], sc["c"][:, :, d * 8:(d + 1) * 8]),
             r=[scb, C.bconst], w=[psb[pl]])
    for d in range(2):
        cl = ps[pl][:, d * 128:(d + 1) * 128].rearrange("p (t h) -> p t h", h=8)
        P.op("act", ACTF(sc["ECL"][:, :, d * 8:(d + 1) * 8], cl, AF.Exp), r=[psb[pl]], w=[scb])
        P.op("dve", TT(t1[:, :, d * 8:(d + 1) * 8], cl, sc["c"][:, :, d * 8:(d + 1) * 8], ALU.subtract),
             r=[psb[pl], scb], w=[t1b])
    P.op("act", ACTF(fl(sc["ekend"]), fl(t1), AF.Exp), r=[t1b], w=[scb])
    P.barrier()
    ph.p = markg0
    mark1 = ph.p
    cwo = VLAY["gdn_conv_w"][0]
    if C.gdn_stop == 0:
        return

    for grp in range(8 // HPG):
        h0 = grp * HPG
        ph.p = mark1
        qT = ph([128, HPG, S], BF16, "qT")
        kT = ph([128, HPG, S], BF16, "kT")
        vtm = ph([128, NT, W], BF16, "vtm")
        ob = ph([128, HPG, S], BF16, "ob")
        qTb = lambda hl, mt: P.buf("qT", hl, mt)
        kTb = lambda hl, mt: P.buf("kT", hl, mt)
        vtb = lambda tt: P.buf("vtm", tt)
        obb = lambda hl, n: P.buf("ob", hl, n)
        mark2 = ph.p
        wblk = [ph([128, 8, 128], BF16, "wblk%d" % i) for i in range(2)]
        Gb = ph([128, S + 2], F32, "gbuf")
        cv = [ph([128, 512], F32, "cv%d" % i) for i in range(2)]
        so = [ph([128, 512], F32, "so%d" % i) for i in range(4)]
        sob = [ph([128, 512], BF16, "sob")] * 2
        sqn = [ph([128, 512], BF16, "sqn%d" % i) for i in range(4)]
        rsn = [ph([128, 512], F32, "rsn")] * 2
        gb_ = lambda mt: P.buf("gbuf", mt)
        gedge = P.buf("gbuf_edge")
        P.op("dve", MS(Gb[:, 0:1], 0.0), w=[gedge])
        P.op("dve", MS(Gb[:, S + 1:S + 2], 0.0), w=[gedge])
        wi = 0
        for kind in range(3):
            for hl in range(HPG):
                blk = kind * 8 + h0 + hl
                wt, wtb = wblk[wi % 2], P.buf("wblk", wi % 2)
                wi += 1
                P.dma("pool", wt[:], C.gdn_win_d[blk], dst=wtb)
                for mt in range(4):
                    pi = next_ps(C)
                    for kc in range(8):
                        P.op("pe", MM(ps[pi][:], wt[:, kc, :], hn[:, kc, mt * 512:(mt + 1) * 512], kc == 0, kc == 7),
                             r=[wtb, hnb(kc, mt)], w=[psb[pi]])
                    P.op("act", ACTF(Gb[:, 1 + mt * 512:1 + (mt + 1) * 512], ps[pi][:], AF.Copy),
                         r=[psb[pi]], w=[gb_(mt)])
                pns = reserve_ps(C, 4) if kind < 2 else None
                for mt in range(4):
                    par = mt % 2
                    tsl = slice(mt * 512, (mt + 1) * 512)
                    cvt, cvb = cv[par], P.buf("cv", par)
                    rd = [gb_(mt), gedge, C.bconst]
                    if mt > 0:
                        rd.append(gb_(mt - 1))
                    if mt < 3:
                        rd.append(gb_(mt + 1))
                    b0 = mt * 512
                    w_ = lambda k: C.vec[:, cwo + blk * 3 + k:cwo + blk * 3 + k + 1]
                    P.op("dve", TS(cvt[:], Gb[:, b0:b0 + 512], w_(0), None, ALU.mult), r=rd, w=[cvb])
                    P.op("dve", STT(cvt[:], Gb[:, b0 + 1:b0 + 513], w_(1), cvt[:], ALU.mult, ALU.add), r=rd + [cvb], w=[cvb])
                    P.op("dve", STT(cvt[:], Gb[:, b0 + 2:b0 + 514], w_(2), cvt[:], ALU.mult, ALU.add), r=rd + [cvb], w=[cvb])
                    if kind < 2:
                        sot, sotb = so[mt], P.buf("so", mt)
                        P.op("act", ACTF(sot[:], cvt[:], AF.Silu), r=[cvb], w=[sotb])
                        sqt_, sqb_ = sqn[mt], P.buf("sqn", mt)
                        P.op("act", ACTF(sqt_[:], sot[:], AF.Square), r=[sotb], w=[sqb_])
                        P.op("pe", MM(ps[pns[mt]][:], C.ones1[:], sqt_[:]), r=[sqb_, P.buf("ones")], w=[psb[pns[mt]]])
                    else:
                        sbt, sbtb = sob[par], P.buf("sob", 0)
                        P.op("act", ACTF(sbt[:], cvt[:], AF.Silu), r=[cvb], w=[sbtb])
                        pt = next_ps(C)
                        for q in range(4):
                            P.op("pe", MM(ps[pt][:, q * 128:(q + 1) * 128], sbt[:, q * 128:(q + 1) * 128], C.identb[:]),
                                 r=[sbtb, P.buf("identb")], w=[psb[pt]])
                        P.op("act", ACTF(vtm[:, mt * 4:(mt + 1) * 4, hl * 128:(hl + 1) * 128],
                                         ps[pt][:].rearrange("p (q v) -> p q v", v=128), AF.Copy),
                             r=[psb[pt]], w=[vtb(mt * 4 + q_) for q_ in range(4)])
                if kind < 2:
                    for mt in range(4):
                        tsl = slice(mt * 512, (mt + 1) * 512)
                        sot, sotb = so[mt], P.buf("so", mt)
                        rt_, rb_ = rsn[0], P.buf("rsn", 0)
                        rsqrt_from_psum(P, rt_[:], ps[pns[mt]][:], psb[pns[mt]], rb_)
                        if kind == 0:
                            P.op("dve", STT(qT[:, hl, tsl], sot[:], 128.0 ** -0.5, rt_[:], ALU.mult, ALU.mult),
                                 r=[sotb, rb_], w=[qTb(hl, mt)])
                        else:
                            P.op("dve", TT(kT[:, hl, tsl], sot[:], rt_[:], ALU.mult), r=[sotb, rb_], w=[kTb(hl, mt)])
                    release_ps(C, pns)
        P.barrier()
        ph.p = mark2
        if C.gdn_stop == 1:
            return

        of_ = ph([128, HPG, S], BF16, "of")
        ofb = lambda hl, n: P.buf("of", hl, n)
        oo = [of_, ob]
        oob = [ofb, obb]
        markT = ph.p
        NV = 2 * HPG
        WW = NV * 128
        VH = [(d, hl) for d in range(2) for hl in range(HPG)]
        HS = [slice(v * 128, (v + 1) * 128) for v in range(NV)]
        t = {}
        for nm in ("Rt", "vn", "Sbf", "mniW", "mnsW", "idW", "id2W"):
            t[nm] = ph([128, WW], BF16, nm)
        t["Sm"] = ph([128, WW], F32, "Sm")
        tsets = []
        for k_ in range(2):
            ts_ = {}
            for nm in ("tmpA", "tmpB"):
                ts_[nm] = ph([128, WW], F32, "%s_%d" % (nm, k_))
            for nm in ("DTi", "ETs", "ECb", "X0", "X1", "Y0", "Y1", "P0", "P1", "TN", "ZT", "Rn"):
                ts_[nm] = ph([128, WW], BF16, "%s_%d" % (nm, k_))
            tsets.append(ts_)
        csets = [{nm: ph([128, WW], BF16, "%s%d" % (nm, i)) for nm in ("MT", "Aqk", "kend", "qdec")} for i in range(3)]
        B_ = lambda nm, *k: P.buf("g2_" + nm, *k)
        for v, (d, hl) in enumerate(VH):
            P.op("pool", CP(t["mniW"][:, HS[v]], C.mneg_incl[d]), r=[C.bconst], w=[B_("maskW")])
            P.op("pool", CP(t["mnsW"][:, HS[v]], C.mneg_strict[d]), r=[C.bconst], w=[B_("maskW")])
            P.op("pool", CP(t["idW"][:, HS[v]], C.ident), r=[C.bconst], w=[B_("maskW")])
            P.op("pool", TS(t["id2W"][:, HS[v]], C.ident, 2.0, None, ALU.mult), r=[C.bconst], w=[B_("maskW")])
        v3 = lambda tt_: tt_[:].rearrange("p (h i) -> p h i", i=128)
        v3p = lambda ap: ap.rearrange("p (h i) -> p h i", i=128)
        order = [list(range(NT)), list(range(NT - 1, -1, -1))]
        hb_pool = list(range(8))

        def hb_alloc():
            assert hb_pool, "out of PSUM banks"
            return hb_pool.pop(0)

        def hb_free(x):
            hb_pool.append(x)

        hp = lambda x: ps[x]
        hbb = lambda x: psb[x]

        def pre(i):
            ci = i % 3
            ti = i % 2
            cs_ = csets[ci]
            tp = tsets[ti]
            Bt = lambda nm, *k: P.buf("g2_" + nm, ti, *k)
            nn = [order[d][i] for (d, hl) in VH]
            cols = [slice(n * 128, (n + 1) * 128) for n in nn]
            mts = [n // 4 for n in nn]
            scol = lambda nm, v: sc[nm][:, nn[v], VH[v][0] * 8 + h0 + VH[v][1]:VH[v][0] * 8 + h0 + VH[v][1] + 1]
            bc = lambda nm, v: scol(nm, v).to_broadcast([128, 128])
            pA, pB = hb_alloc(), hb_alloc()
            for v, (d, hl) in enumerate(VH):
                P.op("pe", MM(hp(pA)[:, HS[v]], bc("g", v), C.mask01[d]), r=[scb, C.bconst], w=[hbb(pA)])
            for v, (d, hl) in enumerate(VH):
                P.op("pe", MM(hp(pB)[:, HS[v]], bc("g", v), C.mask01[d], True, False), r=[scb, C.bconst], w=[hbb(pB)])
                P.op("pe", MM(hp(pB)[:, HS[v]], bc("lnb", v), C.ident, False, True), r=[scb, C.bconst], w=[hbb(pB)])
            yield
            P.op("dve", TT(tp["tmpA"][:], hp(pA)[:], t["mniW"][:], ALU.add), r=[hbb(pA), B_("maskW")], w=[Bt("tmpA")])
            P.op("act", ACTF(tp["ECb"][:], hp(pA)[:], AF.Exp), r=[hbb(pA)], w=[Bt("ECb")])
            P.op("dve", TT(tp["tmpB"][:], hp(pB)[:], t["mnsW"][:], ALU.add), r=[hbb(pB), B_("maskW")], w=[Bt("tmpB")])
            hb_free(pA)
            hb_free(pB)
            yield
            pG, pQ = hb_alloc(), hb_alloc()
            for v, (d, hl) in enumerate(VH):
                P.op("pe", MM(hp(pG)[:, HS[v]], kT[:, hl, cols[v]], kT[:, hl, cols[v]]), r=[kTb(hl, mts[v])], w=[hbb(pG)])
                P.op("pe", MM(hp(pQ)[:, HS[v]], kT[:, hl, cols[v]], qT[:, hl, cols[v]]),
                     r=[kTb(hl, mts[v]), qTb(hl, mts[v])], w=[hbb(pQ)])
            for v in range(NV):
                P.op("act", ACTF(tp["ETs"][:, HS[v]], tp["tmpB"][:, HS[v]], AF.Exp, bias=scol("negc", v)),
                     r=[Bt("tmpB"), scb], w=[Bt("ETs")])
            for v in range(NV):
                P.op("act", ACTF(tp["DTi"][:, HS[v]], tp["tmpA"][:, HS[v]], AF.Exp, bias=scol("negc", v)),
                     r=[Bt("tmpA"), scb], w=[Bt("DTi")])
            for d in range(2):
                vs = slice(d * HPG * 128, (d + 1) * HPG * 128)
                P.op("pool", TT(v3p(cs_["qdec"][:, vs]), qT[:, :, cols[d * HPG]], v3p(tp["ECb"][:, vs]), ALU.mult),
                     r=[Bt("ECb")] + [qTb(hl, mts[d * HPG]) for hl in range(HPG)], w=[B_("qdec", ci)])
            yield
            xb = lambda k: Bt("X", k)
            yb = lambda k: Bt("Y", k)
            pb = lambda k: Bt("P", k)
            Xt, Yt, Pt = [tp["X0"], tp["X1"]], [tp["Y0"], tp["Y1"]], [tp["P0"], tp["P1"]]
            P.op("dve", STT(Xt[0][:], hp(pG)[:], -1.0, tp["ETs"][:], ALU.mult, ALU.mult),
                 r=[hbb(pG), Bt("ETs")], w=[xb(0)])
            P.op("dve", TT(cs_["Aqk"][:], hp(pQ)[:], tp["DTi"][:], ALU.mult), r=[hbb(pQ), Bt("DTi")], w=[B_("Aqk", ci)])
            hb_free(pG)
            hb_free(pQ)
            yield
            pY = hb_alloc()
            for v in range(NV):
                P.op("pe", MM(hp(pY)[:, HS[v]], Xt[0][:, HS[v]], C.identb[:]), r=[xb(0), P.buf("identb")], w=[hbb(pY)])
            P.op("pool", TT(Pt[0][:], Xt[0][:], t["idW"][:], ALU.add), r=[xb(0), B_("maskW")], w=[pb(0)])
            yield
            P.op("act", ACTF(Yt[0][:], hp(pY)[:], AF.Copy), r=[hbb(pY)], w=[yb(0)])
            hb_free(pY)
            P.op("pool", TT(tp["TN"][:], t["idW"][:], Yt[0][:], ALU.subtract), r=[yb(0), B_("maskW")], w=[Bt("TN")])
            yield
            cur = 0
            for k in range(1, NLEV + 1):
                nx = 1 - cur
                pX, pY = hb_alloc(), hb_alloc()
                for v in range(NV):
                    P.op("pe", MM(hp(pX)[:, HS[v]], Yt[cur][:, HS[v]], Xt[cur][:, HS[v]]), r=[xb(cur), yb(cur)], w=[hbb(pX)])
                for v in range(NV):
                    P.op("pe", MM(hp(pY)[:, HS[v]], Xt[cur][:, HS[v]], Yt[cur][:, HS[v]]), r=[xb(cur), yb(cur)], w=[hbb(pY)])
                yield
                P.op("act", ACTF(Yt[nx][:], hp(pY)[:], AF.Copy), r=[hbb(pY)], w=[yb(nx)])
                P.op("act", ACTF(Xt[nx][:], hp(pX)[:], AF.Copy), r=[hbb(pX)], w=[xb(nx)])
                hb_free(pX)
                hb_free(pY)
                yield
                pP = hb_alloc()
                for v in range(NV):
                    P.op("pe", MM(hp(pP)[:, HS[v]], Yt[nx][:, HS[v]], Pt[cur][:, HS[v]]),
                         r=[pb(cur), yb(nx)], w=[hbb(pP)])
                yield
                P.op("dve", TT(Pt[nx][:], hp(pP)[:], Pt[cur][:], ALU.add), r=[hbb(pP), pb(cur)], w=[pb(nx)])
                hb_free(pP)
                cur = nx
            yield
            Z = Pt[cur]
            pZ, pE = hb_alloc(), hb_alloc()
            for v in range(NV):
                P.op("pe", MM(hp(pZ)[:, HS[v]], Z[:, HS[v]], C.identb[:]), r=[pb(cur), P.buf("identb")], w=[hbb(pZ)])
                P.op("pe", MM(hp(pE)[:, HS[v]], tp["TN"][:, HS[v]], Z[:, HS[v]]), r=[pb(cur), Bt("TN")], w=[hbb(pE)])
            yield
            P.op("act", ACTF(tp["ZT"][:], hp(pZ)[:], AF.Copy), r=[hbb(pZ)], w=[Bt("ZT")])
            P.op("dve", STT(tp["Rn"][:], hp(pE)[:], -1.0, t["id2W"][:], ALU.mult, ALU.add),
                 r=[hbb(pE), B_("maskW")], w=[Bt("Rn")])
            hb_free(pZ)
            hb_free(pE)
            yield
            pM = hb_alloc()
            for v in range(NV):
                P.op("pe", MM(hp(pM)[:, HS[v]], tp["ZT"][:, HS[v]], tp["Rn"][:, HS[v]]), r=[Bt("ZT"), Bt("Rn")], w=[hbb(pM)])
            pK = hb_alloc()
            for v, (d, hl) in enumerate(VH):
                P.op("pe", MM(hp(pK)[:, HS[v]], kT[:, hl, cols[v]], C.identb[:]), r=[kTb(hl, mts[v]), P.buf("identb")],
                     w=[hbb(pK)])
            yield
            for v in range(NV):
                P.op("act", ACTF(cs_["kend"][:, HS[v]], hp(pK)[:, HS[v]], AF.Copy, scale=scol("ekend", v)),
                     r=[hbb(pK), scb], w=[B_("kend", ci)])
            hb_free(pK)
            for v in range(NV):
                P.op("act", ACTF(cs_["MT"][:, HS[v]], hp(pM)[:, HS[v]], AF.Copy, scale=scol("beta", v)),
                     r=[hbb(pM), scb], w=[B_("MT", ci)])
            hb_free(pM)

        def scan(i):
            ci = i % 3
            cs_ = csets[ci]
            nn = [order[d][i] for (d, hl) in VH]
            cols = [slice(n * 128, (n + 1) * 128) for n in nn]
            mts = [n // 4 for n in nn]
            scol = lambda nm, v: sc[nm][:, nn[v], VH[v][0] * 8 + h0 + VH[v][1]:VH[v][0] * 8 + h0 + VH[v][1] + 1]
            csb = {nm: B_(nm, ci) for nm in ("MT", "Aqk", "kend", "qdec")}
            Rt, vn, Sm, Sbf = t["Rt"], t["vn"], t["Sm"], t["Sbf"]
            p1 = hb_alloc()
            for v, (d, hl) in enumerate(VH):
                P.op("pe", MM(hp(p1)[:, HS[v]], kT[:, hl, cols[v]], Sbf[:, HS[v]]), r=[kTb(hl, mts[v]), B_("Sbf")], w=[hbb(p1)])
            yield
            for v, (d, hl) in enumerate(VH):
                P.op("dve", STT(Rt[:, HS[v]], hp(p1)[:, HS[v]], scol("nec", v), vtm[:, nn[v], hl * 128:(hl + 1) * 128],
                                ALU.mult, ALU.add), r=[hbb(p1), scb, vtb(nn[v])], w=[B_("Rt")])
            hb_free(p1)
            yield
            p2 = hb_alloc()
            for v in range(NV):
                P.op("pe", MM(hp(p2)[:, HS[v]], cs_["MT"][:, HS[v]], Rt[:, HS[v]]), r=[csb["MT"], B_("Rt")], w=[hbb(p2)])
            yield
            P.op("act", ACTF(vn[:], hp(p2)[:], AF.Copy), r=[hbb(p2)], w=[B_("vn")])
            hb_free(p2)
            yield
            p3, p4 = hb_alloc(), hb_alloc()
            for v in range(NV):
                P.op("pe", MM(hp(p4)[:, HS[v]], cs_["kend"][:, HS[v]], vn[:, HS[v]]), r=[csb["kend"], B_("vn")], w=[hbb(p4)])
            for v in range(NV):
                P.op("pe", MM(hp(p3)[:, HS[v]], Sbf[:, HS[v]], cs_["qdec"][:, HS[v]], True, False),
                     r=[B_("Sbf"), csb["qdec"]], w=[hbb(p3)])
                P.op("pe", MM(hp(p3)[:, HS[v]], vn[:, HS[v]], cs_["Aqk"][:, HS[v]], False, True),
                     r=[B_("vn"), csb["Aqk"]], w=[hbb(p3)])
            yield
            for v in range(NV):
                P.op("dve", STT(Sm[:, HS[v]], Sm[:, HS[v]], scol("ECL", v), hp(p4)[:, HS[v]], ALU.mult, ALU.add),
                     r=[hbb(p4), scb, B_("Sm")], w=[B_("Sm")])
            for d in range(2):
                vs = slice(d * HPG * 128, (d + 1) * HPG * 128)
                P.op("act", ACTF(oo[d][:, :, cols[d * HPG]], v3p(hp(p3)[:, vs]), AF.Copy), r=[hbb(p3)],
                     w=[oob[d](hl, nn[d * HPG]) for hl in range(HPG)])
            hb_free(p3)
            hb_free(p4)
            yield
            P.op("dve", CP(Sbf[:], Sm[:]), r=[B_("Sm")], w=[B_("Sbf")])

        def step(g_):
            try:
                next(g_)
                return True
            except StopIteration:
                return False

        P.op("dve", MS(t["Sm"][:], 0.0), w=[B_("Sm")])
        P.op("dve", MS(t["Sbf"][:], 0.0), w=[B_("Sbf")])
        pres = {0: pre(0), 1: pre(1)}
        while step(pres[0]):
            step(pres[1])
        for i in range(NT):
            if i + 2 < NT:
                pres[i + 2] = pre(i + 2)
            must = [scan(i)]
            if i + 1 < NT:
                must.append(pres[i + 1])
            extra = pres.get(i + 2)
            while must:
                must = [g_ for g_ in must if step(g_)]
                if extra is not None and not step(extra):
                    extra = None

        P.barrier()
        ph.p = markT
        osum = ph([128, 512], F32, "osum")
        sqo = ph([128, 512], BF16, "sqo")
        rso = ph([128, 512], F32, "rso")
        zs = ph([128, 512], BF16, "zs")
        wz = ph([128, 8, 128], BF16, "wz")
        ong = ph([128, HPG, S], BF16, "ong")
        ongb = lambda hl, mt: P.buf("ong", hl, mt)
        woutg = ph([128, HPG, D], BF16, "woutg")
        woutgb = P.buf("woutg")
        for hl in range(HPG):
            P.dma("pool", woutg[:, hl, :], C.gdn_wout_d[:, h0 + hl, :], dst=woutgb)
        Bn = lambda nm: P.buf("g2n_" + nm)
        zs4 = [ph([128, 512], BF16, "zs%d" % i) for i in range(4)]
        for hl in range(HPG):
            h = h0 + hl
            P.dma("pool", wz[:], C.gdn_win_d[24 + h], dst=Bn("wz"))
            for mt in range(4):
                tsl = slice(mt * 512, (mt + 1) * 512)
                pz = next_ps(C)
                for kc in range(8):
                    P.op("pe", MM(ps[pz][:], wz[:, kc, :], hn[:, kc, tsl], kc == 0, kc == 7),
                         r=[Bn("wz"), hnb(kc, mt)], w=[psb[pz]])
                P.op("act", ACTF(zs4[mt][:], ps[pz][:], AF.Silu), r=[psb[pz]], w=[P.buf("g2n_zs", mt)])
            for mt in range(4):
                tsl = slice(mt * 512, (mt + 1) * 512)
                rdo = [oob[d](hl, mt * 4 + q_) for d in range(2) for q_ in range(4)]
                P.op("dve", TT(osum[:], of_[:, hl, tsl], ob[:, hl, tsl], ALU.add), r=rdo, w=[Bn("osum")])
                P.op("act", ACTF(sqo[:], osum[:], AF.Square), r=[Bn("osum")], w=[Bn("sqo")])
                pn = next_ps(C)
                P.op("pe", MM(ps[pn][:], C.ones128[:], sqo[:]), r=[Bn("sqo"), P.buf("ones")], w=[psb[pn]])
                rsqrt_from_psum(P, rso[:], ps[pn][:], psb[pn], Bn("rso"))
                P.op("dve", TT(rso[:], rso[:], zs4[mt][:], ALU.mult), r=[Bn("rso"), P.buf("g2n_zs", mt)], w=[Bn("rso")])
                P.op("dve", STT(ong[:, hl, tsl], osum[:], vcol(C, "gdn_norm"), rso[:], ALU.mult, ALU.mult),
                     r=[Bn("osum"), Bn("rso"), C.bconst], w=[ongb(hl, mt)])
        for mt in range(4):
            tsl = slice(mt * 512, (mt + 1) * 512)
            for db in range(8):
                po = next_ps(C)
                for hl in range(HPG):
                    P.op("pe", MM(ps[po][:], woutg[:, hl, db * 128:(db + 1) * 128], ong[:, hl, tsl], hl == 0, hl == HPG - 1),
                         r=[woutgb, ongb(hl, mt)], w=[psb[po]])
                P.op("dve", TT(C.hT[:, db, tsl], C.hT[:, db, tsl], ps[po][:], ALU.add),
                     r=[psb[po], hbuf(C, db, mt)], w=[hbuf(C, db, mt)])
        P.barrier()


def final_norm_store(C, sq):
    P, ps, psb = C.P, C.ps, C.psb
    ph = C.phase
    sqt = [ph([128, 512], BF16, "sq%d" % i) for i in range(2)]
    rstd = [ph([128, 512], F32, "rstd%d" % i) for i in range(2)]
    yt = ph([128, 8, 512], F32, "y")
    ot = [ph([128, D], F32, "ot%d" % i) for i in range(2)]
    for mt in range(4):
        tsl = slice(mt * 512, (mt + 1) * 512)
        rb = P.buf("rstd", mt % 2)
        rt = rstd[mt % 2]
        rms_rstd(C, mt, sqt, rt, rb)
        yb = P.buf("fy")
        for kb in range(8):
            P.op("dve", lambda e, kb=kb, rt=rt, tsl=tsl: e.scalar_tensor_tensor(
                out=yt[:, kb, :], in0=C.hT[:, kb, tsl], scalar=vcol(C, "norm_final", kb),
                in1=rt[:], op0=ALU.mult, op1=ALU.mult), r=[hbuf(C, kb, mt), rb, C.bconst], w=[yb])
        for q in range(4):
            tt = mt * 4 + q
            ob = P.buf("fot", tt % 2)
            o_t = ot[tt % 2]
            for half in range(2):
                pi2 = next_ps(C)
                for j in range(4):
                    kb = half * 4 + j
                    P.op("pe", lambda e, pi2=pi2, j=j, kb=kb, q=q: e.matmul(
                        ps[pi2][:, j * 128:(j + 1) * 128], yt[:, kb, q * 128:(q + 1) * 128], C.ident,
                        start=True, stop=True), r=[yb, C.bconst], w=[psb[pi2]])
                P.op("act", lambda e, pi2=pi2, half=half, o_t=o_t: e.activation(
                    out=o_t[:, half * 512:(half + 1) * 512], in_=ps[pi2][:], func=AF.Copy),
                    r=[psb[pi2]], w=[ob])
            P.dma("sp", C.out[sq, tt * 128:(tt + 1) * 128, :], o_t[:], src=ob)


def vec_layout():
    lay = {}
    off = [0]

    def add(name, n):
        lay[name] = (off[0], n)
        off[0] += n
    add("norm_final", 8)
    for l in range(2):
        add("norm_mix%d" % l, 8)
        add("norm_ffn%d" % l, 8)
        add("ffn_conv_w%d" % l, NCB * 3)
        add("ffn_conv_b%d" % l, NCB)
    add("gla_norm", 1)
    add("gdn_conv_w", 24 * 3)
    add("gdn_norm", 1)
    return lay, off[0]


VLAY, NVEC = vec_layout()


def make_vecs(inputs):
    v = np.zeros((128, NVEC), np.float32)

    def put(name, arr):
        o, n = VLAY[name]
        v[:, o:o + n] = np.asarray(arr, np.float32).reshape(128, n)
    f = lambda a: np.asarray(a, np.float32)
    put("norm_final", f(inputs["norm_final"]).reshape(8, 128).T)
    for l in range(2):
        put("norm_mix%d" % l, f(inputs["norm_mix"])[l].reshape(8, 128).T)
        put("norm_ffn%d" % l, f(inputs["norm_ffn"])[l].reshape(8, 128).T)
        put("ffn_conv_w%d" % l, f(inputs["ffn_conv_w"])[l].reshape(3, NCB, 128).transpose(2, 1, 0))
        put("ffn_conv_b%d" % l, f(inputs["ffn_conv_b"])[l].reshape(NCB, 128).T)
    put("gla_norm", f(inputs["gla_norm"])[0].reshape(128, 1))
    put("gdn_conv_w", f(inputs["gdn_conv_w"])[0].reshape(3, 24, 128).transpose(2, 1, 0))
    put("gdn_norm", f(inputs["gdn_norm"])[0].reshape(128, 1))
    return v


def make_weights(inputs):
    f = lambda a: np.asarray(a, np.float32)
    w = {}
    wu = f(inputs["ffn_w_up"]).reshape(2, 8, 128, 2 * NCB, 128)
    w["ffn_wup"] = np.ascontiguousarray(wu.transpose(0, 3, 2, 1, 4))
    w["ffn_wdn"] = np.ascontiguousarray(f(inputs["ffn_w_down"]).reshape(2, NCB, 128, D))
    tile_k = lambda W: np.ascontiguousarray(W.reshape(8, 128, -1).transpose(1, 0, 2))
    w["ab_win"] = tile_k(f(inputs["ab_w_in"])[0])
    w["ab_wout"] = tile_k(f(inputs["ab_w_out"])[0])
    wg = np.zeros((2, 32, 256), np.float32)
    wg[0, :16] = f(inputs["gla_w_gate_fwd"])[0]
    wg[0, 16] = f(inputs["gla_b_gate_fwd"])[0]
    wg[1, :16] = f(inputs["gla_w_gate_bwd"])[0]
    wg[1, 16] = f(inputs["gla_b_gate_bwd"])[0]
    w["gla_wg"] = wg
    w["sgu_wsT"] = np.ascontiguousarray(f(inputs["sgu_w_s"])[0].transpose(2, 0, 1))
    w["sgu_ln"] = np.stack([f(inputs["sgu_ln_g"])[0], f(inputs["sgu_ln_b"])[0]])
    w["sgu_bs"] = np.ascontiguousarray(f(inputs["sgu_b_s"])[0].reshape(1, 512))
    gw = f(inputs["gdn_w_in"])[0]
    w["gdn_win"] = np.ascontiguousarray(gw[:, :4096].reshape(8, 128, 32, 128).transpose(2, 1, 0, 3))
    w["gdn_wsm"] = tile_k(gw[:, 4096:4128])
    w["gdn_wout"] = tile_k(f(inputs["gdn_w_out"])[0])
    w["gdn_vec"] = np.concatenate([f(inputs["gdn_dt_bias_fwd"])[0], f(inputs["gdn_dt_bias_bwd"])[0],
                                   f(inputs["gdn_a_log_fwd"])[0], f(inputs["gdn_a_log_bwd"])[0]]).reshape(1, 32)
    return w


def make_consts():
    c = np.zeros((128, NCONST * 128), np.float32)
    j = np.arange(128)[:, None]
    i = np.arange(128)[None, :]
    c[:, 0:128] = np.eye(128)
    c[:, 128:256] = (j <= i)
    c[:, 256:384] = (j >= i)
    c[:, 384:512] = (j <= i) * (-1.0 / 16)
    c[:, 512:640] = (j >= i) * (-1.0 / 16)
    big = 30000.0
    c[:, 640:768] = ((j <= i) - 1.0) * big
    c[:, 768:896] = ((j < i) - 1.0) * big
    c[:, 896:1024] = ((j >= i) - 1.0) * big
    c[:, 1024:1152] = ((j > i) - 1.0) * big
    c[127, 1152:1280] = 1.0
    c[0, 1280:1408] = 1.0
    c[:, 1408:1536] = 1.0
    return c


_NC_CACHE = {}


def make_in_map(inputs, c, nseq):
    x = np.asarray(inputs["x"], np.float32)
    m = {"x": np.ascontiguousarray(x[c * nseq:(c + 1) * nseq]), "vecs": make_vecs(inputs),
         "consts": make_consts()}
    m.update(make_weights(inputs))
    return m


def kernel(**inputs):
    B = np.asarray(inputs["x"]).shape[0]
    nseq = B // N_CORES
    if nseq not in _NC_CACHE:
        _NC_CACHE[nseq] = build_program(nseq)
    nc = _NC_CACHE[nseq]
    in_maps = [make_in_map(inputs, c, nseq) for c in range(N_CORES)]
    res = run_bass_kernel_spmd(nc, in_maps, core_ids=list(range(N_CORES)))
    return np.concatenate([r["out"] for r in res.results], axis=0)
```

```python
import numpy as np
import concourse.bass as bass
import concourse.mybir as mybir
from concourse.bass_utils import run_bass_kernel_spmd

F32 = mybir.dt.float32
BF16 = mybir.dt.bfloat16
AF = mybir.ActivationFunctionType
ALU = mybir.AluOpType
AX = mybir.AxisListType

D = 1024
S = 2048
NT = S // 128
FF = 2816
NCB = FF // 128
NCONST = 12
HPG = 2
NLEV = 4
EPS = 1e-6
N_CORES = 8


class Buf:
    __slots__ = ("name", "w", "r", "wsem", "wcnt", "rsem", "rcnt", "excl")

    def __init__(self, name):
        self.name = name
        self.excl = False
        self.w = None
        self.r = []
        self.wsem = None
        self.wcnt = 0
        self.rsem = None
        self.rcnt = 0


class Op:
    __slots__ = ("eng", "fn", "deps", "dma", "event", "src", "dst", "id")


class Prog:
    MAXV = 60000
    MAXD = 56000

    def __init__(self, nc):
        self.nc = nc
        self.engs = {"pe": nc.tensor, "act": nc.scalar, "dve": nc.vector,
                     "pool": nc.gpsimd, "sp": nc.sync}
        self.ops = []
        self.bufs = {}
        self.last = {}
        self.dma_out = []
        self.nsem = 0

    def buf(self, *key):
        b = self.bufs.get(key)
        if b is None:
            b = Buf(key)
            self.bufs[key] = b
        return b

    def _deps(self, o, r, w, is_dma):
        deps = set()
        for b in r:
            if b.w is not None:
                deps.add(b.w)
            if b.excl:
                for x in b.r:
                    if self.ops[x].eng != o.eng:
                        deps.add(x)
        for b in w:
            if b.w is not None:
                deps.add(b.w)
            last = {}
            for x in b.r:
                ox = self.ops[x]
                if ox.dma:
                    deps.add(x)
                else:
                    last[ox.eng] = max(last.get(ox.eng, -1), x)
            deps.update(last.values())
        if not is_dma:
            raw = set(b.w for b in r if b.w is not None)
            deps = set(d for d in deps
                       if d in raw or self.ops[d].eng != o.eng or self.ops[d].dma)
        o.deps = deps
        for b in r:
            b.r.append(o.id)
        for b in w:
            b.w = o.id
            b.r = []

    def op(self, eng, fn, r=(), w=()):
        o = Op()
        o.id = len(self.ops)
        o.eng = eng
        o.fn = fn
        o.dma = False
        o.event = None
        o.src = o.dst = None
        self._deps(o, r, w, False)
        self.ops.append(o)
        self.last[eng] = o.id
        return o.id

    def dma(self, q, out, in_, src=None, dst=None):
        o = Op()
        o.id = len(self.ops)
        o.eng = q
        o.fn = lambda e, out=out, in_=in_: e.dma_start(out=out, in_=in_)
        o.dma = True
        o.event = None
        o.src = src
        o.dst = dst
        self._deps(o, [src] if src is not None else [], [dst] if dst is not None else [], True)
        self.ops.append(o)
        self.dma_out.append(o.id)
        return o.id

    def barrier(self):
        deps = set(self.last.values()) | set(self.dma_out)
        for eng in ("pe", "act", "dve", "pool", "sp"):
            o = Op()
            o.id = len(self.ops)
            o.eng = eng
            o.fn = None
            o.dma = False
            o.event = None
            o.src = o.dst = None
            o.deps = set(deps)
            self.ops.append(o)
        self.dma_out = []
        for b in self.bufs.values():
            b.w = None
            b.r = []

    def _newsem(self):
        self.nsem += 1
        return self.nc.alloc_semaphore("s%d" % self.nsem)

    def emit(self):
        nc = self.nc
        has_dep = set()
        for o in self.ops:
            has_dep |= o.deps
        cur = {}
        cnt = {}
        known = {e: {} for e in self.engs}
        for o in self.ops:
            E = self.engs[o.eng]
            waits = {}
            for d in o.deps:
                ev = self.ops[d].event
                if ev is None:
                    continue
                s, v = ev
                k = id(s)
                if known[o.eng].get(k, 0) >= v:
                    continue
                if k not in waits or waits[k][1] < v:
                    waits[k] = (s, v)
            wl = list(waits.values())
            for s, v in wl:
                known[o.eng][id(s)] = v
            if o.fn is None:
                for s, v in wl:
                    E.wait_ge(s, v)
                continue
            for s, v in wl[:-1]:
                E.wait_ge(s, v)
            ins = o.fn(E)
            if wl:
                ins._wait_ge(wl[-1][0], wl[-1][1])
            if o.dma:
                b = o.dst if o.dst is not None else o.src
                if o.dst is not None:
                    if b.wsem is None or b.wcnt >= self.MAXD:
                        b.wsem = self._newsem()
                        b.wcnt = 0
                    b.wcnt += 16
                    ins.then_inc(b.wsem, 16)
                    o.event = (b.wsem, b.wcnt)
                else:
                    if b.rsem is None or b.rcnt >= self.MAXD:
                        b.rsem = self._newsem()
                        b.rcnt = 0
                    b.rcnt += 16
                    ins.then_inc(b.rsem, 16)
                    o.event = (b.rsem, b.rcnt)
            elif o.id in has_dep:
                if o.eng not in cur or cnt[o.eng] >= self.MAXV:
                    cur[o.eng] = self._newsem()
                    cnt[o.eng] = 0
                cnt[o.eng] += 1
                ins.then_inc(cur[o.eng], 1)
                o.event = (cur[o.eng], cnt[o.eng])


class Alloc:
    def __init__(self, nc, lo, hi):
        self.nc = nc
        self.lo = lo
        self.hi = hi
        self.p = lo
        self.n = 0

    def reset(self):
        self.p = self.lo

    def __call__(self, shape, dtype, name=None):
        nb = 4 if dtype == F32 else 2
        sz = nb
        for s in shape[1:]:
            sz *= s
        sz = (sz + 63) // 64 * 64
        off = self.p
        self.p += sz
        assert self.p <= self.hi, "SBUF overflow %s %d > %d" % (name, self.p, self.hi)
        self.n += 1
        return self.nc.alloc_sbuf_tensor_at("%s_%d" % (name or "t", self.n), list(shape), dtype,
                                            offset=off)


def rsqrt_from_psum(P, out_ap, ps_ap, ps_buf, out_buf, eps=EPS, scale=1.0):
    P.op("act", lambda e: e.activation(out=out_ap, in_=ps_ap, func=AF.Ln, bias=eps, scale=scale),
         r=[ps_buf], w=[out_buf])
    P.op("act", lambda e: e.activation(out=out_ap, in_=out_ap, func=AF.Exp, scale=-0.5), r=[out_buf], w=[out_buf])


class Ctx:
    pass


def build_program(nseq, stages="all", debug=False):
    nc = bass.Bass("TRN2", target_bir_lowering=False)
    P = Prog(nc)
    C = Ctx()
    C.nc, C.P = nc, P
    C.debug = debug
    import os
    C.gdn_stop = int(os.environ.get('GDN_STOP', '9'))
    C.gdn_sub = int(os.environ.get('GDN_SUB', '99'))
    C.x = nc.dram_tensor("x", [nseq, S, D], F32, kind="ExternalInput").ap()
    C.out = nc.dram_tensor("out", [nseq, S, D], F32, kind="ExternalOutput").ap()
    vecs = nc.dram_tensor("vecs", [128, NVEC], F32, kind="ExternalInput").ap()
    C.wup_d = nc.dram_tensor("ffn_wup", [2, 2 * NCB, 128, 8, 128], F32, kind="ExternalInput").ap()
    C.wdn_d = nc.dram_tensor("ffn_wdn", [2, NCB, 128, D], F32, kind="ExternalInput").ap()
    consts_d = nc.dram_tensor("consts", [128, NCONST * 128], F32, kind="ExternalInput").ap()
    C.ab_win_d = nc.dram_tensor("ab_win", [128, 8, 2592], F32, kind="ExternalInput").ap()
    C.ab_wout_d = nc.dram_tensor("ab_wout", [128, 8, 1024], F32, kind="ExternalInput").ap()
    C.gla_wg_d = nc.dram_tensor("gla_wg", [2, 32, 256], F32, kind="ExternalInput").ap()
    C.sgu_wsT_d = nc.dram_tensor("sgu_wsT", [128, 4, 128], F32, kind="ExternalInput").ap()
    C.sgu_ln_d = nc.dram_tensor("sgu_ln", [2, 512], F32, kind="ExternalInput").ap()
    C.sgu_bs_d = nc.dram_tensor("sgu_bs", [1, 512], F32, kind="ExternalInput").ap()
    C.gdn_win_d = nc.dram_tensor("gdn_win", [32, 128, 8, 128], F32, kind="ExternalInput").ap()
    C.gdn_wsm_d = nc.dram_tensor("gdn_wsm", [128, 8, 32], F32, kind="ExternalInput").ap()
    C.gdn_wout_d = nc.dram_tensor("gdn_wout", [128, 8, 1024], F32, kind="ExternalInput").ap()
    C.gdn_vec_d = nc.dram_tensor("gdn_vec", [1, 32], F32, kind="ExternalInput").ap()

    persist = Alloc(nc, 16 * 1024, 96 * 1024)

    C.ps = [nc.alloc_psum_tensor("ps%d" % i, [128, 512], F32) for i in range(8)]
    C.psb = [P.buf("ps", i) for i in range(8)]
    for b in C.psb:
        b.excl = True
    C.psrr = 0
    C.ps_res = set()

    C.hT = persist([128, 8, S], F32, "hT")
    C.consts = persist([128, NCONST * 128], F32, "consts")
    C.ident = C.consts[:, 0:128]
    C.maskf = C.consts[:, 128:256]
    C.maskb = C.consts[:, 256:384]
    C.tri = [C.consts[:, 384:512], C.consts[:, 512:640]]
    C.mask01 = [C.maskf, C.maskb]
    C.mneg_incl = [C.consts[:, 640:768], C.consts[:, 896:1024]]
    C.mneg_strict = [C.consts[:, 768:896], C.consts[:, 1024:1152]]
    C.esel = [C.consts[:, 1152:1280], C.consts[:, 1280:1408]]
    C.onesf = C.consts[:, 1408:1536]
    C.ones1 = persist([128, 128], BF16, "ones1")
    C.identb = persist([128, 128], BF16, "identb")
    C.onesD = persist([128, 128], BF16, "onesD")
    C.ones128 = persist([128, 128], BF16, "ones128")
    C.vec = persist([128, NVEC], F32, "vecs")
    C.bconst = P.buf("const")
    C.phase = Alloc(nc, persist.p, 224 * 1024 - 256)

    P.dma("sp", C.consts[:], consts_d[:, :], dst=C.bconst)
    P.dma("sp", C.vec[:], vecs[:, :], dst=C.bconst)
    P.op("dve", lambda e: e.tensor_copy(out=C.identb[:], in_=C.ident), r=[C.bconst], w=[P.buf("identb")])
    P.op("dve", lambda e: e.memset(C.onesD[:], 1.0 / D), w=[P.buf("ones")])
    P.op("dve", lambda e: e.memset(C.ones128[:], 1.0 / 128), w=[P.buf("ones")])
    P.op("dve", lambda e: e.memset(C.ones1[:], 1.0), w=[P.buf("ones")])

    for sq in range(nseq):
        load_x(C, sq)
        P.barrier()
        C.phase.reset()
        if stages in ("mix0", "all"):
            mixer0(C)
            P.barrier()
            C.phase.reset()
        if stages in ("ffn0", "all"):
            ffn(C, 0)
            P.barrier()
            C.phase.reset()
        if stages in ("mix1", "all"):
            mixer1(C)
            P.barrier()
            C.phase.reset()
        if stages in ("ffn1", "all"):
            ffn(C, 1)
            P.barrier()
            C.phase.reset()
        final_norm_store(C, sq)
        P.barrier()
        C.phase.reset()

    P.barrier()
    P.emit()
    return nc


def dump(C, name, t, buf_list):
    if not C.debug:
        return
    shape = list(t.shape)
    d = C.nc.dram_tensor("dbg_" + name, shape, t.dtype, kind="ExternalOutput").ap()
    C.P.barrier()
    C.P.dma("sp", d, t[:], src=None, dst=C.P.buf("dbg", name))
    C.P.barrier()


def next_ps(C):
    while True:
        i = C.psrr % 8
        C.psrr += 1
        if i not in C.ps_res:
            return i


def reserve_ps(C, n):
    r = []
    for _ in range(n):
        i = next_ps(C)
        C.ps_res.add(i)
        r.append(i)
    return r


def release_ps(C, banks):
    for i in banks:
        C.ps_res.discard(i)


def hbuf(C, kb, mt):
    return C.P.buf("hT", kb, mt)


def vcol(C, name, j=0, n=1):
    o = VLAY[name][0] + j
    return C.vec[:, o:o + n]


def load_x(C, sq):
    P, ps, psb = C.P, C.ps, C.psb
    xs = [C.phase([128, D], F32, "xin%d" % i) for i in range(2)]
    for tt in range(NT):
        xb = P.buf("xin", tt % 2)
        xt = xs[tt % 2]
        P.dma("sp", xt[:], C.x[sq, tt * 128:(tt + 1) * 128, :], dst=xb)
        for half in range(2):
            pi = next_ps(C)
            for j in range(4):
                kb = half * 4 + j
                P.op("pe", lambda e, pi=pi, j=j, kb=kb, xt=xt: e.matmul(
                    ps[pi][:, j * 128:(j + 1) * 128], xt[:, kb * 128:(kb + 1) * 128], C.ident,
                    start=True, stop=True), r=[xb, C.bconst], w=[psb[pi]])
            mt = tt // 4
            P.op("act", lambda e, pi=pi, half=half, tt=tt: e.activation(
                out=C.hT[:, half * 4:half * 4 + 4, tt * 128:(tt + 1) * 128],
                in_=ps[pi][:].rearrange("p (j t) -> p j t", j=4), func=AF.Copy),
                r=[psb[pi]], w=[hbuf(C, half * 4 + j, mt) for j in range(4)])


def rms_rstd(C, mt, sqt, rt, rb):
    P, ps, psb = C.P, C.ps, C.psb
    tsl = slice(mt * 512, (mt + 1) * 512)
    pi = next_ps(C)
    for kb in range(8):
        sb = P.buf("sqt", kb % 2)
        st = sqt[kb % 2]
        P.op("act", lambda e, st=st, kb=kb: e.activation(out=st[:], in_=C.hT[:, kb, tsl], func=AF.Square),
             r=[hbuf(C, kb, mt)], w=[sb])
        P.op("pe", lambda e, st=st, kb=kb: e.matmul(
            ps[pi][:], C.onesD[:], st[:], start=(kb == 0), stop=(kb == 7)),
            r=[sb, P.buf("ones")], w=[psb[pi]])
    rsqrt_from_psum(P, rt[:], ps[pi][:], psb[pi], rb)


def rmsnorm_to_bf16(C, gname, hn, hnb):
    P = C.P
    sqt = [C.phase([128, 512], BF16, "sq%d" % i) for i in range(2)]
    rstd = [C.phase([128, 512], F32, "rstd%d" % i) for i in range(2)]
    for mt in range(4):
        tsl = slice(mt * 512, (mt + 1) * 512)
        rb = P.buf("rstd", mt % 2)
        rt = rstd[mt % 2]
        rms_rstd(C, mt, sqt, rt, rb)
        for kb in range(8):
            P.op("dve", lambda e, kb=kb, rt=rt, tsl=tsl: e.scalar_tensor_tensor(
                out=hn[:, kb, tsl], in0=C.hT[:, kb, tsl], scalar=vcol(C, gname, kb),
                in1=rt[:], op0=ALU.mult, op1=ALU.mult), r=[hbuf(C, kb, mt), rb, C.bconst], w=[hnb(kb, mt)])


def ffn(C, l):
    P, ps, psb = C.P, C.ps, C.psb
    ph = C.phase
    hn = ph([128, 8, S], BF16, "hn")
    hnb = lambda kb, mt: P.buf("hn", kb, mt)
    rmsnorm_to_bf16(C, "norm_ffn%d" % l, hn, hnb)
    GRP = 8
    act = [ph([128, S], BF16, "act%d" % i) for i in range(GRP)]
    wd = [ph([128, D], BF16, "wd%d" % i) for i in range(2 * GRP)]
    wg = [ph([128, 8, 128], BF16, "wg%d" % i) for i in range(2)]
    wu = [ph([128, 8, 128], BF16, "wu%d" % i) for i in range(2)]
    G = ph([128, S + 2], F32, "gbuf")
    cv = [ph([128, 512], F32, "cv%d" % i) for i in range(2)]
    sg = [ph([128, 512], F32, "sg%d" % i) for i in range(2)]
    gb = lambda mt: P.buf("gbuf", mt)
    gedge = P.buf("gbuf_edge")
    P.op("dve", lambda e: e.memset(G[:, 0:1], 0.0), w=[gedge])
    P.op("dve", lambda e: e.memset(G[:, S + 1:S + 2], 0.0), w=[gedge])
    cwo = VLAY["ffn_conv_w%d" % l][0]
    cbo = VLAY["ffn_conv_b%d" % l][0]
    groups = []
    c0 = 0
    while c0 < NCB:
        groups.append(list(range(c0, min(c0 + GRP, NCB))))
        c0 += GRP
    wdslot = 0
    for grp in groups:
        slots = {}
        for cb in grp:
            a_t = act[cb % GRP]
            ab = lambda mt, cb=cb: P.buf("act", cb % GRP, mt)
            wgt, wut = wg[cb % 2], wu[cb % 2]
            wgb, wub = P.buf("wg", cb % 2), P.buf("wu", cb % 2)
            ws = wdslot % (2 * GRP)
            wdslot += 1
            slots[cb] = ws
            wdb = P.buf("wd", ws)
            P.dma("pool", wgt[:], C.wup_d[l, cb], dst=wgb)
            P.dma("pool", wut[:], C.wup_d[l, NCB + cb], dst=wub)
            P.dma("pool", wd[ws][:], C.wdn_d[l, cb], dst=wdb)
            for mt in range(4):
                pi = next_ps(C)
                for kc in range(8):
                    P.op("pe", lambda e, pi=pi, kc=kc, mt=mt, wgt=wgt: e.matmul(
                        ps[pi][:], wgt[:, kc, :], hn[:, kc, mt * 512:(mt + 1) * 512],
                        start=(kc == 0), stop=(kc == 7)), r=[wgb, hnb(kc, mt)], w=[psb[pi]])
                P.op("act", lambda e, pi=pi, mt=mt: e.activation(
                    out=G[:, 1 + mt * 512:1 + (mt + 1) * 512], in_=ps[pi][:], func=AF.Copy),
                    r=[psb[pi]], w=[gb(mt)])
            for mt in range(4):
                pi = next_ps(C)
                for kc in range(8):
                    P.op("pe", lambda e, pi=pi, kc=kc, mt=mt, wut=wut: e.matmul(
                        ps[pi][:], wut[:, kc, :], hn[:, kc, mt * 512:(mt + 1) * 512],
                        start=(kc == 0), stop=(kc == 7)), r=[wub, hnb(kc, mt)], w=[psb[pi]])
                cvt, cvb = cv[mt % 2], P.buf("cv", mt % 2)
                sgt, sgb = sg[mt % 2], P.buf("sg", mt % 2)
                rd = [gb(mt), gedge, C.bconst]
                if mt > 0:
                    rd.append(gb(mt - 1))
                if mt < 3:
                    rd.append(gb(mt + 1))
                b0 = mt * 512
                w_ = lambda k, cb=cb: C.vec[:, cwo + cb * 3 + k:cwo + cb * 3 + k + 1]
                P.op("dve", lambda e, cvt=cvt, b0=b0, w_=w_: e.tensor_scalar(
                    out=cvt[:], in0=G[:, b0:b0 + 512], scalar1=w_(0), scalar2=None, op0=ALU.mult),
                    r=rd, w=[cvb])
                P.op("dve", lambda e, cvt=cvt, b0=b0, w_=w_: e.scalar_tensor_tensor(
                    out=cvt[:], in0=G[:, b0 + 1:b0 + 513], scalar=w_(1), in1=cvt[:],
                    op0=ALU.mult, op1=ALU.add), r=rd + [cvb], w=[cvb])
                P.op("dve", lambda e, cvt=cvt, b0=b0, w_=w_: e.scalar_tensor_tensor(
                    out=cvt[:], in0=G[:, b0 + 2:b0 + 514], scalar=w_(2), in1=cvt[:],
                    op0=ALU.mult, op1=ALU.add), r=rd + [cvb], w=[cvb])
                P.op("act", lambda e, cvt=cvt, sgt=sgt, cb=cb: e.activation(
                    out=sgt[:], in_=cvt[:], func=AF.Silu, bias=C.vec[:, cbo + cb:cbo + cb + 1], scale=1.0),
                    r=[cvb, C.bconst], w=[sgb])
                P.op("dve", lambda e, sgt=sgt, pi=pi, a_t=a_t, b0=b0: e.tensor_tensor(
                    out=a_t[:, b0:b0 + 512], in0=sgt[:], in1=ps[pi][:], op=ALU.mult),
                    r=[sgb, psb[pi]], w=[ab(mt)])
        for mt in range(4):
            for db in range(8):
                pi = next_ps(C)
                for i, cb in enumerate(grp):
                    P.op("pe", lambda e, pi=pi, cb=cb, db=db, mt=mt, i=i, n=len(grp), ws=slots[cb]: e.matmul(
                        ps[pi][:], wd[ws][:, db * 128:(db + 1) * 128], act[cb % GRP][:, mt * 512:(mt + 1) * 512],
                        start=(i == 0), stop=(i == n - 1)),
                        r=[P.buf("wd", slots[cb]), P.buf("act", cb % GRP, mt)], w=[psb[pi]])
                P.op("dve", lambda e, pi=pi, db=db, mt=mt: e.tensor_tensor(
                    out=C.hT[:, db, mt * 512:(mt + 1) * 512], in0=C.hT[:, db, mt * 512:(mt + 1) * 512],
                    in1=ps[pi][:], op=ALU.add), r=[psb[pi], hbuf(C, db, mt)], w=[hbuf(C, db, mt)])


def MM(out, lhsT, rhs, start=True, stop=True):
    return lambda e: e.matmul(out, lhsT, rhs, start=start, stop=stop)


def ACTF(out, in_, func, **kw):
    return lambda e: e.activation(out=out, in_=in_, func=func, **kw)


def TT(out, in0, in1, op):
    return lambda e: e.tensor_tensor(out=out, in0=in0, in1=in1, op=op)


def STT(out, in0, scalar, in1, op0, op1):
    return lambda e: e.scalar_tensor_tensor(out=out, in0=in0, scalar=scalar, in1=in1, op0=op0, op1=op1)


def TS(out, in0, s1, s2, op0, op1=None):
    if op1 is None:
        return lambda e: e.tensor_scalar(out=out, in0=in0, scalar1=s1, scalar2=None, op0=op0)
    return lambda e: e.tensor_scalar(out=out, in0=in0, scalar1=s1, scalar2=s2, op0=op0, op1=op1)


def CP(out, in_):
    return lambda e: e.tensor_copy(out=out, in_=in_)


def MS(out, val):
    return lambda e: e.memset(out, val)


def RCP(out, in_):
    return lambda e: e.reciprocal(out=out, in_=in_)


def mixer0(C):
    P, ps, psb, ph = C.P, C.ps, C.psb, C.phase
    hn = ph([128, 8, S], BF16, "hn")
    hnb = lambda kb, mt: P.buf("hn", kb, mt)
    oa = ph([128, 4, S], BF16, "oa")
    oab = lambda h, mt: P.buf("oa", h, mt)
    mark1 = ph.p
    rmsnorm_to_bf16(C, "norm_mix0", hn, hnb)
    P.barrier()
    ph.p = mark1
    qd = [ph([128, 2, S], BF16, "qd%d" % d) for d in range(2)]
    ki = [ph([128, 2, S], BF16, "ki%d" % d) for d in range(2)]
    qdb = lambda d, blk, mt: P.buf("qd", d, blk, mt)
    kib = lambda d, blk, mt: P.buf("ki", d, blk, mt)
    vt = ph([128, NT, 512], BF16, "vt")
    vtb = lambda tt: P.buf("vt", tt)
    elast = ph([128, 2, 2, NT], F32, "elast")
    elb = lambda mt: P.buf("elast", mt)
    mark2 = ph.p

    win = ph([128, 8, 1056], BF16, "win")
    winb = P.buf("win")
    for kc in range(8):
        P.dma("pool", win[:, kc, 0:1024], C.ab_win_d[:, kc, 0:1024], dst=winb)
    for kc in range(8):
        P.dma("pool", win[:, kc, 1024:1056], C.ab_win_d[:, kc, 1536:1568], dst=winb)
    wgg = ph([32, 2, 256], BF16, "wgg")
    wggb = P.buf("wgg")
    P.dma("pool", wgg[:], C.gla_wg_d.rearrange("d r c -> r d c"), dst=wggb)
    lra = [ph([32, 512], BF16, "lra%d" % d) for d in range(2)]
    lrab = [P.buf("lra", d) for d in range(2)]
    for d in range(2):
        P.op("dve", MS(lra[d][:], 1.0), w=[lrab[d]])
    e_t = [ph([128, 512], F32, "e%d" % i) for i in range(2)]
    sp_t = [ph([128, 512], F32, "sp%d" % i) for i in range(2)]
    Ep = [ph([128, 512], F32, "Ep%d" % d) for d in range(2)]
    Em = [ph([128, 512], F32, "Em%d" % d) for d in range(2)]
    for mt in range(4):
        tsl = slice(mt * 512, (mt + 1) * 512)
        for d in range(2):
            pi = next_ps(C)
            for kc in range(8):
                P.op("pe", MM(ps[pi][0:16, :], win[:, kc, 1024 + d * 16:1024 + (d + 1) * 16], hn[:, kc, tsl],
                              kc == 0, kc == 7), r=[winb, hnb(kc, mt)], w=[psb[pi]])
            P.op("act", ACTF(lra[d][0:16, :], ps[pi][0:16, :], AF.Copy), r=[psb[pi]], w=[lrab[d]])
        cpi = reserve_ps(C, 4)
        for q in range(4):
            tt = mt * 4 + q
            pi = next_ps(C)
            for d in range(2):
                P.op("pe", MM(ps[pi][:, d * 256:(d + 1) * 256], lra[d][:, q * 128:(q + 1) * 128], wgg[:, d, :]),
                     r=[lrab[d], wggb], w=[psb[pi]])
            et, etb = e_t[q % 2], P.buf("e_t", q % 2)
            spt, spb = sp_t[q % 2], P.buf("sp_t", q % 2)
            P.op("act", ACTF(et[:], ps[pi][:], AF.Exp, scale=-1.0), r=[psb[pi]], w=[etb])
            P.op("act", ACTF(spt[:], et[:], AF.Ln, bias=1.0), r=[etb], w=[spb])
            for d in range(2):
                for blk in range(2):
                    b = cpi[d * 2 + blk]
                    P.op("pe", MM(ps[b][:, q * 128:(q + 1) * 128],
                                  spt[:, d * 256 + blk * 128:d * 256 + (blk + 1) * 128], C.tri[d]),
                         r=[spb, C.bconst], w=[psb[b]])
            pi = next_ps(C)
            for kc in range(8):
                P.op("pe", MM(ps[pi][:], hn[:, kc, tt * 128:(tt + 1) * 128], win[:, kc, 512:1024], kc == 0, kc == 7),
                     r=[winb, hnb(kc, mt)], w=[psb[pi]])
            P.op("act", ACTF(vt[:, tt, :], ps[pi][:], AF.Copy), r=[psb[pi]], w=[vtb(tt)])
        for blk in range(2):
            pq = next_ps(C)
            for kc in range(8):
                P.op("pe", MM(ps[pq][:], win[:, kc, blk * 128:(blk + 1) * 128], hn[:, kc, tsl], kc == 0, kc == 7),
                     r=[winb, hnb(kc, mt)], w=[psb[pq]])
            pk = next_ps(C)
            for kc in range(8):
                P.op("pe", MM(ps[pk][:], win[:, kc, 256 + blk * 128:256 + (blk + 1) * 128], hn[:, kc, tsl],
                              kc == 0, kc == 7), r=[winb, hnb(kc, mt)], w=[psb[pk]])
            for d in range(2):
                b = cpi[d * 2 + blk]
                ep_, epb = Ep[d], P.buf("Ep", d)
                em_, emb = Em[d], P.buf("Em", d)
                P.op("act", ACTF(ep_[:], ps[b][:], AF.Exp), r=[psb[b]], w=[epb])
                P.op("act", ACTF(em_[:], ps[b][:], AF.Exp, scale=-1.0), r=[psb[b]], w=[emb])
                col = 127 if d == 0 else 0
                P.op("dve", CP(elast[:, d, blk, mt * 4:(mt + 1) * 4],
                               ep_[:].rearrange("p (n i) -> p n i", i=128)[:, :, col]), r=[epb], w=[elb(mt)])
                P.op("dve", STT(qd[d][:, blk, tsl], ps[pq][:], 0.125, ep_[:], ALU.mult, ALU.mult),
                     r=[psb[pq], epb], w=[qdb(d, blk, mt)])
                P.op("dve", TT(ki[d][:, blk, tsl], ps[pk][:], em_[:], ALU.mult),
                     r=[psb[pk], emb], w=[kib(d, blk, mt)])
        release_ps(C, cpi)
    dump(C, "qd0", qd[0], None)
    dump(C, "qd1", qd[1], None)
    dump(C, "ki0", ki[0], None)
    dump(C, "ki1", ki[1], None)
    dump(C, "vt", vt, None)
    P.barrier()
    ph.p = mark2

    Sall = [ph([128, NT, 2, 128], BF16, "Sall%d" % d) for d in range(2)]
    mark3 = ph.p
    S2 = [[ph([128, 128], F32, "S2_%d%d" % (d, hp)) for hp in range(2)] for d in range(2)]
    sab = lambda d, n, hp: P.buf("Sall", d, n, hp)
    A_t = [[ph([128, 128], F32, "A%d%d" % (d, hp)) for hp in range(2)] for d in range(2)]
    kin = [ph([128, 128], BF16, "kin%d" % i) for i in range(4)]
    for d in range(2):
        n0 = 0 if d == 0 else NT - 1
        for hp in range(2):
            P.op("dve", MS(S2[d][hp][:], 0.0), w=[P.buf("S2", d, hp)])
            P.op("dve", MS(Sall[d][:, n0, hp, :], 0.0), w=[sab(d, n0, hp)])
    kk = 0
    for step in range(NT):
        for d in range(2):
            n = step if d == 0 else NT - 1 - step
            nxt = n + 1 if d == 0 else n - 1
            if not (0 <= nxt < NT):
                continue
            mt = n // 4
            for hp in range(2):
                pi = next_ps(C)
                kt, ktb = kin[kk % 4], P.buf("kin", kk % 4)
                kk += 1
                P.op("pe", MM(ps[pi][:, 0:128], ki[d][:, hp, n * 128:(n + 1) * 128], C.identb[:]),
                     r=[kib(d, hp, mt), P.buf("identb")], w=[psb[pi]])
                P.op("act", ACTF(kt[:], ps[pi][:, 0:128], AF.Copy), r=[psb[pi]], w=[ktb])
                P.op("pe", MM(ps[pi][:, 128:384], kt[:], vt[:, n, hp * 256:(hp + 1) * 256]),
                     r=[ktb, vtb(n)], w=[psb[pi]])
                sb_ = P.buf("S2", d, hp)
                ab_ = P.buf("A", d, hp)
                P.op("act", ACTF(A_t[d][hp][:], S2[d][hp][:], AF.Copy, scale=elast[:, d, hp, n:n + 1]),
                     r=[sb_, elb(mt)], w=[ab_])
                for hh in range(2):
                    rows = slice(hh * 64, hh * 64 + 64)
                    P.op("dve", STT(S2[d][hp][rows, :], ps[pi][rows, 128 + hh * 128:256 + hh * 128],
                                    elast[rows, d, hp, n:n + 1], A_t[d][hp][rows, :], ALU.mult, ALU.add),
                         r=[psb[pi], ab_, elb(mt)], w=[sb_])
                P.op("act", ACTF(Sall[d][:, nxt, hp, :], S2[d][hp][:], AF.Copy), r=[sb_], w=[sab(d, nxt, hp)])
    dump(C, "Sall0", Sall[0], None)
    dump(C, "Sall1", Sall[1], None)
    P.barrier()
    ph.p = mark3

    wgate = ph([128, 8, 512], BF16, "wgate")
    wgb = P.buf("wgate")
    for kc in range(8):
        P.dma("pool", wgate[:, kc, :], C.ab_win_d[:, kc, 1024:1536], dst=wgb)
    mask4 = [ph([128, 512], BF16, "mask4_%d" % d) for d in range(2)]
    m4b = P.buf("mask4")
    for d in range(2):
        src = C.maskf if d == 0 else C.maskb
        for q in range(4):
            P.op("dve", CP(mask4[d][:, q * 128:(q + 1) * 128], src), r=[C.bconst], w=[m4b])
    PT = [[ph([128, 512], BF16, "PT%d%d" % (d, i)) for i in range(2)] for d in range(2)]
    sqh = [ph([128, 512], BF16, "sqh")] * 2
    rsh = [ph([128, 512], F32, "rsh%d" % i) for i in range(2)]
    sgt = ph([128, 512], F32, "sgt")
    tmp = ph([128, 512], F32, "tmp")
    it = 0
    for mt in range(4):
        tsl = slice(mt * 512, (mt + 1) * 512)
        for h in range(4):
            hp, hh = h // 2, h % 2
            rows = slice(hh * 64, hh * 64 + 64)
            par = it % 2
            it += 1
            for d in range(2):
                pi = next_ps(C)
                for q in range(4):
                    ns = slice((mt * 4 + q) * 128, (mt * 4 + q + 1) * 128)
                    P.op("pe", MM(ps[pi][:, q * 128:(q + 1) * 128], ki[d][rows, hp, ns], qd[d][rows, hp, ns]),
                         r=[kib(d, hp, mt), qdb(d, hp, mt)], w=[psb[pi]])
                P.op("dve", TT(PT[d][par][:], ps[pi][:], mask4[d][:], ALU.mult),
                     r=[psb[pi], m4b], w=[P.buf("PT", d, par)])
            po = next_ps(C)
            for q in range(4):
                n = mt * 4 + q
                cs = slice(q * 128, (q + 1) * 128)
                ns = slice(n * 128, (n + 1) * 128)
                P.op("pe", MM(ps[po][:, cs], vt[:, n, h * 128:(h + 1) * 128], PT[0][par][:, cs], True, False),
                     r=[vtb(n), P.buf("PT", 0, par)], w=[psb[po]])
                P.op("pe", MM(ps[po][:, cs], vt[:, n, h * 128:(h + 1) * 128], PT[1][par][:, cs], False, False),
                     r=[vtb(n), P.buf("PT", 1, par)], w=[psb[po]])
                for d in range(2):
                    P.op("pe", MM(ps[po][:, cs], Sall[d][rows, n, hp, :], qd[d][rows, hp, ns], False, d == 1),
                         r=[sab(d, n, hp), qdb(d, hp, mt)], w=[psb[po]])
            sqt_, sqb_ = sqh[par], P.buf("sqh", 0)
            P.op("act", ACTF(sqt_[:], ps[po][:], AF.Square), r=[psb[po]], w=[sqb_])
            pn = next_ps(C)
            P.op("pe", MM(ps[pn][:], C.ones128[:], sqt_[:]), r=[sqb_, P.buf("ones")], w=[psb[pn]])
            rt_, rb_ = rsh[par], P.buf("rsh", par)
            rsqrt_from_psum(P, rt_[:], ps[pn][:], psb[pn], rb_)
            pg = next_ps(C)
            for kc in range(8):
                P.op("pe", MM(ps[pg][:], wgate[:, kc, h * 128:(h + 1) * 128], hn[:, kc, tsl], kc == 0, kc == 7),
                     r=[wgb, hnb(kc, mt)], w=[psb[pg]])
            sgb_, tmb_ = P.buf("sgt"), P.buf("tmp")
            P.op("act", ACTF(sgt[:], ps[pg][:], AF.Silu), r=[psb[pg]], w=[sgb_])
            P.op("dve", STT(tmp[:], ps[po][:], vcol(C, "gla_norm"), rt_[:], ALU.mult, ALU.mult),
                 r=[psb[po], rb_, C.bconst], w=[tmb_])
            P.op("dve", TT(oa[:, h, tsl], tmp[:], sgt[:], ALU.mult), r=[tmb_, sgb_], w=[oab(h, mt)])
    dump(C, "oa", oa, None)
    P.barrier()
    ph.p = mark1

    wsu = ph([128, 8, 512], BF16, "wsu")
    wsv = ph([128, 8, 512], BF16, "wsv")
    wout = ph([128, 8, D], BF16, "wout")
    wsub, wsvb, woutb = P.buf("wsu"), P.buf("wsv"), P.buf("wout")
    for kc in range(8):
        P.dma("pool", wsu[:, kc, :], C.ab_win_d[:, kc, 1568:2080], dst=wsub)
        P.dma("pool", wsv[:, kc, :], C.ab_win_d[:, kc, 2080:2592], dst=wsvb)
        P.dma("pool", wout[:, kc, :], C.ab_wout_d[:, kc, :], dst=woutb)
    wsT = ph([128, 4, 128], BF16, "wsT")
    wsTb = P.buf("wsT")
    P.dma("pool", wsT[:], C.sgu_wsT_d, dst=wsTb)
    lng = ph([128, 512], F32, "lng")
    lnbt = ph([128, 512], F32, "lnb")
    bsb4 = ph([128, 4, 4, 128], F32, "bsb4")
    lnbuf = P.buf("lnconst")
    P.dma("sp", lng[:], C.sgu_ln_d[0:1, :].partition_broadcast(128), dst=lnbuf)
    P.dma("sp", lnbt[:], C.sgu_ln_d[1:2, :].partition_broadcast(128), dst=lnbuf)
    for n in range(4):
        P.dma("sp", bsb4[:, :, n, :], C.sgu_bs_d.rearrange("o (g i) -> o g i", g=4).partition_broadcast(128),
              dst=lnbuf)
    ut = [ph([128, 4, 512], BF16, "ut%d" % i) for i in range(2)]
    sgu = [ph([128, 4, 512], BF16, "sgu%d" % i) for i in range(2)]
    gv = [ph([128, 512], F32, "gv%d" % i) for i in range(2)]
    vv = [ph([128, 512], BF16, "vv%d" % i) for i in range(2)]
    st6 = [ph([128, 6], F32, "st6_%d" % i) for i in range(2)]
    mv = [ph([128, 2], F32, "mv%d" % i) for i in range(2)]
    rs1 = [ph([128, 1], F32, "rs1_%d" % i) for i in range(2)]
    tmq = [ph([128, 512], F32, "tmq%d" % i) for i in range(2)]
    for mt in range(4):
        tsl = slice(mt * 512, (mt + 1) * 512)
        u_, sg4 = ut[mt % 2], sgu[mt % 2]
        for g in range(4):
            pu = next_ps(C)
            for kc in range(8):
                P.op("pe", MM(ps[pu][:], wsu[:, kc, g * 128:(g + 1) * 128], hn[:, kc, tsl], kc == 0, kc == 7),
                     r=[wsub, hnb(kc, mt)], w=[psb[pu]])
            P.op("act", ACTF(u_[:, g, :], ps[pu][:], AF.Gelu_apprx_tanh), r=[psb[pu]], w=[P.buf("ut", mt % 2, g)])
        pm = reserve_ps(C, 4)
        for q in range(4):
            tt = mt * 4 + q
            par = q % 2
            pv = next_ps(C)
            for kc in range(8):
                P.op("pe", MM(ps[pv][:], hn[:, kc, tt * 128:(tt + 1) * 128], wsv[:, kc, :], kc == 0, kc == 7),
                     r=[wsvb, hnb(kc, mt)], w=[psb[pv]])
            gvb = P.buf("gv", par)
            P.op("act", ACTF(gv[par][:], ps[pv][:], AF.Gelu_apprx_tanh), r=[psb[pv]], w=[gvb])
            stb, mvb, rsb = P.buf("st6", par), P.buf("mv", par), P.buf("rs1", par)
            P.op("dve", lambda e, par=par: e.bn_stats(out=st6[par][:], in_=gv[par][:]), r=[gvb], w=[stb])
            P.op("dve", lambda e, par=par: e.bn_aggr(out=mv[par][:], in_=st6[par][:]), r=[stb], w=[mvb])
            P.op("act", ACTF(rs1[par][:], mv[par][:, 1:2], AF.Sqrt, bias=EPS), r=[mvb], w=[rsb])
            P.op("dve", RCP(rs1[par][:], rs1[par][:]), r=[rsb], w=[rsb])
            P.op("dve", TS(gv[par][:], gv[par][:], mv[par][:, 0:1], rs1[par][:, 0:1], ALU.subtract, ALU.mult),
                 r=[gvb, mvb, rsb], w=[gvb])
            P.op("dve", TT(gv[par][:], gv[par][:], lng[:], ALU.mult), r=[gvb, lnbuf], w=[gvb])
            vvb = P.buf("vv", par)
            P.op("dve", TT(vv[par][:], gv[par][:], lnbt[:], ALU.add), r=[gvb, lnbuf], w=[vvb])
            for g in range(4):
                P.op("pe", MM(ps[pm[g]][:, q * 128:(q + 1) * 128], vv[par][:, g * 128:(g + 1) * 128], wsT[:, g, :]),
                     r=[vvb, wsTb], w=[psb[pm[g]]])
        for g in range(4):
            tq, tqb = tmq[g % 2], P.buf("tmq", g % 2)
            P.op("dve", TT(tq[:], ps[pm[g]][:], bsb4[:, g, :, :].rearrange("p n i -> p (n i)"), ALU.add),
                 r=[psb[pm[g]], lnbuf], w=[tqb])
            P.op("dve", TT(sg4[:, g, :], tq[:], u_[:, g, :], ALU.mult),
                 r=[tqb, P.buf("ut", mt % 2, g)], w=[P.buf("sgu", mt % 2, g)])
        release_ps(C, pm)
        for db in range(8):
            po = next_ps(C)
            for kc in range(8):
                if kc < 4:
                    rhs, rb_ = oa[:, kc, tsl], oab(kc, mt)
                else:
                    rhs, rb_ = sg4[:, kc - 4, :], P.buf("sgu", mt % 2, kc - 4)
                P.op("pe", MM(ps[po][:], wout[:, kc, db * 128:(db + 1) * 128], rhs, kc == 0, kc == 7),
                     r=[woutb, rb_], w=[psb[po]])
            P.op("dve", TT(C.hT[:, db, tsl], C.hT[:, db, tsl], ps[po][:], ALU.add),
                 r=[psb[po], hbuf(C, db, mt)], w=[hbuf(C, db, mt)])
        if mt == 3:
            dump(C, "sgu", sg4, None)


def mixer1(C):
    P, ps, psb, ph = C.P, C.ps, C.psb, C.phase
    W = HPG * 128
    hn = ph([128, 8, S], BF16, "hn")
    hnb = lambda kb, mt: P.buf("hn", kb, mt)
    sc = {}
    for nm in ("g", "lnb", "beta", "c", "negc", "nec", "ECL", "ekend"):
        sc[nm] = ph([128, NT, 16], F32, "sc_" + nm)
    scb = P.buf("gdn_scalars")
    mark0 = ph.p
    markg0 = ph.p
    wsm = ph([128, 8, 32], F32, "wsm")
    wsmb = P.buf("wsm")
    P.dma("sp", wsm[:], C.gdn_wsm_d, dst=wsmb)
    gvec = ph([128, 32], F32, "gvec")
    gvb = P.buf("gvec")
    P.dma("sp", gvec[:], C.gdn_vec_d.partition_broadcast(128), dst=gvb)
    nea = ph([128, 16], F32, "nea")
    neab = P.buf("nea")
    P.op("act", ACTF(nea[:], gvec[:, 16:32], AF.Exp), r=[gvb], w=[neab])
    P.op("dve", TS(nea[:], nea[:], -1.0, None, ALU.mult), r=[neab], w=[neab])
    ab_tm = ph([128, NT, 32], F32, "ab_tm")
    abb = P.buf("ab_tm")
    t1 = ph([128, NT, 16], F32, "t1")
    t2 = ph([128, NT, 16], F32, "t2")
    t1b, t2b = P.buf("t1"), P.buf("t2")
    hn32 = ph([128, 8, 512], F32, "hn32")
    h32b = lambda kb: P.buf("hn32", kb)
    sqt = [ph([128, 512], BF16, "sq%d" % i) for i in range(2)]
    rstd = [ph([128, 512], F32, "rstd%d" % i) for i in range(2)]
    pa = reserve_ps(C, 1)[0]
    for mt in range(4):
        tsl = slice(mt * 512, (mt + 1) * 512)
        rb = P.buf("rstd", mt % 2)
        rt = rstd[mt % 2]
        rms_rstd(C, mt, sqt, rt, rb)
        for kb in range(8):
            P.op("dve", STT(hn32[:, kb, :], C.hT[:, kb, tsl], vcol(C, "norm_mix1", kb), rt[:], ALU.mult, ALU.mult),
                 r=[hbuf(C, kb, mt), rb, C.bconst], w=[h32b(kb)])
            P.op("act", ACTF(hn[:, kb, tsl], hn32[:, kb, :], AF.Copy), r=[h32b(kb)], w=[hnb(kb, mt)])
        for q in range(4):
            tt = mt * 4 + q
            for kc in range(8):
                P.op("pe", MM(ps[pa][:, tt * 32:(tt + 1) * 32], hn32[:, kc, q * 128:(q + 1) * 128], wsm[:, kc, :],
                              kc == 0, kc == 7), r=[wsmb, h32b(kc)], w=[psb[pa]])
    release_ps(C, [pa])
    P.op("act", ACTF(ab_tm[:].rearrange("p t c -> p (t c)"), ps[pa][:], AF.Copy), r=[psb[pa]], w=[abb])
    for tt in range(NT):
        P.op("dve", TT(t1[:, tt, :], ab_tm[:, tt, 16:32], gvec[:, 0:16], ALU.add), r=[abb, gvb], w=[t1b])
    fl = lambda t: t[:].rearrange("p t c -> p (t c)")
    P.op("act", ACTF(fl(t1), fl(t1), AF.Exp), r=[t1b], w=[t1b])
    P.op("act", ACTF(fl(t1), fl(t1), AF.Ln, bias=1.0), r=[t1b], w=[t1b])
    for tt in range(NT):
        P.op("dve", TT(sc["g"][:, tt, :], t1[:, tt, :], nea[:], ALU.mult), r=[t1b, neab], w=[scb])
    P.op("act", ACTF(t2[:], ab_tm[:, :, 0:16], AF.Exp, scale=-1.0), r=[abb], w=[t2b])
    P.op("act", ACTF(fl(t2), fl(t2), AF.Ln, bias=1.0), r=[t2b], w=[t2b])
    P.op("dve", TS(fl(sc["lnb"]), fl(t2), -1.0, None, ALU.mult), r=[t2b], w=[scb])
    P.op("act", ACTF(fl(sc["beta"]), fl(sc["lnb"]), AF.Exp), r=[scb], w=[scb])
    pc = next_ps(C)
    for tt in range(NT):
        for d in range(2):
            P.op("pe", MM(ps[pc][:, tt * 16 + d * 8:tt * 16 + d * 8 + 8], C.mask01[d], sc["g"][:, tt, d * 8:(d + 1) * 8]),
                 r=[scb, C.bconst], w=[psb[pc]])
    P.op("act", ACTF(fl(sc["c"]), ps[pc][:, 0:NT * 16], AF.Copy), r=[psb[pc]], w=[scb])
    P.op("dve", TS(fl(sc["negc"]), fl(sc["c"]), -1.0, None, ALU.mult), r=[scb], w=[scb])
    P.op("act", ACTF(fl(sc["nec"]), fl(sc["c"]), AF.Exp), r=[scb], w=[scb])
    P.op("dve", TS(fl(sc["nec"]), fl(sc["nec"]), -1.0, None, ALU.mult), r=[scb], w=[scb])
    pl = next_ps(C)
    for d in range(2):
        P.op("pe", MM(ps[pl][:, d * 128:(d + 1) * 128], C.esel[d], sc["c"][:, :, d * 8:(d + 1) * 8]),
             r=[scb, C.bconst], w=[psb[pl]])
    for d in range(2):
        cl = ps[pl][:, d * 128:(d + 1) * 128].rearrange("p (t h) -> p t h", h=8)
        P.op("act", ACTF(sc["ECL"][:, :, d * 8:(d + 1) * 8], cl, AF.Exp), r=[psb[pl]], w=[scb])
        P.op("dve", TT(t1[:, :, d * 8:(d + 1) * 8], cl, sc["c"][:, :, d * 8:(d + 1) * 8], ALU.subtract),
             r=[psb[pl], scb], w=[t1b])
    P.op("act", ACTF(fl(sc["ekend"]), fl(t1), AF.Exp), r=[t1b], w=[scb])
    P.barrier()
    ph.p = markg0
    mark1 = ph.p
    cwo = VLAY["gdn_conv_w"][0]
    if C.gdn_stop == 0:
        return

    for grp in range(8 // HPG):
        h0 = grp * HPG
        ph.p = mark1
        qT = ph([128, HPG, S], BF16, "qT")
        kT = ph([128, HPG, S], BF16, "kT")
        vtm = ph([128, NT, W], BF16, "vtm")
        ob = ph([128, HPG, S], BF16, "ob")
        qTb = lambda hl, mt: P.buf("qT", hl, mt)
        kTb = lambda hl, mt: P.buf("kT", hl, mt)
        vtb = lambda tt: P.buf("vtm", tt)
        obb = lambda hl, n: P.buf("ob", hl, n)
        mark2 = ph.p
        wblk = [ph([128, 8, 128], BF16, "wblk%d" % i) for i in range(2)]
        Gb = ph([128, S + 2], F32, "gbuf")
        cv = [ph([128, 512], F32, "cv%d" % i) for i in range(2)]
        so = [ph([128, 512], F32, "so%d" % i) for i in range(4)]
        sob = [ph([128, 512], BF16, "sob")] * 2
        sqn = [ph([128, 512], BF16, "sqn%d" % i) for i in range(4)]
        rsn = [ph([128, 512], F32, "rsn")] * 2
        gb_ = lambda mt: P.buf("gbuf", mt)
        gedge = P.buf("gbuf_edge")
        P.op("dve", MS(Gb[:, 0:1], 0.0), w=[gedge])
        P.op("dve", MS(Gb[:, S + 1:S + 2], 0.0), w=[gedge])
        wi = 0
        for kind in range(3):
            for hl in range(HPG):
                blk = kind * 8 + h0 + hl
                wt, wtb = wblk[wi % 2], P.buf("wblk", wi % 2)
                wi += 1
                P.dma("pool", wt[:], C.gdn_win_d[blk], dst=wtb)
                for mt in range(4):
                    pi = next_ps(C)
                    for kc in range(8):
                        P.op("pe", MM(ps[pi][:], wt[:, kc, :], hn[:, kc, mt * 512:(mt + 1) * 512], kc == 0, kc == 7),
                             r=[wtb, hnb(kc, mt)], w=[psb[pi]])
                    P.op("act", ACTF(Gb[:, 1 + mt * 512:1 + (mt + 1) * 512], ps[pi][:], AF.Copy),
                         r=[psb[pi]], w=[gb_(mt)])
                pns = reserve_ps(C, 4) if kind < 2 else None
                for mt in range(4):
                    par = mt % 2
                    tsl = slice(mt * 512, (mt + 1) * 512)
                    cvt, cvb = cv[par], P.buf("cv", par)
                    rd = [gb_(mt), gedge, C.bconst]
                    if mt > 0:
                        rd.append(gb_(mt - 1))
                    if mt < 3:
                        rd.append(gb_(mt + 1))
                    b0 = mt * 512
                    w_ = lambda k: C.vec[:, cwo + blk * 3 + k:cwo + blk * 3 + k + 1]
                    P.op("dve", TS(cvt[:], Gb[:, b0:b0 + 512], w_(0), None, ALU.mult), r=rd, w=[cvb])
                    P.op("dve", STT(cvt[:], Gb[:, b0 + 1:b0 + 513], w_(1), cvt[:], ALU.mult, ALU.add), r=rd + [cvb], w=[cvb])
                    P.op("dve", STT(cvt[:], Gb[:, b0 + 2:b0 + 514], w_(2), cvt[:], ALU.mult, ALU.add), r=rd + [cvb], w=[cvb])
                    if kind < 2:
                        sot, sotb = so[mt], P.buf("so", mt)
                        P.op("act", ACTF(sot[:], cvt[:], AF.Silu), r=[cvb], w=[sotb])
                        sqt_, sqb_ = sqn[mt], P.buf("sqn", mt)
                        P.op("act", ACTF(sqt_[:], sot[:], AF.Square), r=[sotb], w=[sqb_])
                        P.op("pe", MM(ps[pns[mt]][:], C.ones1[:], sqt_[:]), r=[sqb_, P.buf("ones")], w=[psb[pns[mt]]])
                    else:
                        sbt, sbtb = sob[par], P.buf("sob", 0)
                        P.op("act", ACTF(sbt[:], cvt[:], AF.Silu), r=[cvb], w=[sbtb])
                        pt = next_ps(C)
                        for q in range(4):
                            P.op("pe", MM(ps[pt][:, q * 128:(q + 1) * 128], sbt[:, q * 128:(q + 1) * 128], C.identb[:]),
                                 r=[sbtb, P.buf("identb")], w=[psb[pt]])
                        P.op("act", ACTF(vtm[:, mt * 4:(mt + 1) * 4, hl * 128:(hl + 1) * 128],
                                         ps[pt][:].rearrange("p (q v) -> p q v", v=128), AF.Copy),
                             r=[psb[pt]], w=[vtb(mt * 4 + q_) for q_ in range(4)])
                if kind < 2:
                    for mt in range(4):
                        tsl = slice(mt * 512, (mt + 1) * 512)
                        sot, sotb = so[mt], P.buf("so", mt)
                        rt_, rb_ = rsn[0], P.buf("rsn", 0)
                        rsqrt_from_psum(P, rt_[:], ps[pns[mt]][:], psb[pns[mt]], rb_)
                        if kind == 0:
                            P.op("dve", STT(qT[:, hl, tsl], sot[:], 128.0 ** -0.5, rt_[:], ALU.mult, ALU.mult),
                                 r=[sotb, rb_], w=[qTb(hl, mt)])
                        else:
                            P.op("dve", TT(kT[:, hl, tsl], sot[:], rt_[:], ALU.mult), r=[sotb, rb_], w=[kTb(hl, mt)])
                    release_ps(C, pns)
        P.barrier()
        ph.p = mark2
        if C.gdn_stop == 1:
            return

        of_ = ph([128, HPG, S], BF16, "of")
        ofb = lambda hl, n: P.buf("of", hl, n)
        oo = [of_, ob]
        oob = [ofb, obb]
        markT = ph.p
        NV = 2 * HPG
        WW = NV * 128
        VH = [(d, hl) for d in range(2) for hl in range(HPG)]
        HS = [slice(v * 128, (v + 1) * 128) for v in range(NV)]
        t = {}
        for nm in ("Rt", "vn", "Sbf", "mniW", "mnsW", "idW", "id2W"):
            t[nm] = ph([128, WW], BF16, nm)
        t["Sm"] = ph([128, WW], F32, "Sm")
        tsets = []
        for k_ in range(2):
            ts_ = {}
            for nm in ("tmpA", "tmpB"):
                ts_[nm] = ph([128, WW], F32, "%s_%d" % (nm, k_))
            for nm in ("DTi", "ETs", "ECb", "X0", "X1", "Y0", "Y1", "P0", "P1", "TN", "ZT", "Rn"):
                ts_[nm] = ph([128, WW], BF16, "%s_%d" % (nm, k_))
            tsets.append(ts_)
        csets = [{nm: ph([128, WW], BF16, "%s%d" % (nm, i)) for nm in ("MT", "Aqk", "kend", "qdec")} for i in range(3)]
        B_ = lambda nm, *k: P.buf("g2_" + nm, *k)
        for v, (d, hl) in enumerate(VH):
            P.op("pool", CP(t["mniW"][:, HS[v]], C.mneg_incl[d]), r=[C.bconst], w=[B_("maskW")])
            P.op("pool", CP(t["mnsW"][:, HS[v]], C.mneg_strict[d]), r=[C.bconst], w=[B_("maskW")])
            P.op("pool", CP(t["idW"][:, HS[v]], C.ident), r=[C.bconst], w=[B_("maskW")])
            P.op("pool", TS(t["id2W"][:, HS[v]], C.ident, 2.0, None, ALU.mult), r=[C.bconst], w=[B_("maskW")])
        v3 = lambda tt_: tt_[:].rearrange("p (h i) -> p h i", i=128)
        v3p = lambda ap: ap.rearrange("p (h i) -> p h i", i=128)
        order = [list(range(NT)), list(range(NT - 1, -1, -1))]
        hb_pool = list(range(8))

        def hb_alloc():
            assert hb_pool, "out of PSUM banks"
            return hb_pool.pop(0)

        def hb_free(x):
            hb_pool.append(x)

        hp = lambda x: ps[x]
        hbb = lambda x: psb[x]

        def pre(i):
            ci = i % 3
            ti = i % 2
            cs_ = csets[ci]
            tp = tsets[ti]
            Bt = lambda nm, *k: P.buf("g2_" + nm, ti, *k)
            nn = [order[d][i] for (d, hl) in VH]
            cols = [slice(n * 128, (n + 1) * 128) for n in nn]
            mts = [n // 4 for n in nn]
            scol = lambda nm, v: sc[nm][:, nn[v], VH[v][0] * 8 + h0 + VH[v][1]:VH[v][0] * 8 + h0 + VH[v][1] + 1]
            bc = lambda nm, v: scol(nm, v).to_broadcast([128, 128])
            pA, pB = hb_alloc(), hb_alloc()
            for v, (d, hl) in enumerate(VH):
                P.op("pe", MM(hp(pA)[:, HS[v]], bc("g", v), C.mask01[d]), r=[scb, C.bconst], w=[hbb(pA)])
            for v, (d, hl) in enumerate(VH):
                P.op("pe", MM(hp(pB)[:, HS[v]], bc("g", v), C.mask01[d], True, False), r=[scb, C.bconst], w=[hbb(pB)])
                P.op("pe", MM(hp(pB)[:, HS[v]], bc("lnb", v), C.ident, False, True), r=[scb, C.bconst], w=[hbb(pB)])
            yield
            P.op("dve", TT(tp["tmpA"][:], hp(pA)[:], t["mniW"][:], ALU.add), r=[hbb(pA), B_("maskW")], w=[Bt("tmpA")])
            P.op("act", ACTF(tp["ECb"][:], hp(pA)[:], AF.Exp), r=[hbb(pA)], w=[Bt("ECb")])
            P.op("dve", TT(tp["tmpB"][:], hp(pB)[:], t["mnsW"][:], ALU.add), r=[hbb(pB), B_("maskW")], w=[Bt("tmpB")])
            hb_free(pA)
            hb_free(pB)
            yield
            pG, pQ = hb_alloc(), hb_alloc()
            for v, (d, hl) in enumerate(VH):
                P.op("pe", MM(hp(pG)[:, HS[v]], kT[:, hl, cols[v]], kT[:, hl, cols[v]]), r=[kTb(hl, mts[v])], w=[hbb(pG)])
                P.op("pe", MM(hp(pQ)[:, HS[v]], kT[:, hl, cols[v]], qT[:, hl, cols[v]]),
                     r=[kTb(hl, mts[v]), qTb(hl, mts[v])], w=[hbb(pQ)])
            for v in range(NV):
                P.op("act", ACTF(tp["ETs"][:, HS[v]], tp["tmpB"][:, HS[v]], AF.Exp, bias=scol("negc", v)),
                     r=[Bt("tmpB"), scb], w=[Bt("ETs")])
            for v in range(NV):
                P.op("act", ACTF(tp["DTi"][:, HS[v]], tp["tmpA"][:, HS[v]], AF.Exp, bias=scol("negc", v)),
                     r=[Bt("tmpA"), scb], w=[Bt("DTi")])
            for d in range(2):
                vs = slice(d * HPG * 128, (d + 1) * HPG * 128)
                P.op("pool", TT(v3p(cs_["qdec"][:, vs]), qT[:, :, cols[d * HPG]], v3p(tp["ECb"][:, vs]), ALU.mult),
                     r=[Bt("ECb")] + [qTb(hl, mts[d * HPG]) for hl in range(HPG)], w=[B_("qdec", ci)])
            yield
            xb = lambda k: Bt("X", k)
            yb = lambda k: Bt("Y", k)
            pb = lambda k: Bt("P", k)
            Xt, Yt, Pt = [tp["X0"], tp["X1"]], [tp["Y0"], tp["Y1"]], [tp["P0"], tp["P1"]]
            P.op("dve", STT(Xt[0][:], hp(pG)[:], -1.0, tp["ETs"][:], ALU.mult, ALU.mult),
                 r=[hbb(pG), Bt("ETs")], w=[xb(0)])
            P.op("dve", TT(cs_["Aqk"][:], hp(pQ)[:], tp["DTi"][:], ALU.mult), r=[hbb(pQ), Bt("DTi")], w=[B_("Aqk", ci)])
            hb_free(pG)
            hb_free(pQ)
            yield
            pY = hb_alloc()
            for v in range(NV):
                P.op("pe", MM(hp(pY)[:, HS[v]], Xt[0][:, HS[v]], C.identb[:]), r=[xb(0), P.buf("identb")], w=[hbb(pY)])
            P.op("pool", TT(Pt[0][:], Xt[0][:], t["idW"][:], ALU.add), r=[xb(0), B_("maskW")], w=[pb(0)])
            yield
            P.op("act", ACTF(Yt[0][:], hp(pY)[:], AF.Copy), r=[hbb(pY)], w=[yb(0)])
            hb_free(pY)
            P.op("pool", TT(tp["TN"][:], t["idW"][:], Yt[0][:], ALU.subtract), r=[yb(0), B_("maskW")], w=[Bt("TN")])
            yield
            cur = 0
            for k in range(1, NLEV + 1):
                nx = 1 - cur
                pX, pY = hb_alloc(), hb_alloc()
                for v in range(NV):
                    P.op("pe", MM(hp(pX)[:, HS[v]], Yt[cur][:, HS[v]], Xt[cur][:, HS[v]]), r=[xb(cur), yb(cur)], w=[hbb(pX)])
                for v in range(NV):
                    P.op("pe", MM(hp(pY)[:, HS[v]], Xt[cur][:, HS[v]], Yt[cur][:, HS[v]]), r=[xb(cur), yb(cur)], w=[hbb(pY)])
                yield
                P.op("act", ACTF(Yt[nx][:], hp(pY)[:], AF.Copy), r=[hbb(pY)], w=[yb(nx)])
                P.op("act", ACTF(Xt[nx][:], hp(pX)[:], AF.Copy), r=[hbb(pX)], w=[xb(nx)])
                hb_free(pX)
                hb_free(pY)
                yield
                pP = hb_alloc()
                for v in range(NV):
                    P.op("pe", MM(hp(pP)[:, HS[v]], Yt[nx][:, HS[v]], Pt[cur][:, HS[v]]),
                         r=[pb(cur), yb(nx)], w=[hbb(pP)])
                yield
                P.op("dve", TT(Pt[nx][:], hp(pP)[:], Pt[cur][:], ALU.add), r=[hbb(pP), pb(cur)], w=[pb(nx)])
                hb_free(pP)
                cur = nx
            yield
            Z = Pt[cur]
            pZ, pE = hb_alloc(), hb_alloc()
            for v in range(NV):
                P.op("pe", MM(hp(pZ)[:, HS[v]], Z[:, HS[v]], C.identb[:]), r=[pb(cur), P.buf("identb")], w=[hbb(pZ)])
                P.op("pe", MM(hp(pE)[:, HS[v]], tp["TN"][:, HS[v]], Z[:, HS[v]]), r=[pb(cur), Bt("TN")], w=[hbb(pE)])
            yield
            P.op("act", ACTF(tp["ZT"][:], hp(pZ)[:], AF.Copy), r=[hbb(pZ)], w=[Bt("ZT")])
            P.op("dve", STT(tp["Rn"][:], hp(pE)[:], -1.0, t["id2W"][:], ALU.mult, ALU.add),
                 r=[hbb(pE), B_("maskW")], w=[Bt("Rn")])
            hb_free(pZ)
            hb_free(pE)
            yield
            pM = hb_alloc()
            for v in range(NV):
                P.op("pe", MM(hp(pM)[:, HS[v]], tp["ZT"][:, HS[v]], tp["Rn"][:, HS[v]]), r=[Bt("ZT"), Bt("Rn")], w=[hbb(pM)])
            pK = hb_alloc()
            for v, (d, hl) in enumerate(VH):
                P.op("pe", MM(hp(pK)[:, HS[v]], kT[:, hl, cols[v]], C.identb[:]), r=[kTb(hl, mts[v]), P.buf("identb")],
                     w=[hbb(pK)])
            yield
            for v in range(NV):
                P.op("act", ACTF(cs_["kend"][:, HS[v]], hp(pK)[:, HS[v]], AF.Copy, scale=scol("ekend", v)),
                     r=[hbb(pK), scb], w=[B_("kend", ci)])
            hb_free(pK)
            for v in range(NV):
                P.op("act", ACTF(cs_["MT"][:, HS[v]], hp(pM)[:, HS[v]], AF.Copy, scale=scol("beta", v)),
                     r=[hbb(pM), scb], w=[B_("MT", ci)])
            hb_free(pM)

        def scan(i):
            ci = i % 3
            cs_ = csets[ci]
            nn = [order[d][i] for (d, hl) in VH]
            cols = [slice(n * 128, (n + 1) * 128) for n in nn]
            mts = [n // 4 for n in nn]
            scol = lambda nm, v: sc[nm][:, nn[v], VH[v][0] * 8 + h0 + VH[v][1]:VH[v][0] * 8 + h0 + VH[v][1] + 1]
            csb = {nm: B_(nm, ci) for nm in ("MT", "Aqk", "kend", "qdec")}
            Rt, vn, Sm, Sbf = t["Rt"], t["vn"], t["Sm"], t["Sbf"]
            p1 = hb_alloc()
            for v, (d, hl) in enumerate(VH):
                P.op("pe", MM(hp(p1)[:, HS[v]], kT[:, hl, cols[v]], Sbf[:, HS[v]]), r=[kTb(hl, mts[v]), B_("Sbf")], w=[hbb(p1)])
            yield
            for v, (d, hl) in enumerate(VH):
                P.op("dve", STT(Rt[:, HS[v]], hp(p1)[:, HS[v]], scol("nec", v), vtm[:, nn[v], hl * 128:(hl + 1) * 128],
                                ALU.mult, ALU.add), r=[hbb(p1), scb, vtb(nn[v])], w=[B_("Rt")])
            hb_free(p1)
            yield
            p2 = hb_alloc()
            for v in range(NV):
                P.op("pe", MM(hp(p2)[:, HS[v]], cs_["MT"][:, HS[v]], Rt[:, HS[v]]), r=[csb["MT"], B_("Rt")], w=[hbb(p2)])
            yield
            P.op("act", ACTF(vn[:], hp(p2)[:], AF.Copy), r=[hbb(p2)], w=[B_("vn")])
            hb_free(p2)
            yield
            p3, p4 = hb_alloc(), hb_alloc()
            for v in range(NV):
                P.op("pe", MM(hp(p4)[:, HS[v]], cs_["kend"][:, HS[v]], vn[:, HS[v]]), r=[csb["kend"], B_("vn")], w=[hbb(p4)])
            for v in range(NV):
                P.op("pe", MM(hp(p3)[:, HS[v]], Sbf[:, HS[v]], cs_["qdec"][:, HS[v]], True, False),
                     r=[B_("Sbf"), csb["qdec"]], w=[hbb(p3)])
                P.op("pe", MM(hp(p3)[:, HS[v]], vn[:, HS[v]], cs_["Aqk"][:, HS[v]], False, True),
                     r=[B_("vn"), csb["Aqk"]], w=[hbb(p3)])
            yield
            for v in range(NV):
                P.op("dve", STT(Sm[:, HS[v]], Sm[:, HS[v]], scol("ECL", v), hp(p4)[:, HS[v]], ALU.mult, ALU.add),
                     r=[hbb(p4), scb, B_("Sm")], w=[B_("Sm")])
            for d in range(2):
                vs = slice(d * HPG * 128, (d + 1) * HPG * 128)
                P.op("act", ACTF(oo[d][:, :, cols[d * HPG]], v3p(hp(p3)[:, vs]), AF.Copy), r=[hbb(p3)],
                     w=[oob[d](hl, nn[d * HPG]) for hl in range(HPG)])
            hb_free(p3)
            hb_free(p4)
            yield
            P.op("dve", CP(Sbf[:], Sm[:]), r=[B_("Sm")], w=[B_("Sbf")])

        def step(g_):
            try:
                next(g_)
                return True
            except StopIteration:
                return False

        P.op("dve", MS(t["Sm"][:], 0.0), w=[B_("Sm")])
        P.op("dve", MS(t["Sbf"][:], 0.0), w=[B_("Sbf")])
        pres = {0: pre(0), 1: pre(1)}
        while step(pres[0]):
            step(pres[1])
        for i in range(NT):
            if i + 2 < NT:
                pres[i + 2] = pre(i + 2)
            must = [scan(i)]
            if i + 1 < NT:
                must.append(pres[i + 1])
            extra = pres.get(i + 2)
            while must:
                must = [g_ for g_ in must if step(g_)]
                if extra is not None and not step(extra):
                    extra = None

        P.barrier()
        ph.p = markT
        osum = ph([128, 512], F32, "osum")
        sqo = ph([128, 512], BF16, "sqo")
        rso = ph([128, 512], F32, "rso")
        zs = ph([128, 512], BF16, "zs")
        wz = ph([128, 8, 128], BF16, "wz")
        ong = ph([128, HPG, S], BF16, "ong")
        ongb = lambda hl, mt: P.buf("ong", hl, mt)
        woutg = ph([128, HPG, D], BF16, "woutg")
        woutgb = P.buf("woutg")
        for hl in range(HPG):
            P.dma("pool", woutg[:, hl, :], C.gdn_wout_d[:, h0 + hl, :], dst=woutgb)
        Bn = lambda nm: P.buf("g2n_" + nm)
        zs4 = [ph([128, 512], BF16, "zs%d" % i) for i in range(4)]
        for hl in range(HPG):
            h = h0 + hl
            P.dma("pool", wz[:], C.gdn_win_d[24 + h], dst=Bn("wz"))
            for mt in range(4):
                tsl = slice(mt * 512, (mt + 1) * 512)
                pz = next_ps(C)
                for kc in range(8):
                    P.op("pe", MM(ps[pz][:], wz[:, kc, :], hn[:, kc, tsl], kc == 0, kc == 7),
                         r=[Bn("wz"), hnb(kc, mt)], w=[psb[pz]])
                P.op("act", ACTF(zs4[mt][:], ps[pz][:], AF.Silu), r=[psb[pz]], w=[P.buf("g2n_zs", mt)])
            for mt in range(4):
                tsl = slice(mt * 512, (mt + 1) * 512)
                rdo = [oob[d](hl, mt * 4 + q_) for d in range(2) for q_ in range(4)]
                P.op("dve", TT(osum[:], of_[:, hl, tsl], ob[:, hl, tsl], ALU.add), r=rdo, w=[Bn("osum")])
                P.op("act", ACTF(sqo[:], osum[:], AF.Square), r=[Bn("osum")], w=[Bn("sqo")])
                pn = next_ps(C)
                P.op("pe", MM(ps[pn][:], C.ones128[:], sqo[:]), r=[Bn("sqo"), P.buf("ones")], w=[psb[pn]])
                rsqrt_from_psum(P, rso[:], ps[pn][:], psb[pn], Bn("rso"))
                P.op("dve", TT(rso[:], rso[:], zs4[mt][:], ALU.mult), r=[Bn("rso"), P.buf("g2n_zs", mt)], w=[Bn("rso")])
                P.op("dve", STT(ong[:, hl, tsl], osum[:], vcol(C, "gdn_norm"), rso[:], ALU.mult, ALU.mult),
                     r=[Bn("osum"), Bn("rso"), C.bconst], w=[ongb(hl, mt)])
        for mt in range(4):
            tsl = slice(mt * 512, (mt + 1) * 512)
            for db in range(8):
                po = next_ps(C)
                for hl in range(HPG):
                    P.op("pe", MM(ps[po][:], woutg[:, hl, db * 128:(db + 1) * 128], ong[:, hl, tsl], hl == 0, hl == HPG - 1),
                         r=[woutgb, ongb(hl, mt)], w=[psb[po]])
                P.op("dve", TT(C.hT[:, db, tsl], C.hT[:, db, tsl], ps[po][:], ALU.add),
                     r=[psb[po], hbuf(C, db, mt)], w=[hbuf(C, db, mt)])
        P.barrier()


def final_norm_store(C, sq):
    P, ps, psb = C.P, C.ps, C.psb
    ph = C.phase
    sqt = [ph([128, 512], BF16, "sq%d" % i) for i in range(2)]
    rstd = [ph([128, 512], F32, "rstd%d" % i) for i in range(2)]
    yt = ph([128, 8, 512], F32, "y")
    ot = [ph([128, D], F32, "ot%d" % i) for i in range(2)]
    for mt in range(4):
        tsl = slice(mt * 512, (mt + 1) * 512)
        rb = P.buf("rstd", mt % 2)
        rt = rstd[mt % 2]
        rms_rstd(C, mt, sqt, rt, rb)
        yb = P.buf("fy")
        for kb in range(8):
            P.op("dve", lambda e, kb=kb, rt=rt, tsl=tsl: e.scalar_tensor_tensor(
                out=yt[:, kb, :], in0=C.hT[:, kb, tsl], scalar=vcol(C, "norm_final", kb),
                in1=rt[:], op0=ALU.mult, op1=ALU.mult), r=[hbuf(C, kb, mt), rb, C.bconst], w=[yb])
        for q in range(4):
            tt = mt * 4 + q
            ob = P.buf("fot", tt % 2)
            o_t = ot[tt % 2]
            for half in range(2):
                pi2 = next_ps(C)
                for j in range(4):
                    kb = half * 4 + j
                    P.op("pe", lambda e, pi2=pi2, j=j, kb=kb, q=q: e.matmul(
                        ps[pi2][:, j * 128:(j + 1) * 128], yt[:, kb, q * 128:(q + 1) * 128], C.ident,
                        start=True, stop=True), r=[yb, C.bconst], w=[psb[pi2]])
                P.op("act", lambda e, pi2=pi2, half=half, o_t=o_t: e.activation(
                    out=o_t[:, half * 512:(half + 1) * 512], in_=ps[pi2][:], func=AF.Copy),
                    r=[psb[pi2]], w=[ob])
            P.dma("sp", C.out[sq, tt * 128:(tt + 1) * 128, :], o_t[:], src=ob)


def vec_layout():
    lay = {}
    off = [0]

    def add(name, n):
        lay[name] = (off[0], n)
        off[0] += n
    add("norm_final", 8)
    for l in range(2):
        add("norm_mix%d" % l, 8)
        add("norm_ffn%d" % l, 8)
        add("ffn_conv_w%d" % l, NCB * 3)
        add("ffn_conv_b%d" % l, NCB)
    add("gla_norm", 1)
    add("gdn_conv_w", 24 * 3)
    add("gdn_norm", 1)
    return lay, off[0]


VLAY, NVEC = vec_layout()


def make_vecs(inputs):
    v = np.zeros((128, NVEC), np.float32)

    def put(name, arr):
        o, n = VLAY[name]
        v[:, o:o + n] = np.asarray(arr, np.float32).reshape(128, n)
    f = lambda a: np.asarray(a, np.float32)
    put("norm_final", f(inputs["norm_final"]).reshape(8, 128).T)
    for l in range(2):
        put("norm_mix%d" % l, f(inputs["norm_mix"])[l].reshape(8, 128).T)
        put("norm_ffn%d" % l, f(inputs["norm_ffn"])[l].reshape(8, 128).T)
        put("ffn_conv_w%d" % l, f(inputs["ffn_conv_w"])[l].reshape(3, NCB, 128).transpose(2, 1, 0))
        put("ffn_conv_b%d" % l, f(inputs["ffn_conv_b"])[l].reshape(NCB, 128).T)
    put("gla_norm", f(inputs["gla_norm"])[0].reshape(128, 1))
    put("gdn_conv_w", f(inputs["gdn_conv_w"])[0].reshape(3, 24, 128).transpose(2, 1, 0))
    put("gdn_norm", f(inputs["gdn_norm"])[0].reshape(128, 1))
    return v


def make_weights(inputs):
    f = lambda a: np.asarray(a, np.float32)
    w = {}
    wu = f(inputs["ffn_w_up"]).reshape(2, 8, 128, 2 * NCB, 128)
    w["ffn_wup"] = np.ascontiguousarray(wu.transpose(0, 3, 2, 1, 4))
    w["ffn_wdn"] = np.ascontiguousarray(f(inputs["ffn_w_down"]).reshape(2, NCB, 128, D))
    tile_k = lambda W: np.ascontiguousarray(W.reshape(8, 128, -1).transpose(1, 0, 2))
    w["ab_win"] = tile_k(f(inputs["ab_w_in"])[0])
    w["ab_wout"] = tile_k(f(inputs["ab_w_out"])[0])
    wg = np.zeros((2, 32, 256), np.float32)
    wg[0, :16] = f(inputs["gla_w_gate_fwd"])[0]
    wg[0, 16] = f(inputs["gla_b_gate_fwd"])[0]
    wg[1, :16] = f(inputs["gla_w_gate_bwd"])[0]
    wg[1, 16] = f(inputs["gla_b_gate_bwd"])[0]
    w["gla_wg"] = wg
    w["sgu_wsT"] = np.ascontiguousarray(f(inputs["sgu_w_s"])[0].transpose(2, 0, 1))
    w["sgu_ln"] = np.stack([f(inputs["sgu_ln_g"])[0], f(inputs["sgu_ln_b"])[0]])
    w["sgu_bs"] = np.ascontiguousarray(f(inputs["sgu_b_s"])[0].reshape(1, 512))
    gw = f(inputs["gdn_w_in"])[0]
    w["gdn_win"] = np.ascontiguousarray(gw[:, :4096].reshape(8, 128, 32, 128).transpose(2, 1, 0, 3))
    w["gdn_wsm"] = tile_k(gw[:, 4096:4128])
    w["gdn_wout"] = tile_k(f(inputs["gdn_w_out"])[0])
    w["gdn_vec"] = np.concatenate([f(inputs["gdn_dt_bias_fwd"])[0], f(inputs["gdn_dt_bias_bwd"])[0],
                                   f(inputs["gdn_a_log_fwd"])[0], f(inputs["gdn_a_log_bwd"])[0]]).reshape(1, 32)
    return w


def make_consts():
    c = np.zeros((128, NCONST * 128), np.float32)
    j = np.arange(128)[:, None]
    i = np.arange(128)[None, :]
    c[:, 0:128] = np.eye(128)
    c[:, 128:256] = (j <= i)
    c[:, 256:384] = (j >= i)
    c[:, 384:512] = (j <= i) * (-1.0 / 16)
    c[:, 512:640] = (j >= i) * (-1.0 / 16)
    big = 30000.0
    c[:, 640:768] = ((j <= i) - 1.0) * big
    c[:, 768:896] = ((j < i) - 1.0) * big
    c[:, 896:1024] = ((j >= i) - 1.0) * big
    c[:, 1024:1152] = ((j > i) - 1.0) * big
    c[127, 1152:1280] = 1.0
    c[0, 1280:1408] = 1.0
    c[:, 1408:1536] = 1.0
    return c


_NC_CACHE = {}


def make_in_map(inputs, c, nseq):
    x = np.asarray(inputs["x"], np.float32)
    m = {"x": np.ascontiguousarray(x[c * nseq:(c + 1) * nseq]), "vecs": make_vecs(inputs),
         "consts": make_consts()}
    m.update(make_weights(inputs))
    return m


def kernel(**inputs):
    B = np.asarray(inputs["x"]).shape[0]
    nseq = B // N_CORES
    if nseq not in _NC_CACHE:
        _NC_CACHE[nseq] = build_program(nseq)
    nc = _NC_CACHE[nseq]
    in_maps = [make_in_map(inputs, c, nseq) for c in range(N_CORES)]
    res = run_bass_kernel_spmd(nc, in_maps, core_ids=list(range(N_CORES)))
    return np.concatenate([r["out"] for r in res.results], axis=0)
```

```python
import numpy as np
import concourse.bass as bass
import concourse.mybir as mybir
from concourse.bass_utils import run_bass_kernel_spmd

F32 = mybir.dt.float32
BF16 = mybir.dt.bfloat16
AF = mybir.ActivationFunctionType
ALU = mybir.AluOpType
AX = mybir.AxisListType

D = 1024
S = 2048
NT = S // 128
FF = 2816
NCB = FF // 128
NCONST = 12
HPG = 2
NLEV = 4
EPS = 1e-6
N_CORES = 8


class Buf:
    __slots__ = ("name", "w", "r", "wsem", "wcnt", "rsem", "rcnt", "excl")

    def __init__(self, name):
        self.name = name
        self.excl = False
        self.w = None
        self.r = []
        self.wsem = None
        self.wcnt = 0
        self.rsem = None
        self.rcnt = 0


class Op:
    __slots__ = ("eng", "fn", "deps", "dma", "event", "src", "dst", "id")


class Prog:
    MAXV = 60000
    MAXD = 56000

    def __init__(self, nc):
        self.nc = nc
        self.engs = {"pe": nc.tensor, "act": nc.scalar, "dve": nc.vector,
                     "pool": nc.gpsimd, "sp": nc.sync}
        self.ops = []
        self.bufs = {}
        self.last = {}
        self.dma_out = []
        self.nsem = 0

    def buf(self, *key):
        b = self.bufs.get(key)
        if b is None:
            b = Buf(key)
            self.bufs[key] = b
        return b

    def _deps(self, o, r, w, is_dma):
        deps = set()
        for b in r:
            if b.w is not None:
                deps.add(b.w)
            if b.excl:
                for x in b.r:
                    if self.ops[x].eng != o.eng:
                        deps.add(x)
        for b in w:
            if b.w is not None:
                deps.add(b.w)
            last = {}
            for x in b.r:
                ox = self.ops[x]
                if ox.dma:
                    deps.add(x)
                else:
                    last[ox.eng] = max(last.get(ox.eng, -1), x)
            deps.update(last.values())
        if not is_dma:
            raw = set(b.w for b in r if b.w is not None)
            deps = set(d for d in deps
                       if d in raw or self.ops[d].eng != o.eng or self.ops[d].dma)
        o.deps = deps
        for b in r:
            b.r.append(o.id)
        for b in w:
            b.w = o.id
            b.r = []

    def op(self, eng, fn, r=(), w=()):
        o = Op()
        o.id = len(self.ops)
        o.eng = eng
        o.fn = fn
        o.dma = False
        o.event = None
        o.src = o.dst = None
        self._deps(o, r, w, False)
        self.ops.append(o)
        self.last[eng] = o.id
        return o.id

    def dma(self, q, out, in_, src=None, dst=None):
        o = Op()
        o.id = len(self.ops)
        o.eng = q
        o.fn = lambda e, out=out, in_=in_: e.dma_start(out=out, in_=in_)
        o.dma = True
        o.event = None
        o.src = src
        o.dst = dst
        self._deps(o, [src] if src is not None else [], [dst] if dst is not None else [], True)
        self.ops.append(o)
        self.dma_out.append(o.id)
        return o.id

    def barrier(self):
        deps = set(self.last.values()) | set(self.dma_out)
        for eng in ("pe", "act", "dve", "pool", "sp"):
            o = Op()
            o.id = len(self.ops)
            o.eng = eng
            o.fn = None
            o.dma = False
            o.event = None
            o.src = o.dst = None
            o.deps = set(deps)
            self.ops.append(o)
        self.dma_out = []
        for b in self.bufs.values():
            b.w = None
            b.r = []

    def _newsem(self):
        self.nsem += 1
        return self.nc.alloc_semaphore("s%d" % self.nsem)

    def emit(self):
        nc = self.nc
        has_dep = set()
        for o in self.ops:
            has_dep |= o.deps
        cur = {}
        cnt = {}
        known = {e: {} for e in self.engs}
        for o in self.ops:
            E = self.engs[o.eng]
            waits = {}
            for d in o.deps:
                ev = self.ops[d].event
                if ev is None:
                    continue
                s, v = ev
                k = id(s)
                if known[o.eng].get(k, 0) >= v:
                    continue
                if k not in waits or waits[k][1] < v:
                    waits[k] = (s, v)
            wl = list(waits.values())
            for s, v in wl:
                known[o.eng][id(s)] = v
            if o.fn is None:
                for s, v in wl:
                    E.wait_ge(s, v)
                continue
            for s, v in wl[:-1]:
                E.wait_ge(s, v)
            ins = o.fn(E)
            if wl:
                ins._wait_ge(wl[-1][0], wl[-1][1])
            if o.dma:
                b = o.dst if o.dst is not None else o.src
                if o.dst is not None:
                    if b.wsem is None or b.wcnt >= self.MAXD:
                        b.wsem = self._newsem()
                        b.wcnt = 0
                    b.wcnt += 16
                    ins.then_inc(b.wsem, 16)
                    o.event = (b.wsem, b.wcnt)
                else:
                    if b.rsem is None or b.rcnt >= self.MAXD:
                        b.rsem = self._newsem()
                        b.rcnt = 0
                    b.rcnt += 16
                    ins.then_inc(b.rsem, 16)
                    o.event = (b.rsem, b.rcnt)
            elif o.id in has_dep:
                if o.eng not in cur or cnt[o.eng] >= self.MAXV:
                    cur[o.eng] = self._newsem()
                    cnt[o.eng] = 0
                cnt[o.eng] += 1
                ins.then_inc(cur[o.eng], 1)
                o.event = (cur[o.eng], cnt[o.eng])


class Alloc:
    def __init__(self, nc, lo, hi):
        self.nc = nc
        self.lo = lo
        self.hi = hi
        self.p = lo
        self.n = 0

    def reset(self):
        self.p = self.lo

    def __call__(self, shape, dtype, name=None):
        nb = 4 if dtype == F32 else 2
        sz = nb
        for s in shape[1:]:
            sz *= s
        sz = (sz + 63) // 64 * 64
        off = self.p
        self.p += sz
        assert self.p <= self.hi, "SBUF overflow %s %d > %d" % (name, self.p, self.hi)
        self.n += 1
        return self.nc.alloc_sbuf_tensor_at("%s_%d" % (name or "t", self.n), list(shape), dtype,
                                            offset=off)


def rsqrt_from_psum(P, out_ap, ps_ap, ps_buf, out_buf, eps=EPS, scale=1.0):
    P.op("act", lambda e: e.activation(out=out_ap, in_=ps_ap, func=AF.Ln, bias=eps, scale=scale),
         r=[ps_buf], w=[out_buf])
    P.op("act", lambda e: e.activation(out=out_ap, in_=out_ap, func=AF.Exp, scale=-0.5), r=[out_buf], w=[out_buf])


class Ctx:
    pass


def build_program(nseq, stages="all", debug=False):
    nc = bass.Bass("TRN2", target_bir_lowering=False)
    P = Prog(nc)
    C = Ctx()
    C.nc, C.P = nc, P
    C.debug = debug
    import os
    C.gdn_stop = int(os.environ.get('GDN_STOP', '9'))
    C.gdn_sub = int(os.environ.get('GDN_SUB', '99'))
    C.x = nc.dram_tensor("x", [nseq, S, D], F32, kind="ExternalInput").ap()
    C.out = nc.dram_tensor("out", [nseq, S, D], F32, kind="ExternalOutput").ap()
    vecs = nc.dram_tensor("vecs", [128, NVEC], F32, kind="ExternalInput").ap()
    C.wup_d = nc.dram_tensor("ffn_wup", [2, 2 * NCB, 128, 8, 128], F32, kind="ExternalInput").ap()
    C.wdn_d = nc.dram_tensor("ffn_wdn", [2, NCB, 128, D], F32, kind="ExternalInput").ap()
    consts_d = nc.dram_tensor("consts", [128, NCONST * 128], F32, kind="ExternalInput").ap()
    C.ab_win_d = nc.dram_tensor("ab_win", [128, 8, 2592], F32, kind="ExternalInput").ap()
    C.ab_wout_d = nc.dram_tensor("ab_wout", [128, 8, 1024], F32, kind="ExternalInput").ap()
    C.gla_wg_d = nc.dram_tensor("gla_wg", [2, 32, 256], F32, kind="ExternalInput").ap()
    C.sgu_wsT_d = nc.dram_tensor("sgu_wsT", [128, 4, 128], F32, kind="ExternalInput").ap()
    C.sgu_ln_d = nc.dram_tensor("sgu_ln", [2, 512], F32, kind="ExternalInput").ap()
    C.sgu_bs_d = nc.dram_tensor("sgu_bs", [1, 512], F32, kind="ExternalInput").ap()
    C.gdn_win_d = nc.dram_tensor("gdn_win", [32, 128, 8, 128], F32, kind="ExternalInput").ap()
    C.gdn_wsm_d = nc.dram_tensor("gdn_wsm", [128, 8, 32], F32, kind="ExternalInput").ap()
    C.gdn_wout_d = nc.dram_tensor("gdn_wout", [128, 8, 1024], F32, kind="ExternalInput").ap()
    C.gdn_vec_d = nc.dram_tensor("gdn_vec", [1, 32], F32, kind="ExternalInput").ap()

    persist = Alloc(nc, 16 * 1024, 96 * 1024)

    C.ps = [nc.alloc_psum_tensor("ps%d" % i, [128, 512], F32) for i in range(8)]
    C.psb = [P.buf("ps", i) for i in range(8)]
    for b in C.psb:
        b.excl = True
    C.psrr = 0
    C.ps_res = set()

    C.hT = persist([128, 8, S], F32, "hT")
    C.consts = persist([128, NCONST * 128], F32, "consts")
    C.ident = C.consts[:, 0:128]
    C.maskf = C.consts[:, 128:256]
    C.maskb = C.consts[:, 256:384]
    C.tri = [C.consts[:, 384:512], C.consts[:, 512:640]]
    C.mask01 = [C.maskf, C.maskb]
    C.mneg_incl = [C.consts[:, 640:768], C.consts[:, 896:1024]]
    C.mneg_strict = [C.consts[:, 768:896], C.consts[:, 1024:1152]]
    C.esel = [C.consts[:, 1152:1280], C.consts[:, 1280:1408]]
    C.onesf = C.consts[:, 1408:1536]
    C.ones1 = persist([128, 128], BF16, "ones1")
    C.identb = persist([128, 128], BF16, "identb")
    C.onesD = persist([128, 128], BF16, "onesD")
    C.ones128 = persist([128, 128], BF16, "ones128")
    C.vec = persist([128, NVEC], F32, "vecs")
    C.bconst = P.buf("const")
    C.phase = Alloc(nc, persist.p, 224 * 1024 - 256)

    P.dma("sp", C.consts[:], consts_d[:, :], dst=C.bconst)
    P.dma("sp", C.vec[:], vecs[:, :], dst=C.bconst)
    P.op("dve", lambda e: e.tensor_copy(out=C.identb[:], in_=C.ident), r=[C.bconst], w=[P.buf("identb")])
    P.op("dve", lambda e: e.memset(C.onesD[:], 1.0 / D), w=[P.buf("ones")])
    P.op("dve", lambda e: e.memset(C.ones128[:], 1.0 / 128), w=[P.buf("ones")])
    P.op("dve", lambda e: e.memset(C.ones1[:], 1.0), w=[P.buf("ones")])

    for sq in range(nseq):
        load_x(C, sq)
        P.barrier()
        C.phase.reset()
        if stages in ("mix0", "all"):
            mixer0(C)
            P.barrier()
            C.phase.reset()
        if stages in ("ffn0", "all"):
            ffn(C, 0)
            P.barrier()
            C.phase.reset()
        if stages in ("mix1", "all"):
            mixer1(C)
            P.barrier()
            C.phase.reset()
        if stages in ("ffn1", "all"):
            ffn(C, 1)
            P.barrier()
            C.phase.reset()
        final_norm_store(C, sq)
        P.barrier()
        C.phase.reset()

    P.barrier()
    P.emit()
    return nc


def dump(C, name, t, buf_list):
    if not C.debug:
        return
    shape = list(t.shape)
    d = C.nc.dram_tensor("dbg_" + name, shape, t.dtype, kind="ExternalOutput").ap()
    C.P.barrier()
    C.P.dma("sp", d, t[:], src=None, dst=C.P.buf("dbg", name))
    C.P.barrier()


def next_ps(C):
    while True:
        i = C.psrr % 8
        C.psrr += 1
        if i not in C.ps_res:
            return i


def reserve_ps(C, n):
    r = []
    for _ in range(n):
        i = next_ps(C)
        C.ps_res.add(i)
        r.append(i)
    return r


def release_ps(C, banks):
    for i in banks:
        C.ps_res.discard(i)


def hbuf(C, kb, mt):
    return C.P.buf("hT", kb, mt)


def vcol(C, name, j=0, n=1):
    o = VLAY[name][0] + j
    return C.vec[:, o:o + n]


def load_x(C, sq):
    P, ps, psb = C.P, C.ps, C.psb
    xs = [C.phase([128, D], F32, "xin%d" % i) for i in range(2)]
    for tt in range(NT):
        xb = P.buf("xin", tt % 2)
        xt = xs[tt % 2]
        P.dma("sp", xt[:], C.x[sq, tt * 128:(tt + 1) * 128, :], dst=xb)
        for half in range(2):
            pi = next_ps(C)
            for j in range(4):
                kb = half * 4 + j
                P.op("pe", lambda e, pi=pi, j=j, kb=kb, xt=xt: e.matmul(
                    ps[pi][:, j * 128:(j + 1) * 128], xt[:, kb * 128:(kb + 1) * 128], C.ident,
                    start=True, stop=True), r=[xb, C.bconst], w=[psb[pi]])
            mt = tt // 4
            P.op("act", lambda e, pi=pi, half=half, tt=tt: e.activation(
                out=C.hT[:, half * 4:half * 4 + 4, tt * 128:(tt + 1) * 128],
                in_=ps[pi][:].rearrange("p (j t) -> p j t", j=4), func=AF.Copy),
                r=[psb[pi]], w=[hbuf(C, half * 4 + j, mt) for j in range(4)])


def rms_rstd(C, mt, sqt, rt, rb):
    P, ps, psb = C.P, C.ps, C.psb
    tsl = slice(mt * 512, (mt + 1) * 512)
    pi = next_ps(C)
    for kb in range(8):
        sb = P.buf("sqt", kb % 2)
        st = sqt[kb % 2]
        P.op("act", lambda e, st=st, kb=kb: e.activation(out=st[:], in_=C.hT[:, kb, tsl], func=AF.Square),
             r=[hbuf(C, kb, mt)], w=[sb])
        P.op("pe", lambda e, st=st, kb=kb: e.matmul(
            ps[pi][:], C.onesD[:], st[:], start=(kb == 0), stop=(kb == 7)),
            r=[sb, P.buf("ones")], w=[psb[pi]])
    rsqrt_from_psum(P, rt[:], ps[pi][:], psb[pi], rb)


def rmsnorm_to_bf16(C, gname, hn, hnb):
    P = C.P
    sqt = [C.phase([128, 512], BF16, "sq%d" % i) for i in range(2)]
    rstd = [C.phase([128, 512], F32, "rstd%d" % i) for i in range(2)]
    for mt in range(4):
        tsl = slice(mt * 512, (mt + 1) * 512)
        rb = P.buf("rstd", mt % 2)
        rt = rstd[mt % 2]
        rms_rstd(C, mt, sqt, rt, rb)
        for kb in range(8):
            P.op("dve", lambda e, kb=kb, rt=rt, tsl=tsl: e.scalar_tensor_tensor(
                out=hn[:, kb, tsl], in0=C.hT[:, kb, tsl], scalar=vcol(C, gname, kb),
                in1=rt[:], op0=ALU.mult, op1=ALU.mult), r=[hbuf(C, kb, mt), rb, C.bconst], w=[hnb(kb, mt)])


def ffn(C, l):
    P, ps, psb = C.P, C.ps, C.psb
    ph = C.phase
    hn = ph([128, 8, S], BF16, "hn")
    hnb = lambda kb, mt: P.buf("hn", kb, mt)
    rmsnorm_to_bf16(C, "norm_ffn%d" % l, hn, hnb)
    GRP = 8
    act = [ph([128, S], BF16, "act%d" % i) for i in range(GRP)]
    wd = [ph([128, D], BF16, "wd%d" % i) for i in range(2 * GRP)]
    wg = [ph([128, 8, 128], BF16, "wg%d" % i) for i in range(2)]
    wu = [ph([128, 8, 128], BF16, "wu%d" % i) for i in range(2)]
    G = ph([128, S + 2], F32, "gbuf")
    cv = [ph([128, 512], F32, "cv%d" % i) for i in range(2)]
    sg = [ph([128, 512], F32, "sg%d" % i) for i in range(2)]
    gb = lambda mt: P.buf("gbuf", mt)
    gedge = P.buf("gbuf_edge")
    P.op("dve", lambda e: e.memset(G[:, 0:1], 0.0), w=[gedge])
    P.op("dve", lambda e: e.memset(G[:, S + 1:S + 2], 0.0), w=[gedge])
    cwo = VLAY["ffn_conv_w%d" % l][0]
    cbo = VLAY["ffn_conv_b%d" % l][0]
    groups = []
    c0 = 0
    while c0 < NCB:
        groups.append(list(range(c0, min(c0 + GRP, NCB))))
        c0 += GRP
    wdslot = 0
    for grp in groups:
        slots = {}
        for cb in grp:
            a_t = act[cb % GRP]
            ab = lambda mt, cb=cb: P.buf("act", cb % GRP, mt)
            wgt, wut = wg[cb % 2], wu[cb % 2]
            wgb, wub = P.buf("wg", cb % 2), P.buf("wu", cb % 2)
            ws = wdslot % (2 * GRP)
            wdslot += 1
            slots[cb] = ws
            wdb = P.buf("wd", ws)
            P.dma("pool", wgt[:], C.wup_d[l, cb], dst=wgb)
            P.dma("pool", wut[:], C.wup_d[l, NCB + cb], dst=wub)
            P.dma("pool", wd[ws][:], C.wdn_d[l, cb], dst=wdb)
            for mt in range(4):
                pi = next_ps(C)
                for kc in range(8):
                    P.op("pe", lambda e, pi=pi, kc=kc, mt=mt, wgt=wgt: e.matmul(
                        ps[pi][:], wgt[:, kc, :], hn[:, kc, mt * 512:(mt + 1) * 512],
                        start=(kc == 0), stop=(kc == 7)), r=[wgb, hnb(kc, mt)], w=[psb[pi]])
                P.op("act", lambda e, pi=pi, mt=mt: e.activation(
                    out=G[:, 1 + mt * 512:1 + (mt + 1) * 512], in_=ps[pi][:], func=AF.Copy),
                    r=[psb[pi]], w=[gb(mt)])
            for mt in range(4):
                pi = next_ps(C)
                for kc in range(8):
                    P.op("pe", lambda e, pi=pi, kc=kc, mt=mt, wut=wut: e.matmul(
                        ps[pi][:], wut[:, kc, :], hn[:, kc, mt * 512:(mt + 1) * 512],
                        start=(kc == 0), stop=(kc == 7)), r=[wub, hnb(kc, mt)], w=[psb[pi]])
                cvt, cvb = cv[mt % 2], P.buf("cv", mt % 2)
                sgt, sgb = sg[mt % 2], P.buf("sg", mt % 2)
                rd = [gb(mt), gedge, C.bconst]
                if mt > 0:
                    rd.append(gb(mt - 1))
                if mt < 3:
                    rd.append(gb(mt + 1))
                b0 = mt * 512
                w_ = lambda k, cb=cb: C.vec[:, cwo + cb * 3 + k:cwo + cb * 3 + k + 1]
                P.op("dve", lambda e, cvt=cvt, b0=b0, w_=w_: e.tensor_scalar(
                    out=cvt[:], in0=G[:, b0:b0 + 512], scalar1=w_(0), scalar2=None, op0=ALU.mult),
                    r=rd, w=[cvb])
                P.op("dve", lambda e, cvt=cvt, b0=b0, w_=w_: e.scalar_tensor_tensor(
                    out=cvt[:], in0=G[:, b0 + 1:b0 + 513], scalar=w_(1), in1=cvt[:],
                    op0=ALU.mult, op1=ALU.add), r=rd + [cvb], w=[cvb])
                P.op("dve", lambda e, cvt=cvt, b0=b0, w_=w_: e.scalar_tensor_tensor(
                    out=cvt[:], in0=G[:, b0 + 2:b0 + 514], scalar=w_(2), in1=cvt[:],
                    op0=ALU.mult, op1=ALU.add), r=rd + [cvb], w=[cvb])
                P.op("act", lambda e, cvt=cvt, sgt=sgt, cb=cb: e.activation(
                    out=sgt[:], in_=cvt[:], func=AF.Silu, bias=C.vec[:, cbo + cb:cbo + cb + 1], scale=1.0),
                    r=[cvb, C.bconst], w=[sgb])
                P.op("dve", lambda e, sgt=sgt, pi=pi, a_t=a_t, b0=b0: e.tensor_tensor(
                    out=a_t[:, b0:b0 + 512], in0=sgt[:], in1=ps[pi][:], op=ALU.mult),
                    r=[sgb, psb[pi]], w=[ab(mt)])
        for mt in range(4):
            for db in range(8):
                pi = next_ps(C)
                for i, cb in enumerate(grp):
                    P.op("pe", lambda e, pi=pi, cb=cb, db=db, mt=mt, i=i, n=len(grp), ws=slots[cb]: e.matmul(
                        ps[pi][:], wd[ws][:, db * 128:(db + 1) * 128], act[cb % GRP][:, mt * 512:(mt + 1) * 512],
                        start=(i == 0), stop=(i == n - 1)),
                        r=[P.buf("wd", slots[cb]), P.buf("act", cb % GRP, mt)], w=[psb[pi]])
                P.op("dve", lambda e, pi=pi, db=db, mt=mt: e.tensor_tensor(
                    out=C.hT[:, db, mt * 512:(mt + 1) * 512], in0=C.hT[:, db, mt * 512:(mt + 1) * 512],
                    in1=ps[pi][:], op=ALU.add), r=[psb[pi], hbuf(C, db, mt)], w=[hbuf(C, db, mt)])


def MM(out, lhsT, rhs, start=True, stop=True):
    return lambda e: e.matmul(out, lhsT, rhs, start=start, stop=stop)


def ACTF(out, in_, func, **kw):
    return lambda e: e.activation(out=out, in_=in_, func=func, **kw)


def TT(out, in0, in1, op):
    return lambda e: e.tensor_tensor(out=out, in0=in0, in1=in1, op=op)


def STT(out, in0, scalar, in1, op0, op1):
    return lambda e: e.scalar_tensor_tensor(out=out, in0=in0, scalar=scalar, in1=in1, op0=op0, op1=op1)


def TS(out, in0, s1, s2, op0, op1=None):
    if op1 is None:
        return lambda e: e.tensor_scalar(out=out, in0=in0, scalar1=s1, scalar2=None, op0=op0)
    return lambda e: e.tensor_scalar(out=out, in0=in0, scalar1=s1, scalar2=s2, op0=op0, op1=op1)


def CP(out, in_):
    return lambda e: e.tensor_copy(out=out, in_=in_)


def MS(out, val):
    return lambda e: e.memset(out, val)


def RCP(out, in_):
    return lambda e: e.reciprocal(out=out, in_=in_)


def mixer0(C):
    P, ps, psb, ph = C.P, C.ps, C.psb, C.phase
    hn = ph([128, 8, S], BF16, "hn")
    hnb = lambda kb, mt: P.buf("hn", kb, mt)
    oa = ph([128, 4, S], BF16, "oa")
    oab = lambda h, mt: P.buf("oa", h, mt)
    mark1 = ph.p
    qd = [ph([128, 2, S], BF16, "qd%d" % d) for d in range(2)]
    ki = [ph([128, 2, S], BF16, "ki%d" % d) for d in range(2)]
    qdb = lambda d, blk, mt: P.buf("qd", d, blk, mt)
    kib = lambda d, blk, mt: P.buf("ki", d, blk, mt)
    vt = ph([128, NT, 512], BF16, "vt")
    vtb = lambda tt: P.buf("vt", tt)
    elast = ph([128, 2, 2, NT], F32, "elast")
    elb = lambda mt: P.buf("elast", mt)
    mark2 = ph.p
    win = ph([128, 8, 1056], BF16, "win")
    winb = P.buf("win")
    for kc in range(8):
        P.dma("pool", win[:, kc, 0:1024], C.ab_win_d[:, kc, 0:1024], dst=winb)
    for kc in range(8):
        P.dma("pool", win[:, kc, 1024:1056], C.ab_win_d[:, kc, 1536:1568], dst=winb)
    markW = ph.p
    rmsnorm_to_bf16(C, "norm_mix0", hn, hnb)
    P.barrier()
    ph.p = markW
    winb = P.buf("win")

    wgg = ph([32, 2, 256], BF16, "wgg")
    wggb = P.buf("wgg")
    P.dma("pool", wgg[:], C.gla_wg_d.rearrange("d r c -> r d c"), dst=wggb)
    lra = [ph([32, 512], BF16, "lra%d" % d) for d in range(2)]
    lrab = [P.buf("lra", d) for d in range(2)]
    for d in range(2):
        P.op("dve", MS(lra[d][:], 1.0), w=[lrab[d]])
    e_t = [ph([128, 512], F32, "e%d" % i) for i in range(2)]
    sp_t = [ph([128, 512], F32, "sp%d" % i) for i in range(2)]
    Ep = [ph([128, 512], F32, "Ep%d" % d) for d in range(2)]
    Em = [ph([128, 512], F32, "Em%d" % d) for d in range(2)]
    for mt in range(4):
        tsl = slice(mt * 512, (mt + 1) * 512)
        for d in range(2):
            pi = next_ps(C)
            for kc in range(8):
                P.op("pe", MM(ps[pi][0:16, :], win[:, kc, 1024 + d * 16:1024 + (d + 1) * 16], hn[:, kc, tsl],
                              kc == 0, kc == 7), r=[winb, hnb(kc, mt)], w=[psb[pi]])
            P.op("act", ACTF(lra[d][0:16, :], ps[pi][0:16, :], AF.Copy), r=[psb[pi]], w=[lrab[d]])
        cpi = reserve_ps(C, 4)
        for q in range(4):
            tt = mt * 4 + q
            pi = next_ps(C)
            for d in range(2):
                P.op("pe", MM(ps[pi][:, d * 256:(d + 1) * 256], lra[d][:, q * 128:(q + 1) * 128], wgg[:, d, :]),
                     r=[lrab[d], wggb], w=[psb[pi]])
            et, etb = e_t[q % 2], P.buf("e_t", q % 2)
            spt, spb = sp_t[q % 2], P.buf("sp_t", q % 2)
            P.op("act", ACTF(et[:], ps[pi][:], AF.Exp, scale=-1.0), r=[psb[pi]], w=[etb])
            P.op("act", ACTF(spt[:], et[:], AF.Ln, bias=1.0), r=[etb], w=[spb])
            for d in range(2):
                for blk in range(2):
                    b = cpi[d * 2 + blk]
                    P.op("pe", MM(ps[b][:, q * 128:(q + 1) * 128],
                                  spt[:, d * 256 + blk * 128:d * 256 + (blk + 1) * 128], C.tri[d]),
                         r=[spb, C.bconst], w=[psb[b]])
            pi = next_ps(C)
            for kc in range(8):
                P.op("pe", MM(ps[pi][:], hn[:, kc, tt * 128:(tt + 1) * 128], win[:, kc, 512:1024], kc == 0, kc == 7),
                     r=[winb, hnb(kc, mt)], w=[psb[pi]])
            P.op("act", ACTF(vt[:, tt, :], ps[pi][:], AF.Copy), r=[psb[pi]], w=[vtb(tt)])
        for blk in range(2):
            pq = next_ps(C)
            for kc in range(8):
                P.op("pe", MM(ps[pq][:], win[:, kc, blk * 128:(blk + 1) * 128], hn[:, kc, tsl], kc == 0, kc == 7),
                     r=[winb, hnb(kc, mt)], w=[psb[pq]])
            pk = next_ps(C)
            for kc in range(8):
                P.op("pe", MM(ps[pk][:], win[:, kc, 256 + blk * 128:256 + (blk + 1) * 128], hn[:, kc, tsl],
                              kc == 0, kc == 7), r=[winb, hnb(kc, mt)], w=[psb[pk]])
            for d in range(2):
                b = cpi[d * 2 + blk]
                ep_, epb = Ep[d], P.buf("Ep", d)
                em_, emb = Em[d], P.buf("Em", d)
                P.op("act", ACTF(ep_[:], ps[b][:], AF.Exp), r=[psb[b]], w=[epb])
                P.op("act", ACTF(em_[:], ps[b][:], AF.Exp, scale=-1.0), r=[psb[b]], w=[emb])
                col = 127 if d == 0 else 0
                P.op("dve", CP(elast[:, d, blk, mt * 4:(mt + 1) * 4],
                               ep_[:].rearrange("p (n i) -> p n i", i=128)[:, :, col]), r=[epb], w=[elb(mt)])
                P.op("dve", STT(qd[d][:, blk, tsl], ps[pq][:], 0.125, ep_[:], ALU.mult, ALU.mult),
                     r=[psb[pq], epb], w=[qdb(d, blk, mt)])
                P.op("dve", TT(ki[d][:, blk, tsl], ps[pk][:], em_[:], ALU.mult),
                     r=[psb[pk], emb], w=[kib(d, blk, mt)])
        release_ps(C, cpi)
    dump(C, "qd0", qd[0], None)
    dump(C, "qd1", qd[1], None)
    dump(C, "ki0", ki[0], None)
    dump(C, "ki1", ki[1], None)
    dump(C, "vt", vt, None)
    P.barrier()
    ph.p = mark2

    Sall = [ph([128, NT, 2, 128], BF16, "Sall%d" % d) for d in range(2)]
    mark3 = ph.p
    S2 = [[ph([128, 128], F32, "S2_%d%d" % (d, hp)) for hp in range(2)] for d in range(2)]
    sab = lambda d, n, hp: P.buf("Sall", d, n, hp)
    A_t = [[ph([128, 128], F32, "A%d%d" % (d, hp)) for hp in range(2)] for d in range(2)]
    kin = [ph([128, 128], BF16, "kin%d" % i) for i in range(4)]
    for d in range(2):
        n0 = 0 if d == 0 else NT - 1
        for hp in range(2):
            P.op("dve", MS(S2[d][hp][:], 0.0), w=[P.buf("S2", d, hp)])
            P.op("dve", MS(Sall[d][:, n0, hp, :], 0.0), w=[sab(d, n0, hp)])
    kk = 0
    for step in range(NT):
        chains = []
        for d in range(2):
            n = step if d == 0 else NT - 1 - step
            nxt = n + 1 if d == 0 else n - 1
            if not (0 <= nxt < NT):
                continue
            for hp in range(2):
                chains.append((d, n, nxt, hp, next_ps(C), kk % 4))
                kk += 1
        for (d, n, nxt, hp, pi, ki_) in chains:
            P.op("pe", MM(ps[pi][:, 0:128], ki[d][:, hp, n * 128:(n + 1) * 128], C.identb[:]),
                 r=[kib(d, hp, n // 4), P.buf("identb")], w=[psb[pi]])
        for (d, n, nxt, hp, pi, ki_) in chains:
            P.op("act", ACTF(kin[ki_][:], ps[pi][:, 0:128], AF.Copy), r=[psb[pi]], w=[P.buf("kin", ki_)])
            P.op("act", ACTF(A_t[d][hp][:], S2[d][hp][:], AF.Copy, scale=elast[:, d, hp, n:n + 1]),
                 r=[P.buf("S2", d, hp), elb(n // 4)], w=[P.buf("A", d, hp)])
        for (d, n, nxt, hp, pi, ki_) in chains:
            P.op("pe", MM(ps[pi][:, 128:384], kin[ki_][:], vt[:, n, hp * 256:(hp + 1) * 256]),
                 r=[P.buf("kin", ki_), vtb(n)], w=[psb[pi]])
        for (d, n, nxt, hp, pi, ki_) in chains:
            for hh in range(2):
                rows = slice(hh * 64, hh * 64 + 64)
                P.op("dve", STT(S2[d][hp][rows, :], ps[pi][rows, 128 + hh * 128:256 + hh * 128],
                                elast[rows, d, hp, n:n + 1], A_t[d][hp][rows, :], ALU.mult, ALU.add),
                     r=[psb[pi], P.buf("A", d, hp), elb(n // 4)], w=[P.buf("S2", d, hp)])
        for (d, n, nxt, hp, pi, ki_) in chains:
            P.op("act", ACTF(Sall[d][:, nxt, hp, :], S2[d][hp][:], AF.Copy), r=[P.buf("S2", d, hp)], w=[sab(d, nxt, hp)])
    dump(C, "Sall0", Sall[0], None)
    dump(C, "Sall1", Sall[1], None)
    P.barrier()
    ph.p = mark3

    wgate = ph([128, 8, 512], BF16, "wgate")
    wgb = P.buf("wgate")
    for kc in range(8):
        P.dma("pool", wgate[:, kc, :], C.ab_win_d[:, kc, 1024:1536], dst=wgb)
    mask4 = [ph([128, 512], BF16, "mask4_%d" % d) for d in range(2)]
    m4b = P.buf("mask4")
    for d in range(2):
        src = C.maskf if d == 0 else C.maskb
        for q in range(4):
            P.op("dve", CP(mask4[d][:, q * 128:(q + 1) * 128], src), r=[C.bconst], w=[m4b])
    PT = [[ph([128, 512], BF16, "PT%d%d" % (d, i)) for i in range(2)] for d in range(2)]
    sqh = [ph([128, 512], BF16, "sqh")] * 2
    rsh = [ph([128, 512], F32, "rsh%d" % i) for i in range(2)]
    sgt = ph([128, 512], F32, "sgt")
    tmp = ph([128, 512], F32, "tmp")
    it = 0
    for mt in range(4):
        tsl = slice(mt * 512, (mt + 1) * 512)
        for h in range(4):
            hp, hh = h // 2, h % 2
            rows = slice(hh * 64, hh * 64 + 64)
            par = it % 2
            it += 1
            for d in range(2):
                pi = next_ps(C)
                for q in range(4):
                    ns = slice((mt * 4 + q) * 128, (mt * 4 + q + 1) * 128)
                    P.op("pe", MM(ps[pi][:, q * 128:(q + 1) * 128], ki[d][rows, hp, ns], qd[d][rows, hp, ns]),
                         r=[kib(d, hp, mt), qdb(d, hp, mt)], w=[psb[pi]])
                P.op("dve", TT(PT[d][par][:], ps[pi][:], mask4[d][:], ALU.mult),
                     r=[psb[pi], m4b], w=[P.buf("PT", d, par)])
            po = next_ps(C)
            for q in range(4):
                n = mt * 4 + q
                cs = slice(q * 128, (q + 1) * 128)
                ns = slice(n * 128, (n + 1) * 128)
                P.op("pe", MM(ps[po][:, cs], vt[:, n, h * 128:(h + 1) * 128], PT[0][par][:, cs], True, False),
                     r=[vtb(n), P.buf("PT", 0, par)], w=[psb[po]])
                P.op("pe", MM(ps[po][:, cs], vt[:, n, h * 128:(h + 1) * 128], PT[1][par][:, cs], False, False),
                     r=[vtb(n), P.buf("PT", 1, par)], w=[psb[po]])
                for d in range(2):
                    P.op("pe", MM(ps[po][:, cs], Sall[d][rows, n, hp, :], qd[d][rows, hp, ns], False, d == 1),
                         r=[sab(d, n, hp), qdb(d, hp, mt)], w=[psb[po]])
            sqt_, sqb_ = sqh[par], P.buf("sqh", 0)
            P.op("act", ACTF(sqt_[:], ps[po][:], AF.Square), r=[psb[po]], w=[sqb_])
            pn = next_ps(C)
            P.op("pe", MM(ps[pn][:], C.ones128[:], sqt_[:]), r=[sqb_, P.buf("ones")], w=[psb[pn]])
            rt_, rb_ = rsh[par], P.buf("rsh", par)
            rsqrt_from_psum(P, rt_[:], ps[pn][:], psb[pn], rb_)
            pg = next_ps(C)
            for kc in range(8):
                P.op("pe", MM(ps[pg][:], wgate[:, kc, h * 128:(h + 1) * 128], hn[:, kc, tsl], kc == 0, kc == 7),
                     r=[wgb, hnb(kc, mt)], w=[psb[pg]])
            sgb_, tmb_ = P.buf("sgt"), P.buf("tmp")
            P.op("act", ACTF(sgt[:], ps[pg][:], AF.Silu), r=[psb[pg]], w=[sgb_])
            P.op("dve", STT(tmp[:], ps[po][:], vcol(C, "gla_norm"), rt_[:], ALU.mult, ALU.mult),
                 r=[psb[po], rb_, C.bconst], w=[tmb_])
            P.op("dve", TT(oa[:, h, tsl], tmp[:], sgt[:], ALU.mult), r=[tmb_, sgb_], w=[oab(h, mt)])
    dump(C, "oa", oa, None)
    P.barrier()
    ph.p = mark1

    wsu = ph([128, 8, 512], BF16, "wsu")
    wsv = ph([128, 8, 512], BF16, "wsv")
    wout = ph([128, 8, D], BF16, "wout")
    wsub, wsvb, woutb = P.buf("wsu"), P.buf("wsv"), P.buf("wout")
    for kc in range(8):
        P.dma("pool", wsu[:, kc, :], C.ab_win_d[:, kc, 1568:2080], dst=wsub)
        P.dma("pool", wsv[:, kc, :], C.ab_win_d[:, kc, 2080:2592], dst=wsvb)
        P.dma("pool", wout[:, kc, :], C.ab_wout_d[:, kc, :], dst=woutb)
    wsT = ph([128, 4, 128], BF16, "wsT")
    wsTb = P.buf("wsT")
    P.dma("pool", wsT[:], C.sgu_wsT_d, dst=wsTb)
    lng = ph([128, 512], F32, "lng")
    lnbt = ph([128, 512], F32, "lnb")
    bsb4 = ph([128, 4, 4, 128], F32, "bsb4")
    lnbuf = P.buf("lnconst")
    P.dma("sp", lng[:], C.sgu_ln_d[0:1, :].partition_broadcast(128), dst=lnbuf)
    P.dma("sp", lnbt[:], C.sgu_ln_d[1:2, :].partition_broadcast(128), dst=lnbuf)
    for n in range(4):
        P.dma("sp", bsb4[:, :, n, :], C.sgu_bs_d.rearrange("o (g i) -> o g i", g=4).partition_broadcast(128),
              dst=lnbuf)
    ut = [ph([128, 4, 512], BF16, "ut%d" % i) for i in range(2)]
    sgu = [ph([128, 4, 512], BF16, "sgu%d" % i) for i in range(2)]
    gv = [ph([128, 512], F32, "gv%d" % i) for i in range(2)]
    vv = [ph([128, 512], BF16, "vv%d" % i) for i in range(2)]
    st6 = [ph([128, 6], F32, "st6_%d" % i) for i in range(2)]
    mv = [ph([128, 2], F32, "mv%d" % i) for i in range(2)]
    rs1 = [ph([128, 1], F32, "rs1_%d" % i) for i in range(2)]
    tmq = [ph([128, 512], F32, "tmq%d" % i) for i in range(2)]
    for mt in range(4):
        tsl = slice(mt * 512, (mt + 1) * 512)
        u_, sg4 = ut[mt % 2], sgu[mt % 2]
        for g in range(4):
            pu = next_ps(C)
            for kc in range(8):
                P.op("pe", MM(ps[pu][:], wsu[:, kc, g * 128:(g + 1) * 128], hn[:, kc, tsl], kc == 0, kc == 7),
                     r=[wsub, hnb(kc, mt)], w=[psb[pu]])
            P.op("act", ACTF(u_[:, g, :], ps[pu][:], AF.Gelu_apprx_tanh), r=[psb[pu]], w=[P.buf("ut", mt % 2, g)])
        pm = reserve_ps(C, 4)
        for q in range(4):
            tt = mt * 4 + q
            par = q % 2
            pv = next_ps(C)
            for kc in range(8):
                P.op("pe", MM(ps[pv][:], hn[:, kc, tt * 128:(tt + 1) * 128], wsv[:, kc, :], kc == 0, kc == 7),
                     r=[wsvb, hnb(kc, mt)], w=[psb[pv]])
            gvb = P.buf("gv", par)
            P.op("act", ACTF(gv[par][:], ps[pv][:], AF.Gelu_apprx_tanh), r=[psb[pv]], w=[gvb])
            stb, mvb, rsb = P.buf("st6", par), P.buf("mv", par), P.buf("rs1", par)
            P.op("dve", lambda e, par=par: e.bn_stats(out=st6[par][:], in_=gv[par][:]), r=[gvb], w=[stb])
            P.op("dve", lambda e, par=par: e.bn_aggr(out=mv[par][:], in_=st6[par][:]), r=[stb], w=[mvb])
            P.op("act", ACTF(rs1[par][:], mv[par][:, 1:2], AF.Sqrt, bias=EPS), r=[mvb], w=[rsb])
            P.op("dve", RCP(rs1[par][:], rs1[par][:]), r=[rsb], w=[rsb])
            P.op("dve", TS(gv[par][:], gv[par][:], mv[par][:, 0:1], rs1[par][:, 0:1], ALU.subtract, ALU.mult),
                 r=[gvb, mvb, rsb], w=[gvb])
            P.op("dve", TT(gv[par][:], gv[par][:], lng[:], ALU.mult), r=[gvb, lnbuf], w=[gvb])
            vvb = P.buf("vv", par)
            P.op("dve", TT(vv[par][:], gv[par][:], lnbt[:], ALU.add), r=[gvb, lnbuf], w=[vvb])
            for g in range(4):
                P.op("pe", MM(ps[pm[g]][:, q * 128:(q + 1) * 128], vv[par][:, g * 128:(g + 1) * 128], wsT[:, g, :]),
                     r=[vvb, wsTb], w=[psb[pm[g]]])
        for g in range(4):
            tq, tqb = tmq[g % 2], P.buf("tmq", g % 2)
            P.op("dve", TT(tq[:], ps[pm[g]][:], bsb4[:, g, :, :].rearrange("p n i -> p (n i)"), ALU.add),
                 r=[psb[pm[g]], lnbuf], w=[tqb])
            P.op("dve", TT(sg4[:, g, :], tq[:], u_[:, g, :], ALU.mult),
                 r=[tqb, P.buf("ut", mt % 2, g)], w=[P.buf("sgu", mt % 2, g)])
        release_ps(C, pm)
        for db in range(8):
            po = next_ps(C)
            for kc in range(8):
                if kc < 4:
                    rhs, rb_ = oa[:, kc, tsl], oab(kc, mt)
                else:
                    rhs, rb_ = sg4[:, kc - 4, :], P.buf("sgu", mt % 2, kc - 4)
                P.op("pe", MM(ps[po][:], wout[:, kc, db * 128:(db + 1) * 128], rhs, kc == 0, kc == 7),
                     r=[woutb, rb_], w=[psb[po]])
            P.op("dve", TT(C.hT[:, db, tsl], C.hT[:, db, tsl], ps[po][:], ALU.add),
                 r=[psb[po], hbuf(C, db, mt)], w=[hbuf(C, db, mt)])
        if mt == 3:
            dump(C, "sgu", sg4, None)


def mixer1(C):
    P, ps, psb, ph = C.P, C.ps, C.psb, C.phase
    W = HPG * 128
    hn = ph([128, 8, S], BF16, "hn")
    hnb = lambda kb, mt: P.buf("hn", kb, mt)
    sc = {}
    for nm in ("g", "lnb", "beta", "c", "negc", "nec", "ECL", "ekend"):
        sc[nm] = ph([128, NT, 16], F32, "sc_" + nm)
    scb = P.buf("gdn_scalars")
    mark0 = ph.p
    markg0 = ph.p
    wsm = ph([128, 8, 32], F32, "wsm")
    wsmb = P.buf("wsm")
    P.dma("sp", wsm[:], C.gdn_wsm_d, dst=wsmb)
    gvec = ph([128, 32], F32, "gvec")
    gvb = P.buf("gvec")
    P.dma("sp", gvec[:], C.gdn_vec_d.partition_broadcast(128), dst=gvb)
    nea = ph([128, 16], F32, "nea")
    neab = P.buf("nea")
    P.op("act", ACTF(nea[:], gvec[:, 16:32], AF.Exp), r=[gvb], w=[neab])
    P.op("dve", TS(nea[:], nea[:], -1.0, None, ALU.mult), r=[neab], w=[neab])
    ab_tm = ph([128, NT, 32], F32, "ab_tm")
    abb = P.buf("ab_tm")
    t1 = ph([128, NT, 16], F32, "t1")
    t2 = ph([128, NT, 16], F32, "t2")
    t1b, t2b = P.buf("t1"), P.buf("t2")
    hn32 = ph([128, 8, 512], F32, "hn32")
    h32b = lambda kb: P.buf("hn32", kb)
    sqt = [ph([128, 512], BF16, "sq%d" % i) for i in range(2)]
    rstd = [ph([128, 512], F32, "rstd%d" % i) for i in range(2)]
    pa = reserve_ps(C, 1)[0]
    for mt in range(4):
        tsl = slice(mt * 512, (mt + 1) * 512)
        rb = P.buf("rstd", mt % 2)
        rt = rstd[mt % 2]
        rms_rstd(C, mt, sqt, rt, rb)
        for kb in range(8):
            P.op("dve", STT(hn32[:, kb, :], C.hT[:, kb, tsl], vcol(C, "norm_mix1", kb), rt[:], ALU.mult, ALU.mult),
                 r=[hbuf(C, kb, mt), rb, C.bconst], w=[h32b(kb)])
            P.op("act", ACTF(hn[:, kb, tsl], hn32[:, kb, :], AF.Copy), r=[h32b(kb)], w=[hnb(kb, mt)])
        for q in range(4):
            tt = mt * 4 + q
            for kc in range(8):
                P.op("pe", MM(ps[pa][:, tt * 32:(tt + 1) * 32], hn32[:, kc, q * 128:(q + 1) * 128], wsm[:, kc, :],
                              kc == 0, kc == 7), r=[wsmb, h32b(kc)], w=[psb[pa]])
    release_ps(C, [pa])
    P.op("act", ACTF(ab_tm[:].rearrange("p t c -> p (t c)"), ps[pa][:], AF.Copy), r=[psb[pa]], w=[abb])
    for tt in range(NT):
        P.op("dve", TT(t1[:, tt, :], ab_tm[:, tt, 16:32], gvec[:, 0:16], ALU.add), r=[abb, gvb], w=[t1b])
    fl = lambda t: t[:].rearrange("p t c -> p (t c)")
    P.op("act", ACTF(fl(t1), fl(t1), AF.Exp), r=[t1b], w=[t1b])
    P.op("act", ACTF(fl(t1), fl(t1), AF.Ln, bias=1.0), r=[t1b], w=[t1b])
    for tt in range(NT):
        P.op("dve", TT(sc["g"][:, tt, :], t1[:, tt, :], nea[:], ALU.mult), r=[t1b, neab], w=[scb])
    P.op("act", ACTF(t2[:], ab_tm[:, :, 0:16], AF.Exp, scale=-1.0), r=[abb], w=[t2b])
    P.op("act", ACTF(fl(t2), fl(t2), AF.Ln, bias=1.0), r=[t2b], w=[t2b])
    P.op("dve", TS(fl(sc["lnb"]), fl(t2), -1.0, None, ALU.mult), r=[t2b], w=[scb])
    P.op("act", ACTF(fl(sc["beta"]), fl(sc["lnb"]), AF.Exp), r=[scb], w=[scb])
    pc = next_ps(C)
    for tt in range(NT):
        for d in range(2):
            P.op("pe", MM(ps[pc][:, tt * 16 + d * 8:tt * 16 + d * 8 + 8], C.mask01[d], sc["g"][:, tt, d * 8:(d + 1) * 8]),
                 r=[scb, C.bconst], w=[psb[pc]])
    P.op("act", ACTF(fl(sc["c"]), ps[pc][:, 0:NT * 16], AF.Copy), r=[psb[pc]], w=[scb])
    P.op("dve", TS(fl(sc["negc"]), fl(sc["c"]), -1.0, None, ALU.mult), r=[scb], w=[scb])
    P.op("act", ACTF(fl(sc["nec"]), fl(sc["c"]), AF.Exp), r=[scb], w=[scb])
    P.op("dve", TS(fl(sc["nec"]), fl(sc["nec"]), -1.0, None, ALU.mult), r=[scb], w=[scb])
    pl = next_ps(C)
    for d in range(2):
        P.op("pe", MM(ps[pl][:, d * 128:(d + 1) * 128], C.esel[d], sc["c"][:, :, d * 8:(d + 1) * 8]),
             r=[scb, C.bconst], w=[psb[pl]])
    for d in range(2):
        cl = ps[pl][:, d * 128:(d + 1) * 128].rearrange("p (t h) -> p t h", h=8)
        P.op("act", ACTF(sc["ECL"][:, :, d * 8:(d + 1) * 8], cl, AF.Exp), r=[psb[pl]], w=[scb])
        P.op("dve", TT(t1[:, :, d * 8:(d + 1) * 8], cl, sc["c"][:, :, d * 8:(d + 1) * 8], ALU.subtract),
             r=[psb[pl], scb], w=[t1b])
    P.op("act", ACTF(fl(sc["ekend"]), fl(t1), AF.Exp), r=[t1b], w=[scb])
    P.barrier()
    ph.p = markg0
    mark1 = ph.p
    cwo = VLAY["gdn_conv_w"][0]
    if C.gdn_stop == 0:
        return

    for grp in range(8 // HPG):
        h0 = grp * HPG
        ph.p = mark1
        qT = ph([128, HPG, S], BF16, "qT")
        kT = ph([128, HPG, S], BF16, "kT")
        vtm = ph([128, NT, W], BF16, "vtm")
        ob = ph([128, HPG, S], BF16, "ob")
        qTb = lambda hl, mt: P.buf("qT", hl, mt)
        kTb = lambda hl, mt: P.buf("kT", hl, mt)
        vtb = lambda tt: P.buf("vtm", tt)
        obb = lambda hl, n: P.buf("ob", hl, n)
        mark2 = ph.p
        wblk = [ph([128, 8, 128], BF16, "wblk%d" % i) for i in range(2)]
        Gb = ph([128, S + 2], F32, "gbuf")
        cv = [ph([128, 512], F32, "cv%d" % i) for i in range(2)]
        so = [ph([128, 512], F32, "so%d" % i) for i in range(4)]
        sob = [ph([128, 512], BF16, "sob")] * 2
        sqn = [ph([128, 512], BF16, "sqn%d" % i) for i in range(4)]
        rsn = [ph([128, 512], F32, "rsn")] * 2
        gb_ = lambda mt: P.buf("gbuf", mt)
        gedge = P.buf("gbuf_edge")
        P.op("dve", MS(Gb[:, 0:1], 0.0), w=[gedge])
        P.op("dve", MS(Gb[:, S + 1:S + 2], 0.0), w=[gedge])
        wi = 0
        for kind in range(3):
            for hl in range(HPG):
                blk = kind * 8 + h0 + hl
                wt, wtb = wblk[wi % 2], P.buf("wblk", wi % 2)
                wi += 1
                P.dma("pool", wt[:], C.gdn_win_d[blk], dst=wtb)
                for mt in range(4):
                    pi = next_ps(C)
                    for kc in range(8):
                        P.op("pe", MM(ps[pi][:], wt[:, kc, :], hn[:, kc, mt * 512:(mt + 1) * 512], kc == 0, kc == 7),
                             r=[wtb, hnb(kc, mt)], w=[psb[pi]])
                    P.op("act", ACTF(Gb[:, 1 + mt * 512:1 + (mt + 1) * 512], ps[pi][:], AF.Copy),
                         r=[psb[pi]], w=[gb_(mt)])
                pns = reserve_ps(C, 4) if kind < 2 else None
                for mt in range(4):
                    par = mt % 2
                    tsl = slice(mt * 512, (mt + 1) * 512)
                    cvt, cvb = cv[par], P.buf("cv", par)
                    rd = [gb_(mt), gedge, C.bconst]
                    if mt > 0:
                        rd.append(gb_(mt - 1))
                    if mt < 3:
                        rd.append(gb_(mt + 1))
                    b0 = mt * 512
                    w_ = lambda k: C.vec[:, cwo + blk * 3 + k:cwo + blk * 3 + k + 1]
                    P.op("dve", TS(cvt[:], Gb[:, b0:b0 + 512], w_(0), None, ALU.mult), r=rd, w=[cvb])
                    P.op("dve", STT(cvt[:], Gb[:, b0 + 1:b0 + 513], w_(1), cvt[:], ALU.mult, ALU.add), r=rd + [cvb], w=[cvb])
                    P.op("dve", STT(cvt[:], Gb[:, b0 + 2:b0 + 514], w_(2), cvt[:], ALU.mult, ALU.add), r=rd + [cvb], w=[cvb])
                    if kind < 2:
                        sot, sotb = so[mt], P.buf("so", mt)
                        P.op("act", ACTF(sot[:], cvt[:], AF.Silu), r=[cvb], w=[sotb])
                        sqt_, sqb_ = sqn[mt], P.buf("sqn", mt)
                        P.op("act", ACTF(sqt_[:], sot[:], AF.Square), r=[sotb], w=[sqb_])
                        P.op("pe", MM(ps[pns[mt]][:], C.ones1[:], sqt_[:]), r=[sqb_, P.buf("ones")], w=[psb[pns[mt]]])
                    else:
                        sbt, sbtb = sob[par], P.buf("sob", 0)
                        P.op("act", ACTF(sbt[:], cvt[:], AF.Silu), r=[cvb], w=[sbtb])
                        pt = next_ps(C)
                        for q in range(4):
                            P.op("pe", MM(ps[pt][:, q * 128:(q + 1) * 128], sbt[:, q * 128:(q + 1) * 128], C.identb[:]),
                                 r=[sbtb, P.buf("identb")], w=[psb[pt]])
                        P.op("act", ACTF(vtm[:, mt * 4:(mt + 1) * 4, hl * 128:(hl + 1) * 128],
                                         ps[pt][:].rearrange("p (q v) -> p q v", v=128), AF.Copy),
                             r=[psb[pt]], w=[vtb(mt * 4 + q_) for q_ in range(4)])
                if kind < 2:
                    for mt in range(4):
                        tsl = slice(mt * 512, (mt + 1) * 512)
                        sot, sotb = so[mt], P.buf("so", mt)
                        rt_, rb_ = rsn[0], P.buf("rsn", 0)
                        rsqrt_from_psum(P, rt_[:], ps[pns[mt]][:], psb[pns[mt]], rb_)
                        if kind == 0:
                            P.op("dve", STT(qT[:, hl, tsl], sot[:], 128.0 ** -0.5, rt_[:], ALU.mult, ALU.mult),
                                 r=[sotb, rb_], w=[qTb(hl, mt)])
                        else:
                            P.op("dve", TT(kT[:, hl, tsl], sot[:], rt_[:], ALU.mult), r=[sotb, rb_], w=[kTb(hl, mt)])
                    release_ps(C, pns)
        P.barrier()
        ph.p = mark2
        if C.gdn_stop == 1:
            return

        of_ = ph([128, HPG, S], BF16, "of")
        ofb = lambda hl, n: P.buf("of", hl, n)
        oo = [of_, ob]
        oob = [ofb, obb]
        markT = ph.p
        NV = 2 * HPG
        WW = NV * 128
        VH = [(d, hl) for d in range(2) for hl in range(HPG)]
        HS = [slice(v * 128, (v + 1) * 128) for v in range(NV)]
        t = {}
        for nm in ("Rt", "vn", "Sbf", "mniW", "mnsW", "idW", "id2W"):
            t[nm] = ph([128, WW], BF16, nm)
        t["Sm"] = ph([128, WW], F32, "Sm")
        tsets = []
        for k_ in range(2):
            ts_ = {}
            for nm in ("tmpA", "tmpB"):
                ts_[nm] = ph([128, WW], F32, "%s_%d" % (nm, k_))
            for nm in ("DTi", "ETs", "ECb", "X0", "X1", "Y0", "Y1", "P0", "P1", "TN", "ZT", "Rn"):
                ts_[nm] = ph([128, WW], BF16, "%s_%d" % (nm, k_))
            tsets.append(ts_)
        csets = [{nm: ph([128, WW], BF16, "%s%d" % (nm, i)) for nm in ("MT", "Aqk", "kend", "qdec")} for i in range(3)]
        B_ = lambda nm, *k: P.buf("g2_" + nm, *k)
        for v, (d, hl) in enumerate(VH):
            P.op("pool", CP(t["mniW"][:, HS[v]], C.mneg_incl[d]), r=[C.bconst], w=[B_("maskW")])
            P.op("pool", CP(t["mnsW"][:, HS[v]], C.mneg_strict[d]), r=[C.bconst], w=[B_("maskW")])
            P.op("pool", CP(t["idW"][:, HS[v]], C.ident), r=[C.bconst], w=[B_("maskW")])
            P.op("pool", TS(t["id2W"][:, HS[v]], C.ident, 2.0, None, ALU.mult), r=[C.bconst], w=[B_("maskW")])
        v3 = lambda tt_: tt_[:].rearrange("p (h i) -> p h i", i=128)
        v3p = lambda ap: ap.rearrange("p (h i) -> p h i", i=128)
        order = [list(range(NT)), list(range(NT - 1, -1, -1))]
        hb_pool = list(range(8))

        def hb_alloc():
            assert hb_pool, "out of PSUM banks"
            return hb_pool.pop(0)

        def hb_free(x):
            hb_pool.append(x)

        hp = lambda x: ps[x]
        hbb = lambda x: psb[x]

        def pre(i):
            ci = i % 3
            ti = i % 2
            cs_ = csets[ci]
            tp = tsets[ti]
            Bt = lambda nm, *k: P.buf("g2_" + nm, ti, *k)
            nn = [order[d][i] for (d, hl) in VH]
            cols = [slice(n * 128, (n + 1) * 128) for n in nn]
            mts = [n // 4 for n in nn]
            scol = lambda nm, v: sc[nm][:, nn[v], VH[v][0] * 8 + h0 + VH[v][1]:VH[v][0] * 8 + h0 + VH[v][1] + 1]
            bc = lambda nm, v: scol(nm, v).to_broadcast([128, 128])
            pA, pB = hb_alloc(), hb_alloc()
            for v, (d, hl) in enumerate(VH):
                P.op("pe", MM(hp(pA)[:, HS[v]], bc("g", v), C.mask01[d]), r=[scb, C.bconst], w=[hbb(pA)])
            for v, (d, hl) in enumerate(VH):
                P.op("pe", MM(hp(pB)[:, HS[v]], bc("g", v), C.mask01[d], True, False), r=[scb, C.bconst], w=[hbb(pB)])
                P.op("pe", MM(hp(pB)[:, HS[v]], bc("lnb", v), C.ident, False, True), r=[scb, C.bconst], w=[hbb(pB)])
            yield
            P.op("dve", TT(tp["tmpA"][:], hp(pA)[:], t["mniW"][:], ALU.add), r=[hbb(pA), B_("maskW")], w=[Bt("tmpA")])
            P.op("act", ACTF(tp["ECb"][:], hp(pA)[:], AF.Exp), r=[hbb(pA)], w=[Bt("ECb")])
            P.op("dve", TT(tp["tmpB"][:], hp(pB)[:], t["mnsW"][:], ALU.add), r=[hbb(pB), B_("maskW")], w=[Bt("tmpB")])
            hb_free(pA)
            hb_free(pB)
            yield
            pG, pQ = hb_alloc(), hb_alloc()
            for v, (d, hl) in enumerate(VH):
                P.op("pe", MM(hp(pG)[:, HS[v]], kT[:, hl, cols[v]], kT[:, hl, cols[v]]), r=[kTb(hl, mts[v])], w=[hbb(pG)])
                P.op("pe", MM(hp(pQ)[:, HS[v]], kT[:, hl, cols[v]], qT[:, hl, cols[v]]),
                     r=[kTb(hl, mts[v]), qTb(hl, mts[v])], w=[hbb(pQ)])
            for v in range(NV):
                P.op("act", ACTF(tp["ETs"][:, HS[v]], tp["tmpB"][:, HS[v]], AF.Exp, bias=scol("negc", v)),
                     r=[Bt("tmpB"), scb], w=[Bt("ETs")])
            for v in range(NV):
                P.op("act", ACTF(tp["DTi"][:, HS[v]], tp["tmpA"][:, HS[v]], AF.Exp, bias=scol("negc", v)),
                     r=[Bt("tmpA"), scb], w=[Bt("DTi")])
            for d in range(2):
                vs = slice(d * HPG * 128, (d + 1) * HPG * 128)
                P.op("pool", TT(v3p(cs_["qdec"][:, vs]), qT[:, :, cols[d * HPG]], v3p(tp["ECb"][:, vs]), ALU.mult),
                     r=[Bt("ECb")] + [qTb(hl, mts[d * HPG]) for hl in range(HPG)], w=[B_("qdec", ci)])
            yield
            xb = lambda k: Bt("X", k)
            yb = lambda k: Bt("Y", k)
            pb = lambda k: Bt("P", k)
            Xt, Yt, Pt = [tp["X0"], tp["X1"]], [tp["Y0"], tp["Y1"]], [tp["P0"], tp["P1"]]
            P.op("dve", STT(Xt[0][:], hp(pG)[:], -1.0, tp["ETs"][:], ALU.mult, ALU.mult),
                 r=[hbb(pG), Bt("ETs")], w=[xb(0)])
            P.op("dve", TT(cs_["Aqk"][:], hp(pQ)[:], tp["DTi"][:], ALU.mult), r=[hbb(pQ), Bt("DTi")], w=[B_("Aqk", ci)])
            hb_free(pG)
            hb_free(pQ)
            yield
            pY = hb_alloc()
            for v in range(NV):
                P.op("pe", MM(hp(pY)[:, HS[v]], Xt[0][:, HS[v]], C.identb[:]), r=[xb(0), P.buf("identb")], w=[hbb(pY)])
            P.op("pool", TT(Pt[0][:], Xt[0][:], t["idW"][:], ALU.add), r=[xb(0), B_("maskW")], w=[pb(0)])
            yield
            P.op("act", ACTF(Yt[0][:], hp(pY)[:], AF.Copy), r=[hbb(pY)], w=[yb(0)])
            hb_free(pY)
            P.op("pool", TT(tp["TN"][:], t["idW"][:], Yt[0][:], ALU.subtract), r=[yb(0), B_("maskW")], w=[Bt("TN")])
            yield
            cur = 0
            for k in range(1, NLEV + 1):
                nx = 1 - cur
                pX, pY = hb_alloc(), hb_alloc()
                for v in range(NV):
                    P.op("pe", MM(hp(pX)[:, HS[v]], Yt[cur][:, HS[v]], Xt[cur][:, HS[v]]), r=[xb(cur), yb(cur)], w=[hbb(pX)])
                for v in range(NV):
                    P.op("pe", MM(hp(pY)[:, HS[v]], Xt[cur][:, HS[v]], Yt[cur][:, HS[v]]), r=[xb(cur), yb(cur)], w=[hbb(pY)])
                yield
                P.op("dve", CP(Yt[nx][:], hp(pY)[:]), r=[hbb(pY)], w=[yb(nx)])
                P.op("act", ACTF(Xt[nx][:], hp(pX)[:], AF.Copy), r=[hbb(pX)], w=[xb(nx)])
                hb_free(pX)
                hb_free(pY)
                yield
                pP = hb_alloc()
                for v in range(NV):
                    P.op("pe", MM(hp(pP)[:, HS[v]], Yt[nx][:, HS[v]], Pt[cur][:, HS[v]]),
                         r=[pb(cur), yb(nx)], w=[hbb(pP)])
                yield
                P.op("dve", TT(Pt[nx][:], hp(pP)[:], Pt[cur][:], ALU.add), r=[hbb(pP), pb(cur)], w=[pb(nx)])
                hb_free(pP)
                cur = nx
            yield
            Z = Pt[cur]
            pZ, pE = hb_alloc(), hb_alloc()
            for v in range(NV):
                P.op("pe", MM(hp(pZ)[:, HS[v]], Z[:, HS[v]], C.identb[:]), r=[pb(cur), P.buf("identb")], w=[hbb(pZ)])
                P.op("pe", MM(hp(pE)[:, HS[v]], tp["TN"][:, HS[v]], Z[:, HS[v]]), r=[pb(cur), Bt("TN")], w=[hbb(pE)])
            yield
            P.op("act", ACTF(tp["ZT"][:], hp(pZ)[:], AF.Copy), r=[hbb(pZ)], w=[Bt("ZT")])
            P.op("dve", STT(tp["Rn"][:], hp(pE)[:], -1.0, t["id2W"][:], ALU.mult, ALU.add),
                 r=[hbb(pE), B_("maskW")], w=[Bt("Rn")])
            hb_free(pZ)
            hb_free(pE)
            yield
            pM = hb_alloc()
            for v in range(NV):
                P.op("pe", MM(hp(pM)[:, HS[v]], tp["ZT"][:, HS[v]], tp["Rn"][:, HS[v]]), r=[Bt("ZT"), Bt("Rn")], w=[hbb(pM)])
            pK = hb_alloc()
            for v, (d, hl) in enumerate(VH):
                P.op("pe", MM(hp(pK)[:, HS[v]], kT[:, hl, cols[v]], C.identb[:]), r=[kTb(hl, mts[v]), P.buf("identb")],
                     w=[hbb(pK)])
            yield
            for v in range(NV):
                P.op("act", ACTF(cs_["kend"][:, HS[v]], hp(pK)[:, HS[v]], AF.Copy, scale=scol("ekend", v)),
                     r=[hbb(pK), scb], w=[B_("kend", ci)])
            hb_free(pK)
            for v in range(NV):
                P.op("act", ACTF(cs_["MT"][:, HS[v]], hp(pM)[:, HS[v]], AF.Copy, scale=scol("beta", v)),
                     r=[hbb(pM), scb], w=[B_("MT", ci)])
            hb_free(pM)

        def scan(i):
            ci = i % 3
            cs_ = csets[ci]
            nn = [order[d][i] for (d, hl) in VH]
            cols = [slice(n * 128, (n + 1) * 128) for n in nn]
            mts = [n // 4 for n in nn]
            scol = lambda nm, v: sc[nm][:, nn[v], VH[v][0] * 8 + h0 + VH[v][1]:VH[v][0] * 8 + h0 + VH[v][1] + 1]
            csb = {nm: B_(nm, ci) for nm in ("MT", "Aqk", "kend", "qdec")}
            Rt, vn, Sm, Sbf = t["Rt"], t["vn"], t["Sm"], t["Sbf"]
            p1 = hb_alloc()
            for v, (d, hl) in enumerate(VH):
                P.op("pe", MM(hp(p1)[:, HS[v]], kT[:, hl, cols[v]], Sbf[:, HS[v]]), r=[kTb(hl, mts[v]), B_("Sbf")], w=[hbb(p1)])
            yield
            for v, (d, hl) in enumerate(VH):
                P.op("dve", STT(Rt[:, HS[v]], hp(p1)[:, HS[v]], scol("nec", v), vtm[:, nn[v], hl * 128:(hl + 1) * 128],
                                ALU.mult, ALU.add), r=[hbb(p1), scb, vtb(nn[v])], w=[B_("Rt")])
            hb_free(p1)
            yield
            p2 = hb_alloc()
            for v in range(NV):
                P.op("pe", MM(hp(p2)[:, HS[v]], cs_["MT"][:, HS[v]], Rt[:, HS[v]]), r=[csb["MT"], B_("Rt")], w=[hbb(p2)])
            yield
            P.op("act", ACTF(vn[:], hp(p2)[:], AF.Copy), r=[hbb(p2)], w=[B_("vn")])
            hb_free(p2)
            yield
            p3, p4 = hb_alloc(), hb_alloc()
            for v in range(NV):
                P.op("pe", MM(hp(p4)[:, HS[v]], cs_["kend"][:, HS[v]], vn[:, HS[v]]), r=[csb["kend"], B_("vn")], w=[hbb(p4)])
            for v in range(NV):
                P.op("pe", MM(hp(p3)[:, HS[v]], Sbf[:, HS[v]], cs_["qdec"][:, HS[v]], True, False),
                     r=[B_("Sbf"), csb["qdec"]], w=[hbb(p3)])
                P.op("pe", MM(hp(p3)[:, HS[v]], vn[:, HS[v]], cs_["Aqk"][:, HS[v]], False, True),
                     r=[B_("vn"), csb["Aqk"]], w=[hbb(p3)])
            yield
            for v in range(NV):
                P.op("dve", STT(Sm[:, HS[v]], Sm[:, HS[v]], scol("ECL", v), hp(p4)[:, HS[v]], ALU.mult, ALU.add),
                     r=[hbb(p4), scb, B_("Sm")], w=[B_("Sm")])
            for d in range(2):
                vs = slice(d * HPG * 128, (d + 1) * HPG * 128)
                P.op("act", ACTF(oo[d][:, :, cols[d * HPG]], v3p(hp(p3)[:, vs]), AF.Copy), r=[hbb(p3)],
                     w=[oob[d](hl, nn[d * HPG]) for hl in range(HPG)])
            hb_free(p3)
            hb_free(p4)
            yield
            P.op("dve", CP(Sbf[:], Sm[:]), r=[B_("Sm")], w=[B_("Sbf")])

        def step(g_):
            try:
                next(g_)
                return True
            except StopIteration:
                return False

        P.op("dve", MS(t["Sm"][:], 0.0), w=[B_("Sm")])
        P.op("dve", MS(t["Sbf"][:], 0.0), w=[B_("Sbf")])
        pres = {0: pre(0), 1: pre(1)}
        while step(pres[0]):
            step(pres[1])
        for i in range(NT):
            if i + 2 < NT:
                pres[i + 2] = pre(i + 2)
            must = [scan(i)]
            if i + 1 < NT:
                must.append(pres[i + 1])
            extra = pres.get(i + 2)
            while must:
                must = [g_ for g_ in must if step(g_)]
                if extra is not None and not step(extra):
                    extra = None

        P.barrier()
        ph.p = markT
        osum = ph([128, 512], F32, "osum")
        sqo = ph([128, 512], BF16, "sqo")
        rso = ph([128, 512], F32, "rso")
        zs = ph([128, 512], BF16, "zs")
        wz = ph([128, 8, 128], BF16, "wz")
        ong = ph([128, HPG, S], BF16, "ong")
        ongb = lambda hl, mt: P.buf("ong", hl, mt)
        woutg = ph([128, HPG, D], BF16, "woutg")
        woutgb = P.buf("woutg")
        for hl in range(HPG):
            P.dma("pool", woutg[:, hl, :], C.gdn_wout_d[:, h0 + hl, :], dst=woutgb)
        Bn = lambda nm: P.buf("g2n_" + nm)
        zs4 = [ph([128, 512], BF16, "zs%d" % i) for i in range(4)]
        for hl in range(HPG):
            h = h0 + hl
            P.dma("pool", wz[:], C.gdn_win_d[24 + h], dst=Bn("wz"))
            for mt in range(4):
                tsl = slice(mt * 512, (mt + 1) * 512)
                pz = next_ps(C)
                for kc in range(8):
                    P.op("pe", MM(ps[pz][:], wz[:, kc, :], hn[:, kc, tsl], kc == 0, kc == 7),
                         r=[Bn("wz"), hnb(kc, mt)], w=[psb[pz]])
                P.op("act", ACTF(zs4[mt][:], ps[pz][:], AF.Silu), r=[psb[pz]], w=[P.buf("g2n_zs", mt)])
            for mt in range(4):
                tsl = slice(mt * 512, (mt + 1) * 512)
                rdo = [oob[d](hl, mt * 4 + q_) for d in range(2) for q_ in range(4)]
                P.op("dve", TT(osum[:], of_[:, hl, tsl], ob[:, hl, tsl], ALU.add), r=rdo, w=[Bn("osum")])
                P.op("act", ACTF(sqo[:], osum[:], AF.Square), r=[Bn("osum")], w=[Bn("sqo")])
                pn = next_ps(C)
                P.op("pe", MM(ps[pn][:], C.ones128[:], sqo[:]), r=[Bn("sqo"), P.buf("ones")], w=[psb[pn]])
                rsqrt_from_psum(P, rso[:], ps[pn][:], psb[pn], Bn("rso"))
                P.op("dve", TT(rso[:], rso[:], zs4[mt][:], ALU.mult), r=[Bn("rso"), P.buf("g2n_zs", mt)], w=[Bn("rso")])
                P.op("dve", STT(ong[:, hl, tsl], osum[:], vcol(C, "gdn_norm"), rso[:], ALU.mult, ALU.mult),
                     r=[Bn("osum"), Bn("rso"), C.bconst], w=[ongb(hl, mt)])
        for mt in range(4):
            tsl = slice(mt * 512, (mt + 1) * 512)
            for db in range(8):
                po = next_ps(C)
                for hl in range(HPG):
                    P.op("pe", MM(ps[po][:], woutg[:, hl, db * 128:(db + 1) * 128], ong[:, hl, tsl], hl == 0, hl == HPG - 1),
                         r=[woutgb, ongb(hl, mt)], w=[psb[po]])
                P.op("dve", TT(C.hT[:, db, tsl], C.hT[:, db, tsl], ps[po][:], ALU.add),
                     r=[psb[po], hbuf(C, db, mt)], w=[hbuf(C, db, mt)])
        P.barrier()


def final_norm_store(C, sq):
    P, ps, psb = C.P, C.ps, C.psb
    ph = C.phase
    sqt = [ph([128, 512], BF16, "sq%d" % i) for i in range(2)]
    rstd = [ph([128, 512], F32, "rstd%d" % i) for i in range(2)]
    yt = ph([128, 8, 512], F32, "y")
    ot = [ph([128, D], F32, "ot%d" % i) for i in range(2)]
    for mt in range(4):
        tsl = slice(mt * 512, (mt + 1) * 512)
        rb = P.buf("rstd", mt % 2)
        rt = rstd[mt % 2]
        rms_rstd(C, mt, sqt, rt, rb)
        yb = P.buf("fy")
        for kb in range(8):
            P.op("dve", lambda e, kb=kb, rt=rt, tsl=tsl: e.scalar_tensor_tensor(
                out=yt[:, kb, :], in0=C.hT[:, kb, tsl], scalar=vcol(C, "norm_final", kb),
                in1=rt[:], op0=ALU.mult, op1=ALU.mult), r=[hbuf(C, kb, mt), rb, C.bconst], w=[yb])
        for q in range(4):
            tt = mt * 4 + q
            ob = P.buf("fot", tt % 2)
            o_t = ot[tt % 2]
            for half in range(2):
                pi2 = next_ps(C)
                for j in range(4):
                    kb = half * 4 + j
                    P.op("pe", lambda e, pi2=pi2, j=j, kb=kb, q=q: e.matmul(
                        ps[pi2][:, j * 128:(j + 1) * 128], yt[:, kb, q * 128:(q + 1) * 128], C.ident,
                        start=True, stop=True), r=[yb, C.bconst], w=[psb[pi2]])
                P.op("act", lambda e, pi2=pi2, half=half, o_t=o_t: e.activation(
                    out=o_t[:, half * 512:(half + 1) * 512], in_=ps[pi2][:], func=AF.Copy),
                    r=[psb[pi2]], w=[ob])
            P.dma("sp", C.out[sq, tt * 128:(tt + 1) * 128, :], o_t[:], src=ob)


def vec_layout():
    lay = {}
    off = [0]

    def add(name, n):
        lay[name] = (off[0], n)
        off[0] += n
    add("norm_final", 8)
    for l in range(2):
        add("norm_mix%d" % l, 8)
        add("norm_ffn%d" % l, 8)
        add("ffn_conv_w%d" % l, NCB * 3)
        add("ffn_conv_b%d" % l, NCB)
    add("gla_norm", 1)
    add("gdn_conv_w", 24 * 3)
    add("gdn_norm", 1)
    return lay, off[0]


VLAY, NVEC = vec_layout()


def make_vecs(inputs):
    v = np.zeros((128, NVEC), np.float32)

    def put(name, arr):
        o, n = VLAY[name]
        v[:, o:o + n] = np.asarray(arr, np.float32).reshape(128, n)
    f = lambda a: np.asarray(a, np.float32)
    put("norm_final", f(inputs["norm_final"]).reshape(8, 128).T)
    for l in range(2):
        put("norm_mix%d" % l, f(inputs["norm_mix"])[l].reshape(8, 128).T)
        put("norm_ffn%d" % l, f(inputs["norm_ffn"])[l].reshape(8, 128).T)
        put("ffn_conv_w%d" % l, f(inputs["ffn_conv_w"])[l].reshape(3, NCB, 128).transpose(2, 1, 0))
        put("ffn_conv_b%d" % l, f(inputs["ffn_conv_b"])[l].reshape(NCB, 128).T)
    put("gla_norm", f(inputs["gla_norm"])[0].reshape(128, 1))
    put("gdn_conv_w", f(inputs["gdn_conv_w"])[0].reshape(3, 24, 128).transpose(2, 1, 0))
    put("gdn_norm", f(inputs["gdn_norm"])[0].reshape(128, 1))
    return v


def make_weights(inputs):
    f = lambda a: np.asarray(a, np.float32)
    w = {}
    wu = f(inputs["ffn_w_up"]).reshape(2, 8, 128, 2 * NCB, 128)
    w["ffn_wup"] = np.ascontiguousarray(wu.transpose(0, 3, 2, 1, 4))
    w["ffn_wdn"] = np.ascontiguousarray(f(inputs["ffn_w_down"]).reshape(2, NCB, 128, D))
    tile_k = lambda W: np.ascontiguousarray(W.reshape(8, 128, -1).transpose(1, 0, 2))
    w["ab_win"] = tile_k(f(inputs["ab_w_in"])[0])
    w["ab_wout"] = tile_k(f(inputs["ab_w_out"])[0])
    wg = np.zeros((2, 32, 256), np.float32)
    wg[0, :16] = f(inputs["gla_w_gate_fwd"])[0]
    wg[0, 16] = f(inputs["gla_b_gate_fwd"])[0]
    wg[1, :16] = f(inputs["gla_w_gate_bwd"])[0]
    wg[1, 16] = f(inputs["gla_b_gate_bwd"])[0]
    w["gla_wg"] = wg
    w["sgu_wsT"] = np.ascontiguousarray(f(inputs["sgu_w_s"])[0].transpose(2, 0, 1))
    w["sgu_ln"] = np.stack([f(inputs["sgu_ln_g"])[0], f(inputs["sgu_ln_b"])[0]])
    w["sgu_bs"] = np.ascontiguousarray(f(inputs["sgu_b_s"])[0].reshape(1, 512))
    gw = f(inputs["gdn_w_in"])[0]
    w["gdn_win"] = np.ascontiguousarray(gw[:, :4096].reshape(8, 128, 32, 128).transpose(2, 1, 0, 3))
    w["gdn_wsm"] = tile_k(gw[:, 4096:4128])
    w["gdn_wout"] = tile_k(f(inputs["gdn_w_out"])[0])
    w["gdn_vec"] = np.concatenate([f(inputs["gdn_dt_bias_fwd"])[0], f(inputs["gdn_dt_bias_bwd"])[0],
                                   f(inputs["gdn_a_log_fwd"])[0], f(inputs["gdn_a_log_bwd"])[0]]).reshape(1, 32)
    return w


def make_consts():
    c = np.zeros((128, NCONST * 128), np.float32)
    j = np.arange(128)[:, None]
    i = np.arange(128)[None, :]
    c[:, 0:128] = np.eye(128)
    c[:, 128:256] = (j <= i)
    c[:, 256:384] = (j >= i)
    c[:, 384:512] = (j <= i) * (-1.0 / 16)
    c[:, 512:640] = (j >= i) * (-1.0 / 16)
    big = 30000.0
    c[:, 640:768] = ((j <= i) - 1.0) * big
    c[:, 768:896] = ((j < i) - 1.0) * big
    c[:, 896:1024] = ((j >= i) - 1.0) * big
    c[:, 1024:1152] = ((j > i) - 1.0) * big
    c[127, 1152:1280] = 1.0
    c[0, 1280:1408] = 1.0
    c[:, 1408:1536] = 1.0
    return c


_NC_CACHE = {}


def make_in_map(inputs, c, nseq):
    x = np.asarray(inputs["x"], np.float32)
    m = {"x": np.ascontiguousarray(x[c * nseq:(c + 1) * nseq]), "vecs": make_vecs(inputs),
         "consts": make_consts()}
    m.update(make_weights(inputs))
    return m


def kernel(**inputs):
    B = np.asarray(inputs["x"]).shape[0]
    nseq = B // N_CORES
    if nseq not in _NC_CACHE:
        _NC_CACHE[nseq] = build_program(nseq)
    nc = _NC_CACHE[nseq]
    in_maps = [make_in_map(inputs, c, nseq) for c in range(N_CORES)]
    res = run_bass_kernel_spmd(nc, in_maps, core_ids=list(range(N_CORES)))
    return np.concatenate([r["out"] for r in res.results], axis=0)
```

```python
import numpy as np
import concourse.bass as bass
import concourse.mybir as mybir
from concourse.bass_utils import run_bass_kernel_spmd

F32 = mybir.dt.float32
BF16 = mybir.dt.bfloat16
AF = mybir.ActivationFunctionType
ALU = mybir.AluOpType
AX = mybir.AxisListType

D = 1024
S = 2048
NT = S // 128
FF = 2816
NCB = FF // 128
NCONST = 12
HPG = 2
NLEV = 4
EPS = 1e-6
N_CORES = 8


class Buf:
    __slots__ = ("name", "w", "r", "wsem", "wcnt", "rsem", "rcnt", "excl")

    def __init__(self, name):
        self.name = name
        self.excl = False
        self.w = None
        self.r = []
        self.wsem = None
        self.wcnt = 0
        self.rsem = None
        self.rcnt = 0


class Op:
    __slots__ = ("eng", "fn", "deps", "dma", "event", "src", "dst", "id")


class Prog:
    MAXV = 60000
    MAXD = 56000

    def __init__(self, nc):
        self.nc = nc
        self.engs = {"pe": nc.tensor, "act": nc.scalar, "dve": nc.vector,
                     "pool": nc.gpsimd, "sp": nc.sync}
        self.ops = []
        self.bufs = {}
        self.last = {}
        self.dma_out = []
        self.nsem = 0

    def buf(self, *key):
        b = self.bufs.get(key)
        if b is None:
            b = Buf(key)
            self.bufs[key] = b
        return b

    def _deps(self, o, r, w, is_dma):
        deps = set()
        for b in r:
            if b.w is not None:
                deps.add(b.w)
            if b.excl:
                for x in b.r:
                    if self.ops[x].eng != o.eng:
                        deps.add(x)
        for b in w:
            if b.w is not None:
                deps.add(b.w)
            last = {}
            for x in b.r:
                ox = self.ops[x]
                if ox.dma:
                    deps.add(x)
                else:
                    last[ox.eng] = max(last.get(ox.eng, -1), x)
            deps.update(last.values())
        if not is_dma:
            raw = set(b.w for b in r if b.w is not None)
            deps = set(d for d in deps
                       if d in raw or self.ops[d].eng != o.eng or self.ops[d].dma)
        o.deps = deps
        for b in r:
            b.r.append(o.id)
        for b in w:
            b.w = o.id
            b.r = []

    def op(self, eng, fn, r=(), w=()):
        o = Op()
        o.id = len(self.ops)
        o.eng = eng
        o.fn = fn
        o.dma = False
        o.event = None
        o.src = o.dst = None
        self._deps(o, r, w, False)
        self.ops.append(o)
        self.last[eng] = o.id
        return o.id

    def dma(self, q, out, in_, src=None, dst=None):
        o = Op()
        o.id = len(self.ops)
        o.eng = q
        o.fn = lambda e, out=out, in_=in_: e.dma_start(out=out, in_=in_)
        o.dma = True
        o.event = None
        o.src = src
        o.dst = dst
        self._deps(o, [src] if src is not None else [], [dst] if dst is not None else [], True)
        self.ops.append(o)
        self.dma_out.append(o.id)
        return o.id

    def barrier(self):
        deps = set(self.last.values()) | set(self.dma_out)
        for eng in ("pe", "act", "dve", "pool", "sp"):
            o = Op()
            o.id = len(self.ops)
            o.eng = eng
            o.fn = None
            o.dma = False
            o.event = None
            o.src = o.dst = None
            o.deps = set(deps)
            self.ops.append(o)
        self.dma_out = []
        for b in self.bufs.values():
            b.w = None
            b.r = []

    def _newsem(self):
        self.nsem += 1
        return self.nc.alloc_semaphore("s%d" % self.nsem)

    def emit(self):
        nc = self.nc
        has_dep = set()
        for o in self.ops:
            has_dep |= o.deps
        cur = {}
        cnt = {}
        known = {e: {} for e in self.engs}
        for o in self.ops:
            E = self.engs[o.eng]
            waits = {}
            for d in o.deps:
                ev = self.ops[d].event
                if ev is None:
                    continue
                s, v = ev
                k = id(s)
                if known[o.eng].get(k, 0) >= v:
                    continue
                if k not in waits or waits[k][1] < v:
                    waits[k] = (s, v)
            wl = list(waits.values())
            for s, v in wl:
                known[o.eng][id(s)] = v
            if o.fn is None:
                for s, v in wl:
                    E.wait_ge(s, v)
                continue
            for s, v in wl[:-1]:
                E.wait_ge(s, v)
            ins = o.fn(E)
            if wl:
                ins._wait_ge(wl[-1][0], wl[-1][1])
            if o.dma:
                b = o.dst if o.dst is not None else o.src
                if o.dst is not None:
                    if b.wsem is None or b.wcnt >= self.MAXD:
                        b.wsem = self._newsem()
                        b.wcnt = 0
                    b.wcnt += 16
                    ins.then_inc(b.wsem, 16)
                    o.event = (b.wsem, b.wcnt)
                else:
                    if b.rsem is None or b.rcnt >= self.MAXD:
                        b.rsem = self._newsem()
                        b.rcnt = 0
                    b.rcnt += 16
                    ins.then_inc(b.rsem, 16)
                    o.event = (b.rsem, b.rcnt)
            elif o.id in has_dep:
                if o.eng not in cur or cnt[o.eng] >= self.MAXV:
                    cur[o.eng] = self._newsem()
                    cnt[o.eng] = 0
                cnt[o.eng] += 1
                ins.then_inc(cur[o.eng], 1)
                o.event = (cur[o.eng], cnt[o.eng])


class Alloc:
    def __init__(self, nc, lo, hi):
        self.nc = nc
        self.lo = lo
        self.hi = hi
        self.p = lo
        self.n = 0

    def reset(self):
        self.p = self.lo

    def __call__(self, shape, dtype, name=None):
        nb = 4 if dtype == F32 else 2
        sz = nb
        for s in shape[1:]:
            sz *= s
        sz = (sz + 63) // 64 * 64
        off = self.p
        self.p += sz
        assert self.p <= self.hi, "SBUF overflow %s %d > %d" % (name, self.p, self.hi)
        self.n += 1
        return self.nc.alloc_sbuf_tensor_at("%s_%d" % (name or "t", self.n), list(shape), dtype,
                                            offset=off)


def rsqrt_from_psum(P, out_ap, ps_ap, ps_buf, out_buf, eps=EPS, scale=1.0):
    P.op("act", lambda e: e.activation(out=out_ap, in_=ps_ap, func=AF.Ln, bias=eps, scale=scale),
         r=[ps_buf], w=[out_buf])
    P.op("act", lambda e: e.activation(out=out_ap, in_=out_ap, func=AF.Exp, scale=-0.5), r=[out_buf], w=[out_buf])


class Ctx:
    pass


def build_program(nseq, stages="all", debug=False):
    nc = bass.Bass("TRN2", target_bir_lowering=False)
    P = Prog(nc)
    C = Ctx()
    C.nc, C.P = nc, P
    C.debug = debug
    import os
    C.gdn_stop = int(os.environ.get('GDN_STOP', '9'))
    C.gdn_sub = int(os.environ.get('GDN_SUB', '99'))
    C.x = nc.dram_tensor("x", [nseq, S, D], F32, kind="ExternalInput").ap()
    C.out = nc.dram_tensor("out", [nseq, S, D], F32, kind="ExternalOutput").ap()
    vecs = nc.dram_tensor("vecs", [128, NVEC], F32, kind="ExternalInput").ap()
    C.wup_d = nc.dram_tensor("ffn_wup", [2, 2 * NCB, 128, 8, 128], F32, kind="ExternalInput").ap()
    C.wdn_d = nc.dram_tensor("ffn_wdn", [2, NCB, 128, D], F32, kind="ExternalInput").ap()
    consts_d = nc.dram_tensor("consts", [128, NCONST * 128], F32, kind="ExternalInput").ap()
    C.ab_win_d = nc.dram_tensor("ab_win", [128, 8, 2592], F32, kind="ExternalInput").ap()
    C.ab_wout_d = nc.dram_tensor("ab_wout", [128, 8, 1024], F32, kind="ExternalInput").ap()
    C.gla_wg_d = nc.dram_tensor("gla_wg", [2, 32, 256], F32, kind="ExternalInput").ap()
    C.sgu_wsT_d = nc.dram_tensor("sgu_wsT", [128, 4, 128], F32, kind="ExternalInput").ap()
    C.sgu_ln_d = nc.dram_tensor("sgu_ln", [2, 512], F32, kind="ExternalInput").ap()
    C.sgu_bs_d = nc.dram_tensor("sgu_bs", [1, 512], F32, kind="ExternalInput").ap()
    C.gdn_win_d = nc.dram_tensor("gdn_win", [32, 128, 8, 128], F32, kind="ExternalInput").ap()
    C.gdn_wsm_d = nc.dram_tensor("gdn_wsm", [128, 8, 32], F32, kind="ExternalInput").ap()
    C.gdn_wout_d = nc.dram_tensor("gdn_wout", [128, 8, 1024], F32, kind="ExternalInput").ap()
    C.gdn_vec_d = nc.dram_tensor("gdn_vec", [1, 32], F32, kind="ExternalInput").ap()

    persist = Alloc(nc, 16 * 1024, 96 * 1024)

    C.ps = [nc.alloc_psum_tensor("ps%d" % i, [128, 512], F32) for i in range(8)]
    C.psb = [P.buf("ps", i) for i in range(8)]
    for b in C.psb:
        b.excl = True
    C.psrr = 0
    C.ps_res = set()

    C.hT = persist([128, 8, S], F32, "hT")
    C.consts = persist([128, NCONST * 128], F32, "consts")
    C.ident = C.consts[:, 0:128]
    C.maskf = C.consts[:, 128:256]
    C.maskb = C.consts[:, 256:384]
    C.tri = [C.consts[:, 384:512], C.consts[:, 512:640]]
    C.mask01 = [C.maskf, C.maskb]
    C.mneg_incl = [C.consts[:, 640:768], C.consts[:, 896:1024]]
    C.mneg_strict = [C.consts[:, 768:896], C.consts[:, 1024:1152]]
    C.esel = [C.consts[:, 1152:1280], C.consts[:, 1280:1408]]
    C.onesf = C.consts[:, 1408:1536]
    C.ones1 = persist([128, 128], BF16, "ones1")
    C.identb = persist([128, 128], BF16, "identb")
    C.onesD = persist([128, 128], BF16, "onesD")
    C.ones128 = persist([128, 128], BF16, "ones128")
    C.vec = persist([128, NVEC], F32, "vecs")
    C.bconst = P.buf("const")
    C.phase = Alloc(nc, persist.p, 224 * 1024 - 256)

    P.dma("sp", C.consts[:], consts_d[:, :], dst=C.bconst)
    P.dma("sp", C.vec[:], vecs[:, :], dst=C.bconst)
    P.op("dve", lambda e: e.tensor_copy(out=C.identb[:], in_=C.ident), r=[C.bconst], w=[P.buf("identb")])
    P.op("dve", lambda e: e.memset(C.onesD[:], 1.0 / D), w=[P.buf("ones")])
    P.op("dve", lambda e: e.memset(C.ones128[:], 1.0 / 128), w=[P.buf("ones")])
    P.op("dve", lambda e: e.memset(C.ones1[:], 1.0), w=[P.buf("ones")])

    for sq in range(nseq):
        load_x(C, sq)
        P.barrier()
        C.phase.reset()
        if stages in ("mix0", "all"):
            mixer0(C)
            P.barrier()
            C.phase.reset()
        if stages in ("ffn0", "all"):
            ffn(C, 0)
            P.barrier()
            C.phase.reset()
        if stages in ("mix1", "all"):
            mixer1(C)
            P.barrier()
            C.phase.reset()
        if stages in ("ffn1", "all"):
            ffn(C, 1)
            P.barrier()
            C.phase.reset()
        final_norm_store(C, sq)
        P.barrier()
        C.phase.reset()

    P.barrier()
    P.emit()
    return nc


def dump(C, name, t, buf_list):
    if not C.debug:
        return
    shape = list(t.shape)
    d = C.nc.dram_tensor("dbg_" + name, shape, t.dtype, kind="ExternalOutput").ap()
    C.P.barrier()
    C.P.dma("sp", d, t[:], src=None, dst=C.P.buf("dbg", name))
    C.P.barrier()


def next_ps(C):
    while True:
        i = C.psrr % 8
        C.psrr += 1
        if i not in C.ps_res:
            return i


def reserve_ps(C, n):
    r = []
    for _ in range(n):
        i = next_ps(C)
        C.ps_res.add(i)
        r.append(i)
    return r


def release_ps(C, banks):
    for i in banks:
        C.ps_res.discard(i)


def hbuf(C, kb, mt):
    return C.P.buf("hT", kb, mt)


def vcol(C, name, j=0, n=1):
    o = VLAY[name][0] + j
    return C.vec[:, o:o + n]


def load_x(C, sq):
    P, ps, psb = C.P, C.ps, C.psb
    NXB = 6
    xs = [C.phase([128, D], F32, "xin%d" % i) for i in range(NXB)]
    for tt in range(NT):
        xb = P.buf("xin", tt % NXB)
        xt = xs[tt % NXB]
        P.dma("sp", xt[:], C.x[sq, tt * 128:(tt + 1) * 128, :], dst=xb)
        for half in range(2):
            pi = next_ps(C)
            for j in range(4):
                kb = half * 4 + j
                P.op("pe", lambda e, pi=pi, j=j, kb=kb, xt=xt: e.matmul(
                    ps[pi][:, j * 128:(j + 1) * 128], xt[:, kb * 128:(kb + 1) * 128], C.ident,
                    start=True, stop=True), r=[xb, C.bconst], w=[psb[pi]])
            mt = tt // 4
            P.op("act", lambda e, pi=pi, half=half, tt=tt: e.activation(
                out=C.hT[:, half * 4:half * 4 + 4, tt * 128:(tt + 1) * 128],
                in_=ps[pi][:].rearrange("p (j t) -> p j t", j=4), func=AF.Copy),
                r=[psb[pi]], w=[hbuf(C, half * 4 + j, mt) for j in range(4)])


def rms_rstd(C, mt, sqt, rt, rb):
    P, ps, psb = C.P, C.ps, C.psb
    tsl = slice(mt * 512, (mt + 1) * 512)
    pi = next_ps(C)
    for kb in range(8):
        sb = P.buf("sqt", kb % 2)
        st = sqt[kb % 2]
        P.op("act", lambda e, st=st, kb=kb: e.activation(out=st[:], in_=C.hT[:, kb, tsl], func=AF.Square),
             r=[hbuf(C, kb, mt)], w=[sb])
        P.op("pe", lambda e, st=st, kb=kb: e.matmul(
            ps[pi][:], C.onesD[:], st[:], start=(kb == 0), stop=(kb == 7)),
            r=[sb, P.buf("ones")], w=[psb[pi]])
    rsqrt_from_psum(P, rt[:], ps[pi][:], psb[pi], rb)


def rmsnorm_to_bf16(C, gname, hn, hnb):
    P = C.P
    sqt = [C.phase([128, 512], BF16, "sq%d" % i) for i in range(2)]
    rstd = [C.phase([128, 512], F32, "rstd%d" % i) for i in range(2)]
    for mt in range(4):
        tsl = slice(mt * 512, (mt + 1) * 512)
        rb = P.buf("rstd", mt % 2)
        rt = rstd[mt % 2]
        rms_rstd(C, mt, sqt, rt, rb)
        for kb in range(8):
            P.op("dve", lambda e, kb=kb, rt=rt, tsl=tsl: e.scalar_tensor_tensor(
                out=hn[:, kb, tsl], in0=C.hT[:, kb, tsl], scalar=vcol(C, gname, kb),
                in1=rt[:], op0=ALU.mult, op1=ALU.mult), r=[hbuf(C, kb, mt), rb, C.bconst], w=[hnb(kb, mt)])


def ffn(C, l):
    P, ps, psb = C.P, C.ps, C.psb
    ph = C.phase
    hn = ph([128, 8, S], BF16, "hn")
    hnb = lambda kb, mt: P.buf("hn", kb, mt)
    rmsnorm_to_bf16(C, "norm_ffn%d" % l, hn, hnb)
    GRP = 8
    act = [ph([128, S], BF16, "act%d" % i) for i in range(GRP)]
    wd = [ph([128, D], BF16, "wd%d" % i) for i in range(2 * GRP)]
    wg = [ph([128, 8, 128], BF16, "wg%d" % i) for i in range(2)]
    wu = [ph([128, 8, 128], BF16, "wu%d" % i) for i in range(2)]
    G = ph([128, S + 2], F32, "gbuf")
    cv = [ph([128, 512], F32, "cv%d" % i) for i in range(2)]
    sg = [ph([128, 512], F32, "sg%d" % i) for i in range(2)]
    gb = lambda mt: P.buf("gbuf", mt)
    gedge = P.buf("gbuf_edge")
    P.op("dve", lambda e: e.memset(G[:, 0:1], 0.0), w=[gedge])
    P.op("dve", lambda e: e.memset(G[:, S + 1:S + 2], 0.0), w=[gedge])
    cwo = VLAY["ffn_conv_w%d" % l][0]
    cbo = VLAY["ffn_conv_b%d" % l][0]
    groups = []
    c0 = 0
    while c0 < NCB:
        groups.append(list(range(c0, min(c0 + GRP, NCB))))
        c0 += GRP
    wdslot = 0
    for grp in groups:
        slots = {}
        for cb in grp:
            a_t = act[cb % GRP]
            ab = lambda mt, cb=cb: P.buf("act", cb % GRP, mt)
            wgt, wut = wg[cb % 2], wu[cb % 2]
            wgb, wub = P.buf("wg", cb % 2), P.buf("wu", cb % 2)
            ws = wdslot % (2 * GRP)
            wdslot += 1
            slots[cb] = ws
            wdb = P.buf("wd", ws)
            P.dma("pool", wgt[:], C.wup_d[l, cb], dst=wgb)
            P.dma("pool", wut[:], C.wup_d[l, NCB + cb], dst=wub)
            P.dma("pool", wd[ws][:], C.wdn_d[l, cb], dst=wdb)
            for mt in range(4):
                pi = next_ps(C)
                for kc in range(8):
                    P.op("pe", lambda e, pi=pi, kc=kc, mt=mt, wgt=wgt: e.matmul(
                        ps[pi][:], wgt[:, kc, :], hn[:, kc, mt * 512:(mt + 1) * 512],
                        start=(kc == 0), stop=(kc == 7)), r=[wgb, hnb(kc, mt)], w=[psb[pi]])
                P.op("act", lambda e, pi=pi, mt=mt: e.activation(
                    out=G[:, 1 + mt * 512:1 + (mt + 1) * 512], in_=ps[pi][:], func=AF.Copy),
                    r=[psb[pi]], w=[gb(mt)])
            for mt in range(4):
                pi = next_ps(C)
                for kc in range(8):
                    P.op("pe", lambda e, pi=pi, kc=kc, mt=mt, wut=wut: e.matmul(
                        ps[pi][:], wut[:, kc, :], hn[:, kc, mt * 512:(mt + 1) * 512],
                        start=(kc == 0), stop=(kc == 7)), r=[wub, hnb(kc, mt)], w=[psb[pi]])
                cvt, cvb = cv[mt % 2], P.buf("cv", mt % 2)
                sgt, sgb = sg[mt % 2], P.buf("sg", mt % 2)
                rd = [gb(mt), gedge, C.bconst]
                if mt > 0:
                    rd.append(gb(mt - 1))
                if mt < 3:
                    rd.append(gb(mt + 1))
                b0 = mt * 512
                w_ = lambda k, cb=cb: C.vec[:, cwo + cb * 3 + k:cwo + cb * 3 + k + 1]
                P.op("dve", lambda e, cvt=cvt, b0=b0, w_=w_: e.tensor_scalar(
                    out=cvt[:], in0=G[:, b0:b0 + 512], scalar1=w_(0), scalar2=None, op0=ALU.mult),
                    r=rd, w=[cvb])
                P.op("dve", lambda e, cvt=cvt, b0=b0, w_=w_: e.scalar_tensor_tensor(
                    out=cvt[:], in0=G[:, b0 + 1:b0 + 513], scalar=w_(1), in1=cvt[:],
                    op0=ALU.mult, op1=ALU.add), r=rd + [cvb], w=[cvb])
                P.op("dve", lambda e, cvt=cvt, b0=b0, w_=w_: e.scalar_tensor_tensor(
                    out=cvt[:], in0=G[:, b0 + 2:b0 + 514], scalar=w_(2), in1=cvt[:],
                    op0=ALU.mult, op1=ALU.add), r=rd + [cvb], w=[cvb])
                P.op("act", lambda e, cvt=cvt, sgt=sgt, cb=cb: e.activation(
                    out=sgt[:], in_=cvt[:], func=AF.Silu, bias=C.vec[:, cbo + cb:cbo + cb + 1], scale=1.0),
                    r=[cvb, C.bconst], w=[sgb])
                P.op("dve", lambda e, sgt=sgt, pi=pi, a_t=a_t, b0=b0: e.tensor_tensor(
                    out=a_t[:, b0:b0 + 512], in0=sgt[:], in1=ps[pi][:], op=ALU.mult),
                    r=[sgb, psb[pi]], w=[ab(mt)])
        for mt in range(4):
            for db in range(8):
                pi = next_ps(C)
                for i, cb in enumerate(grp):
                    P.op("pe", lambda e, pi=pi, cb=cb, db=db, mt=mt, i=i, n=len(grp), ws=slots[cb]: e.matmul(
                        ps[pi][:], wd[ws][:, db * 128:(db + 1) * 128], act[cb % GRP][:, mt * 512:(mt + 1) * 512],
                        start=(i == 0), stop=(i == n - 1)),
                        r=[P.buf("wd", slots[cb]), P.buf("act", cb % GRP, mt)], w=[psb[pi]])
                P.op("dve", lambda e, pi=pi, db=db, mt=mt: e.tensor_tensor(
                    out=C.hT[:, db, mt * 512:(mt + 1) * 512], in0=C.hT[:, db, mt * 512:(mt + 1) * 512],
                    in1=ps[pi][:], op=ALU.add), r=[psb[pi], hbuf(C, db, mt)], w=[hbuf(C, db, mt)])


def MM(out, lhsT, rhs, start=True, stop=True):
    return lambda e: e.matmul(out, lhsT, rhs, start=start, stop=stop)


def ACTF(out, in_, func, **kw):
    return lambda e: e.activation(out=out, in_=in_, func=func, **kw)


def TT(out, in0, in1, op):
    return lambda e: e.tensor_tensor(out=out, in0=in0, in1=in1, op=op)


def STT(out, in0, scalar, in1, op0, op1):
    return lambda e: e.scalar_tensor_tensor(out=out, in0=in0, scalar=scalar, in1=in1, op0=op0, op1=op1)


def TS(out, in0, s1, s2, op0, op1=None):
    if op1 is None:
        return lambda e: e.tensor_scalar(out=out, in0=in0, scalar1=s1, scalar2=None, op0=op0)
    return lambda e: e.tensor_scalar(out=out, in0=in0, scalar1=s1, scalar2=s2, op0=op0, op1=op1)


def CP(out, in_):
    return lambda e: e.tensor_copy(out=out, in_=in_)


def MS(out, val):
    return lambda e: e.memset(out, val)


def RCP(out, in_):
    return lambda e: e.reciprocal(out=out, in_=in_)


def mixer0(C):
    P, ps, psb, ph = C.P, C.ps, C.psb, C.phase
    hn = ph([128, 8, S], BF16, "hn")
    hnb = lambda kb, mt: P.buf("hn", kb, mt)
    oa = ph([128, 4, S], BF16, "oa")
    oab = lambda h, mt: P.buf("oa", h, mt)
    mark1 = ph.p
    qd = [ph([128, 2, S], BF16, "qd%d" % d) for d in range(2)]
    ki = [ph([128, 2, S], BF16, "ki%d" % d) for d in range(2)]
    qdb = lambda d, blk, mt: P.buf("qd", d, blk, mt)
    kib = lambda d, blk, mt: P.buf("ki", d, blk, mt)
    vt = ph([128, NT, 512], BF16, "vt")
    vtb = lambda tt: P.buf("vt", tt)
    elast = ph([128, 2, 2, NT], F32, "elast")
    elb = lambda mt: P.buf("elast", mt)
    mark2 = ph.p
    win = ph([128, 8, 1056], BF16, "win")
    winb = P.buf("win")
    for kc in range(8):
        P.dma("pool", win[:, kc, 0:1024], C.ab_win_d[:, kc, 0:1024], dst=winb)
    for kc in range(8):
        P.dma("pool", win[:, kc, 1024:1056], C.ab_win_d[:, kc, 1536:1568], dst=winb)
    markW = ph.p
    rmsnorm_to_bf16(C, "norm_mix0", hn, hnb)
    P.barrier()
    ph.p = markW
    winb = P.buf("win")

    wgg = ph([32, 2, 256], BF16, "wgg")
    wggb = P.buf("wgg")
    P.dma("pool", wgg[:], C.gla_wg_d.rearrange("d r c -> r d c"), dst=wggb)
    lra = [ph([32, 512], BF16, "lra%d" % d) for d in range(2)]
    lrab = [P.buf("lra", d) for d in range(2)]
    for d in range(2):
        P.op("dve", MS(lra[d][:], 1.0), w=[lrab[d]])
    e_t = [ph([128, 512], F32, "e%d" % i) for i in range(2)]
    sp_t = [ph([128, 512], F32, "sp%d" % i) for i in range(2)]
    Ep = [ph([128, 512], F32, "Ep%d" % d) for d in range(2)]
    Em = [ph([128, 512], F32, "Em%d" % d) for d in range(2)]
    for mt in range(4):
        tsl = slice(mt * 512, (mt + 1) * 512)
        for d in range(2):
            pi = next_ps(C)
            for kc in range(8):
                P.op("pe", MM(ps[pi][0:16, :], win[:, kc, 1024 + d * 16:1024 + (d + 1) * 16], hn[:, kc, tsl],
                              kc == 0, kc == 7), r=[winb, hnb(kc, mt)], w=[psb[pi]])
            P.op("act", ACTF(lra[d][0:16, :], ps[pi][0:16, :], AF.Copy), r=[psb[pi]], w=[lrab[d]])
        cpi = reserve_ps(C, 4)
        for q in range(4):
            tt = mt * 4 + q
            pi = next_ps(C)
            for d in range(2):
                P.op("pe", MM(ps[pi][:, d * 256:(d + 1) * 256], lra[d][:, q * 128:(q + 1) * 128], wgg[:, d, :]),
                     r=[lrab[d], wggb], w=[psb[pi]])
            et, etb = e_t[q % 2], P.buf("e_t", q % 2)
            spt, spb = sp_t[q % 2], P.buf("sp_t", q % 2)
            P.op("act", ACTF(et[:], ps[pi][:], AF.Exp, scale=-1.0), r=[psb[pi]], w=[etb])
            P.op("act", ACTF(spt[:], et[:], AF.Ln, bias=1.0), r=[etb], w=[spb])
            for d in range(2):
                for blk in range(2):
                    b = cpi[d * 2 + blk]
                    P.op("pe", MM(ps[b][:, q * 128:(q + 1) * 128],
                                  spt[:, d * 256 + blk * 128:d * 256 + (blk + 1) * 128], C.tri[d]),
                         r=[spb, C.bconst], w=[psb[b]])
            pi = next_ps(C)
            for kc in range(8):
                P.op("pe", MM(ps[pi][:], hn[:, kc, tt * 128:(tt + 1) * 128], win[:, kc, 512:1024], kc == 0, kc == 7),
                     r=[winb, hnb(kc, mt)], w=[psb[pi]])
            P.op("act", ACTF(vt[:, tt, :], ps[pi][:], AF.Copy), r=[psb[pi]], w=[vtb(tt)])
        for blk in range(2):
            pq = next_ps(C)
            for kc in range(8):
                P.op("pe", MM(ps[pq][:], win[:, kc, blk * 128:(blk + 1) * 128], hn[:, kc, tsl], kc == 0, kc == 7),
                     r=[winb, hnb(kc, mt)], w=[psb[pq]])
            pk = next_ps(C)
            for kc in range(8):
                P.op("pe", MM(ps[pk][:], win[:, kc, 256 + blk * 128:256 + (blk + 1) * 128], hn[:, kc, tsl],
                              kc == 0, kc == 7), r=[winb, hnb(kc, mt)], w=[psb[pk]])
            for d in range(2):
                b = cpi[d * 2 + blk]
                ep_, epb = Ep[d], P.buf("Ep", d)
                em_, emb = Em[d], P.buf("Em", d)
                P.op("act", ACTF(ep_[:], ps[b][:], AF.Exp), r=[psb[b]], w=[epb])
                P.op("act", ACTF(em_[:], ps[b][:], AF.Exp, scale=-1.0), r=[psb[b]], w=[emb])
                col = 127 if d == 0 else 0
                P.op("dve", CP(elast[:, d, blk, mt * 4:(mt + 1) * 4],
                               ep_[:].rearrange("p (n i) -> p n i", i=128)[:, :, col]), r=[epb], w=[elb(mt)])
                P.op("dve", STT(qd[d][:, blk, tsl], ps[pq][:], 0.125, ep_[:], ALU.mult, ALU.mult),
                     r=[psb[pq], epb], w=[qdb(d, blk, mt)])
                P.op("dve", TT(ki[d][:, blk, tsl], ps[pk][:], em_[:], ALU.mult),
                     r=[psb[pk], emb], w=[kib(d, blk, mt)])
        release_ps(C, cpi)
    dump(C, "qd0", qd[0], None)
    dump(C, "qd1", qd[1], None)
    dump(C, "ki0", ki[0], None)
    dump(C, "ki1", ki[1], None)
    dump(C, "vt", vt, None)
    P.barrier()
    ph.p = mark2

    Sall = [ph([128, NT, 2, 128], BF16, "Sall%d" % d) for d in range(2)]
    mark3 = ph.p
    S2 = [[ph([128, 128], F32, "S2_%d%d" % (d, hp)) for hp in range(2)] for d in range(2)]
    sab = lambda d, n, hp: P.buf("Sall", d, n, hp)
    A_t = [[ph([128, 128], F32, "A%d%d" % (d, hp)) for hp in range(2)] for d in range(2)]
    kin = [ph([128, 128], BF16, "kin%d" % i) for i in range(4)]
    for d in range(2):
        n0 = 0 if d == 0 else NT - 1
        for hp in range(2):
            P.op("dve", MS(S2[d][hp][:], 0.0), w=[P.buf("S2", d, hp)])
            P.op("dve", MS(Sall[d][:, n0, hp, :], 0.0), w=[sab(d, n0, hp)])
    kk = 0
    for step in range(NT):
        chains = []
        for d in range(2):
            n = step if d == 0 else NT - 1 - step
            nxt = n + 1 if d == 0 else n - 1
            if not (0 <= nxt < NT):
                continue
            for hp in range(2):
                chains.append((d, n, nxt, hp, next_ps(C), kk % 4))
                kk += 1
        for (d, n, nxt, hp, pi, ki_) in chains:
            P.op("pe", MM(ps[pi][:, 0:128], ki[d][:, hp, n * 128:(n + 1) * 128], C.identb[:]),
                 r=[kib(d, hp, n // 4), P.buf("identb")], w=[psb[pi]])
        for (d, n, nxt, hp, pi, ki_) in chains:
            P.op("act", ACTF(kin[ki_][:], ps[pi][:, 0:128], AF.Copy), r=[psb[pi]], w=[P.buf("kin", ki_)])
            P.op("act", ACTF(A_t[d][hp][:], S2[d][hp][:], AF.Copy, scale=elast[:, d, hp, n:n + 1]),
                 r=[P.buf("S2", d, hp), elb(n // 4)], w=[P.buf("A", d, hp)])
        for (d, n, nxt, hp, pi, ki_) in chains:
            P.op("pe", MM(ps[pi][:, 128:384], kin[ki_][:], vt[:, n, hp * 256:(hp + 1) * 256]),
                 r=[P.buf("kin", ki_), vtb(n)], w=[psb[pi]])
        for (d, n, nxt, hp, pi, ki_) in chains:
            for hh in range(2):
                rows = slice(hh * 64, hh * 64 + 64)
                P.op("dve", STT(S2[d][hp][rows, :], ps[pi][rows, 128 + hh * 128:256 + hh * 128],
                                elast[rows, d, hp, n:n + 1], A_t[d][hp][rows, :], ALU.mult, ALU.add),
                     r=[psb[pi], P.buf("A", d, hp), elb(n // 4)], w=[P.buf("S2", d, hp)])
        for (d, n, nxt, hp, pi, ki_) in chains:
            P.op("act", ACTF(Sall[d][:, nxt, hp, :], S2[d][hp][:], AF.Copy), r=[P.buf("S2", d, hp)], w=[sab(d, nxt, hp)])
    dump(C, "Sall0", Sall[0], None)
    dump(C, "Sall1", Sall[1], None)
    P.barrier()
    ph.p = mark3

    wgate = ph([128, 8, 512], BF16, "wgate")
    wgb = P.buf("wgate")
    for kc in range(8):
        P.dma("pool", wgate[:, kc, :], C.ab_win_d[:, kc, 1024:1536], dst=wgb)
    mask4 = [ph([128, 512], BF16, "mask4_%d" % d) for d in range(2)]
    m4b = P.buf("mask4")
    for d in range(2):
        src = C.maskf if d == 0 else C.maskb
        for q in range(4):
            P.op("dve", CP(mask4[d][:, q * 128:(q + 1) * 128], src), r=[C.bconst], w=[m4b])
    PT = [[ph([128, 512], BF16, "PT%d%d" % (d, i)) for i in range(2)] for d in range(2)]
    sqh = [ph([128, 512], BF16, "sqh")] * 2
    rsh = [ph([128, 512], F32, "rsh%d" % i) for i in range(2)]
    sgt = ph([128, 512], F32, "sgt")
    tmp = ph([128, 512], F32, "tmp")
    it = 0
    for mt in range(4):
        tsl = slice(mt * 512, (mt + 1) * 512)
        for h in range(4):
            hp, hh = h // 2, h % 2
            rows = slice(hh * 64, hh * 64 + 64)
            par = it % 2
            it += 1
            for d in range(2):
                pi = next_ps(C)
                for q in range(4):
                    ns = slice((mt * 4 + q) * 128, (mt * 4 + q + 1) * 128)
                    P.op("pe", MM(ps[pi][:, q * 128:(q + 1) * 128], ki[d][rows, hp, ns], qd[d][rows, hp, ns]),
                         r=[kib(d, hp, mt), qdb(d, hp, mt)], w=[psb[pi]])
                P.op("dve", TT(PT[d][par][:], ps[pi][:], mask4[d][:], ALU.mult),
                     r=[psb[pi], m4b], w=[P.buf("PT", d, par)])
            po = next_ps(C)
            for q in range(4):
                n = mt * 4 + q
                cs = slice(q * 128, (q + 1) * 128)
                ns = slice(n * 128, (n + 1) * 128)
                P.op("pe", MM(ps[po][:, cs], vt[:, n, h * 128:(h + 1) * 128], PT[0][par][:, cs], True, False),
                     r=[vtb(n), P.buf("PT", 0, par)], w=[psb[po]])
                P.op("pe", MM(ps[po][:, cs], vt[:, n, h * 128:(h + 1) * 128], PT[1][par][:, cs], False, False),
                     r=[vtb(n), P.buf("PT", 1, par)], w=[psb[po]])
                for d in range(2):
                    P.op("pe", MM(ps[po][:, cs], Sall[d][rows, n, hp, :], qd[d][rows, hp, ns], False, d == 1),
                         r=[sab(d, n, hp), qdb(d, hp, mt)], w=[psb[po]])
            sqt_, sqb_ = sqh[par], P.buf("sqh", 0)
            P.op("act", ACTF(sqt_[:], ps[po][:], AF.Square), r=[psb[po]], w=[sqb_])
            pn = next_ps(C)
            P.op("pe", MM(ps[pn][:], C.ones128[:], sqt_[:]), r=[sqb_, P.buf("ones")], w=[psb[pn]])
            rt_, rb_ = rsh[par], P.buf("rsh", par)
            rsqrt_from_psum(P, rt_[:], ps[pn][:], psb[pn], rb_)
            pg = next_ps(C)
            for kc in range(8):
                P.op("pe", MM(ps[pg][:], wgate[:, kc, h * 128:(h + 1) * 128], hn[:, kc, tsl], kc == 0, kc == 7),
                     r=[wgb, hnb(kc, mt)], w=[psb[pg]])
            sgb_, tmb_ = P.buf("sgt"), P.buf("tmp")
            P.op("act", ACTF(sgt[:], ps[pg][:], AF.Silu), r=[psb[pg]], w=[sgb_])
            P.op("dve", STT(tmp[:], ps[po][:], vcol(C, "gla_norm"), rt_[:], ALU.mult, ALU.mult),
                 r=[psb[po], rb_, C.bconst], w=[tmb_])
            P.op("dve", TT(oa[:, h, tsl], tmp[:], sgt[:], ALU.mult), r=[tmb_, sgb_], w=[oab(h, mt)])
    dump(C, "oa", oa, None)
    P.barrier()
    ph.p = mark1

    wsu = ph([128, 8, 512], BF16, "wsu")
    wsv = ph([128, 8, 512], BF16, "wsv")
    wout = ph([128, 8, D], BF16, "wout")
    wsub, wsvb, woutb = P.buf("wsu"), P.buf("wsv"), P.buf("wout")
    for kc in range(8):
        P.dma("pool", wsu[:, kc, :], C.ab_win_d[:, kc, 1568:2080], dst=wsub)
        P.dma("pool", wsv[:, kc, :], C.ab_win_d[:, kc, 2080:2592], dst=wsvb)
        P.dma("pool", wout[:, kc, :], C.ab_wout_d[:, kc, :], dst=woutb)
    wsT = ph([128, 4, 128], BF16, "wsT")
    wsTb = P.buf("wsT")
    P.dma("pool", wsT[:], C.sgu_wsT_d, dst=wsTb)
    lng = ph([128, 512], F32, "lng")
    lnbt = ph([128, 512], F32, "lnb")
    bsb4 = ph([128, 4, 4, 128], F32, "bsb4")
    lnbuf = P.buf("lnconst")
    P.dma("sp", lng[:], C.sgu_ln_d[0:1, :].partition_broadcast(128), dst=lnbuf)
    P.dma("sp", lnbt[:], C.sgu_ln_d[1:2, :].partition_broadcast(128), dst=lnbuf)
    for n in range(4):
        P.dma("sp", bsb4[:, :, n, :], C.sgu_bs_d.rearrange("o (g i) -> o g i", g=4).partition_broadcast(128),
              dst=lnbuf)
    ut = [ph([128, 4, 512], BF16, "ut%d" % i) for i in range(2)]
    sgu = [ph([128, 4, 512], BF16, "sgu%d" % i) for i in range(2)]
    gv = [ph([128, 512], F32, "gv%d" % i) for i in range(4)]
    vv = [ph([128, 512], BF16, "vv%d" % i) for i in range(4)]
    st6 = [ph([128, 6], F32, "st6_%d" % i) for i in range(4)]
    mv = [ph([128, 2], F32, "mv%d" % i) for i in range(4)]
    rs1 = [ph([128, 1], F32, "rs1_%d" % i) for i in range(4)]
    tmq = [ph([128, 512], F32, "tmq%d" % i) for i in range(2)]
    for mt in range(4):
        tsl = slice(mt * 512, (mt + 1) * 512)
        u_, sg4 = ut[mt % 2], sgu[mt % 2]
        for g in range(4):
            pu = next_ps(C)
            for kc in range(8):
                P.op("pe", MM(ps[pu][:], wsu[:, kc, g * 128:(g + 1) * 128], hn[:, kc, tsl], kc == 0, kc == 7),
                     r=[wsub, hnb(kc, mt)], w=[psb[pu]])
            P.op("act", ACTF(u_[:, g, :], ps[pu][:], AF.Gelu_apprx_tanh), r=[psb[pu]], w=[P.buf("ut", mt % 2, g)])
        pm = reserve_ps(C, 4)
        pvs = []
        for q in range(4):
            tt = mt * 4 + q
            pv = next_ps(C)
            pvs.append(pv)
            for kc in range(8):
                P.op("pe", MM(ps[pv][:], hn[:, kc, tt * 128:(tt + 1) * 128], wsv[:, kc, :], kc == 0, kc == 7),
                     r=[wsvb, hnb(kc, mt)], w=[psb[pv]])
            P.op("act", ACTF(gv[q][:], ps[pv][:], AF.Gelu_apprx_tanh), r=[psb[pv]], w=[P.buf("gv", q)])
        for q in range(4):
            P.op("dve", lambda e, q=q: e.bn_stats(out=st6[q][:], in_=gv[q][:]), r=[P.buf("gv", q)], w=[P.buf("st6", q)])
        for q in range(4):
            P.op("dve", lambda e, q=q: e.bn_aggr(out=mv[q][:], in_=st6[q][:]), r=[P.buf("st6", q)], w=[P.buf("mv", q)])
        for q in range(4):
            P.op("act", ACTF(rs1[q][:], mv[q][:, 1:2], AF.Ln, bias=EPS), r=[P.buf("mv", q)], w=[P.buf("rs1", q)])
        for q in range(4):
            P.op("act", ACTF(rs1[q][:], rs1[q][:], AF.Exp, scale=-0.5), r=[P.buf("rs1", q)], w=[P.buf("rs1", q)])
        for q in range(4):
            P.op("dve", TS(gv[q][:], gv[q][:], mv[q][:, 0:1], rs1[q][:, 0:1], ALU.subtract, ALU.mult),
                 r=[P.buf("gv", q), P.buf("mv", q), P.buf("rs1", q)], w=[P.buf("gv", q)])
        for q in range(4):
            P.op("pool", TT(gv[q][:], gv[q][:], lng[:], ALU.mult), r=[P.buf("gv", q), lnbuf], w=[P.buf("gv", q)])
        for q in range(4):
            P.op("dve", TT(vv[q][:], gv[q][:], lnbt[:], ALU.add), r=[P.buf("gv", q), lnbuf], w=[P.buf("vv", q)])
        for q in range(4):
            for g in range(4):
                P.op("pe", MM(ps[pm[g]][:, q * 128:(q + 1) * 128], vv[q][:, g * 128:(g + 1) * 128], wsT[:, g, :]),
                     r=[P.buf("vv", q), wsTb], w=[psb[pm[g]]])
        for g in range(4):
            tq, tqb = tmq[g % 2], P.buf("tmq", g % 2)
            P.op("dve", TT(tq[:], ps[pm[g]][:], bsb4[:, g, :, :].rearrange("p n i -> p (n i)"), ALU.add),
                 r=[psb[pm[g]], lnbuf], w=[tqb])
            P.op("dve", TT(sg4[:, g, :], tq[:], u_[:, g, :], ALU.mult),
                 r=[tqb, P.buf("ut", mt % 2, g)], w=[P.buf("sgu", mt % 2, g)])
        release_ps(C, pm)
        for db in range(8):
            po = next_ps(C)
            for kc in range(8):
                if kc < 4:
                    rhs, rb_ = oa[:, kc, tsl], oab(kc, mt)
                else:
                    rhs, rb_ = sg4[:, kc - 4, :], P.buf("sgu", mt % 2, kc - 4)
                P.op("pe", MM(ps[po][:], wout[:, kc, db * 128:(db + 1) * 128], rhs, kc == 0, kc == 7),
                     r=[woutb, rb_], w=[psb[po]])
            P.op("dve", TT(C.hT[:, db, tsl], C.hT[:, db, tsl], ps[po][:], ALU.add),
                 r=[psb[po], hbuf(C, db, mt)], w=[hbuf(C, db, mt)])
        if mt == 3:
            dump(C, "sgu", sg4, None)


def mixer1(C):
    P, ps, psb, ph = C.P, C.ps, C.psb, C.phase
    W = HPG * 128
    hn = ph([128, 8, S], BF16, "hn")
    hnb = lambda kb, mt: P.buf("hn", kb, mt)
    sc = {}
    for nm in ("g", "lnb", "beta", "c", "negc", "nec", "ECL", "ekend"):
        sc[nm] = ph([128, NT, 16], F32, "sc_" + nm)
    scb = P.buf("gdn_scalars")
    mark0 = ph.p
    markg0 = ph.p
    wsm = ph([128, 8, 32], F32, "wsm")
    wsmb = P.buf("wsm")
    P.dma("sp", wsm[:], C.gdn_wsm_d, dst=wsmb)
    gvec = ph([128, 32], F32, "gvec")
    gvb = P.buf("gvec")
    P.dma("sp", gvec[:], C.gdn_vec_d.partition_broadcast(128), dst=gvb)
    nea = ph([128, 16], F32, "nea")
    neab = P.buf("nea")
    P.op("act", ACTF(nea[:], gvec[:, 16:32], AF.Exp), r=[gvb], w=[neab])
    P.op("dve", TS(nea[:], nea[:], -1.0, None, ALU.mult), r=[neab], w=[neab])
    ab_tm = ph([128, NT, 32], F32, "ab_tm")
    abb = P.buf("ab_tm")
    t1 = ph([128, NT, 16], F32, "t1")
    t2 = ph([128, NT, 16], F32, "t2")
    t1b, t2b = P.buf("t1"), P.buf("t2")
    hn32 = ph([128, 8, 512], F32, "hn32")
    h32b = lambda kb: P.buf("hn32", kb)
    sqt = [ph([128, 512], BF16, "sq%d" % i) for i in range(2)]
    rstd = [ph([128, 512], F32, "rstd%d" % i) for i in range(2)]
    pa = reserve_ps(C, 1)[0]
    for mt in range(4):
        tsl = slice(mt * 512, (mt + 1) * 512)
        rb = P.buf("rstd", mt % 2)
        rt = rstd[mt % 2]
        rms_rstd(C, mt, sqt, rt, rb)
        for kb in range(8):
            P.op("dve", STT(hn32[:, kb, :], C.hT[:, kb, tsl], vcol(C, "norm_mix1", kb), rt[:], ALU.mult, ALU.mult),
                 r=[hbuf(C, kb, mt), rb, C.bconst], w=[h32b(kb)])
            P.op("act", ACTF(hn[:, kb, tsl], hn32[:, kb, :], AF.Copy), r=[h32b(kb)], w=[hnb(kb, mt)])
        for q in range(4):
            tt = mt * 4 + q
            for kc in range(8):
                P.op("pe", MM(ps[pa][:, tt * 32:(tt + 1) * 32], hn32[:, kc, q * 128:(q + 1) * 128], wsm[:, kc, :],
                              kc == 0, kc == 7), r=[wsmb, h32b(kc)], w=[psb[pa]])
    release_ps(C, [pa])
    P.op("act", ACTF(ab_tm[:].rearrange("p t c -> p (t c)"), ps[pa][:], AF.Copy), r=[psb[pa]], w=[abb])
    for tt in range(NT):
        P.op("dve", TT(t1[:, tt, :], ab_tm[:, tt, 16:32], gvec[:, 0:16], ALU.add), r=[abb, gvb], w=[t1b])
    fl = lambda t: t[:].rearrange("p t c -> p (t c)")
    P.op("act", ACTF(fl(t1), fl(t1), AF.Exp), r=[t1b], w=[t1b])
    P.op("act", ACTF(fl(t1), fl(t1), AF.Ln, bias=1.0), r=[t1b], w=[t1b])
    for tt in range(NT):
        P.op("dve", TT(sc["g"][:, tt, :], t1[:, tt, :], nea[:], ALU.mult), r=[t1b, neab], w=[scb])
    P.op("act", ACTF(t2[:], ab_tm[:, :, 0:16], AF.Exp, scale=-1.0), r=[abb], w=[t2b])
    P.op("act", ACTF(fl(t2), fl(t2), AF.Ln, bias=1.0), r=[t2b], w=[t2b])
    P.op("dve", TS(fl(sc["lnb"]), fl(t2), -1.0, None, ALU.mult), r=[t2b], w=[scb])
    P.op("act", ACTF(fl(sc["beta"]), fl(sc["lnb"]), AF.Exp), r=[scb], w=[scb])
    pc = next_ps(C)
    for tt in range(NT):
        for d in range(2):
            P.op("pe", MM(ps[pc][:, tt * 16 + d * 8:tt * 16 + d * 8 + 8], C.mask01[d], sc["g"][:, tt, d * 8:(d + 1) * 8]),
                 r=[scb, C.bconst], w=[psb[pc]])
    P.op("act", ACTF(fl(sc["c"]), ps[pc][:, 0:NT * 16], AF.Copy), r=[psb[pc]], w=[scb])
    P.op("dve", TS(fl(sc["negc"]), fl(sc["c"]), -1.0, None, ALU.mult), r=[scb], w=[scb])
    P.op("act", ACTF(fl(sc["nec"]), fl(sc["c"]), AF.Exp), r=[scb], w=[scb])
    P.op("dve", TS(fl(sc["nec"]), fl(sc["nec"]), -1.0, None, ALU.mult), r=[scb], w=[scb])
    pl = next_ps(C)
    for d in range(2):
        P.op("pe", MM(ps[pl][:, d * 128:(d + 1) * 128], C.esel[d], sc["c"][:, :, d * 8:(d + 1) * 8]),
             r=[scb, C.bconst], w=[psb[pl]])
    for d in range(2):
        cl = ps[pl][:, d * 128:(d + 1) * 128].rearrange("p (t h) -> p t h", h=8)
        P.op("act", ACTF(sc["ECL"][:, :, d * 8:(d + 1) * 8], cl, AF.Exp), r=[psb[pl]], w=[scb])
        P.op("dve", TT(t1[:, :, d * 8:(d + 1) * 8], cl, sc["c"][:, :, d * 8:(d + 1) * 8], ALU.subtract),
             r=[psb[pl], scb], w=[t1b])
    P.op("act", ACTF(fl(sc["ekend"]), fl(t1), AF.Exp), r=[t1b], w=[scb])
    P.barrier()
    ph.p = markg0
    mark1 = ph.p
    cwo = VLAY["gdn_conv_w"][0]
    if C.gdn_stop == 0:
        return

    for grp in range(8 // HPG):
        h0 = grp * HPG
        ph.p = mark1
        qT = ph([128, HPG, S], BF16, "qT")
        kT = ph([128, HPG, S], BF16, "kT")
        vtm = ph([128, NT, W], BF16, "vtm")
        ob = ph([128, HPG, S], BF16, "ob")
        qTb = lambda hl, mt: P.buf("qT", hl, mt)
        kTb = lambda hl, mt: P.buf("kT", hl, mt)
        vtb = lambda tt: P.buf("vtm", tt)
        obb = lambda hl, n: P.buf("ob", hl, n)
        mark2 = ph.p
        wblk = [ph([128, 8, 128], BF16, "wblk%d" % i) for i in range(2)]
        Gb = ph([128, S + 2], F32, "gbuf")
        cv = [ph([128, 512], F32, "cv%d" % i) for i in range(2)]
        so = [ph([128, 512], F32, "so%d" % i) for i in range(4)]
        sob = [ph([128, 512], BF16, "sob")] * 2
        sqn = [ph([128, 512], BF16, "sqn%d" % i) for i in range(4)]
        rsn = [ph([128, 512], F32, "rsn")] * 2
        gb_ = lambda mt: P.buf("gbuf", mt)
        gedge = P.buf("gbuf_edge")
        P.op("dve", MS(Gb[:, 0:1], 0.0), w=[gedge])
        P.op("dve", MS(Gb[:, S + 1:S + 2], 0.0), w=[gedge])
        wi = 0
        for kind in range(3):
            for hl in range(HPG):
                blk = kind * 8 + h0 + hl
                wt, wtb = wblk[wi % 2], P.buf("wblk", wi % 2)
                wi += 1
                P.dma("pool", wt[:], C.gdn_win_d[blk], dst=wtb)
                for mt in range(4):
                    pi = next_ps(C)
                    for kc in range(8):
                        P.op("pe", MM(ps[pi][:], wt[:, kc, :], hn[:, kc, mt * 512:(mt + 1) * 512], kc == 0, kc == 7),
                             r=[wtb, hnb(kc, mt)], w=[psb[pi]])
                    P.op("act", ACTF(Gb[:, 1 + mt * 512:1 + (mt + 1) * 512], ps[pi][:], AF.Copy),
                         r=[psb[pi]], w=[gb_(mt)])
                pns = reserve_ps(C, 4) if kind < 2 else None
                for mt in range(4):
                    par = mt % 2
                    tsl = slice(mt * 512, (mt + 1) * 512)
                    cvt, cvb = cv[par], P.buf("cv", par)
                    rd = [gb_(mt), gedge, C.bconst]
                    if mt > 0:
                        rd.append(gb_(mt - 1))
                    if mt < 3:
                        rd.append(gb_(mt + 1))
                    b0 = mt * 512
                    w_ = lambda k: C.vec[:, cwo + blk * 3 + k:cwo + blk * 3 + k + 1]
                    P.op("dve", TS(cvt[:], Gb[:, b0:b0 + 512], w_(0), None, ALU.mult), r=rd, w=[cvb])
                    P.op("dve", STT(cvt[:], Gb[:, b0 + 1:b0 + 513], w_(1), cvt[:], ALU.mult, ALU.add), r=rd + [cvb], w=[cvb])
                    P.op("dve", STT(cvt[:], Gb[:, b0 + 2:b0 + 514], w_(2), cvt[:], ALU.mult, ALU.add), r=rd + [cvb], w=[cvb])
                    if kind < 2:
                        sot, sotb = so[mt], P.buf("so", mt)
                        P.op("act", ACTF(sot[:], cvt[:], AF.Silu), r=[cvb], w=[sotb])
                        sqt_, sqb_ = sqn[mt], P.buf("sqn", mt)
                        P.op("act", ACTF(sqt_[:], sot[:], AF.Square), r=[sotb], w=[sqb_])
                        P.op("pe", MM(ps[pns[mt]][:], C.ones1[:], sqt_[:]), r=[sqb_, P.buf("ones")], w=[psb[pns[mt]]])
                    else:
                        sbt, sbtb = sob[par], P.buf("sob", 0)
                        P.op("act", ACTF(sbt[:], cvt[:], AF.Silu), r=[cvb], w=[sbtb])
                        pt = next_ps(C)
                        for q in range(4):
                            P.op("pe", MM(ps[pt][:, q * 128:(q + 1) * 128], sbt[:, q * 128:(q + 1) * 128], C.identb[:]),
                                 r=[sbtb, P.buf("identb")], w=[psb[pt]])
                        P.op("act", ACTF(vtm[:, mt * 4:(mt + 1) * 4, hl * 128:(hl + 1) * 128],
                                         ps[pt][:].rearrange("p (q v) -> p q v", v=128), AF.Copy),
                             r=[psb[pt]], w=[vtb(mt * 4 + q_) for q_ in range(4)])
                if kind < 2:
                    for mt in range(4):
                        tsl = slice(mt * 512, (mt + 1) * 512)
                        sot, sotb = so[mt], P.buf("so", mt)
                        rt_, rb_ = rsn[0], P.buf("rsn", 0)
                        rsqrt_from_psum(P, rt_[:], ps[pns[mt]][:], psb[pns[mt]], rb_)
                        if kind == 0:
                            P.op("dve", STT(qT[:, hl, tsl], sot[:], 128.0 ** -0.5, rt_[:], ALU.mult, ALU.mult),
                                 r=[sotb, rb_], w=[qTb(hl, mt)])
                        else:
                            P.op("dve", TT(kT[:, hl, tsl], sot[:], rt_[:], ALU.mult), r=[sotb, rb_], w=[kTb(hl, mt)])
                    release_ps(C, pns)
        P.barrier()
        ph.p = mark2
        if C.gdn_stop == 1:
            return

        of_ = ph([128, HPG, S], BF16, "of")
        ofb = lambda hl, n: P.buf("of", hl, n)
        oo = [of_, ob]
        oob = [ofb, obb]
        markT = ph.p
        NV = 2 * HPG
        WW = NV * 128
        VH = [(d, hl) for d in range(2) for hl in range(HPG)]
        HS = [slice(v * 128, (v + 1) * 128) for v in range(NV)]
        t = {}
        for nm in ("Rt", "vn", "Sbf", "mniW", "mnsW", "idW", "id2W"):
            t[nm] = ph([128, WW], BF16, nm)
        t["Sm"] = ph([128, WW], F32, "Sm")
        tsets = []
        for k_ in range(2):
            ts_ = {}
            for nm in ("tmpA", "tmpB"):
                ts_[nm] = ph([128, WW], F32, "%s_%d" % (nm, k_))
            for nm in ("DTi", "ETs", "ECb", "X0", "X1", "Y0", "Y1", "P0", "P1", "TN", "ZT", "Rn"):
                ts_[nm] = ph([128, WW], BF16, "%s_%d" % (nm, k_))
            tsets.append(ts_)
        csets = [{nm: ph([128, WW], BF16, "%s%d" % (nm, i)) for nm in ("MT", "Aqk", "kend", "qdec")} for i in range(3)]
        B_ = lambda nm, *k: P.buf("g2_" + nm, *k)
        for v, (d, hl) in enumerate(VH):
            P.op("pool", CP(t["mniW"][:, HS[v]], C.mneg_incl[d]), r=[C.bconst], w=[B_("maskW")])
            P.op("pool", CP(t["mnsW"][:, HS[v]], C.mneg_strict[d]), r=[C.bconst], w=[B_("maskW")])
            P.op("pool", CP(t["idW"][:, HS[v]], C.ident), r=[C.bconst], w=[B_("maskW")])
            P.op("pool", TS(t["id2W"][:, HS[v]], C.ident, 2.0, None, ALU.mult), r=[C.bconst], w=[B_("maskW")])
        v3 = lambda tt_: tt_[:].rearrange("p (h i) -> p h i", i=128)
        v3p = lambda ap: ap.rearrange("p (h i) -> p h i", i=128)
        order = [list(range(NT)), list(range(NT - 1, -1, -1))]
        hb_pool = list(range(8))

        def hb_alloc():
            assert hb_pool, "out of PSUM banks"
            return hb_pool.pop(0)

        def hb_free(x):
            hb_pool.append(x)

        hp = lambda x: ps[x]
        hbb = lambda x: psb[x]

        def pre(i):
            ci = i % 3
            ti = i % 2
            cs_ = csets[ci]
            tp = tsets[ti]
            Bt = lambda nm, *k: P.buf("g2_" + nm, ti, *k)
            nn = [order[d][i] for (d, hl) in VH]
            cols = [slice(n * 128, (n + 1) * 128) for n in nn]
            mts = [n // 4 for n in nn]
            scol = lambda nm, v: sc[nm][:, nn[v], VH[v][0] * 8 + h0 + VH[v][1]:VH[v][0] * 8 + h0 + VH[v][1] + 1]
            bc = lambda nm, v: scol(nm, v).to_broadcast([128, 128])
            pA, pB = hb_alloc(), hb_alloc()
            for v, (d, hl) in enumerate(VH):
                P.op("pe", MM(hp(pA)[:, HS[v]], bc("g", v), C.mask01[d]), r=[scb, C.bconst], w=[hbb(pA)])
            for v, (d, hl) in enumerate(VH):
                P.op("pe", MM(hp(pB)[:, HS[v]], bc("g", v), C.mask01[d], True, False), r=[scb, C.bconst], w=[hbb(pB)])
                P.op("pe", MM(hp(pB)[:, HS[v]], bc("lnb", v), C.ident, False, True), r=[scb, C.bconst], w=[hbb(pB)])
            yield
            P.op("dve", TT(tp["tmpA"][:], hp(pA)[:], t["mniW"][:], ALU.add), r=[hbb(pA), B_("maskW")], w=[Bt("tmpA")])
            P.op("act", ACTF(tp["ECb"][:], hp(pA)[:], AF.Exp), r=[hbb(pA)], w=[Bt("ECb")])
            P.op("dve", TT(tp["tmpB"][:], hp(pB)[:], t["mnsW"][:], ALU.add), r=[hbb(pB), B_("maskW")], w=[Bt("tmpB")])
            hb_free(pA)
            hb_free(pB)
            yield
            pG, pQ = hb_alloc(), hb_alloc()
            for v, (d, hl) in enumerate(VH):
                P.op("pe", MM(hp(pG)[:, HS[v]], kT[:, hl, cols[v]], kT[:, hl, cols[v]]), r=[kTb(hl, mts[v])], w=[hbb(pG)])
                P.op("pe", MM(hp(pQ)[:, HS[v]], kT[:, hl, cols[v]], qT[:, hl, cols[v]]),
                     r=[kTb(hl, mts[v]), qTb(hl, mts[v])], w=[hbb(pQ)])
            for v in range(NV):
                P.op("act", ACTF(tp["ETs"][:, HS[v]], tp["tmpB"][:, HS[v]], AF.Exp, bias=scol("negc", v)),
                     r=[Bt("tmpB"), scb], w=[Bt("ETs")])
            for v in range(NV):
                P.op("act", ACTF(tp["DTi"][:, HS[v]], tp["tmpA"][:, HS[v]], AF.Exp, bias=scol("negc", v)),
                     r=[Bt("tmpA"), scb], w=[Bt("DTi")])
            for d in range(2):
                vs = slice(d * HPG * 128, (d + 1) * HPG * 128)
                P.op("pool", TT(v3p(cs_["qdec"][:, vs]), qT[:, :, cols[d * HPG]], v3p(tp["ECb"][:, vs]), ALU.mult),
                     r=[Bt("ECb")] + [qTb(hl, mts[d * HPG]) for hl in range(HPG)], w=[B_("qdec", ci)])
            yield
            xb = lambda k: Bt("X", k)
            yb = lambda k: Bt("Y", k)
            pb = lambda k: Bt("P", k)
            Xt, Yt, Pt = [tp["X0"], tp["X1"]], [tp["Y0"], tp["Y1"]], [tp["P0"], tp["P1"]]
            P.op("dve", STT(Xt[0][:], hp(pG)[:], -1.0, tp["ETs"][:], ALU.mult, ALU.mult),
                 r=[hbb(pG), Bt("ETs")], w=[xb(0)])
            P.op("dve", TT(cs_["Aqk"][:], hp(pQ)[:], tp["DTi"][:], ALU.mult), r=[hbb(pQ), Bt("DTi")], w=[B_("Aqk", ci)])
            hb_free(pG)
            hb_free(pQ)
            yield
            pY = hb_alloc()
            for v in range(NV):
                P.op("pe", MM(hp(pY)[:, HS[v]], Xt[0][:, HS[v]], C.identb[:]), r=[xb(0), P.buf("identb")], w=[hbb(pY)])
            P.op("pool", TT(Pt[0][:], Xt[0][:], t["idW"][:], ALU.add), r=[xb(0), B_("maskW")], w=[pb(0)])
            yield
            P.op("act", ACTF(Yt[0][:], hp(pY)[:], AF.Copy), r=[hbb(pY)], w=[yb(0)])
            hb_free(pY)
            P.op("pool", TT(tp["TN"][:], t["idW"][:], Yt[0][:], ALU.subtract), r=[yb(0), B_("maskW")], w=[Bt("TN")])
            yield
            cur = 0
            for k in range(1, NLEV + 1):
                nx = 1 - cur
                pX, pY = hb_alloc(), hb_alloc()
                for v in range(NV):
                    P.op("pe", MM(hp(pX)[:, HS[v]], Yt[cur][:, HS[v]], Xt[cur][:, HS[v]]), r=[xb(cur), yb(cur)], w=[hbb(pX)])
                for v in range(NV):
                    P.op("pe", MM(hp(pY)[:, HS[v]], Xt[cur][:, HS[v]], Yt[cur][:, HS[v]]), r=[xb(cur), yb(cur)], w=[hbb(pY)])
                yield
                P.op("dve", CP(Yt[nx][:], hp(pY)[:]), r=[hbb(pY)], w=[yb(nx)])
                P.op("act", ACTF(Xt[nx][:], hp(pX)[:], AF.Copy), r=[hbb(pX)], w=[xb(nx)])
                hb_free(pX)
                hb_free(pY)
                yield
                pP = hb_alloc()
                for v in range(NV):
                    P.op("pe", MM(hp(pP)[:, HS[v]], Yt[nx][:, HS[v]], Pt[cur][:, HS[v]]),
                         r=[pb(cur), yb(nx)], w=[hbb(pP)])
                yield
                P.op("dve", TT(Pt[nx][:], hp(pP)[:], Pt[cur][:], ALU.add), r=[hbb(pP), pb(cur)], w=[pb(nx)])
                hb_free(pP)
                cur = nx
            yield
            Z = Pt[cur]
            pZ, pE = hb_alloc(), hb_alloc()
            for v in range(NV):
                P.op("pe", MM(hp(pZ)[:, HS[v]], Z[:, HS[v]], C.identb[:]), r=[pb(cur), P.buf("identb")], w=[hbb(pZ)])
                P.op("pe", MM(hp(pE)[:, HS[v]], tp["TN"][:, HS[v]], Z[:, HS[v]]), r=[pb(cur), Bt("TN")], w=[hbb(pE)])
            yield
            P.op("act", ACTF(tp["ZT"][:], hp(pZ)[:], AF.Copy), r=[hbb(pZ)], w=[Bt("ZT")])
            P.op("dve", STT(tp["Rn"][:], hp(pE)[:], -1.0, t["id2W"][:], ALU.mult, ALU.add),
                 r=[hbb(pE), B_("maskW")], w=[Bt("Rn")])
            hb_free(pZ)
            hb_free(pE)
            yield
            pM = hb_alloc()
            for v in range(NV):
                P.op("pe", MM(hp(pM)[:, HS[v]], tp["ZT"][:, HS[v]], tp["Rn"][:, HS[v]]), r=[Bt("ZT"), Bt("Rn")], w=[hbb(pM)])
            pK = hb_alloc()
            for v, (d, hl) in enumerate(VH):
                P.op("pe", MM(hp(pK)[:, HS[v]], kT[:, hl, cols[v]], C.identb[:]), r=[kTb(hl, mts[v]), P.buf("identb")],
                     w=[hbb(pK)])
            yield
            for v in range(NV):
                P.op("act", ACTF(cs_["kend"][:, HS[v]], hp(pK)[:, HS[v]], AF.Copy, scale=scol("ekend", v)),
                     r=[hbb(pK), scb], w=[B_("kend", ci)])
            hb_free(pK)
            for v in range(NV):
                P.op("act", ACTF(cs_["MT"][:, HS[v]], hp(pM)[:, HS[v]], AF.Copy, scale=scol("beta", v)),
                     r=[hbb(pM), scb], w=[B_("MT", ci)])
            hb_free(pM)

        def scan(i):
            ci = i % 3
            cs_ = csets[ci]
            nn = [order[d][i] for (d, hl) in VH]
            cols = [slice(n * 128, (n + 1) * 128) for n in nn]
            mts = [n // 4 for n in nn]
            scol = lambda nm, v: sc[nm][:, nn[v], VH[v][0] * 8 + h0 + VH[v][1]:VH[v][0] * 8 + h0 + VH[v][1] + 1]
            csb = {nm: B_(nm, ci) for nm in ("MT", "Aqk", "kend", "qdec")}
            Rt, vn, Sm, Sbf = t["Rt"], t["vn"], t["Sm"], t["Sbf"]
            p1 = hb_alloc()
            for v, (d, hl) in enumerate(VH):
                P.op("pe", MM(hp(p1)[:, HS[v]], kT[:, hl, cols[v]], Sbf[:, HS[v]]), r=[kTb(hl, mts[v]), B_("Sbf")], w=[hbb(p1)])
            yield
            for v, (d, hl) in enumerate(VH):
                P.op("dve", STT(Rt[:, HS[v]], hp(p1)[:, HS[v]], scol("nec", v), vtm[:, nn[v], hl * 128:(hl + 1) * 128],
                                ALU.mult, ALU.add), r=[hbb(p1), scb, vtb(nn[v])], w=[B_("Rt")])
            hb_free(p1)
            yield
            p2 = hb_alloc()
            for v in range(NV):
                P.op("pe", MM(hp(p2)[:, HS[v]], cs_["MT"][:, HS[v]], Rt[:, HS[v]]), r=[csb["MT"], B_("Rt")], w=[hbb(p2)])
            yield
            P.op("act", ACTF(vn[:], hp(p2)[:], AF.Copy), r=[hbb(p2)], w=[B_("vn")])
            hb_free(p2)
            yield
            p3, p4 = hb_alloc(), hb_alloc()
            for v in range(NV):
                P.op("pe", MM(hp(p4)[:, HS[v]], cs_["kend"][:, HS[v]], vn[:, HS[v]]), r=[csb["kend"], B_("vn")], w=[hbb(p4)])
            for v in range(NV):
                P.op("pe", MM(hp(p3)[:, HS[v]], Sbf[:, HS[v]], cs_["qdec"][:, HS[v]], True, False),
                     r=[B_("Sbf"), csb["qdec"]], w=[hbb(p3)])
                P.op("pe", MM(hp(p3)[:, HS[v]], vn[:, HS[v]], cs_["Aqk"][:, HS[v]], False, True),
                     r=[B_("vn"), csb["Aqk"]], w=[hbb(p3)])
            yield
            for v in range(NV):
                P.op("dve", STT(Sm[:, HS[v]], Sm[:, HS[v]], scol("ECL", v), hp(p4)[:, HS[v]], ALU.mult, ALU.add),
                     r=[hbb(p4), scb, B_("Sm")], w=[B_("Sm")])
            for d in range(2):
                vs = slice(d * HPG * 128, (d + 1) * HPG * 128)
                P.op("act", ACTF(oo[d][:, :, cols[d * HPG]], v3p(hp(p3)[:, vs]), AF.Copy), r=[hbb(p3)],
                     w=[oob[d](hl, nn[d * HPG]) for hl in range(HPG)])
            hb_free(p3)
            hb_free(p4)
            yield
            P.op("dve", CP(Sbf[:], Sm[:]), r=[B_("Sm")], w=[B_("Sbf")])

        def step(g_):
            try:
                next(g_)
                return True
            except StopIteration:
                return False

        P.op("dve", MS(t["Sm"][:], 0.0), w=[B_("Sm")])
        P.op("dve", MS(t["Sbf"][:], 0.0), w=[B_("Sbf")])
        pres = {0: pre(0), 1: pre(1)}
        while step(pres[0]):
            step(pres[1])
        for i in range(NT):
            if i + 2 < NT:
                pres[i + 2] = pre(i + 2)
            must = [scan(i)]
            if i + 1 < NT:
                must.append(pres[i + 1])
            extra = pres.get(i + 2)
            while must:
                must = [g_ for g_ in must if step(g_)]
                if extra is not None and not step(extra):
                    extra = None

        P.barrier()
        ph.p = markT
        osum = ph([128, 512], F32, "osum")
        sqo = ph([128, 512], BF16, "sqo")
        rso = ph([128, 512], F32, "rso")
        zs = ph([128, 512], BF16, "zs")
        wz = ph([128, 8, 128], BF16, "wz")
        ong = ph([128, HPG, S], BF16, "ong")
        ongb = lambda hl, mt: P.buf("ong", hl, mt)
        woutg = ph([128, HPG, D], BF16, "woutg")
        woutgb = P.buf("woutg")
        for hl in range(HPG):
            P.dma("pool", woutg[:, hl, :], C.gdn_wout_d[:, h0 + hl, :], dst=woutgb)
        Bn = lambda nm: P.buf("g2n_" + nm)
        zs4 = [ph([128, 512], BF16, "zs%d" % i) for i in range(4)]
        for hl in range(HPG):
            h = h0 + hl
            P.dma("pool", wz[:], C.gdn_win_d[24 + h], dst=Bn("wz"))
            for mt in range(4):
                tsl = slice(mt * 512, (mt + 1) * 512)
                pz = next_ps(C)
                for kc in range(8):
                    P.op("pe", MM(ps[pz][:], wz[:, kc, :], hn[:, kc, tsl], kc == 0, kc == 7),
                         r=[Bn("wz"), hnb(kc, mt)], w=[psb[pz]])
                P.op("act", ACTF(zs4[mt][:], ps[pz][:], AF.Silu), r=[psb[pz]], w=[P.buf("g2n_zs", mt)])
            for mt in range(4):
                tsl = slice(mt * 512, (mt + 1) * 512)
                rdo = [oob[d](hl, mt * 4 + q_) for d in range(2) for q_ in range(4)]
                P.op("dve", TT(osum[:], of_[:, hl, tsl], ob[:, hl, tsl], ALU.add), r=rdo, w=[Bn("osum")])
                P.op("act", ACTF(sqo[:], osum[:], AF.Square), r=[Bn("osum")], w=[Bn("sqo")])
                pn = next_ps(C)
                P.op("pe", MM(ps[pn][:], C.ones128[:], sqo[:]), r=[Bn("sqo"), P.buf("ones")], w=[psb[pn]])
                rsqrt_from_psum(P, rso[:], ps[pn][:], psb[pn], Bn("rso"))
                P.op("dve", TT(rso[:], rso[:], zs4[mt][:], ALU.mult), r=[Bn("rso"), P.buf("g2n_zs", mt)], w=[Bn("rso")])
                P.op("dve", STT(ong[:, hl, tsl], osum[:], vcol(C, "gdn_norm"), rso[:], ALU.mult, ALU.mult),
                     r=[Bn("osum"), Bn("rso"), C.bconst], w=[ongb(hl, mt)])
        for mt in range(4):
            tsl = slice(mt * 512, (mt + 1) * 512)
            for db in range(8):
                po = next_ps(C)
                for hl in range(HPG):
                    P.op("pe", MM(ps[po][:], woutg[:, hl, db * 128:(db + 1) * 128], ong[:, hl, tsl], hl == 0, hl == HPG - 1),
                         r=[woutgb, ongb(hl, mt)], w=[psb[po]])
                P.op("dve", TT(C.hT[:, db, tsl], C.hT[:, db, tsl], ps[po][:], ALU.add),
                     r=[psb[po], hbuf(C, db, mt)], w=[hbuf(C, db, mt)])
        P.barrier()


def final_norm_store(C, sq):
    P, ps, psb = C.P, C.ps, C.psb
    ph = C.phase
    sqt = [ph([128, 512], BF16, "sq%d" % i) for i in range(2)]
    rstd = [ph([128, 512], F32, "rstd%d" % i) for i in range(2)]
    yt = ph([128, 8, 512], F32, "y")
    NOB = 6
    ot = [ph([128, D], F32, "ot%d" % i) for i in range(NOB)]
    for mt in range(4):
        tsl = slice(mt * 512, (mt + 1) * 512)
        rb = P.buf("rstd", mt % 2)
        rt = rstd[mt % 2]
        rms_rstd(C, mt, sqt, rt, rb)
        yb = P.buf("fy")
        for kb in range(8):
            P.op("dve", lambda e, kb=kb, rt=rt, tsl=tsl: e.scalar_tensor_tensor(
                out=yt[:, kb, :], in0=C.hT[:, kb, tsl], scalar=vcol(C, "norm_final", kb),
                in1=rt[:], op0=ALU.mult, op1=ALU.mult), r=[hbuf(C, kb, mt), rb, C.bconst], w=[yb])
        for q in range(4):
            tt = mt * 4 + q
            ob = P.buf("fot", tt % NOB)
            o_t = ot[tt % NOB]
            for half in range(2):
                pi2 = next_ps(C)
                for j in range(4):
                    kb = half * 4 + j
                    P.op("pe", lambda e, pi2=pi2, j=j, kb=kb, q=q: e.matmul(
                        ps[pi2][:, j * 128:(j + 1) * 128], yt[:, kb, q * 128:(q + 1) * 128], C.ident,
                        start=True, stop=True), r=[yb, C.bconst], w=[psb[pi2]])
                P.op("act", lambda e, pi2=pi2, half=half, o_t=o_t: e.activation(
                    out=o_t[:, half * 512:(half + 1) * 512], in_=ps[pi2][:], func=AF.Copy),
                    r=[psb[pi2]], w=[ob])
            P.dma("sp", C.out[sq, tt * 128:(tt + 1) * 128, :], o_t[:], src=ob)


def vec_layout():
    lay = {}
    off = [0]

    def add(name, n):
        lay[name] = (off[0], n)
        off[0] += n
    add("norm_final", 8)
    for l in range(2):
        add("norm_mix%d" % l, 8)
        add("norm_ffn%d" % l, 8)
        add("ffn_conv_w%d" % l, NCB * 3)
        add("ffn_conv_b%d" % l, NCB)
    add("gla_norm", 1)
    add("gdn_conv_w", 24 * 3)
    add("gdn_norm", 1)
    return lay, off[0]


VLAY, NVEC = vec_layout()


def make_vecs(inputs):
    v = np.zeros((128, NVEC), np.float32)

    def put(name, arr):
        o, n = VLAY[name]
        v[:, o:o + n] = np.asarray(arr, np.float32).reshape(128, n)
    f = lambda a: np.asarray(a, np.float32)
    put("norm_final", f(inputs["norm_final"]).reshape(8, 128).T)
    for l in range(2):
        put("norm_mix%d" % l, f(inputs["norm_mix"])[l].reshape(8, 128).T)
        put("norm_ffn%d" % l, f(inputs["norm_ffn"])[l].reshape(8, 128).T)
        put("ffn_conv_w%d" % l, f(inputs["ffn_conv_w"])[l].reshape(3, NCB, 128).transpose(2, 1, 0))
        put("ffn_conv_b%d" % l, f(inputs["ffn_conv_b"])[l].reshape(NCB, 128).T)
    put("gla_norm", f(inputs["gla_norm"])[0].reshape(128, 1))
    put("gdn_conv_w", f(inputs["gdn_conv_w"])[0].reshape(3, 24, 128).transpose(2, 1, 0))
    put("gdn_norm", f(inputs["gdn_norm"])[0].reshape(128, 1))
    return v


def make_weights(inputs):
    f = lambda a: np.asarray(a, np.float32)
    w = {}
    wu = f(inputs["ffn_w_up"]).reshape(2, 8, 128, 2 * NCB, 128)
    w["ffn_wup"] = np.ascontiguousarray(wu.transpose(0, 3, 2, 1, 4))
    w["ffn_wdn"] = np.ascontiguousarray(f(inputs["ffn_w_down"]).reshape(2, NCB, 128, D))
    tile_k = lambda W: np.ascontiguousarray(W.reshape(8, 128, -1).transpose(1, 0, 2))
    w["ab_win"] = tile_k(f(inputs["ab_w_in"])[0])
    w["ab_wout"] = tile_k(f(inputs["ab_w_out"])[0])
    wg = np.zeros((2, 32, 256), np.float32)
    wg[0, :16] = f(inputs["gla_w_gate_fwd"])[0]
    wg[0, 16] = f(inputs["gla_b_gate_fwd"])[0]
    wg[1, :16] = f(inputs["gla_w_gate_bwd"])[0]
    wg[1, 16] = f(inputs["gla_b_gate_bwd"])[0]
    w["gla_wg"] = wg
    w["sgu_wsT"] = np.ascontiguousarray(f(inputs["sgu_w_s"])[0].transpose(2, 0, 1))
    w["sgu_ln"] = np.stack([f(inputs["sgu_ln_g"])[0], f(inputs["sgu_ln_b"])[0]])
    w["sgu_bs"] = np.ascontiguousarray(f(inputs["sgu_b_s"])[0].reshape(1, 512))
    gw = f(inputs["gdn_w_in"])[0]
    w["gdn_win"] = np.ascontiguousarray(gw[:, :4096].reshape(8, 128, 32, 128).transpose(2, 1, 0, 3))
    w["gdn_wsm"] = tile_k(gw[:, 4096:4128])
    w["gdn_wout"] = tile_k(f(inputs["gdn_w_out"])[0])
    w["gdn_vec"] = np.concatenate([f(inputs["gdn_dt_bias_fwd"])[0], f(inputs["gdn_dt_bias_bwd"])[0],
                                   f(inputs["gdn_a_log_fwd"])[0], f(inputs["gdn_a_log_bwd"])[0]]).reshape(1, 32)
    return w


def make_consts():
    c = np.zeros((128, NCONST * 128), np.float32)
    j = np.arange(128)[:, None]
    i = np.arange(128)[None, :]
    c[:, 0:128] = np.eye(128)
    c[:, 128:256] = (j <= i)
    c[:, 256:384] = (j >= i)
    c[:, 384:512] = (j <= i) * (-1.0 / 16)
    c[:, 512:640] = (j >= i) * (-1.0 / 16)
    big = 30000.0
    c[:, 640:768] = ((j <= i) - 1.0) * big
    c[:, 768:896] = ((j < i) - 1.0) * big
    c[:, 896:1024] = ((j >= i) - 1.0) * big
    c[:, 1024:1152] = ((j > i) - 1.0) * big
    c[127, 1152:1280] = 1.0
    c[0, 1280:1408] = 1.0
    c[:, 1408:1536] = 1.0
    return c


_NC_CACHE = {}


def make_in_map(inputs, c, nseq):
    x = np.asarray(inputs["x"], np.float32)
    m = {"x": np.ascontiguousarray(x[c * nseq:(c + 1) * nseq]), "vecs": make_vecs(inputs),
         "consts": make_consts()}
    m.update(make_weights(inputs))
    return m


def kernel(**inputs):
    B = np.asarray(inputs["x"]).shape[0]
    nseq = B // N_CORES
    if nseq not in _NC_CACHE:
        _NC_CACHE[nseq] = build_program(nseq)
    nc = _NC_CACHE[nseq]
    in_maps = [make_in_map(inputs, c, nseq) for c in range(N_CORES)]
    res = run_bass_kernel_spmd(nc, in_maps, core_ids=list(range(N_CORES)))
    return np.concatenate([r["out"] for r in res.results], axis=0)
```
